# Optimizing a Trainium2 kernel written in Bass

```python
import math
import jax, jax.numpy as jnp
from jax import lax
import numpy as np

D_MODEL = 1024
BATCH = 32
SEQ = 2048
DEPTH = 4
DEC_BATCH = 8
DEC_SEQ = 2048
PAST_LEN = 128

GRID_W = 64
HEAD_DIM = 64
N_MIXERS = 4
GROUP_W = D_MODEL // N_MIXERS
N_HEADS_G = GROUP_W // HEAD_DIM
N_KV = 2
REP = N_HEADS_G // N_KV
KV_W = N_KV * HEAD_DIM
D_MIX = N_MIXERS * GROUP_W
RWKV_DECAY_RANK = 64
RWKV_A_RANK = 64
RWKV_SHIFT_W = 3 * GROUP_W + RWKV_DECAY_RANK + RWKV_A_RANK
LRU_C = 8.0
LRU_BLOCKS = 4
LRU_BLK = GROUP_W // LRU_BLOCKS
CONV_W = 4
CONV_LEFT = 2
Q_BLOCK = 128
WINDOW = 128
ROPE_THETA = 10000.0
NORM_EPS = 1e-6
GN_EPS = 64e-5
NEG = -1e30
A_W = RWKV_SHIFT_W + GROUP_W
B_W = GROUP_W + 2 * KV_W + GROUP_W
C_W = 2 * GROUP_W
D_W = GROUP_W + 2 * KV_W + GROUP_W
D_IN = A_W + B_W + C_W + D_W

kernel_name = "hybrid_bidir_parallel_heads_encoder"


def _rmsnorm(x, g, eps=NORM_EPS):
    xf = x.astype(jnp.float32)
    return xf * lax.rsqrt(jnp.mean(xf * xf, -1, keepdims=True) + eps) * g.astype(jnp.float32)


def _rwkv_scan(r, w, k, v, kk, a, reverse):
    xs = tuple(jnp.moveaxis(t, 1, 0) for t in (r, w, k, v, kk, a))
    bsz, nh, n = r.shape[0], r.shape[2], r.shape[3]

    def step(S, inp):
        rt, wt, kt, vt, kkt, at = inp
        sk = jnp.einsum('bhij,bhj->bhi', S, kkt)
        S = S * wt[:, :, None, :] - sk[..., None] * (kkt * at)[:, :, None, :] + vt[..., None] * kt[:, :, None, :]
        return S, jnp.einsum('bhij,bhj->bhi', S, rt)

    S0 = jnp.zeros((bsz, nh, n, n), jnp.float32)
    _, y = lax.scan(step, S0, xs, reverse=reverse)
    return jnp.moveaxis(y, 0, 1)


def _rwkv_mixer(z, shift, w0, w_up, a0, a_up, k_k, k_a, r_k, ln_g, ln_b):
    b_, t_, _ = z.shape
    xs = z[..., :RWKV_SHIFT_W]
    gate = z[..., RWKV_SHIFT_W:]
    prev = jnp.pad(xs, ((0, 0), (1, 0), (0, 0)))[:, :t_]
    nxt = jnp.pad(xs, ((0, 0), (0, 1), (0, 0)))[:, 1:]
    xs = xs + shift[0] * (prev - xs) + shift[1] * (nxt - xs)
    r = xs[..., :GROUP_W]
    k = xs[..., GROUP_W:2 * GROUP_W]
    v = xs[..., 2 * GROUP_W:3 * GROUP_W]
    wl = jnp.tanh(xs[..., 3 * GROUP_W:3 * GROUP_W + RWKV_DECAY_RANK])
    al = xs[..., 3 * GROUP_W + RWKV_DECAY_RANK:]
    hd = lambda t: t.reshape(b_, t_, N_HEADS_G, HEAD_DIM)
    kk = hd(k * k_k)
    kk = kk / jnp.maximum(jnp.sqrt(jnp.sum(kk * kk, -1, keepdims=True)), 1e-12)
    y = jnp.zeros((b_, t_, N_HEADS_G, HEAD_DIM), jnp.float32)
    ksum = jnp.zeros_like(k)
    for d, rev in ((0, False), (1, True)):
        logw = -jnp.exp(-jax.nn.softplus(-(w0[d] + wl @ w_up[d])) - 0.5)
        a = jax.nn.sigmoid(a0[d] + al @ a_up[d])
        kd = k * (1.0 + (a - 1.0) * k_a)
        y = y + _rwkv_scan(hd(r), hd(jnp.exp(logw)), hd(kd), hd(v), kk, hd(a), rev)
        ksum = ksum + kd
    mu = jnp.mean(y, -1, keepdims=True)
    var = jnp.mean(jnp.square(y - mu), -1, keepdims=True)
    y = ((y - mu) * lax.rsqrt(var + GN_EPS)).reshape(b_, t_, GROUP_W) * ln_g + ln_b
    bonus = jnp.sum(hd(r) * hd(ksum) * r_k, -1, keepdims=True) * hd(v)
    y = y + bonus.reshape(b_, t_, GROUP_W)
    return y * jax.nn.silu(gate)


def _axial_angles(t_):
    rows = t_ // GRID_W
    row = jnp.repeat(jnp.arange(rows), GRID_W).astype(jnp.float32)
    col = jnp.tile(jnp.arange(GRID_W), rows).astype(jnp.float32)
    half = HEAD_DIM // 2
    inv = ROPE_THETA ** (-jnp.arange(0, half, 2, dtype=jnp.float32) / half)
    return row[:, None] * inv, col[:, None] * inv


def _rope_half(x, ang):
    n = x.shape[-1] // 2
    c = jnp.cos(ang)[:, None, :]
    s = jnp.sin(ang)[:, None, :]
    x1, x2 = x[..., :n], x[..., n:]
    return jnp.concatenate([x1 * c - x2 * s, x1 * s + x2 * c], -1)


def _axial_rope(x, ang_r, ang_c):
    h = HEAD_DIM // 2
    return jnp.concatenate([_rope_half(x[..., :h], ang_r), _rope_half(x[..., h:], ang_c)], -1)


def _global_attn_mixer(z, q_norm, k_norm):
    b_, t_, _ = z.shape
    q = z[..., :GROUP_W].reshape(b_, t_, N_HEADS_G, HEAD_DIM)
    k = z[..., GROUP_W:GROUP_W + KV_W].reshape(b_, t_, N_KV, HEAD_DIM)
    v = z[..., GROUP_W + KV_W:GROUP_W + 2 * KV_W].reshape(b_, t_, N_KV, HEAD_DIM)
    gate = z[..., GROUP_W + 2 * KV_W:]
    ang_r, ang_c = _axial_angles(t_)
    q = _axial_rope(_rmsnorm(q, q_norm), ang_r, ang_c)
    k = _axial_rope(_rmsnorm(k, k_norm), ang_r, ang_c)
    scale = HEAD_DIM ** -0.5
    nb = t_ // Q_BLOCK
    qb = q.reshape(b_, nb, Q_BLOCK, N_KV, REP, HEAD_DIM).transpose(1, 0, 2, 3, 4, 5)

    def block(qi):
        s = jnp.einsum('bqgrd,bkgd->bgrqk', qi, k) * scale
        p = jax.nn.softmax(s, axis=-1)
        return jnp.einsum('bgrqk,bkgd->bqgrd', p, v)

    o = lax.map(block, qb)
    o = o.transpose(1, 0, 2, 3, 4, 5).reshape(b_, t_, GROUP_W)
    return o * jax.nn.silu(gate)


def _lin_comb(e1, e2):
    a1, b1 = e1
    a2, b2 = e2
    return a1 * a2, a2 * b1 + b2


def _rglru_mixer(z, conv_w, conv_b, gate_w, gate_b, lam):
    b_, t_, _ = z.shape
    xb = z[..., :GROUP_W]
    gate = z[..., GROUP_W:]
    xp = jnp.pad(xb, ((0, 0), (CONV_LEFT, CONV_W - 1 - CONV_LEFT), (0, 0)))
    xc = conv_b + sum(conv_w[j] * xp[:, j:j + t_] for j in range(CONV_W))
    xh = xc.reshape(b_, t_, LRU_BLOCKS, LRU_BLK)
    h = jnp.zeros_like(xc)
    for d, rev in ((0, False), (1, True)):
        g = jnp.einsum('btnd,knde->kbtne', xh, gate_w[d]).reshape(2, b_, t_, GROUP_W) + gate_b[d][:, None, None, :]
        r = jax.nn.sigmoid(g[0])
        i = jax.nn.sigmoid(g[1])
        log_a = -LRU_C * r * jax.nn.softplus(-lam[d])
        a = jnp.exp(log_a)
        bterm = jnp.sqrt(-jnp.expm1(2.0 * log_a)) * (i * xc)
        _, hd = lax.associative_scan(_lin_comb, (a, bterm), axis=1, reverse=rev)
        h = h + hd
    return h * jax.nn.silu(gate)


def _window_attn_mixer(z, sink):
    b_, t_, _ = z.shape
    q = z[..., :GROUP_W].reshape(b_, t_, N_KV, REP, HEAD_DIM)
    k = z[..., GROUP_W:GROUP_W + KV_W].reshape(b_, t_, N_KV, HEAD_DIM)
    v = z[..., GROUP_W + KV_W:GROUP_W + 2 * KV_W].reshape(b_, t_, N_KV, HEAD_DIM)
    gate = z[..., GROUP_W + 2 * KV_W:]
    nb = t_ // Q_BLOCK
    pad = ((0, 0), (WINDOW, WINDOW), (0, 0), (0, 0))
    kp = jnp.pad(k, pad).reshape(b_, nb + 2, Q_BLOCK, N_KV, HEAD_DIM)
    vp = jnp.pad(v, pad).reshape(b_, nb + 2, Q_BLOCK, N_KV, HEAD_DIM)
    kw = jnp.concatenate([kp[:, :-2], kp[:, 1:-1], kp[:, 2:]], axis=2)
    vw = jnp.concatenate([vp[:, :-2], vp[:, 1:-1], vp[:, 2:]], axis=2)
    qb = q.reshape(b_, nb, Q_BLOCK, N_KV, REP, HEAD_DIM)
    s = jnp.einsum('bnqgrd,bnkgd->bngrqk', qb, kw) * (HEAD_DIM ** -0.5)
    qpos = jnp.arange(nb)[:, None] * Q_BLOCK + jnp.arange(Q_BLOCK)[None]
    kpos = jnp.arange(nb)[:, None] * Q_BLOCK - WINDOW + jnp.arange(3 * Q_BLOCK)[None]
    dist = jnp.abs(kpos[:, None, :] - qpos[:, :, None])
    valid = (dist <= WINDOW) & (kpos >= 0)[:, None, :] & (kpos < t_)[:, None, :]
    slopes = jnp.exp2(-8.0 * jnp.arange(1, N_HEADS_G + 1, dtype=jnp.float32) / N_HEADS_G).reshape(N_KV, REP)
    bias = -slopes[None, :, :, None, None] * dist[:, None, None].astype(jnp.float32)
    s = jnp.where(valid[None, :, None, None], s + bias[None], NEG)
    sink_l = sink.astype(jnp.float32).reshape(N_KV, REP)[None, None, :, :, None, None]
    m = jnp.maximum(jnp.max(s, -1, keepdims=True), sink_l)
    p = jnp.exp(s - m)
    p = p / (jnp.sum(p, -1, keepdims=True) + jnp.exp(sink_l - m))
    o = jnp.einsum('bngrqk,bnkgd->bnqgrd', p, vw).reshape(b_, t_, GROUP_W)
    return o * jax.nn.silu(gate)


def _trunk(x, p):
    for l in range(DEPTH):
        h = _rmsnorm(x, p['norm_g'][l])
        zz = h @ p['w_in'][l].astype(jnp.float32)
        zA = zz[..., :A_W]
        zB = zz[..., A_W:A_W + B_W]
        zC = zz[..., A_W + B_W:A_W + B_W + C_W]
        zD = zz[..., A_W + B_W + C_W:]
        oA = _rwkv_mixer(zA, p['rwkv_shift'][l], p['rwkv_w0'][l], p['rwkv_w_up'][l], p['rwkv_a0'][l],
                         p['rwkv_a_up'][l], p['rwkv_k_k'][l], p['rwkv_k_a'][l], p['rwkv_r_k'][l],
                         p['rwkv_ln_g'][l], p['rwkv_ln_b'][l])
        oB = _global_attn_mixer(zB, p['attn_q_norm'][l], p['attn_k_norm'][l])
        oC = _rglru_mixer(zC, p['lru_conv_w'][l], p['lru_conv_b'][l], p['lru_gate_w'][l],
                          p['lru_gate_b'][l], p['lru_lambda'][l])
        oD = _window_attn_mixer(zD, p['swa_sink'][l])
        o = jnp.concatenate([oA, oB, oC, oD], -1) @ p['w_out'][l].astype(jnp.float32)
        x = x + o.astype(x.dtype)
    return _rmsnorm(x, p['final_g']).astype(x.dtype)


def setup_inputs(seed: int = 0) -> dict:
    key = jax.random.key(seed)
    ks = jax.random.split(key, 24)
    L, G = DEPTH, GROUP_W
    nrm = lambda k, s: jax.random.normal(k, s, jnp.float32)
    a_target = jax.random.uniform(ks[20], (L, 2, G), jnp.float32, 0.9, 0.999)
    a_base = a_target ** (1.0 / LRU_C)
    lam = jnp.log(a_base) - jnp.log1p(-a_base)
    return {
        "x_prompt": nrm(ks[0], (BATCH, SEQ, D_MODEL)),
        "x_sample": nrm(ks[1], (DEC_BATCH, DEC_SEQ, D_MODEL)),
        "norm_g": 1.0 + 0.02 * nrm(ks[2], (L, D_MODEL)),
        "w_in": nrm(ks[3], (L, D_MODEL, D_IN)) * D_MODEL ** -0.5,
        "w_out": nrm(ks[4], (L, D_MIX, D_MODEL)) * D_MIX ** -0.5,
        "rwkv_shift": jax.random.uniform(ks[5], (L, 2, RWKV_SHIFT_W), jnp.float32, 0.0, 0.5),
        "rwkv_w0": jax.random.uniform(ks[6], (L, 2, G), jnp.float32, -6.0, 1.0),
        "rwkv_w_up": 0.1 * nrm(ks[7], (L, 2, RWKV_DECAY_RANK, G)),
        "rwkv_a0": 0.5 * nrm(ks[8], (L, 2, G)),
        "rwkv_a_up": 0.5 * RWKV_A_RANK ** -0.5 * nrm(ks[9], (L, 2, RWKV_A_RANK, G)),
        "rwkv_k_k": 0.85 + 0.02 * nrm(ks[10], (L, G)),
        "rwkv_k_a": 1.0 + 0.02 * nrm(ks[11], (L, G)),
        "rwkv_r_k": 0.1 * nrm(ks[12], (L, N_HEADS_G, HEAD_DIM)),
        "rwkv_ln_g": 1.0 + 0.02 * nrm(ks[13], (L, G)),
        "rwkv_ln_b": 0.02 * nrm(ks[14], (L, G)),
        "attn_q_norm": 1.0 + 0.02 * nrm(ks[15], (L, HEAD_DIM)),
        "attn_k_norm": 1.0 + 0.02 * nrm(ks[16], (L, HEAD_DIM)),
        "lru_conv_w": CONV_W ** -0.5 * nrm(ks[17], (L, CONV_W, G)),
        "lru_conv_b": 0.02 * nrm(ks[18], (L, G)),
        "lru_gate_w": LRU_BLK ** -0.5 * nrm(ks[19], (L, 2, 2, LRU_BLOCKS, LRU_BLK, LRU_BLK)),
        "lru_gate_b": 0.02 * nrm(ks[21], (L, 2, 2, G)),
        "lru_lambda": lam,
        "swa_sink": nrm(ks[22], (L, N_HEADS_G)),
        "final_g": 1.0 + 0.02 * nrm(ks[23], (D_MODEL,)),
    }


def reference(x_prompt, x_sample, norm_g, w_in, w_out, rwkv_shift, rwkv_w0, rwkv_w_up, rwkv_a0,
              rwkv_a_up, rwkv_k_k, rwkv_k_a, rwkv_r_k, rwkv_ln_g, rwkv_ln_b, attn_q_norm, attn_k_norm,
              lru_conv_w, lru_conv_b, lru_gate_w, lru_gate_b, lru_lambda, swa_sink, final_g):
    p = dict(norm_g=norm_g, w_in=w_in, w_out=w_out, rwkv_shift=rwkv_shift, rwkv_w0=rwkv_w0,
             rwkv_w_up=rwkv_w_up, rwkv_a0=rwkv_a0, rwkv_a_up=rwkv_a_up, rwkv_k_k=rwkv_k_k,
             rwkv_k_a=rwkv_k_a, rwkv_r_k=rwkv_r_k, rwkv_ln_g=rwkv_ln_g, rwkv_ln_b=rwkv_ln_b,
             attn_q_norm=attn_q_norm, attn_k_norm=attn_k_norm, lru_conv_w=lru_conv_w,
             lru_conv_b=lru_conv_b, lru_gate_w=lru_gate_w, lru_gate_b=lru_gate_b,
             lru_lambda=lru_lambda, swa_sink=swa_sink, final_g=final_g)
    y_prompt = _trunk(x_prompt, p)
    y_sample = _trunk(x_sample, p)
    return (y_prompt, y_sample)
```

```python
import math
from contextlib import ExitStack
import numpy as np
import ml_dtypes
import concourse.bass as bass
import concourse.mybir as mybir
from concourse.bass_utils import run_bass_kernel_spmd

F32 = mybir.dt.float32
BF16 = mybir.dt.bfloat16
AF = mybir.ActivationFunctionType
ALU = mybir.AluOpType

D_MODEL = 1024
GRID_W = 64
D_IN = 3200
A_W, B_W, C_W, D_W = 1152, 768, 512, 768
OFF_A, OFF_B, OFF_C, OFF_D = 0, 1152, 1920, 2432
NORM_EPS = 1e-6
GN_EPS = 64e-5
LAMW = math.exp(-0.5)
SEM_LIMIT = 30000


class Cfg:
    def __init__(self, T=2048, NSEQ=5, DEPTH=4, MIX="ABCD", NCORES=8, taps=()):
        self.T, self.NSEQ, self.DEPTH, self.MIX, self.NCORES = T, NSEQ, DEPTH, MIX, NCORES
        self.taps = tuple(taps)


PV_FIELDS = [("norm_g", 8), ("sh0", 7), ("sh1", 7), ("w0", 4), ("a0", 4), ("k_k", 2), ("k_a", 2), ("r_k", 2),
             ("ln_g", 2), ("ln_b", 2), ("qn", 1), ("kn", 1), ("cw", 8), ("cb", 2), ("gb", 8), ("lam", 4),
             ("sink", 2), ("sinksw", 2)]
DV_FIELDS = [("c0", 7), ("omka", 2), ("cneg", 4), ("esinksw", 2), ("gbh", 8)]


def _offsets(fields):
    off, d = 0, {}
    for n, w in fields:
        d[n] = off
        off += w
    return d, off


PV_OFF, PV_W = _offsets(PV_FIELDS)
DV_OFF, DV_W = _offsets(DV_FIELDS)


def pv_col(L, name, l, i=0):
    if name == "final_g":
        return L * PV_W + i
    return l * PV_W + PV_OFF[name] + i


def dv_col(name, l, i=0):
    return l * DV_W + DV_OFF[name] + i


CST_FIELDS = [("ident", 128), ("blk64", 128), ("ones", 128), ("swap", 128), ("perm", 128), ("mfwd", 256),
              ("mbwd", 256), ("mfwdT", 64), ("mbwdT", 64), ("rst", 512)]
CST_OFF, CST_W = _offsets(CST_FIELDS)


def make_consts(T):
    c = np.zeros((128, CST_W), np.float32)
    alibi = np.zeros((128, 4 * 384), np.float32)
    p = np.arange(128)
    o = CST_OFF
    c[:, o["ident"]:o["ident"] + 128] = np.eye(128)
    c[:, o["blk64"]:o["blk64"] + 128] = (p[:, None] // 64 == p[None, :] // 64)
    c[:, o["ones"]:o["ones"] + 128] = 1.0
    c[:, o["swap"]:o["swap"] + 128] = (p[:, None] == (p[None, :] + 64) % 128)
    d = p % 64
    partner = np.where((d % 32) < 16, p + 16, p - 16)
    c[:, o["perm"]:o["perm"] + 128] = (p[:, None] == partner[None, :])
    s = (p % 64)[:, None]
    t = np.arange(64)[None, :]
    strict_f, incl_f = (s < t), (s <= t)
    strict_b, incl_b = (s > t), (s >= t)
    c[:, o["mfwd"]:o["mfwd"] + 256] = np.concatenate([strict_f, incl_f, strict_f, incl_f], 1)
    c[:, o["mbwd"]:o["mbwd"] + 256] = np.concatenate([strict_b, incl_b, strict_b, incl_b], 1)
    c[:, o["mfwdT"]:o["mfwdT"] + 64] = (t < s)
    c[:, o["mbwdT"]:o["mbwdT"] + 64] = (t > s)
    c[:, o["rst"]:o["rst"] + 512] = (np.arange(512)[None, :] % 64 != 0)
    cc = np.arange(384)[None, :]
    dist = np.abs(cc - 128 - p[:, None]).astype(np.float64)
    for h in range(4):
        slope = 2.0 ** (-8.0 * (h + 1) / 4)
        e = np.where(dist <= 128, np.exp(-slope * dist), 0.0)
        alibi[:, h * 384:(h + 1) * 384] = e
    row = (np.arange(T) // GRID_W).astype(np.float64)
    col = (np.arange(T) % GRID_W).astype(np.float64)
    inv = 10000.0 ** (-np.arange(0, 32, 2, dtype=np.float64) / 32)
    C = np.zeros((128, T), np.float32)
    S = np.zeros((128, T), np.float32)
    for pp in range(128):
        dd = pp % 64
        pos = row if dd < 32 else col
        f = inv[dd % 16]
        ang = pos * f
        C[pp] = np.cos(ang)
        S[pp] = (-np.sin(ang)) if (dd % 32) < 16 else np.sin(ang)
    return c, alibi, C, S


def pack_params(inp, L):
    pv = np.zeros((128, L * PV_W + 8), np.float32)

    def put(name, l, i, vec128):
        pv[:, pv_col(L, name, l, i)] = vec128

    for l in range(L):
        for i in range(8):
            put("norm_g", l, i, inp["norm_g"][l, i * 128:(i + 1) * 128])
        for i in range(7):
            put("sh0", l, i, inp["rwkv_shift"][l, 0, i * 128:(i + 1) * 128])
            put("sh1", l, i, inp["rwkv_shift"][l, 1, i * 128:(i + 1) * 128])
        for d in range(2):
            for hp in range(2):
                put("w0", l, d * 2 + hp, inp["rwkv_w0"][l, d, hp * 128:(hp + 1) * 128])
                put("a0", l, d * 2 + hp, inp["rwkv_a0"][l, d, hp * 128:(hp + 1) * 128])
        rk = inp["rwkv_r_k"][l].reshape(256)
        for hp in range(2):
            sl = slice(hp * 128, (hp + 1) * 128)
            put("k_k", l, hp, inp["rwkv_k_k"][l, sl])
            put("k_a", l, hp, inp["rwkv_k_a"][l, sl])
            put("r_k", l, hp, rk[sl])
            put("ln_g", l, hp, inp["rwkv_ln_g"][l, sl])
            put("ln_b", l, hp, inp["rwkv_ln_b"][l, sl])
        put("qn", l, 0, np.tile(inp["attn_q_norm"][l], 2))
        put("kn", l, 0, np.tile(inp["attn_k_norm"][l], 2))
        for j in range(4):
            for cc in range(2):
                put("cw", l, j * 2 + cc, inp["lru_conv_w"][l, j, cc * 128:(cc + 1) * 128])
        for cc in range(2):
            put("cb", l, cc, inp["lru_conv_b"][l, cc * 128:(cc + 1) * 128])
        for d in range(2):
            for k in range(2):
                for cc in range(2):
                    put("gb", l, (d * 2 + k) * 2 + cc, inp["lru_gate_b"][l, d, k, cc * 128:(cc + 1) * 128])
            for cc in range(2):
                put("lam", l, d * 2 + cc, inp["lru_lambda"][l, d, cc * 128:(cc + 1) * 128])
        sk = inp["swa_sink"][l]
        for cc in range(2):
            put("sink", l, cc, np.repeat(sk[2 * cc:2 * cc + 2], 64))
            put("sinksw", l, cc, np.repeat(sk[2 * cc:2 * cc + 2][::-1], 64))
    for i in range(8):
        pv[:, L * PV_W + i] = inp["final_g"][i * 128:(i + 1) * 128]
    upw = np.zeros((128, L * 2 * 256), np.float32)
    for l in range(L):
        for d in range(2):
            upw[0:64, (l * 2 + d) * 256:(l * 2 + d + 1) * 256] = inp["rwkv_w_up"][l, d]
            upw[64:128, (l * 2 + d) * 256:(l * 2 + d + 1) * 256] = inp["rwkv_a_up"][l, d]
    gw = np.zeros((128, L * 8 * 128), np.float32)
    for l in range(L):
        for d in range(2):
            for k in range(2):
                for cc in range(2):
                    base = (((l * 2 + d) * 2 + k) * 2 + cc) * 128
                    for b in range(2):
                        gw[b * 64:(b + 1) * 64, base + b * 64:base + (b + 1) * 64] = inp["lru_gate_w"][l, d, k, 2 * cc + b]
    return pv, upw, gw


class SemCounter:
    def __init__(self, bld, name, step):
        self.bld, self.name, self.step = bld, name, step
        self.n = 0
        self._new()

    def _new(self):
        self.sem = self.bld.es.enter_context(self.bld.nc.semaphore(f"{self.name}_{self.n}"))
        self.sname = f"{self.name}_{self.n}"
        self.n += 1
        self.val = 0

    def next(self):
        if self.val + self.step > SEM_LIMIT:
            self._new()
        self.val += self.step
        return (self.sname, self.sem, self.val)


class Builder:
    def __init__(self, nc, es):
        self.nc, self.es = nc, es
        self.eng = {"pe": nc.tensor, "act": nc.scalar, "dve": nc.vector, "pool": nc.gpsimd, "sp": nc.sync}
        self.cnt = {e: SemCounter(self, e, 1) for e in ("pe", "act", "dve", "pool")}
        self.dcnt = {}
        self.waited = {e: {} for e in self.eng}
        self.last_w = {}
        self.readers = {}
        self.pending = {e: ([], []) for e in self.eng}
        self.n_ins = 0

    def _deps(self, reads, writes):
        toks = []
        for k in reads:
            t = self.last_w.get(k)
            if t is not None:
                toks.append(t)
        for k in writes:
            t = self.last_w.get(k)
            if t is not None:
                toks.append(t)
            r = self.readers.get(k)
            if r:
                toks.extend(r.values())
        return toks

    def _wait(self, e, toks):
        w = self.waited[e]
        need = {}
        for (sn, sem, val) in toks:
            if w.get(sn, 0) < val and need.get(sn, (None, 0))[1] < val:
                need[sn] = (sem, val)
        for sn, (sem, val) in need.items():
            self.eng[e].wait_ge(sem, val)
            w[sn] = val

    def _commit(self, tok, reads, writes):
        for k in writes:
            self.last_w[k] = tok
            self.readers[k] = {}
        for k in reads:
            self.readers.setdefault(k, {})[tok[0]] = tok

    def op(self, e, fn, reads=(), writes=(), inc=True):
        self._wait(e, self._deps(reads, writes))
        ins = fn(self.eng[e])
        self.n_ins += 1
        pr, pw = self.pending[e]
        if not inc:
            pr.extend(reads)
            pw.extend(writes)
            return
        tok = self.cnt[e].next()
        ins.then_inc(tok[1], 1)
        self._commit(tok, list(reads) + pr, list(writes) + pw)
        self.pending[e] = ([], [])

    def dma(self, out, in_, reads=(), writes=(), q="sp", semkey=None):
        self._wait(q, self._deps(reads, writes))
        ins = self.eng[q].dma_start(out=out, in_=in_)
        self.n_ins += 1
        if semkey not in self.dcnt:
            self.dcnt[semkey] = SemCounter(self, "d" + semkey, 16)
        tok = self.dcnt[semkey].next()
        ins.then_inc(tok[1], 16)
        self._commit(tok, reads, writes)
        return tok

    def barrier(self):
        toks = []
        for c in list(self.cnt.values()) + list(self.dcnt.values()):
            if c.val > 0:
                toks.append((c.sname, c.sem, c.val))
        for e in self.eng:
            self._wait(e, toks)

    def final_wait(self, e, keys):
        toks = []
        for k in keys:
            t = self.last_w.get(k)
            if t is not None:
                toks.append(t)
        self._wait(e, toks)


SCRW = 2048
NSCR = 8


def build(cfg):
    T, NSEQ, L = cfg.T, cfg.NSEQ, cfg.DEPTH
    NSEG, NT, NCH = T // 512, T // 128, T // 64
    assert T % 512 == 0 and T <= 2048
    nc = bass.Bass("TRN2", target_bir_lowering=False)

    def din(name, shape, dt=F32):
        return nc.dram_tensor(name, list(shape), dt, kind="ExternalInput").ap()

    x_d = din("x", [NSEQ * 1024, T])
    win_d = din("w_in", [L * 1024, D_IN])
    wout_d = din("w_out", [L * 1024, 1024])
    pv_d = din("pvec", [128, L * PV_W + 8])
    upw_d = din("upw", [128, L * 512])
    gw_d = din("gw", [128, L * 1024])
    cst_d = din("cst", [128, CST_W])
    alibi_d = din("alibi", [128, 1536])
    ropeC_d = din("ropeC", [128, T])
    ropeS_d = din("ropeS", [128, T])
    y_d = nc.dram_tensor("y", [NSEQ * 1024, T], F32, kind="ExternalOutput").ap()
    tap_d = {}
    for (tname, tshape) in cfg.taps:
        tap_d[tname] = nc.dram_tensor("tap_" + tname, list(tshape), F32, kind="ExternalOutput").ap()

    es = ExitStack()
    with es:
        B = Builder(nc, es)

        def sb(name, shape, dt):
            return es.enter_context(nc.sbuf_tensor(name, list(shape), dt))

        xT = sb("xT", [128, 8, T], F32)
        hT = sb("hT", [128, 8, max(T, 2048)], BF16)
        pvt = sb("pvt", [128, L * PV_W + 8], F32)
        dvt = sb("dvt", [128, L * DV_W], F32)
        CBW = CST_W
        cbf = sb("cbf", [128, CBW], BF16)
        swapf = sb("swapf", [128, 128], F32)
        upw_f = sb("upw_f", [128, 512], F32)
        upw_b = sb("upw_b", [128, 512], BF16)
        gw_b = sb("gw_b", [128, 1024], BF16)
        NWS = 3
        wst = [sb(f"wst{i}", [128, 8, 128], F32) for i in range(2)]
        wbf = [sb(f"wbf{i}", [128, 8, 128], BF16) for i in range(NWS)]
        wost = [sb(f"wost{i}", [128, 256], F32) for i in range(2)]
        wobf = [sb(f"wobf{i}", [128, 256], BF16) for i in range(2)]
        obuf = sb("obuf", [128, 2, T], BF16)
        sqb = sb("sqb", [128, 2, 512], BF16)
        rstd = sb("rstd", [128, 512], F32)
        epsb = sb("epsb", [128, 4], F32)
        tmpA = sb("tmpA", [128, 512], F32)
        tmpB = sb("tmpB", [128, 512], F32)
        scr = [sb(f"scr{i}", [128, SCRW], F32) for i in range(NSCR)]
        psum = es.enter_context(nc.psum_tensor("psum", [128, 8 * 512], F32))

        def sk(i):
            return ("scr", i)

        def sbf(i):
            return scr[i][:].bitcast(BF16)

        def PS(b, n=1):
            return psum[:, b * 512:(b + n) * 512]

        def psk(b, n=1):
            return [("ps", b + i) for i in range(n)]

        def cb(name, w=None, off=0):
            o = CST_OFF[name] + off
            return cbf[:, o:o + (w if w is not None else dict(CST_FIELDS)[name])]

        def pvc(name, l, i=0):
            c = pv_col(L, name, l, i)
            return pvt[:, c:c + 1]

        def dvc(name, l, i=0):
            c = dv_col(name, l, i)
            return dvt[:, c:c + 1]

        def tap(name, ap_sb, rkeys, rows=128):
            if name in tap_d:
                B.dma(tap_d[name], ap_sb, reads=rkeys, writes=[("tap", name)], q="pool", semkey="tap" + name)

        B.dma(scr[0][:, 0:CST_W], cst_d[:, :], writes=[sk(0)], q="pool", semkey="cst")
        B.dma(pvt[:], pv_d[:, :], writes=["pvt"], q="pool", semkey="pvt")
        B.op("dve", lambda e: e.tensor_copy(out=cbf[:], in_=scr[0][:, 0:CST_W]), reads=[sk(0)], writes=["cbf"])
        B.op("dve", lambda e: e.tensor_copy(out=swapf[:], in_=scr[0][:, CST_OFF["swap"]:CST_OFF["swap"] + 128]), reads=[sk(0)], writes=["swapf"])
        B.op("dve", lambda e: e.memset(epsb[:, 0:1], NORM_EPS), writes=["epsb"])
        B.op("dve", lambda e: e.memset(epsb[:, 1:2], 1.0), writes=["epsb"])
        B.op("dve", lambda e: e.memset(epsb[:, 2:3], GN_EPS), writes=["epsb"])
        B.op("dve", lambda e: e.memset(epsb[:, 3:4], 0.0), writes=["epsb"])
        for l in range(L):
            for i in range(7):
                B.op("dve", lambda e, l=l, i=i: e.tensor_tensor(out=dvc("c0", l, i), in0=pvc("sh0", l, i), in1=pvc("sh1", l, i), op=ALU.add),
                     reads=["pvt"], writes=["dvt"])
                B.op("dve", lambda e, l=l, i=i: e.tensor_scalar(out=dvc("c0", l, i), in0=dvc("c0", l, i), scalar1=-1.0, scalar2=1.0, op0=ALU.mult, op1=ALU.add),
                     reads=["dvt"], writes=["dvt"])
            for i in range(2):
                B.op("dve", lambda e, l=l, i=i: e.tensor_scalar(out=dvc("omka", l, i), in0=pvc("k_a", l, i), scalar1=-1.0, scalar2=1.0, op0=ALU.mult, op1=ALU.add),
                     reads=["pvt"], writes=["dvt"])
                B.op("act", lambda e, l=l, i=i: e.activation(out=dvc("esinksw", l, i), in_=pvc("sinksw", l, i), func=AF.Exp),
                     reads=["pvt"], writes=["dvt"])
            for i in range(4):
                B.op("act", lambda e, l=l, i=i: e.activation(out=dvc("cneg", l, i), in_=pvc("lam", l, i), func=AF.Exp, scale=-1.0),
                     reads=["pvt"], writes=["dvt"])
                B.op("act", lambda e, l=l, i=i: e.activation(out=dvc("cneg", l, i), in_=dvc("cneg", l, i), func=AF.Ln, bias=epsb[:, 1:2]),
                     reads=["dvt", "epsb"], writes=["dvt"])
                B.op("dve", lambda e, l=l, i=i: e.tensor_scalar(out=dvc("cneg", l, i), in0=dvc("cneg", l, i), scalar1=-8.0, scalar2=None, op0=ALU.mult),
                     reads=["dvt"], writes=["dvt"])

        wslot = [0]
        wstslot = [0]

        def load_win(l, ranges):
            si = wstslot[0] % 2
            wstslot[0] += 1
            bi = wslot[0] % NWS
            wslot[0] += 1
            off = 0
            src = win_d[l * 1024:(l + 1) * 1024, :].rearrange("(k p) c -> p k c", p=128)
            for (c0, n) in ranges:
                B.dma(wst[si][:, :, off:off + n], src[:, :, c0:c0 + n], writes=[f"wst{si}"], q="sp", semkey=f"wst{si}")
                off += n
            B.op("pool", lambda e: e.tensor_copy(out=wbf[bi][:, :, 0:off], in_=wst[si][:, :, 0:off]),
                 reads=[f"wst{si}"], writes=[f"wbf{bi}"])
            return bi, off

        def inproj(l, ranges, bank0):
            bi, m = load_win(l, ranges)
            for sg in range(NSEG):
                for k in range(8):
                    B.op("pe", lambda e, k=k, sg=sg: e.matmul(PS(bank0 + sg)[0:m, :], lhsT=wbf[bi][:, k, 0:m], rhs=hT[:, k, sg * 512:(sg + 1) * 512],
                                                              start=(k == 0), stop=(k == 7)),
                         reads=[f"wbf{bi}", "hT"], writes=psk(bank0 + sg), inc=(k == 7))
            return m

        def inproj_tok(l, ranges, dst_fn, post):
            bi, m = load_win(l, ranges)
            assert m == 64
            for tb in range((NT + 7) // 8):
                bank = 4 + tb % 2
                ntt = min(8, NT - tb * 8)
                for q in range(ntt):
                    tt = tb * 8 + q
                    for k in range(8):
                        B.op("pe", lambda e, k=k, tt=tt, q=q, bank=bank: e.matmul(PS(bank)[:, q * 64:(q + 1) * 64], lhsT=hT[:, k, tt * 128:(tt + 1) * 128],
                                                                                  rhs=wbf[bi][:, k, 0:64], start=(k == 0), stop=(k == 7)),
                             reads=[f"wbf{bi}", "hT"], writes=psk(bank), inc=(k == 7))
                post(tb, bank, ntt)

        wo_slot = [0]

        def outproj(l, g):
            for co in range(8):
                si = wo_slot[0] % 2
                wo_slot[0] += 1
                for kc in range(2):
                    r0 = l * 1024 + g * 256 + kc * 128
                    B.dma(wost[si][:, kc * 128:(kc + 1) * 128], wout_d[r0:r0 + 128, co * 128:(co + 1) * 128],
                          writes=[f"wost{si}"], q="sp", semkey=f"wost{si}")
                B.op("pool", lambda e, si=si: e.tensor_copy(out=wobf[si][:], in_=wost[si][:]),
                     reads=[f"wost{si}"], writes=[f"wobf{si}"])
                nb = min(2, NSEG)
                for half in range(NSEG // nb):
                    pb = 4 + 2 * ((co * (NSEG // nb) + half) % 2)
                    for j in range(nb):
                        sg = half * nb + j
                        for kc in range(2):
                            B.op("pe", lambda e, si=si, kc=kc, sg=sg, j=j, pb=pb: e.matmul(
                                PS(pb + j), lhsT=wobf[si][:, kc * 128:(kc + 1) * 128], rhs=obuf[:, kc, sg * 512:(sg + 1) * 512],
                                start=(kc == 0), stop=(kc == 1)),
                                reads=[f"wobf{si}", "obuf"], writes=psk(pb + j), inc=(kc == 1))
                    t0 = half * nb * 512
                    w = nb * 512
                    B.op("dve", lambda e, co=co, pb=pb, t0=t0, w=w, nb=nb: e.tensor_tensor(
                        out=xT[:, co, t0:t0 + w], in0=PS(pb, nb), in1=xT[:, co, t0:t0 + w], op=ALU.add),
                        reads=psk(pb, nb) + [("xT", co)], writes=[("xT", co)])

        def rmsnorm_to(l, dest_kind):
            for sg in range(NSEG):
                sl = slice(sg * 512, (sg + 1) * 512)
                for c in range(8):
                    B.op("act", lambda e, c=c: e.activation(out=sqb[:, c % 2, :], in_=xT[:, c, sl], func=AF.Square),
                         reads=[("xT", c)], writes=[("sqb", c % 2)])
                    B.op("pe", lambda e, c=c: e.matmul(PS(0), lhsT=cb("ones"), rhs=sqb[:, c % 2, :], start=(c == 0), stop=(c == 7)),
                         reads=[("sqb", c % 2), "cbf"], writes=psk(0), inc=True)
                B.op("act", lambda e: e.activation(out=rstd[:], in_=PS(0), func=AF.Sqrt, scale=1.0 / 1024, bias=epsb[:, 0:1]),
                     reads=psk(0) + ["epsb"], writes=["rstd"])
                B.op("dve", lambda e: e.reciprocal(out=rstd[:], in_=rstd[:]), reads=["rstd"], writes=["rstd"])
                for c in range(8):
                    if dest_kind == "h":
                        gcol = pvc("norm_g", l, c)
                        B.op("dve", lambda e, c=c, gcol=gcol: e.scalar_tensor_tensor(out=hT[:, c, sl], in0=xT[:, c, sl], scalar=gcol, in1=rstd[:],
                                                                                      op0=ALU.mult, op1=ALU.mult),
                             reads=[("xT", c), "rstd", "pvt"], writes=["hT"])
                    else:
                        gcol = pvt[:, L * PV_W + c:L * PV_W + c + 1]
                        B.op("dve", lambda e, c=c, gcol=gcol: e.scalar_tensor_tensor(out=xT[:, c, sl], in0=xT[:, c, sl], scalar=gcol, in1=rstd[:],
                                                                                      op0=ALU.mult, op1=ALU.mult),
                             reads=[("xT", c), "rstd", "pvt"], writes=[("xT", c)])

        def mixer_C(l):
            for cc in range(2):
                inproj(l, [(OFF_C + cc * 128, 128)], 0)
                Z = PS(0, NSEG)
                zk = psk(0, NSEG)
                xc = scr[0][:, 0:T]
                B.op("act", lambda e: e.activation(out=xc, in_=Z, func=AF.Identity, scale=pvc("cw", l, 2 * 2 + cc), bias=pvc("cb", l, cc)),
                     reads=zk + ["pvt"], writes=[sk(0)])
                for (j, sh) in ((0, -2), (1, -1), (3, 1)):
                    if sh < 0:
                        o_ap, i_ap = xc[:, -sh:T], Z[:, 0:T + sh]
                    else:
                        o_ap, i_ap = xc[:, 0:T - sh], Z[:, sh:T]
                    B.op("dve", lambda e, o_ap=o_ap, i_ap=i_ap, j=j: e.scalar_tensor_tensor(out=o_ap, in0=i_ap, scalar=pvc("cw", l, j * 2 + cc), in1=o_ap,
                                                                                            op0=ALU.mult, op1=ALU.add),
                         reads=zk + [sk(0), "pvt"], writes=[sk(0)])
                xcb = sbf(1)[:, 0:T]
                B.op("act", lambda e: e.activation(out=xcb, in_=xc, func=AF.Copy), reads=[sk(0)], writes=[sk(1)])
                for d in range(2):
                    rb, ib, sbuf_, hb = scr[2][:, 0:T], scr[3][:, 0:T], scr[4][:, 0:T], scr[5 + d][:, 0:T]
                    for sg in range(NSEG):
                        sl = slice(sg * 512, (sg + 1) * 512)
                        for k in range(2):
                            bank = 4 + 2 * k + sg % 2
                            wcol = (((d * 2 + k) * 2 + cc)) * 128
                            B.op("pe", lambda e, bank=bank, wcol=wcol, sl=sl: e.matmul(PS(bank), lhsT=gw_b[:, wcol:wcol + 128], rhs=xcb[:, sl], start=True, stop=True),
                                 reads=["gw_b", sk(1)], writes=psk(bank))
                            dst = rb if k == 0 else ib
                            B.op("act", lambda e, bank=bank, dst=dst, sl=sl, k=k: e.activation(out=dst[:, sl], in_=PS(bank), func=AF.Sigmoid,
                                                                                               bias=pvc("gb", l, (d * 2 + k) * 2 + cc)),
                                 reads=psk(bank) + ["pvt"], writes=[sk(2 + k)])
                    B.op("act", lambda e: e.activation(out=rb, in_=rb, func=AF.Exp, scale=dvc("cneg", l, d * 2 + cc)), reads=[sk(2), "dvt"], writes=[sk(2)])
                    B.op("act", lambda e: e.activation(out=sbuf_, in_=rb, func=AF.Square), reads=[sk(2)], writes=[sk(4)])
                    B.op("act", lambda e: e.activation(out=sbuf_, in_=sbuf_, func=AF.Sqrt, scale=-1.0, bias=epsb[:, 1:2]), reads=[sk(4), "epsb"], writes=[sk(4)])
                    B.op("dve", lambda e: e.tensor_tensor(out=ib, in0=ib, in1=xc, op=ALU.mult), reads=[sk(3), sk(0)], writes=[sk(3)])
                    B.op("dve", lambda e: e.tensor_tensor(out=ib, in0=ib, in1=sbuf_, op=ALU.mult), reads=[sk(3), sk(4)], writes=[sk(3)])
                    if d == 0:
                        B.op("dve", lambda e: e.tensor_tensor_scan(out=hb, data0=rb, data1=ib, initial=0.0, op0=ALU.mult, op1=ALU.add),
                             reads=[sk(2), sk(3)], writes=[sk(5 + d)])
                    else:
                        B.op("dve", lambda e: e.tensor_tensor_scan(out=hb[:, ::-1], data0=rb[:, ::-1], data1=ib[:, ::-1], initial=0.0, op0=ALU.mult, op1=ALU.add),
                             reads=[sk(2), sk(3)], writes=[sk(5 + d)])
                hf, hbk = scr[5][:, 0:T], scr[6][:, 0:T]
                B.op("dve", lambda e: e.tensor_tensor(out=hf, in0=hf, in1=hbk, op=ALU.add), reads=[sk(5), sk(6)], writes=[sk(5)])
                inproj(l, [(OFF_C + 256 + cc * 128, 128)], 0)
                sgt = scr[2][:, 0:T]
                B.op("act", lambda e: e.activation(out=sgt, in_=PS(0, NSEG), func=AF.Silu), reads=psk(0, NSEG), writes=[sk(2)])
                B.op("dve", lambda e: e.tensor_tensor(out=obuf[:, cc, :], in0=hf, in1=sgt, op=ALU.mult), reads=[sk(5), sk(2)], writes=["obuf"])

        def load_qk(l, off_q, off_k, cc, prep):
            qT = sbf(0)[:, 0:T]
            kT = sbf(0)[:, SCRW:SCRW + T]
            prep([(off_q + cc * 128, 128)], qT, True)
            prep([(off_k + cc * 64, 64), (off_k + cc * 64, 64)], kT, False)
            return qT, kT

        def load_vaug(l, off_v, cc):
            va = sbf(1)
            vav = va[:, 0:NT * 192].rearrange("p (t c) -> p t c", c=192)
            B.op("pool", lambda e: e.memset(vav[:, :, 64:128], 1.0), writes=[sk(1)])

            def post(tb, bank, ntt):
                src = PS(bank)[:, 0:ntt * 64].rearrange("p (t c) -> p t c", c=64)
                B.op("act", lambda e: e.activation(out=vav[:, tb * 8:tb * 8 + ntt, 0:64], in_=src, func=AF.Copy), reads=psk(bank), writes=[sk(1)])
                B.op("dve", lambda e: e.tensor_copy(out=vav[:, tb * 8:tb * 8 + ntt, 128:192], in_=src), reads=psk(bank), writes=[sk(1)])
            inproj_tok(l, [(off_v + cc * 64, 64)], None, post)
            return vav

        def load_gate(l, off_g, cc):
            inproj(l, [(off_g + cc * 128, 128)], 0)
            sgt = scr[2][:, 0:T]
            B.op("act", lambda e: e.activation(out=sgt, in_=PS(0, NSEG), func=AF.Silu), reads=psk(0, NSEG), writes=[sk(2)])
            return sgt

        def attn_post(l, cc, qg, bx, by, sgt, sink):
            X, Y = PS(bx), PS(by)
            gsl = slice(qg * 512, (qg + 1) * 512)
            if sink:
                B.op("dve", lambda e: e.tensor_scalar(out=tmpA[0:64, :], in0=Y[0:64, :], scalar1=dvc("esinksw", l, cc)[0:64, :], scalar2=None, op0=ALU.add),
                     reads=psk(by) + ["dvt"], writes=["tmpA"])
                B.op("dve", lambda e: e.tensor_scalar(out=tmpA[64:128, :], in0=X[64:128, :], scalar1=dvc("esinksw", l, cc)[64:128, :], scalar2=None, op0=ALU.add),
                     reads=psk(bx) + ["dvt"], writes=["tmpA"])
                B.op("dve", lambda e: e.reciprocal(out=tmpA[:], in_=tmpA[:]), reads=["tmpA"], writes=["tmpA"])
            else:
                B.op("dve", lambda e: e.reciprocal(out=tmpA[0:64, :], in_=Y[0:64, :]), reads=psk(by), writes=["tmpA"])
                B.op("dve", lambda e: e.reciprocal(out=tmpA[64:128, :], in_=X[64:128, :]), reads=psk(bx), writes=["tmpA"])
            B.op("pe", lambda e: e.matmul(PS(0), lhsT=swapf[:], rhs=tmpA[:], start=True, stop=True), reads=["swapf", "tmpA"], writes=psk(0))
            B.op("dve", lambda e: e.tensor_tensor(out=tmpB[:], in0=PS(0), in1=sgt[:, gsl], op=ALU.mult), reads=psk(0) + [sk(2)], writes=["tmpB"])
            B.op("dve", lambda e: e.tensor_tensor(out=obuf[0:64, cc, gsl], in0=X[0:64, :], in1=tmpB[0:64, :], op=ALU.mult),
                 reads=psk(bx) + ["tmpB"], writes=["obuf"])
            B.op("dve", lambda e: e.tensor_tensor(out=obuf[64:128, cc, gsl], in0=Y[64:128, :], in1=tmpB[64:128, :], op=ALU.mult),
                 reads=psk(by) + ["tmpB"], writes=["obuf"])

        def mixer_D(l):
            B.dma(scr[3][:, 0:1536], alibi_d[:, :], writes=[sk(3)], q="pool", semkey="alibi")

            def prep(ranges, dst, isq):
                inproj(l, ranges, 0)
                B.op("act", lambda e: e.activation(out=dst, in_=PS(0, NSEG), func=AF.Copy), reads=psk(0, NSEG), writes=[sk(0)])
            for cc in range(2):
                qT, kT = load_qk(l, OFF_D, OFF_D + 256, cc, prep)
                vav = load_vaug(l, OFF_D + 384, cc)
                sgt = load_gate(l, OFF_D + 512, cc)
                ptv = sbf(4)
                exv = sbf(5)

                def qrange(j):
                    b0, b1 = max(j - 1, 0), min(j + 1, NT - 1)
                    return b0, b1

                def pv_block(i, hh):
                    qg = i // 4
                    bank = 4 + 2 * (qg % 2) + hh
                    js = [j for j in (i - 1, i, i + 1) if 0 <= j < NT]
                    for n, j in enumerate(js):
                        b0, _ = qrange(j)
                        c0 = (i - b0) * 128
                        pt = ptv[:, (hh * 4 + j % 4) * 384 + c0:(hh * 4 + j % 4) * 384 + c0 + 128]
                        B.op("pe", lambda e, pt=pt, j=j, n=n, bank=bank: e.matmul(
                            PS(bank)[:, (i % 4) * 128:(i % 4 + 1) * 128], lhsT=vav[:, j, hh * 64:hh * 64 + 128], rhs=pt,
                            start=(n == 0), stop=(n == len(js) - 1)),
                            reads=[sk(1), ("pt", hh, j % 4)], writes=psk(bank), inc=(n == len(js) - 1))

                for j in range(NT):
                    b0, b1 = qrange(j)
                    ncol = (b1 - b0 + 1) * 128
                    tcol0 = (b0 - (j - 1)) * 128
                    for hh in range(2):
                        ph = slice(hh * 64, hh * 64 + 64)
                        h = 2 * cc + hh
                        bank = hh * 2 + j % 2
                        B.op("pe", lambda e, ph=ph, bank=bank, b0=b0, ncol=ncol, j=j: e.matmul(
                            PS(bank)[:, 0:ncol], lhsT=kT[ph, j * 128:(j + 1) * 128], rhs=qT[ph, b0 * 128:b0 * 128 + ncol], start=True, stop=True),
                            reads=[sk(0)], writes=psk(bank))
                        ex = exv[:, (hh * 2 + j % 2) * 384:(hh * 2 + j % 2) * 384 + ncol]
                        B.op("act", lambda e, ex=ex, bank=bank, ncol=ncol: e.activation(out=ex, in_=PS(bank)[:, 0:ncol], func=AF.Exp, scale=0.125),
                             reads=psk(bank), writes=[("ex", hh, j % 2)])
                        pt = ptv[:, (hh * 4 + j % 4) * 384:(hh * 4 + j % 4) * 384 + ncol]
                        B.op("dve", lambda e, pt=pt, ex=ex, h=h, tcol0=tcol0, ncol=ncol: e.tensor_tensor(
                            out=pt, in0=ex, in1=scr[3][:, h * 384 + tcol0:h * 384 + tcol0 + ncol], op=ALU.mult),
                            reads=[("ex", hh, j % 2), sk(3)], writes=[("pt", hh, j % 4)])
                    blocks = []
                    if j >= 1:
                        blocks.append(j - 1)
                    if j == NT - 1:
                        blocks.append(j)
                    for i in blocks:
                        for hh in range(2):
                            pv_block(i, hh)
                        if i % 4 == 3:
                            qg = i // 4
                            attn_post(l, cc, qg, 4 + 2 * (qg % 2), 4 + 2 * (qg % 2) + 1, sgt, True)

        def mixer_B(l):
            B.dma(scr[6][:, 0:T], ropeC_d[:, :], writes=[sk(6)], q="pool", semkey="ropeC")
            B.dma(scr[7][:, 0:T], ropeS_d[:, :], writes=[sk(7)], q="pool", semkey="ropeS")
            t1b, t2b = scr[3][:, 0:512], scr[3][:, 512:1024]
            qgb = sbf(3)[:, 2048:2560]

            def prep(ranges, dst, isq):
                inproj(l, ranges, 0)
                gcol = pvc("qn" if isq else "kn", l, 0)
                for sg in range(NSEG):
                    sl = slice(sg * 512, (sg + 1) * 512)
                    Z = PS(sg)
                    B.op("act", lambda e: e.activation(out=qgb, in_=Z, func=AF.Copy, scale=gcol), reads=psk(sg) + ["pvt"], writes=[("qgb",)])
                    B.op("act", lambda e: e.activation(out=sqb[:, 0, :], in_=Z, func=AF.Square), reads=psk(sg), writes=[("sqb", 0)])
                    B.op("pe", lambda e: e.matmul(PS(4), lhsT=cb("blk64"), rhs=sqb[:, 0, :], start=True, stop=True), reads=[("sqb", 0), "cbf"], writes=psk(4))
                    B.op("pe", lambda e: e.matmul(PS(5), lhsT=cb("perm"), rhs=qgb, start=True, stop=True), reads=[("qgb",), "cbf"], writes=psk(5))
                    B.op("act", lambda e: e.activation(out=rstd[:], in_=PS(4), func=AF.Sqrt, scale=1.0 / 64, bias=epsb[:, 0:1]), reads=psk(4) + ["epsb"], writes=["rstd"])
                    B.op("dve", lambda e: e.reciprocal(out=rstd[:], in_=rstd[:]), reads=["rstd"], writes=["rstd"])
                    B.op("dve", lambda e: e.tensor_tensor(out=t1b, in0=qgb, in1=scr[6][:, sl], op=ALU.mult), reads=[("qgb",), sk(6)], writes=[("t1b",)])
                    B.op("dve", lambda e: e.tensor_tensor(out=t2b, in0=PS(5), in1=scr[7][:, sl], op=ALU.mult), reads=psk(5) + [sk(7)], writes=[("t2b",)])
                    B.op("dve", lambda e: e.tensor_tensor(out=t1b, in0=t1b, in1=t2b, op=ALU.add), reads=[("t1b",), ("t2b",)], writes=[("t1b",)])
                    B.op("dve", lambda e: e.tensor_tensor(out=dst[:, sl], in0=t1b, in1=rstd[:], op=ALU.mult), reads=[("t1b",), "rstd"], writes=[sk(0)])
            for cc in range(2):
                qT, kT = load_qk(l, OFF_B, OFF_B + 256, cc, prep)
                vav = load_vaug(l, OFF_B + 384, cc)
                sgt = load_gate(l, OFF_B + 512, cc)
                ptv = sbf(4)
                for qg in range(NSEG):
                    gsl = slice(qg * 512, (qg + 1) * 512)
                    for j in range(NT):
                        for hh in range(2):
                            ph = slice(hh * 64, hh * 64 + 64)
                            bank = hh * 2 + j % 2
                            B.op("pe", lambda e, ph=ph, bank=bank, j=j: e.matmul(PS(bank), lhsT=kT[ph, j * 128:(j + 1) * 128], rhs=qT[ph, gsl], start=True, stop=True),
                                 reads=[sk(0)], writes=psk(bank))
                            pt = ptv[:, (hh * 2 + j % 2) * 512:(hh * 2 + j % 2 + 1) * 512]
                            B.op("act", lambda e, pt=pt, bank=bank: e.activation(out=pt, in_=PS(bank), func=AF.Exp, scale=0.125),
                                 reads=psk(bank), writes=[("pt", hh, j % 2)])
                            abank = 4 + 2 * (qg % 2) + hh
                            B.op("pe", lambda e, pt=pt, abank=abank, j=j, hh=hh: e.matmul(PS(abank), lhsT=vav[:, j, hh * 64:hh * 64 + 128], rhs=pt,
                                                                                         start=(j == 0), stop=(j == NT - 1)),
                                 reads=[sk(1), ("pt", hh, j % 2)], writes=psk(abank), inc=True)
                    attn_post(l, cc, qg, 4 + 2 * (qg % 2), 4 + 2 * (qg % 2) + 1, sgt, False)

        def mixer_A(l):
            SA = min(512, T)
            NS = T // SA
            nch = SA // 64
            arena_b = hT[:].rearrange("p k t -> p (k t)")

            def AF32(off, w):
                return arena_b[:, 2 * off:2 * (off + w)].bitcast(F32)

            def ABF(off, w):
                return arena_b[:, 2 * off:2 * off + w]

            rB = [sbf(0)[:, 0:T], sbf(0)[:, SCRW:SCRW + T]]
            kB = [sbf(1)[:, 0:T], sbf(1)[:, SCRW:SCRW + T]]
            vB = [sbf(2)[:, 0:T], sbf(2)[:, SCRW:SCRW + T]]
            waB = sbf(3)[:, 0:T]
            kkB = [sbf(4)[:, 0:T], sbf(4)[:, SCRW:SCRW + T]]
            yfB = [sbf(5)[:, 0:T], sbf(5)[:, SCRW:SCRW + T]]
            bcB = [sbf(6)[:, 0:T], sbf(6)[:, SCRW:SCRW + T]]
            Xt = scr[7][:, 0:T]
            for ci in range(7):
                inproj(l, [(OFF_A + ci * 128, 128)], 0)
                Z = PS(0, NSEG)
                zk = psk(0, NSEG)
                B.op("act", lambda e: e.activation(out=Xt, in_=Z, func=AF.Copy, scale=dvc("c0", l, ci)), reads=zk + ["dvt"], writes=[sk(7)])
                B.op("dve", lambda e: e.scalar_tensor_tensor(out=Xt[:, 1:T], in0=Z[:, 0:T - 1], scalar=pvc("sh0", l, ci), in1=Xt[:, 1:T], op0=ALU.mult, op1=ALU.add),
                     reads=zk + [sk(7), "pvt"], writes=[sk(7)])
                if ci < 6:
                    dst = (rB, kB, vB)[ci // 2][ci % 2]
                    dk_ = sk(ci // 2)
                    B.op("dve", lambda e: e.scalar_tensor_tensor(out=dst[:, 0:T - 1], in0=Z[:, 1:T], scalar=pvc("sh1", l, ci), in1=Xt[:, 0:T - 1], op0=ALU.mult, op1=ALU.add),
                         reads=zk + [sk(7), "pvt"], writes=[dk_])
                    B.op("dve", lambda e: e.tensor_copy(out=dst[:, T - 1:T], in_=Xt[:, T - 1:T]), reads=[sk(7)], writes=[dk_])
                else:
                    B.op("dve", lambda e: e.scalar_tensor_tensor(out=Xt[:, 0:T - 1], in0=Z[:, 1:T], scalar=pvc("sh1", l, ci), in1=Xt[:, 0:T - 1], op0=ALU.mult, op1=ALU.add),
                         reads=zk + [sk(7), "pvt"], writes=[sk(7)])
                    B.op("act", lambda e: e.activation(out=waB[0:64, :], in_=Xt[0:64, :], func=AF.Tanh), reads=[sk(7)], writes=[sk(3)])
                    B.op("act", lambda e: e.activation(out=waB[64:128, :], in_=Xt[64:128, :], func=AF.Copy), reads=[sk(7)], writes=[sk(3)])
            for hp in range(2):
                for sg in range(NSEG):
                    sl = slice(sg * 512, (sg + 1) * 512)
                    B.op("dve", lambda e: e.tensor_scalar(out=tmpA[:], in0=kB[hp][:, sl], scalar1=pvc("k_k", l, hp), scalar2=None, op0=ALU.mult),
                         reads=[sk(1), "pvt"], writes=["tmpA"])
                    B.op("act", lambda e: e.activation(out=sqb[:, 0, :], in_=tmpA[:], func=AF.Square), reads=["tmpA"], writes=[("sqb", 0)])
                    B.op("pe", lambda e: e.matmul(PS(4), lhsT=cb("blk64"), rhs=sqb[:, 0, :], start=True, stop=True), reads=[("sqb", 0), "cbf"], writes=psk(4))
                    B.op("act", lambda e: e.activation(out=rstd[:], in_=PS(4), func=AF.Sqrt), reads=psk(4), writes=["rstd"])
                    B.op("dve", lambda e: e.tensor_scalar(out=rstd[:], in0=rstd[:], scalar1=1e-12, scalar2=None, op0=ALU.max), reads=["rstd"], writes=["rstd"])
                    B.op("dve", lambda e: e.reciprocal(out=rstd[:], in_=rstd[:]), reads=["rstd"], writes=["rstd"])
                    B.op("dve", lambda e: e.tensor_tensor(out=kkB[hp][:, sl], in0=tmpA[:], in1=rstd[:], op=ALU.mult), reads=["tmpA", "rstd"], writes=[sk(4)])
            for cc in range(2):
                inproj(l, [(OFF_A + 896 + cc * 128, 128)], 0)
                B.op("act", lambda e: e.activation(out=obuf[:, cc, :], in_=PS(0, NSEG), func=AF.Silu), reads=psk(0, NSEG), writes=["obuf"])
            B.barrier()
            o = 0
            Fb = []
            for i in range(5):
                Fb.append(AF32(o, SA)); o += SA
            H0 = AF32(o, SA); o += SA
            H1 = AF32(o, SA); o += SA
            H2 = ABF(o, SA); o += SA // 2
            BK = ABF(o, 2 * SA); o += SA
            ARb = ABF(o, 2 * SA); o += SA
            HK = ABF(o, 2 * SA); o += SA
            Gm = ABF(o, 4 * SA); o += 2 * SA
            GT = ABF(o, SA); o += SA // 2
            Xs, XTs, Ps_ = [], [], []
            for i in range(2):
                Xs.append(ABF(o, SA)); o += SA // 2
                XTs.append(ABF(o, SA)); o += SA // 2
                Ps_.append(ABF(o, SA)); o += SA // 2
            assert o <= 8192, o
            s7 = sbf(7)
            VT, BhT, KhT = s7[:, 0:SA], s7[:, SA:2 * SA], s7[:, 2 * SA:3 * SA]
            Q1 = scr[7][:, 768:768 + SA]
            Qb = s7[:, 2560:2624]
            UTb = [s7[:, 2624:2688], s7[:, 2688:2752]]
            STf = [[scr[7][:, 1408 + (hp * 2 + i) * 64:1408 + (hp * 2 + i + 1) * 64] for i in range(2)] for hp in range(2)]
            STb = [[s7[:, 3328 + (hp * 2 + i) * 64:3328 + (hp * 2 + i + 1) * 64] for i in range(2)] for hp in range(2)]
            id64 = s7[:, 3600:3664]
            blkf = scr[7][:, 1856:1984]
            F0, F1, F2, F3, F4 = Fb
            B.op("dve", lambda e: e.tensor_tensor(out=id64, in0=cb("mfwd", 64, 64), in1=cb("mfwd", 64, 0), op=ALU.subtract), reads=["cbf"], writes=["id64"])
            B.op("dve", lambda e: e.tensor_copy(out=blkf, in_=cb("blk64")), reads=["cbf"], writes=["blkf"])
            BK4 = BK.rearrange("p (c two j) -> p c two j", two=2, j=64)
            AR4 = ARb.rearrange("p (c two j) -> p c two j", two=2, j=64)
            HK4 = HK.rearrange("p (c two j) -> p c two j", two=2, j=64)
            Gm3 = Gm.rearrange("p (c x) -> p c x", x=256)

            def v3(ap):
                return ap.rearrange("p (c j) -> p c j", j=64)

            def hs(hd):
                return slice(hd * 64, hd * 64 + 64)

            for d in range(2):
                mname = "mfwd" if d == 0 else "mbwd"
                mTname = "mfwdT" if d == 0 else "mbwdT"
                endc = 63 if d == 0 else 0
                cur = [0, 0]
                for hp in range(2):
                    B.op("dve", lambda e: e.memset(STf[hp][0], 0.0), writes=[("STf", hp, 0)])
                    B.op("dve", lambda e: e.memset(STb[hp][0], 0.0), writes=[("STb", hp, 0)])
                seg_order = range(NS) if d == 0 else range(NS - 1, -1, -1)
                for sg in seg_order:
                    ssl = slice(sg * SA, (sg + 1) * SA)
                    for hp in range(2):
                        wc = d * 256 + hp * 128
                        B.op("pe", lambda e: e.matmul(PS(0)[:, 0:SA], lhsT=upw_b[0:64, wc:wc + 128], rhs=waB[0:64, ssl], start=True, stop=True),
                             reads=["upw_b", sk(3)], writes=psk(0))
                        B.op("pe", lambda e: e.matmul(PS(1)[:, 0:SA], lhsT=upw_b[64:128, wc:wc + 128], rhs=waB[64:128, ssl], start=True, stop=True),
                             reads=["upw_b", sk(3)], writes=psk(1))
                        B.op("act", lambda e: e.activation(out=F0, in_=PS(0)[:, 0:SA], func=AF.Sigmoid, bias=pvc("w0", l, d * 2 + hp)), reads=psk(0) + ["pvt"], writes=["F0"])
                        B.op("act", lambda e: e.activation(out=F1, in_=PS(1)[:, 0:SA], func=AF.Sigmoid, bias=pvc("a0", l, d * 2 + hp)), reads=psk(1) + ["pvt"], writes=["F1"])
                        rst = cb("rst", SA)
                        if d == 0:
                            B.op("dve", lambda e: e.tensor_tensor_scan(out=F2, data0=rst, data1=F0, initial=0.0, op0=ALU.mult, op1=ALU.add), reads=["cbf", "F0"], writes=["F2"])
                        else:
                            B.op("dve", lambda e: e.tensor_tensor_scan(out=F2[:, ::-1], data0=rst, data1=F0[:, ::-1], initial=0.0, op0=ALU.mult, op1=ALU.add),
                                 reads=["cbf", "F0"], writes=["F2"])
                        B.op("dve", lambda e: e.tensor_tensor(out=F3, in0=F2, in1=F0, op=ALU.subtract), reads=["F2", "F0"], writes=["F3"])
                        endb = v3(F2)[:, :, endc:endc + 1].broadcast_to([128, nch, 64])
                        B.op("dve", lambda e: e.tensor_tensor(out=v3(F0), in0=endb, in1=v3(F2), op=ALU.subtract), reads=["F2"], writes=["F0"])
                        B.op("act", lambda e: e.activation(out=F4, in_=F2, func=AF.Exp, scale=LAMW), reads=["F2"], writes=["F4"])
                        B.op("act", lambda e: e.activation(out=F2, in_=F2, func=AF.Exp, scale=-LAMW), reads=["F2"], writes=["F2"])
                        B.op("act", lambda e: e.activation(out=F3, in_=F3, func=AF.Exp, scale=-LAMW), reads=["F3"], writes=["F3"])
                        B.op("act", lambda e: e.activation(out=F0, in_=F0, func=AF.Exp, scale=-LAMW), reads=["F0"], writes=["F0"])
                        kk_, r_, k_, v_ = kkB[hp][:, ssl], rB[hp][:, ssl], kB[hp][:, ssl], vB[hp][:, ssl]
                        B.op("dve", lambda e: e.tensor_tensor(out=H0, in0=kk_, in1=F1, op=ALU.mult), reads=[sk(4), "F1"], writes=["H0"])
                        B.op("dve", lambda e: e.tensor_scalar(out=F1, in0=F1, scalar1=pvc("k_a", l, hp), scalar2=dvc("omka", l, hp), op0=ALU.mult, op1=ALU.add),
                             reads=["F1", "pvt", "dvt"], writes=["F1"])
                        B.op("dve", lambda e: e.tensor_tensor(out=H1, in0=F1, in1=k_, op=ALU.mult), reads=["F1", sk(1)], writes=["H1"])
                        B.op("dve", lambda e: e.tensor_tensor(out=BK4[:, :, 0, :], in0=v3(H0), in1=v3(F4), op=ALU.mult), reads=["H0", "F4"], writes=["BK"])
                        B.op("dve", lambda e: e.tensor_tensor(out=BK4[:, :, 1, :], in0=v3(H1), in1=v3(F4), op=ALU.mult), reads=["H1", "F4"], writes=["BK"])
                        B.op("dve", lambda e: e.scalar_tensor_tensor(out=AR4[:, :, 0, :], in0=v3(kk_), scalar=-1.0, in1=v3(F3), op0=ALU.mult, op1=ALU.mult),
                             reads=[sk(4), "F3"], writes=["AR"])
                        B.op("dve", lambda e: e.tensor_tensor(out=AR4[:, :, 1, :], in0=v3(r_), in1=v3(F2), op=ALU.mult), reads=[sk(0), "F2"], writes=["AR"])
                        B.op("dve", lambda e: e.tensor_tensor(out=HK4[:, :, 0, :], in0=v3(H0), in1=v3(F0), op=ALU.mult), reads=["H0", "F0"], writes=["HK"])
                        B.op("dve", lambda e: e.tensor_tensor(out=HK4[:, :, 1, :], in0=v3(H1), in1=v3(F0), op=ALU.mult), reads=["H1", "F0"], writes=["HK"])
                        B.op("dve", lambda e: e.scalar_tensor_tensor(out=H2, in0=r_, scalar=pvc("r_k", l, hp), in1=H1, op0=ALU.mult, op1=ALU.mult),
                             reads=[sk(0), "pvt", "H1"], writes=["H2"])
                        B.op("pe", lambda e: e.matmul(PS(2)[:, 0:SA], lhsT=cb("blk64"), rhs=H2, start=True, stop=True), reads=["cbf", "H2"], writes=psk(2))
                        if d == 0:
                            B.op("act", lambda e: e.activation(out=bcB[hp][:, ssl], in_=PS(2)[:, 0:SA], func=AF.Copy), reads=psk(2), writes=[sk(6)])
                        else:
                            B.op("dve", lambda e: e.tensor_tensor(out=bcB[hp][:, ssl], in0=PS(2)[:, 0:SA], in1=bcB[hp][:, ssl], op=ALU.add), reads=psk(2) + [sk(6)], writes=[sk(6)])
                        Gps = PS(4, 4)
                        n_items = 2 * nch
                        it = 0
                        for c in range(nch):
                            for hd in range(2):
                                it += 1
                                last = (it == n_items)
                                B.op("pe", lambda e: e.matmul(Gps[hs(hd), c * 256:c * 256 + 128], lhsT=BK4[hs(hd), c, 0, :], rhs=AR4[hs(hd), c, :, :], start=True, stop=True),
                                     reads=["BK", "AR"], writes=psk(4, 4), inc=False)
                                B.op("pe", lambda e: e.matmul(Gps[hs(hd), c * 256 + 128:c * 256 + 256], lhsT=BK4[hs(hd), c, 1, :], rhs=AR4[hs(hd), c, :, :], start=True, stop=True),
                                     reads=["BK", "AR"], writes=psk(4, 4), inc=False)
                                B.op("pe", lambda e: e.matmul(PS(3)[hs(hd), c * 64:(c + 1) * 64], lhsT=AR4[hs(hd), c, 0, :], rhs=BK4[hs(hd), c, 0, :], start=True, stop=True),
                                     reads=["BK", "AR"], writes=psk(3), inc=last)
                        mk = cb(mname).rearrange("p (o x) -> p o x", o=1).broadcast_to([128, nch, 256])
                        B.op("dve", lambda e: e.tensor_tensor(out=Gm3, in0=Gps[:, 0:nch * 256].rearrange("p (c x) -> p c x", x=256), in1=mk, op=ALU.mult),
                             reads=psk(4, 4) + ["cbf"], writes=["Gm"])
                        mkT = cb(mTname).rearrange("p (o x) -> p o x", o=1).broadcast_to([128, nch, 64])
                        B.op("dve", lambda e: e.tensor_tensor(out=v3(GT), in0=v3(PS(3)[:, 0:SA]), in1=mkT, op=ALU.mult), reads=psk(3) + ["cbf"], writes=["GT"])
                        idb = id64.rearrange("p (o x) -> p o x", o=1).broadcast_to([128, nch, 64])
                        B.op("dve", lambda e: e.tensor_tensor(out=v3(Ps_[0]), in0=Gm3[:, :, 0:64], in1=idb, op=ALU.add), reads=["Gm", "id64"], writes=[("P", 0)])

                        def Xap(level, buf, hd, c):
                            if level == 0:
                                return Gm[hs(hd), c * 256:c * 256 + 64]
                            return Xs[buf][hs(hd), c * 64:(c + 1) * 64]

                        def XTap(level, buf, hd, c):
                            if level == 0:
                                return GT[hs(hd), c * 64:(c + 1) * 64]
                            return XTs[buf][hs(hd), c * 64:(c + 1) * 64]
                        pcur = 0
                        for lev in range(1, 6):
                            src_b, dst_b = (lev - 1) % 2, lev % 2
                            xk_src = ["Gm"] if lev == 1 else [("X", src_b)]
                            xtk_src = ["GT"] if lev == 1 else [("XT", src_b)]
                            it = 0
                            for c in range(nch):
                                for hd in range(2):
                                    it += 1
                                    last = (it == n_items)
                                    if lev < 5:
                                        B.op("pe", lambda e: e.matmul(PS(0)[hs(hd), c * 64:(c + 1) * 64], lhsT=XTap(lev - 1, src_b, hd, c), rhs=Xap(lev - 1, src_b, hd, c), start=True, stop=True),
                                             reads=xk_src + xtk_src, writes=psk(0), inc=False)
                                    B.op("pe", lambda e: e.matmul(PS(1)[hs(hd), c * 64:(c + 1) * 64], lhsT=Xap(lev - 1, src_b, hd, c), rhs=XTap(lev - 1, src_b, hd, c), start=True, stop=True),
                                         reads=xk_src + xtk_src, writes=psk(1) + (psk(0) if lev < 5 else []), inc=last)
                            if lev < 5:
                                B.op("act", lambda e: e.activation(out=Xs[dst_b], in_=PS(0)[:, 0:SA], func=AF.Copy), reads=psk(0), writes=[("X", dst_b)])
                            B.op("dve", lambda e: e.tensor_copy(out=XTs[dst_b], in_=PS(1)[:, 0:SA]), reads=psk(1), writes=[("XT", dst_b)])
                            it = 0
                            for c in range(nch):
                                for hd in range(2):
                                    it += 1
                                    last = (it == n_items)
                                    B.op("pe", lambda e: e.matmul(PS(2)[hs(hd), c * 64:(c + 1) * 64], lhsT=XTs[dst_b][hs(hd), c * 64:(c + 1) * 64], rhs=Ps_[pcur][hs(hd), c * 64:(c + 1) * 64], start=True, stop=True),
                                         reads=[("XT", dst_b), ("P", pcur)], writes=psk(2), inc=last)
                            B.op("dve", lambda e: e.tensor_tensor(out=Ps_[1 - pcur], in0=PS(2)[:, 0:SA], in1=Ps_[pcur], op=ALU.add), reads=psk(2) + [("P", pcur)], writes=[("P", 1 - pcur)])
                            pcur = 1 - pcur
                        Tm = Ps_[pcur]
                        tkey = ("P", pcur)
                        for (srcf, dstb, dkey, rk_, bank) in ((lambda c, hd: v_[hs(hd), c * 64:(c + 1) * 64], VT, "VT", [sk(2)], 0),
                                                               (lambda c, hd: HK4[hs(hd), c, 0, :], BhT, "BhT", ["HK"], 1),
                                                               (lambda c, hd: HK4[hs(hd), c, 1, :], KhT, "KhT", ["HK"], 3)):
                            pb = PS(bank).bitcast(BF16)
                            it = 0
                            for c in range(nch):
                                for hd in range(2):
                                    it += 1
                                    B.op("pe", lambda e: e.transpose(pb[hs(hd), c * 64:(c + 1) * 64], srcf(c, hd), cb("ident")[hs(hd), hd * 64:hd * 64 + 64]),
                                         reads=rk_ + ["cbf"], writes=psk(bank), inc=(it == n_items))
                            B.op("act", lambda e: e.activation(out=dstb, in_=pb[:, 0:SA], func=AF.Copy), reads=psk(bank), writes=[dkey])
                        it = 0
                        for c in range(nch):
                            for hd in range(2):
                                it += 1
                                B.op("pe", lambda e: e.matmul(PS(2)[hs(hd), c * 64:(c + 1) * 64], lhsT=Gm[hs(hd), c * 256 + 128:c * 256 + 192], rhs=VT[hs(hd), c * 64:(c + 1) * 64], start=True, stop=True),
                                     reads=["Gm", "VT"], writes=psk(2), inc=(it == n_items))
                        B.op("dve", lambda e: e.tensor_copy(out=Q1, in_=PS(2)[:, 0:SA]), reads=psk(2), writes=["Q1"])
                        corder = range(nch) if d == 0 else range(nch - 1, -1, -1)
                        for ci_, c in enumerate(corder):
                            cu = cur[hp]
                            nx = 1 - cu
                            cs_ = slice(c * 64, (c + 1) * 64)
                            ub = ci_ % 2
                            for hd in range(2):
                                B.op("pe", lambda e: e.matmul(PS(0)[hs(hd), 0:64], lhsT=AR4[hs(hd), c, 0, :], rhs=STb[hp][cu][hs(hd), :], start=True, stop=True),
                                     reads=["AR", ("STb", hp, cu)], writes=psk(0), inc=(hd == 1))
                            B.op("dve", lambda e: e.tensor_tensor(out=Qb, in0=PS(0)[:, 0:64], in1=Q1[:, cs_], op=ALU.add), reads=psk(0) + ["Q1"], writes=["Qb"])
                            for hd in range(2):
                                B.op("pe", lambda e: e.matmul(PS(1)[hs(hd), 0:64], lhsT=Tm[hs(hd), cs_], rhs=Qb[hs(hd), :], start=True, stop=True),
                                     reads=[tkey, "Qb"], writes=psk(1), inc=(hd == 1))
                            B.op("act", lambda e: e.activation(out=UTb[ub], in_=PS(1)[:, 0:64], func=AF.Copy), reads=psk(1), writes=[("UTb", ub)])
                            for hd in range(2):
                                B.op("pe", lambda e: e.matmul(PS(2)[hs(hd), 0:64], lhsT=BhT[hs(hd), cs_], rhs=UTb[ub][hs(hd), :], start=True, stop=False),
                                     reads=["BhT", ("UTb", ub)], writes=psk(2), inc=False)
                                B.op("pe", lambda e: e.matmul(PS(2)[hs(hd), 0:64], lhsT=KhT[hs(hd), cs_], rhs=VT[hs(hd), cs_], start=False, stop=True),
                                     reads=["KhT", "VT"], writes=psk(2), inc=(hd == 1))
                            gcol = F2[:, c * 64 + endc:c * 64 + endc + 1]
                            B.op("dve", lambda e: e.scalar_tensor_tensor(out=STf[hp][nx], in0=STf[hp][cu], scalar=gcol, in1=PS(2)[:, 0:64], op0=ALU.mult, op1=ALU.add),
                                 reads=[("STf", hp, cu), "F2"] + psk(2), writes=[("STf", hp, nx)])
                            B.op("act", lambda e: e.activation(out=STb[hp][nx], in_=STf[hp][nx], func=AF.Copy), reads=[("STf", hp, nx)], writes=[("STb", hp, nx)])
                            for hd in range(2):
                                B.op("pe", lambda e: e.matmul(PS(3)[hs(hd), cs_], lhsT=STb[hp][cu][hs(hd), :], rhs=AR4[hs(hd), c, 1, :], start=True, stop=False),
                                     reads=[("STb", hp, cu), "AR"], writes=psk(3), inc=False)
                                B.op("pe", lambda e: e.matmul(PS(3)[hs(hd), cs_], lhsT=UTb[ub][hs(hd), :], rhs=Gm[hs(hd), c * 256 + 64:c * 256 + 128], start=False, stop=False),
                                     reads=[("UTb", ub), "Gm"], writes=psk(3), inc=False)
                                B.op("pe", lambda e: e.matmul(PS(3)[hs(hd), cs_], lhsT=VT[hs(hd), cs_], rhs=Gm[hs(hd), c * 256 + 192:c * 256 + 256], start=False, stop=True),
                                     reads=["VT", "Gm"], writes=psk(3), inc=(hd == 1))
                            cur[hp] = nx
                        if d == 0:
                            B.op("act", lambda e: e.activation(out=yfB[hp][:, ssl], in_=PS(3)[:, 0:SA], func=AF.Copy), reads=psk(3), writes=[sk(5)])
                        else:
                            B.op("dve", lambda e: e.tensor_tensor(out=F0, in0=PS(3)[:, 0:SA], in1=yfB[hp][:, ssl], op=ALU.add), reads=psk(3) + [sk(5)], writes=["F0"])
                            B.op("pe", lambda e: e.matmul(PS(0)[:, 0:SA], lhsT=blkf, rhs=F0, start=True, stop=True), reads=["blkf", "F0"], writes=psk(0))
                            B.op("dve", lambda e: e.scalar_tensor_tensor(out=F1, in0=PS(0)[:, 0:SA], scalar=-1.0 / 64, in1=F0, op0=ALU.mult, op1=ALU.add), reads=psk(0) + ["F0"], writes=["F1"])
                            B.op("act", lambda e: e.activation(out=F3, in_=F1, func=AF.Square), reads=["F1"], writes=["F3"])
                            B.op("pe", lambda e: e.matmul(PS(1)[:, 0:SA], lhsT=blkf, rhs=F3, start=True, stop=True), reads=["blkf", "F3"], writes=psk(1))
                            B.op("act", lambda e: e.activation(out=F4, in_=PS(1)[:, 0:SA], func=AF.Sqrt, scale=1.0 / 64, bias=epsb[:, 2:3]), reads=psk(1) + ["epsb"], writes=["F4"])
                            B.op("dve", lambda e: e.reciprocal(out=F4, in_=F4), reads=["F4"], writes=["F4"])
                            B.op("dve", lambda e: e.tensor_tensor(out=F1, in0=F1, in1=F4, op=ALU.mult), reads=["F1", "F4"], writes=["F1"])
                            B.op("dve", lambda e: e.tensor_scalar(out=F1, in0=F1, scalar1=pvc("ln_g", l, hp), scalar2=pvc("ln_b", l, hp), op0=ALU.mult, op1=ALU.add), reads=["F1", "pvt"], writes=["F1"])
                            B.op("dve", lambda e: e.tensor_tensor(out=F3, in0=bcB[hp][:, ssl], in1=v_, op=ALU.mult), reads=[sk(6), sk(2)], writes=["F3"])
                            B.op("dve", lambda e: e.tensor_tensor(out=F1, in0=F1, in1=F3, op=ALU.add), reads=["F1", "F3"], writes=["F1"])
                            B.op("dve", lambda e: e.tensor_tensor(out=obuf[:, hp, ssl], in0=F1, in1=obuf[:, hp, ssl], op=ALU.mult), reads=["F1", "obuf"], writes=["obuf"])
            B.barrier()

        for s in range(NSEQ):
            for c in range(8):
                B.dma(xT[:, c, :], x_d[s * 1024 + c * 128:s * 1024 + (c + 1) * 128, :], writes=[("xT", c)], q="pool", semkey=f"xin{c}")
            for l in range(L):
                rmsnorm_to(l, "h")
                B.dma(scr[7][:, 0:1024], gw_d[:, l * 1024:(l + 1) * 1024], writes=[sk(7)], q="pool", semkey="gw")
                B.op("pool", lambda e: e.tensor_copy(out=gw_b[:], in_=scr[7][:, 0:1024]), reads=[sk(7)], writes=["gw_b"])
                B.dma(upw_f[:], upw_d[:, l * 512:(l + 1) * 512], writes=["upw_f"], q="pool", semkey="upw")
                B.op("pool", lambda e: e.tensor_copy(out=upw_b[:], in_=upw_f[:]), reads=["upw_f"], writes=["upw_b"])
                B.barrier()
                if "C" in cfg.MIX:
                    mixer_C(l)
                    outproj(l, 2)
                if "D" in cfg.MIX:
                    B.barrier()
                    mixer_D(l)
                    outproj(l, 3)
                if "B" in cfg.MIX:
                    B.barrier()
                    mixer_B(l)
                    outproj(l, 1)
                if "A" in cfg.MIX:
                    B.barrier()
                    mixer_A(l)
                    outproj(l, 0)
                B.barrier()
            rmsnorm_to(None, "x")
            for c in range(8):
                B.dma(y_d[s * 1024 + c * 128:s * 1024 + (c + 1) * 128, :], xT[:, c, :], reads=[("xT", c)], q="pool", semkey=f"yout{c}")
        toks = []
        for c in range(8):
            toks.extend(B.readers.get(("xT", c), {}).values())
        for tname in tap_d:
            t = B.last_w.get(("tap", tname))
            if t is not None:
                toks.append(t)
        B._wait("pool", toks)
        print(f"[build] instructions: {B.n_ins}")
    return nc


_NC_CACHE = {}


def run_trunk(cfg, xs, inp):
    T, L = cfg.T, cfg.DEPTH
    ncore = cfg.NCORES
    assert xs.shape[0] == ncore * cfg.NSEQ
    key = (cfg.T, cfg.NSEQ, cfg.DEPTH, cfg.MIX, cfg.taps)
    if key not in _NC_CACHE:
        _NC_CACHE[key] = build(cfg)
    nc = _NC_CACHE[key]
    cst, alibi, C, S = make_consts(T)
    pv, upw, gw = pack_params(inp, L)
    w_in = np.ascontiguousarray(np.asarray(inp["w_in"], np.float32)[:L].reshape(L * 1024, D_IN))
    w_out = np.ascontiguousarray(np.asarray(inp["w_out"], np.float32)[:L].reshape(L * 1024, 1024))
    in_maps = []
    for c in range(ncore):
        xc = xs[c * cfg.NSEQ:(c + 1) * cfg.NSEQ]
        xt = np.ascontiguousarray(xc.transpose(0, 2, 1)).reshape(cfg.NSEQ * 1024, T)
        in_maps.append({"x": xt, "w_in": w_in, "w_out": w_out, "pvec": pv, "upw": upw, "gw": gw, "cst": cst,
                        "alibi": alibi, "ropeC": C, "ropeS": S})
    res = run_bass_kernel_spmd(nc, in_maps, core_ids=list(range(ncore)))
    outs = []
    for c in range(ncore):
        yt = np.asarray(res.results[c]["y"]).reshape(cfg.NSEQ, 1024, T)
        outs.append(yt.transpose(0, 2, 1))
    return np.ascontiguousarray(np.concatenate(outs, 0)).astype(np.float32), res


def kernel(**inputs):
    inp = {k: np.asarray(v) for k, v in inputs.items()}
    xp, xs_ = inp["x_prompt"], inp["x_sample"]
    xs = np.concatenate([xp, xs_], 0).astype(np.float32)
    cfg = Cfg(T=xs.shape[1], NSEQ=xs.shape[0] // 8, DEPTH=inp["w_in"].shape[0], MIX="ABCD", NCORES=8)
    y, _ = run_trunk(cfg, xs, inp)
    return (y[:xp.shape[0]], y[xp.shape[0]:])
```

```python
import math
from contextlib import ExitStack
import numpy as np
import ml_dtypes
import concourse.bass as bass
import concourse.mybir as mybir
from concourse.bass_utils import run_bass_kernel_spmd

F32 = mybir.dt.float32
BF16 = mybir.dt.bfloat16
AF = mybir.ActivationFunctionType
ALU = mybir.AluOpType

D_MODEL = 1024
GRID_W = 64
D_IN = 3200
A_W, B_W, C_W, D_W = 1152, 768, 512, 768
OFF_A, OFF_B, OFF_C, OFF_D = 0, 1152, 1920, 2432
NORM_EPS = 1e-6
GN_EPS = 64e-5
LAMW = math.exp(-0.5)
SEM_LIMIT = 30000


class Cfg:
    def __init__(self, T=2048, NSEQ=5, DEPTH=4, MIX="ABCD", NCORES=8, taps=()):
        self.T, self.NSEQ, self.DEPTH, self.MIX, self.NCORES = T, NSEQ, DEPTH, MIX, NCORES
        self.taps = tuple(taps)


PV_FIELDS = [("norm_g", 8), ("sh0", 7), ("sh1", 7), ("w0", 4), ("a0", 4), ("k_k", 2), ("k_a", 2), ("r_k", 2),
             ("ln_g", 2), ("ln_b", 2), ("qn", 1), ("kn", 1), ("cw", 8), ("cb", 2), ("gb", 8), ("lam", 4),
             ("sink", 2), ("sinksw", 2)]
DV_FIELDS = [("c0", 7), ("omka", 2), ("cneg", 4), ("esinksw", 2), ("gbh", 8)]


def _offsets(fields):
    off, d = 0, {}
    for n, w in fields:
        d[n] = off
        off += w
    return d, off


PV_OFF, PV_W = _offsets(PV_FIELDS)
DV_OFF, DV_W = _offsets(DV_FIELDS)


def pv_col(L, name, l, i=0):
    if name == "final_g":
        return L * PV_W + i
    return l * PV_W + PV_OFF[name] + i


def dv_col(name, l, i=0):
    return l * DV_W + DV_OFF[name] + i


CST_FIELDS = [("ident", 128), ("blk64", 128), ("ones", 128), ("swap", 128), ("perm", 128), ("mfwd", 256),
              ("mbwd", 256), ("mfwdT", 64), ("mbwdT", 64), ("rst", 512)]
CST_OFF, CST_W = _offsets(CST_FIELDS)


def make_consts(T):
    c = np.zeros((128, CST_W), np.float32)
    alibi = np.zeros((128, 4 * 384), np.float32)
    p = np.arange(128)
    o = CST_OFF
    c[:, o["ident"]:o["ident"] + 128] = np.eye(128)
    c[:, o["blk64"]:o["blk64"] + 128] = (p[:, None] // 64 == p[None, :] // 64)
    c[:, o["ones"]:o["ones"] + 128] = 1.0
    c[:, o["swap"]:o["swap"] + 128] = (p[:, None] == (p[None, :] + 64) % 128)
    d = p % 64
    partner = np.where((d % 32) < 16, p + 16, p - 16)
    c[:, o["perm"]:o["perm"] + 128] = (p[:, None] == partner[None, :])
    s = (p % 64)[:, None]
    t = np.arange(64)[None, :]
    strict_f, incl_f = (s < t), (s <= t)
    strict_b, incl_b = (s > t), (s >= t)
    c[:, o["mfwd"]:o["mfwd"] + 256] = np.concatenate([strict_f, incl_f, strict_f, incl_f], 1)
    c[:, o["mbwd"]:o["mbwd"] + 256] = np.concatenate([strict_b, incl_b, strict_b, incl_b], 1)
    c[:, o["mfwdT"]:o["mfwdT"] + 64] = (t < s)
    c[:, o["mbwdT"]:o["mbwdT"] + 64] = (t > s)
    c[:, o["rst"]:o["rst"] + 512] = (np.arange(512)[None, :] % 64 != 0)
    cc = np.arange(384)[None, :]
    dist = np.abs(cc - 128 - p[:, None]).astype(np.float64)
    for h in range(4):
        slope = 2.0 ** (-8.0 * (h + 1) / 4)
        e = np.where(dist <= 128, np.exp(-slope * dist), 0.0)
        alibi[:, h * 384:(h + 1) * 384] = e
    row = (np.arange(T) // GRID_W).astype(np.float64)
    col = (np.arange(T) % GRID_W).astype(np.float64)
    inv = 10000.0 ** (-np.arange(0, 32, 2, dtype=np.float64) / 32)
    C = np.zeros((128, T), np.float32)
    S = np.zeros((128, T), np.float32)
    for pp in range(128):
        dd = pp % 64
        pos = row if dd < 32 else col
        f = inv[dd % 16]
        ang = pos * f
        C[pp] = np.cos(ang)
        S[pp] = (-np.sin(ang)) if (dd % 32) < 16 else np.sin(ang)
    return c, alibi, C, S


def pack_params(inp, L):
    pv = np.zeros((128, L * PV_W + 8), np.float32)

    def put(name, l, i, vec128):
        pv[:, pv_col(L, name, l, i)] = vec128

    for l in range(L):
        for i in range(8):
            put("norm_g", l, i, inp["norm_g"][l, i * 128:(i + 1) * 128])
        for i in range(7):
            put("sh0", l, i, inp["rwkv_shift"][l, 0, i * 128:(i + 1) * 128])
            put("sh1", l, i, inp["rwkv_shift"][l, 1, i * 128:(i + 1) * 128])
        for d in range(2):
            for hp in range(2):
                put("w0", l, d * 2 + hp, inp["rwkv_w0"][l, d, hp * 128:(hp + 1) * 128])
                put("a0", l, d * 2 + hp, inp["rwkv_a0"][l, d, hp * 128:(hp + 1) * 128])
        rk = inp["rwkv_r_k"][l].reshape(256)
        for hp in range(2):
            sl = slice(hp * 128, (hp + 1) * 128)
            put("k_k", l, hp, inp["rwkv_k_k"][l, sl])
            put("k_a", l, hp, inp["rwkv_k_a"][l, sl])
            put("r_k", l, hp, rk[sl])
            put("ln_g", l, hp, inp["rwkv_ln_g"][l, sl])
            put("ln_b", l, hp, inp["rwkv_ln_b"][l, sl])
        put("qn", l, 0, np.tile(inp["attn_q_norm"][l], 2))
        put("kn", l, 0, np.tile(inp["attn_k_norm"][l], 2))
        for j in range(4):
            for cc in range(2):
                put("cw", l, j * 2 + cc, inp["lru_conv_w"][l, j, cc * 128:(cc + 1) * 128])
        for cc in range(2):
            put("cb", l, cc, inp["lru_conv_b"][l, cc * 128:(cc + 1) * 128])
        for d in range(2):
            for k in range(2):
                for cc in range(2):
                    put("gb", l, (d * 2 + k) * 2 + cc, inp["lru_gate_b"][l, d, k, cc * 128:(cc + 1) * 128])
            for cc in range(2):
                put("lam", l, d * 2 + cc, inp["lru_lambda"][l, d, cc * 128:(cc + 1) * 128])
        sk = inp["swa_sink"][l]
        for cc in range(2):
            put("sink", l, cc, np.repeat(sk[2 * cc:2 * cc + 2], 64))
            put("sinksw", l, cc, np.repeat(sk[2 * cc:2 * cc + 2][::-1], 64))
    for i in range(8):
        pv[:, L * PV_W + i] = inp["final_g"][i * 128:(i + 1) * 128]
    upw = np.zeros((128, L * 2 * 256), np.float32)
    for l in range(L):
        for d in range(2):
            upw[0:64, (l * 2 + d) * 256:(l * 2 + d + 1) * 256] = inp["rwkv_w_up"][l, d]
            upw[64:128, (l * 2 + d) * 256:(l * 2 + d + 1) * 256] = inp["rwkv_a_up"][l, d]
    gw = np.zeros((128, L * 8 * 128), np.float32)
    for l in range(L):
        for d in range(2):
            for k in range(2):
                for cc in range(2):
                    base = (((l * 2 + d) * 2 + k) * 2 + cc) * 128
                    for b in range(2):
                        gw[b * 64:(b + 1) * 64, base + b * 64:base + (b + 1) * 64] = inp["lru_gate_w"][l, d, k, 2 * cc + b]
    return pv, upw, gw


class SemCounter:
    def __init__(self, bld, name, step):
        self.bld, self.name, self.step = bld, name, step
        self.n = 0
        self._new()

    def _new(self):
        self.sem = self.bld.es.enter_context(self.bld.nc.semaphore(f"{self.name}_{self.n}"))
        self.sname = f"{self.name}_{self.n}"
        self.n += 1
        self.val = 0

    def next(self):
        if self.val + self.step > SEM_LIMIT:
            self._new()
        self.val += self.step
        return (self.sname, self.sem, self.val)


class Builder:
    def __init__(self, nc, es):
        self.nc, self.es = nc, es
        self.eng = {"pe": nc.tensor, "act": nc.scalar, "dve": nc.vector, "pool": nc.gpsimd, "sp": nc.sync}
        self.cnt = {e: SemCounter(self, e, 1) for e in ("pe", "act", "dve", "pool")}
        self.dcnt = {}
        self.waited = {e: {} for e in self.eng}
        self.last_w = {}
        self.readers = {}
        self.pending = {e: ([], []) for e in self.eng}
        self.n_ins = 0

    def _deps(self, reads, writes):
        toks = []
        for k in reads:
            t = self.last_w.get(k)
            if t is not None:
                toks.append(t)
        for k in writes:
            t = self.last_w.get(k)
            if t is not None:
                toks.append(t)
            r = self.readers.get(k)
            if r:
                toks.extend(r.values())
        return toks

    def _wait(self, e, toks):
        w = self.waited[e]
        need = {}
        for (sn, sem, val) in toks:
            if w.get(sn, 0) < val and need.get(sn, (None, 0))[1] < val:
                need[sn] = (sem, val)
        for sn, (sem, val) in need.items():
            self.eng[e].wait_ge(sem, val)
            w[sn] = val

    def _commit(self, tok, reads, writes):
        for k in writes:
            self.last_w[k] = tok
            self.readers[k] = {}
        for k in reads:
            self.readers.setdefault(k, {})[tok[0]] = tok

    def op(self, e, fn, reads=(), writes=(), inc=True):
        self._wait(e, self._deps(reads, writes))
        ins = fn(self.eng[e])
        self.n_ins += 1
        pr, pw = self.pending[e]
        if not inc:
            pr.extend(reads)
            pw.extend(writes)
            return
        tok = self.cnt[e].next()
        ins.then_inc(tok[1], 1)
        self._commit(tok, list(reads) + pr, list(writes) + pw)
        self.pending[e] = ([], [])

    def dma(self, out, in_, reads=(), writes=(), q="sp", semkey=None):
        self._wait(q, self._deps(reads, writes))
        ins = self.eng[q].dma_start(out=out, in_=in_)
        self.n_ins += 1
        if semkey not in self.dcnt:
            self.dcnt[semkey] = SemCounter(self, "d" + semkey, 16)
        tok = self.dcnt[semkey].next()
        ins.then_inc(tok[1], 16)
        self._commit(tok, reads, writes)
        return tok

    def barrier(self):
        toks = []
        for c in list(self.cnt.values()) + list(self.dcnt.values()):
            if c.val > 0:
                toks.append((c.sname, c.sem, c.val))
        for e in self.eng:
            self._wait(e, toks)

    def final_wait(self, e, keys):
        toks = []
        for k in keys:
            t = self.last_w.get(k)
            if t is not None:
                toks.append(t)
        self._wait(e, toks)


SCRW = 2048
NSCR = 8


def build(cfg):
    T, NSEQ, L = cfg.T, cfg.NSEQ, cfg.DEPTH
    NSEG, NT, NCH = T // 512, T // 128, T // 64
    assert T % 512 == 0 and T <= 2048
    nc = bass.Bass("TRN2", target_bir_lowering=False)

    def din(name, shape, dt=F32):
        return nc.dram_tensor(name, list(shape), dt, kind="ExternalInput").ap()

    x_d = din("x", [NSEQ * 1024, T])
    win_d = din("w_in", [L * 1024, D_IN])
    wout_d = din("w_out", [L * 1024, 1024])
    pv_d = din("pvec", [128, L * PV_W + 8])
    upw_d = din("upw", [128, L * 512])
    gw_d = din("gw", [128, L * 1024])
    cst_d = din("cst", [128, CST_W])
    alibi_d = din("alibi", [128, 1536])
    ropeC_d = din("ropeC", [128, T])
    ropeS_d = din("ropeS", [128, T])
    y_d = nc.dram_tensor("y", [NSEQ * 1024, T], F32, kind="ExternalOutput").ap()
    tap_d = {}
    for (tname, tshape) in cfg.taps:
        tap_d[tname] = nc.dram_tensor("tap_" + tname, list(tshape), F32, kind="ExternalOutput").ap()

    es = ExitStack()
    with es:
        B = Builder(nc, es)

        def sb(name, shape, dt):
            return es.enter_context(nc.sbuf_tensor(name, list(shape), dt))

        xT = sb("xT", [128, 8, T], F32)
        hT = sb("hT", [128, 8, max(T, 2048)], BF16)
        pvt = sb("pvt", [128, L * PV_W + 8], F32)
        dvt = sb("dvt", [128, L * DV_W], F32)
        CBW = CST_W
        cbf = sb("cbf", [128, CBW], BF16)
        swapf = sb("swapf", [128, 128], F32)
        upw_f = sb("upw_f", [128, 512], F32)
        upw_b = sb("upw_b", [128, 512], BF16)
        gw_b = sb("gw_b", [128, 1024], BF16)
        NWS = 3
        wst = [sb(f"wst{i}", [128, 8, 128], F32) for i in range(2)]
        wbf = [sb(f"wbf{i}", [128, 8, 128], BF16) for i in range(NWS)]
        wost = [sb(f"wost{i}", [128, 256], F32) for i in range(2)]
        wobf = [sb(f"wobf{i}", [128, 256], BF16) for i in range(2)]
        obuf = sb("obuf", [128, 2, T], BF16)
        sqb = sb("sqb", [128, 2, 512], BF16)
        rstd = sb("rstd", [128, 512], F32)
        epsb = sb("epsb", [128, 4], F32)
        tmpA = sb("tmpA", [128, 512], F32)
        tmpB = sb("tmpB", [128, 512], F32)
        scr = [sb(f"scr{i}", [128, SCRW], F32) for i in range(NSCR)]
        psum = es.enter_context(nc.psum_tensor("psum", [128, 8 * 512], F32))

        def sk(i):
            return ("scr", i)

        def sbf(i):
            return scr[i][:].bitcast(BF16)

        def PS(b, n=1):
            return psum[:, b * 512:(b + n) * 512]

        def psk(b, n=1):
            return [("ps", b + i) for i in range(n)]

        def cb(name, w=None, off=0):
            o = CST_OFF[name] + off
            return cbf[:, o:o + (w if w is not None else dict(CST_FIELDS)[name])]

        def pvc(name, l, i=0):
            c = pv_col(L, name, l, i)
            return pvt[:, c:c + 1]

        def dvc(name, l, i=0):
            c = dv_col(name, l, i)
            return dvt[:, c:c + 1]

        def tap(name, ap_sb, rkeys, rows=128):
            if name in tap_d:
                B.dma(tap_d[name], ap_sb, reads=rkeys, writes=[("tap", name)], q="pool", semkey="tap" + name)

        B.dma(scr[0][:, 0:CST_W], cst_d[:, :], writes=[sk(0)], q="pool", semkey="cst")
        B.dma(pvt[:], pv_d[:, :], writes=["pvt"], q="pool", semkey="pvt")
        B.op("dve", lambda e: e.tensor_copy(out=cbf[:], in_=scr[0][:, 0:CST_W]), reads=[sk(0)], writes=["cbf"])
        B.op("dve", lambda e: e.tensor_copy(out=swapf[:], in_=scr[0][:, CST_OFF["swap"]:CST_OFF["swap"] + 128]), reads=[sk(0)], writes=["swapf"])
        B.op("dve", lambda e: e.memset(epsb[:, 0:1], NORM_EPS), writes=["epsb"])
        B.op("dve", lambda e: e.memset(epsb[:, 1:2], 1.0), writes=["epsb"])
        B.op("dve", lambda e: e.memset(epsb[:, 2:3], GN_EPS), writes=["epsb"])
        B.op("dve", lambda e: e.memset(epsb[:, 3:4], 0.0), writes=["epsb"])
        for l in range(L):
            for i in range(7):
                B.op("dve", lambda e, l=l, i=i: e.tensor_tensor(out=dvc("c0", l, i), in0=pvc("sh0", l, i), in1=pvc("sh1", l, i), op=ALU.add),
                     reads=["pvt"], writes=["dvt"])
                B.op("dve", lambda e, l=l, i=i: e.tensor_scalar(out=dvc("c0", l, i), in0=dvc("c0", l, i), scalar1=-1.0, scalar2=1.0, op0=ALU.mult, op1=ALU.add),
                     reads=["dvt"], writes=["dvt"])
            for i in range(2):
                B.op("dve", lambda e, l=l, i=i: e.tensor_scalar(out=dvc("omka", l, i), in0=pvc("k_a", l, i), scalar1=-1.0, scalar2=1.0, op0=ALU.mult, op1=ALU.add),
                     reads=["pvt"], writes=["dvt"])
                B.op("act", lambda e, l=l, i=i: e.activation(out=dvc("esinksw", l, i), in_=pvc("sinksw", l, i), func=AF.Exp),
                     reads=["pvt"], writes=["dvt"])
            for i in range(4):
                B.op("act", lambda e, l=l, i=i: e.activation(out=dvc("cneg", l, i), in_=pvc("lam", l, i), func=AF.Exp, scale=-1.0),
                     reads=["pvt"], writes=["dvt"])
                B.op("act", lambda e, l=l, i=i: e.activation(out=dvc("cneg", l, i), in_=dvc("cneg", l, i), func=AF.Ln, bias=epsb[:, 1:2]),
                     reads=["dvt", "epsb"], writes=["dvt"])
                B.op("dve", lambda e, l=l, i=i: e.tensor_scalar(out=dvc("cneg", l, i), in0=dvc("cneg", l, i), scalar1=-8.0, scalar2=None, op0=ALU.mult),
                     reads=["dvt"], writes=["dvt"])

        wslot = [0]
        wstslot = [0]

        def load_win(l, ranges):
            si = wstslot[0] % 2
            wstslot[0] += 1
            bi = wslot[0] % NWS
            wslot[0] += 1
            off = 0
            src = win_d[l * 1024:(l + 1) * 1024, :].rearrange("(k p) c -> p k c", p=128)
            for (c0, n) in ranges:
                B.dma(wst[si][:, :, off:off + n], src[:, :, c0:c0 + n], writes=[f"wst{si}"], q="sp", semkey=f"wst{si}")
                off += n
            B.op("pool", lambda e: e.tensor_copy(out=wbf[bi][:, :, 0:off], in_=wst[si][:, :, 0:off]),
                 reads=[f"wst{si}"], writes=[f"wbf{bi}"])
            return bi, off

        def inproj(l, ranges, bank0):
            bi, m = load_win(l, ranges)
            for sg in range(NSEG):
                for k in range(8):
                    B.op("pe", lambda e, k=k, sg=sg: e.matmul(PS(bank0 + sg)[0:m, :], lhsT=wbf[bi][:, k, 0:m], rhs=hT[:, k, sg * 512:(sg + 1) * 512],
                                                              start=(k == 0), stop=(k == 7)),
                         reads=[f"wbf{bi}", "hT"], writes=psk(bank0 + sg), inc=(k == 7))
            return m

        def inproj_tok(l, ranges, dst_fn, post):
            bi, m = load_win(l, ranges)
            assert m == 64
            for tb in range((NT + 7) // 8):
                bank = 4 + tb % 2
                ntt = min(8, NT - tb * 8)
                for q in range(ntt):
                    tt = tb * 8 + q
                    for k in range(8):
                        B.op("pe", lambda e, k=k, tt=tt, q=q, bank=bank: e.matmul(PS(bank)[:, q * 64:(q + 1) * 64], lhsT=hT[:, k, tt * 128:(tt + 1) * 128],
                                                                                  rhs=wbf[bi][:, k, 0:64], start=(k == 0), stop=(k == 7)),
                             reads=[f"wbf{bi}", "hT"], writes=psk(bank), inc=(k == 7))
                post(tb, bank, ntt)

        wo_slot = [0]

        def outproj(l, g):
            for co in range(8):
                si = wo_slot[0] % 2
                wo_slot[0] += 1
                for kc in range(2):
                    r0 = l * 1024 + g * 256 + kc * 128
                    B.dma(wost[si][:, kc * 128:(kc + 1) * 128], wout_d[r0:r0 + 128, co * 128:(co + 1) * 128],
                          writes=[f"wost{si}"], q="sp", semkey=f"wost{si}")
                B.op("pool", lambda e, si=si: e.tensor_copy(out=wobf[si][:], in_=wost[si][:]),
                     reads=[f"wost{si}"], writes=[f"wobf{si}"])
                nb = min(2, NSEG)
                for half in range(NSEG // nb):
                    pb = 4 + 2 * ((co * (NSEG // nb) + half) % 2)
                    for j in range(nb):
                        sg = half * nb + j
                        for kc in range(2):
                            B.op("pe", lambda e, si=si, kc=kc, sg=sg, j=j, pb=pb: e.matmul(
                                PS(pb + j), lhsT=wobf[si][:, kc * 128:(kc + 1) * 128], rhs=obuf[:, kc, sg * 512:(sg + 1) * 512],
                                start=(kc == 0), stop=(kc == 1)),
                                reads=[f"wobf{si}", "obuf"], writes=psk(pb + j), inc=(kc == 1))
                    t0 = half * nb * 512
                    w = nb * 512
                    B.op("dve", lambda e, co=co, pb=pb, t0=t0, w=w, nb=nb: e.tensor_tensor(
                        out=xT[:, co, t0:t0 + w], in0=PS(pb, nb), in1=xT[:, co, t0:t0 + w], op=ALU.add),
                        reads=psk(pb, nb) + [("xT", co)], writes=[("xT", co)])

        def rmsnorm_to(l, dest_kind):
            for sg in range(NSEG):
                sl = slice(sg * 512, (sg + 1) * 512)
                for c in range(8):
                    B.op("act", lambda e, c=c: e.activation(out=sqb[:, c % 2, :], in_=xT[:, c, sl], func=AF.Square),
                         reads=[("xT", c)], writes=[("sqb", c % 2)])
                    B.op("pe", lambda e, c=c: e.matmul(PS(0), lhsT=cb("ones"), rhs=sqb[:, c % 2, :], start=(c == 0), stop=(c == 7)),
                         reads=[("sqb", c % 2), "cbf"], writes=psk(0), inc=True)
                B.op("act", lambda e: e.activation(out=rstd[:], in_=PS(0), func=AF.Sqrt, scale=1.0 / 1024, bias=epsb[:, 0:1]),
                     reads=psk(0) + ["epsb"], writes=["rstd"])
                B.op("dve", lambda e: e.reciprocal(out=rstd[:], in_=rstd[:]), reads=["rstd"], writes=["rstd"])
                for c in range(8):
                    if dest_kind == "h":
                        gcol = pvc("norm_g", l, c)
                        B.op("dve", lambda e, c=c, gcol=gcol: e.scalar_tensor_tensor(out=hT[:, c, sl], in0=xT[:, c, sl], scalar=gcol, in1=rstd[:],
                                                                                      op0=ALU.mult, op1=ALU.mult),
                             reads=[("xT", c), "rstd", "pvt"], writes=["hT"])
                    else:
                        gcol = pvt[:, L * PV_W + c:L * PV_W + c + 1]
                        B.op("dve", lambda e, c=c, gcol=gcol: e.scalar_tensor_tensor(out=xT[:, c, sl], in0=xT[:, c, sl], scalar=gcol, in1=rstd[:],
                                                                                      op0=ALU.mult, op1=ALU.mult),
                             reads=[("xT", c), "rstd", "pvt"], writes=[("xT", c)])

        def mixer_C(l):
            for cc in range(2):
                inproj(l, [(OFF_C + cc * 128, 128)], 0)
                Z = PS(0, NSEG)
                zk = psk(0, NSEG)
                xc = scr[0][:, 0:T]
                B.op("act", lambda e: e.activation(out=xc, in_=Z, func=AF.Identity, scale=pvc("cw", l, 2 * 2 + cc), bias=pvc("cb", l, cc)),
                     reads=zk + ["pvt"], writes=[sk(0)])
                for (j, sh) in ((0, -2), (1, -1), (3, 1)):
                    if sh < 0:
                        o_ap, i_ap = xc[:, -sh:T], Z[:, 0:T + sh]
                    else:
                        o_ap, i_ap = xc[:, 0:T - sh], Z[:, sh:T]
                    B.op("dve", lambda e, o_ap=o_ap, i_ap=i_ap, j=j: e.scalar_tensor_tensor(out=o_ap, in0=i_ap, scalar=pvc("cw", l, j * 2 + cc), in1=o_ap,
                                                                                            op0=ALU.mult, op1=ALU.add),
                         reads=zk + [sk(0), "pvt"], writes=[sk(0)])
                xcb = sbf(1)[:, 0:T]
                B.op("act", lambda e: e.activation(out=xcb, in_=xc, func=AF.Copy), reads=[sk(0)], writes=[sk(1)])
                for d in range(2):
                    rb, ib, sbuf_, hb = scr[2][:, 0:T], scr[3][:, 0:T], scr[4][:, 0:T], scr[5 + d][:, 0:T]
                    for sg in range(NSEG):
                        sl = slice(sg * 512, (sg + 1) * 512)
                        for k in range(2):
                            bank = 4 + 2 * k + sg % 2
                            wcol = (((d * 2 + k) * 2 + cc)) * 128
                            B.op("pe", lambda e, bank=bank, wcol=wcol, sl=sl: e.matmul(PS(bank), lhsT=gw_b[:, wcol:wcol + 128], rhs=xcb[:, sl], start=True, stop=True),
                                 reads=["gw_b", sk(1)], writes=psk(bank))
                            dst = rb if k == 0 else ib
                            B.op("act", lambda e, bank=bank, dst=dst, sl=sl, k=k: e.activation(out=dst[:, sl], in_=PS(bank), func=AF.Sigmoid,
                                                                                               bias=pvc("gb", l, (d * 2 + k) * 2 + cc)),
                                 reads=psk(bank) + ["pvt"], writes=[sk(2 + k)])
                    B.op("act", lambda e: e.activation(out=rb, in_=rb, func=AF.Exp, scale=dvc("cneg", l, d * 2 + cc)), reads=[sk(2), "dvt"], writes=[sk(2)])
                    B.op("act", lambda e: e.activation(out=sbuf_, in_=rb, func=AF.Square), reads=[sk(2)], writes=[sk(4)])
                    B.op("act", lambda e: e.activation(out=sbuf_, in_=sbuf_, func=AF.Sqrt, scale=-1.0, bias=epsb[:, 1:2]), reads=[sk(4), "epsb"], writes=[sk(4)])
                    B.op("dve", lambda e: e.tensor_tensor(out=ib, in0=ib, in1=xc, op=ALU.mult), reads=[sk(3), sk(0)], writes=[sk(3)])
                    B.op("dve", lambda e: e.tensor_tensor(out=ib, in0=ib, in1=sbuf_, op=ALU.mult), reads=[sk(3), sk(4)], writes=[sk(3)])
                    if d == 0:
                        B.op("dve", lambda e: e.tensor_tensor_scan(out=hb, data0=rb, data1=ib, initial=0.0, op0=ALU.mult, op1=ALU.add),
                             reads=[sk(2), sk(3)], writes=[sk(5 + d)])
                    else:
                        B.op("dve", lambda e: e.tensor_tensor_scan(out=hb[:, ::-1], data0=rb[:, ::-1], data1=ib[:, ::-1], initial=0.0, op0=ALU.mult, op1=ALU.add),
                             reads=[sk(2), sk(3)], writes=[sk(5 + d)])
                hf, hbk = scr[5][:, 0:T], scr[6][:, 0:T]
                B.op("dve", lambda e: e.tensor_tensor(out=hf, in0=hf, in1=hbk, op=ALU.add), reads=[sk(5), sk(6)], writes=[sk(5)])
                inproj(l, [(OFF_C + 256 + cc * 128, 128)], 0)
                sgt = scr[2][:, 0:T]
                B.op("act", lambda e: e.activation(out=sgt, in_=PS(0, NSEG), func=AF.Silu), reads=psk(0, NSEG), writes=[sk(2)])
                B.op("dve", lambda e: e.tensor_tensor(out=obuf[:, cc, :], in0=hf, in1=sgt, op=ALU.mult), reads=[sk(5), sk(2)], writes=["obuf"])

        def load_qk(l, off_q, off_k, cc, prep):
            qT = sbf(0)[:, 0:T]
            kT = sbf(0)[:, SCRW:SCRW + T]
            prep([(off_q + cc * 128, 128)], qT, True)
            prep([(off_k + cc * 64, 64), (off_k + cc * 64, 64)], kT, False)
            return qT, kT

        def load_vaug(l, off_v, cc):
            va = sbf(1)
            vav = va[:, 0:NT * 192].rearrange("p (t c) -> p t c", c=192)
            B.op("pool", lambda e: e.memset(vav[:, :, 64:128], 1.0), writes=[sk(1)])

            def post(tb, bank, ntt):
                src = PS(bank)[:, 0:ntt * 64].rearrange("p (t c) -> p t c", c=64)
                B.op("act", lambda e: e.activation(out=vav[:, tb * 8:tb * 8 + ntt, 0:64], in_=src, func=AF.Copy), reads=psk(bank), writes=[sk(1)])
                B.op("dve", lambda e: e.tensor_copy(out=vav[:, tb * 8:tb * 8 + ntt, 128:192], in_=src), reads=psk(bank), writes=[sk(1)])
            inproj_tok(l, [(off_v + cc * 64, 64)], None, post)
            return vav

        def load_gate(l, off_g, cc):
            inproj(l, [(off_g + cc * 128, 128)], 0)
            sgt = scr[2][:, 0:T]
            B.op("act", lambda e: e.activation(out=sgt, in_=PS(0, NSEG), func=AF.Silu), reads=psk(0, NSEG), writes=[sk(2)])
            return sgt

        def attn_post(l, cc, qg, bx, by, sgt, sink):
            X, Y = PS(bx), PS(by)
            gsl = slice(qg * 512, (qg + 1) * 512)
            if sink:
                B.op("dve", lambda e: e.tensor_scalar(out=tmpA[0:64, :], in0=Y[0:64, :], scalar1=dvc("esinksw", l, cc)[0:64, :], scalar2=None, op0=ALU.add),
                     reads=psk(by) + ["dvt"], writes=["tmpA"])
                B.op("dve", lambda e: e.tensor_scalar(out=tmpA[64:128, :], in0=X[64:128, :], scalar1=dvc("esinksw", l, cc)[64:128, :], scalar2=None, op0=ALU.add),
                     reads=psk(bx) + ["dvt"], writes=["tmpA"])
                B.op("dve", lambda e: e.reciprocal(out=tmpA[:], in_=tmpA[:]), reads=["tmpA"], writes=["tmpA"])
            else:
                B.op("dve", lambda e: e.reciprocal(out=tmpA[0:64, :], in_=Y[0:64, :]), reads=psk(by), writes=["tmpA"])
                B.op("dve", lambda e: e.reciprocal(out=tmpA[64:128, :], in_=X[64:128, :]), reads=psk(bx), writes=["tmpA"])
            B.op("pe", lambda e: e.matmul(PS(0), lhsT=swapf[:], rhs=tmpA[:], start=True, stop=True), reads=["swapf", "tmpA"], writes=psk(0))
            B.op("dve", lambda e: e.tensor_tensor(out=tmpB[:], in0=PS(0), in1=sgt[:, gsl], op=ALU.mult), reads=psk(0) + [sk(2)], writes=["tmpB"])
            B.op("dve", lambda e: e.tensor_tensor(out=obuf[0:64, cc, gsl], in0=X[0:64, :], in1=tmpB[0:64, :], op=ALU.mult),
                 reads=psk(bx) + ["tmpB"], writes=["obuf"])
            B.op("dve", lambda e: e.tensor_tensor(out=obuf[64:128, cc, gsl], in0=Y[64:128, :], in1=tmpB[64:128, :], op=ALU.mult),
                 reads=psk(by) + ["tmpB"], writes=["obuf"])

        def mixer_D(l):
            B.dma(scr[3][:, 0:1536], alibi_d[:, :], writes=[sk(3)], q="pool", semkey="alibi")

            def prep(ranges, dst, isq):
                inproj(l, ranges, 0)
                B.op("act", lambda e: e.activation(out=dst, in_=PS(0, NSEG), func=AF.Copy), reads=psk(0, NSEG), writes=[sk(0)])
            for cc in range(2):
                qT, kT = load_qk(l, OFF_D, OFF_D + 256, cc, prep)
                vav = load_vaug(l, OFF_D + 384, cc)
                sgt = load_gate(l, OFF_D + 512, cc)
                ptv = sbf(4)
                exv = sbf(5)
                atab = scr[3][:, 2 * cc * 384:(2 * cc + 2) * 384].rearrange("p (h c) -> p h c", h=2)

                def qrange(j):
                    return max(j - 1, 0), min(j + 1, NT - 1)

                def qk(j):
                    b0, b1 = qrange(j)
                    ncol = (b1 - b0 + 1) * 128
                    for hh in range(2):
                        ph = slice(hh * 64, hh * 64 + 64)
                        bank = 2 * (j % 2) + hh
                        B.op("pe", lambda e: e.matmul(PS(bank)[:, 0:ncol], lhsT=kT[ph, j * 128:(j + 1) * 128], rhs=qT[ph, b0 * 128:b0 * 128 + ncol], start=True, stop=True),
                             reads=[sk(0)], writes=psk(bank), inc=(hh == 1))

                def ex(j):
                    b0, b1 = qrange(j)
                    ncol = (b1 - b0 + 1) * 128
                    tcol0 = (b0 - (j - 1)) * 128
                    src = PS(2 * (j % 2), 2).rearrange("p (b c) -> p b c", b=2)[:, :, 0:ncol]
                    exs = exv[:, (j % 2) * 768:(j % 2 + 1) * 768].rearrange("p (h c) -> p h c", h=2)[:, :, 0:ncol]
                    pts = ptv[:, (j % 4) * 768:(j % 4 + 1) * 768].rearrange("p (h c) -> p h c", h=2)[:, :, 0:ncol]
                    B.op("act", lambda e: e.activation(out=exs, in_=src, func=AF.Exp, scale=0.125), reads=psk(2 * (j % 2), 2), writes=[("ex", j % 2)])
                    B.op("dve", lambda e: e.tensor_tensor(out=pts, in0=exs, in1=atab[:, :, tcol0:tcol0 + ncol], op=ALU.mult),
                         reads=[("ex", j % 2), sk(3)], writes=[("pt", j % 4)])

                def pv_block(i):
                    qg = i // 4
                    js = [j for j in (i - 1, i, i + 1) if 0 <= j < NT]
                    for hh in range(2):
                        bank = 4 + 2 * (qg % 2) + hh
                        for n, j in enumerate(js):
                            b0, _ = qrange(j)
                            c0 = (j % 4) * 768 + hh * 384 + (i - b0) * 128
                            B.op("pe", lambda e: e.matmul(PS(bank)[:, (i % 4) * 128:(i % 4 + 1) * 128], lhsT=vav[:, j, hh * 64:hh * 64 + 128], rhs=ptv[:, c0:c0 + 128],
                                                         start=(n == 0), stop=(n == len(js) - 1)),
                                 reads=[sk(1), ("pt", j % 4)], writes=psk(bank), inc=(n == len(js) - 1))

                qk(0)
                for j in range(NT):
                    if j + 1 < NT:
                        qk(j + 1)
                    ex(j)
                    blocks = []
                    if j >= 1:
                        blocks.append(j - 1)
                    if j == NT - 1:
                        blocks.append(j)
                    for i in blocks:
                        pv_block(i)
                        if i % 4 == 3:
                            qg = i // 4
                            attn_post(l, cc, qg, 4 + 2 * (qg % 2), 4 + 2 * (qg % 2) + 1, sgt, True)

        def mixer_B(l):
            B.dma(scr[6][:, 0:T], ropeC_d[:, :], writes=[sk(6)], q="pool", semkey="ropeC")
            B.dma(scr[7][:, 0:T], ropeS_d[:, :], writes=[sk(7)], q="pool", semkey="ropeS")
            t1b, t2b = scr[3][:, 0:512], scr[3][:, 512:1024]
            qgb = sbf(3)[:, 2048:2560]

            def prep(ranges, dst, isq):
                inproj(l, ranges, 0)
                gcol = pvc("qn" if isq else "kn", l, 0)
                for sg in range(NSEG):
                    sl = slice(sg * 512, (sg + 1) * 512)
                    Z = PS(sg)
                    B.op("act", lambda e: e.activation(out=qgb, in_=Z, func=AF.Copy, scale=gcol), reads=psk(sg) + ["pvt"], writes=[("qgb",)])
                    B.op("act", lambda e: e.activation(out=sqb[:, 0, :], in_=Z, func=AF.Square), reads=psk(sg), writes=[("sqb", 0)])
                    B.op("pe", lambda e: e.matmul(PS(4), lhsT=cb("blk64"), rhs=sqb[:, 0, :], start=True, stop=True), reads=[("sqb", 0), "cbf"], writes=psk(4))
                    B.op("pe", lambda e: e.matmul(PS(5), lhsT=cb("perm"), rhs=qgb, start=True, stop=True), reads=[("qgb",), "cbf"], writes=psk(5))
                    B.op("act", lambda e: e.activation(out=rstd[:], in_=PS(4), func=AF.Sqrt, scale=1.0 / 64, bias=epsb[:, 0:1]), reads=psk(4) + ["epsb"], writes=["rstd"])
                    B.op("dve", lambda e: e.reciprocal(out=rstd[:], in_=rstd[:]), reads=["rstd"], writes=["rstd"])
                    B.op("dve", lambda e: e.tensor_tensor(out=t1b, in0=qgb, in1=scr[6][:, sl], op=ALU.mult), reads=[("qgb",), sk(6)], writes=[("t1b",)])
                    B.op("dve", lambda e: e.tensor_tensor(out=t2b, in0=PS(5), in1=scr[7][:, sl], op=ALU.mult), reads=psk(5) + [sk(7)], writes=[("t2b",)])
                    B.op("dve", lambda e: e.tensor_tensor(out=t1b, in0=t1b, in1=t2b, op=ALU.add), reads=[("t1b",), ("t2b",)], writes=[("t1b",)])
                    B.op("dve", lambda e: e.tensor_tensor(out=dst[:, sl], in0=t1b, in1=rstd[:], op=ALU.mult), reads=[("t1b",), "rstd"], writes=[sk(0)])
            for cc in range(2):
                qT, kT = load_qk(l, OFF_B, OFF_B + 256, cc, prep)
                vav = load_vaug(l, OFF_B + 384, cc)
                sgt = load_gate(l, OFF_B + 512, cc)
                ptv = sbf(4)
                for qg in range(NSEG):
                    gsl = slice(qg * 512, (qg + 1) * 512)
                    ab = 4 + 2 * (qg % 2)

                    def qk(j):
                        for hh in range(2):
                            ph = slice(hh * 64, hh * 64 + 64)
                            bank = 2 * (j % 2) + hh
                            B.op("pe", lambda e: e.matmul(PS(bank), lhsT=kT[ph, j * 128:(j + 1) * 128], rhs=qT[ph, gsl], start=True, stop=True),
                                 reads=[sk(0)], writes=psk(bank), inc=(hh == 1))

                    def ex(j):
                        slot = j % 3
                        B.op("act", lambda e: e.activation(out=ptv[:, slot * 1024:(slot + 1) * 1024], in_=PS(2 * (j % 2), 2), func=AF.Exp, scale=0.125),
                             reads=psk(2 * (j % 2), 2), writes=[("pt", slot)])

                    def pv(j):
                        slot = j % 3
                        for hh in range(2):
                            B.op("pe", lambda e: e.matmul(PS(ab + hh), lhsT=vav[:, j, hh * 64:hh * 64 + 128], rhs=ptv[:, slot * 1024 + hh * 512:slot * 1024 + (hh + 1) * 512],
                                                         start=(j == 0), stop=(j == NT - 1)),
                                 reads=[sk(1), ("pt", slot)], writes=psk(ab + hh), inc=(hh == 1))
                    qk(0)
                    for j in range(NT):
                        if j + 1 < NT:
                            qk(j + 1)
                        ex(j)
                        pv(j)
                    attn_post(l, cc, qg, ab, ab + 1, sgt, False)

        def mixer_A(l):
            SA = min(512, T)
            NS = T // SA
            nch = SA // 64
            arena_b = hT[:].rearrange("p k t -> p (k t)")

            def AF32(off, w):
                return arena_b[:, 2 * off:2 * (off + w)].bitcast(F32)

            def ABF(off, w):
                return arena_b[:, 2 * off:2 * off + w]

            rB = [sbf(0)[:, 0:T], sbf(0)[:, SCRW:SCRW + T]]
            kB = [sbf(1)[:, 0:T], sbf(1)[:, SCRW:SCRW + T]]
            vB = [sbf(2)[:, 0:T], sbf(2)[:, SCRW:SCRW + T]]
            waB = sbf(3)[:, 0:T]
            kkB = [sbf(4)[:, 0:T], sbf(4)[:, SCRW:SCRW + T]]
            yfB = [sbf(5)[:, 0:T], sbf(5)[:, SCRW:SCRW + T]]
            bcB = [sbf(6)[:, 0:T], sbf(6)[:, SCRW:SCRW + T]]
            Xt = scr[7][:, 0:T]
            for ci in range(7):
                inproj(l, [(OFF_A + ci * 128, 128)], 0)
                Z = PS(0, NSEG)
                zk = psk(0, NSEG)
                B.op("act", lambda e: e.activation(out=Xt, in_=Z, func=AF.Copy, scale=dvc("c0", l, ci)), reads=zk + ["dvt"], writes=[sk(7)])
                B.op("dve", lambda e: e.scalar_tensor_tensor(out=Xt[:, 1:T], in0=Z[:, 0:T - 1], scalar=pvc("sh0", l, ci), in1=Xt[:, 1:T], op0=ALU.mult, op1=ALU.add),
                     reads=zk + [sk(7), "pvt"], writes=[sk(7)])
                if ci < 6:
                    dst = (rB, kB, vB)[ci // 2][ci % 2]
                    dk_ = sk(ci // 2)
                    B.op("dve", lambda e: e.scalar_tensor_tensor(out=dst[:, 0:T - 1], in0=Z[:, 1:T], scalar=pvc("sh1", l, ci), in1=Xt[:, 0:T - 1], op0=ALU.mult, op1=ALU.add),
                         reads=zk + [sk(7), "pvt"], writes=[dk_])
                    B.op("dve", lambda e: e.tensor_copy(out=dst[:, T - 1:T], in_=Xt[:, T - 1:T]), reads=[sk(7)], writes=[dk_])
                else:
                    B.op("dve", lambda e: e.scalar_tensor_tensor(out=Xt[:, 0:T - 1], in0=Z[:, 1:T], scalar=pvc("sh1", l, ci), in1=Xt[:, 0:T - 1], op0=ALU.mult, op1=ALU.add),
                         reads=zk + [sk(7), "pvt"], writes=[sk(7)])
                    B.op("act", lambda e: e.activation(out=waB[0:64, :], in_=Xt[0:64, :], func=AF.Tanh), reads=[sk(7)], writes=[sk(3)])
                    B.op("act", lambda e: e.activation(out=waB[64:128, :], in_=Xt[64:128, :], func=AF.Copy), reads=[sk(7)], writes=[sk(3)])
            for hp in range(2):
                for sg in range(NSEG):
                    sl = slice(sg * 512, (sg + 1) * 512)
                    B.op("dve", lambda e: e.tensor_scalar(out=tmpA[:], in0=kB[hp][:, sl], scalar1=pvc("k_k", l, hp), scalar2=None, op0=ALU.mult),
                         reads=[sk(1), "pvt"], writes=["tmpA"])
                    B.op("act", lambda e: e.activation(out=sqb[:, 0, :], in_=tmpA[:], func=AF.Square), reads=["tmpA"], writes=[("sqb", 0)])
                    B.op("pe", lambda e: e.matmul(PS(4), lhsT=cb("blk64"), rhs=sqb[:, 0, :], start=True, stop=True), reads=[("sqb", 0), "cbf"], writes=psk(4))
                    B.op("act", lambda e: e.activation(out=rstd[:], in_=PS(4), func=AF.Sqrt), reads=psk(4), writes=["rstd"])
                    B.op("dve", lambda e: e.tensor_scalar(out=rstd[:], in0=rstd[:], scalar1=1e-12, scalar2=None, op0=ALU.max), reads=["rstd"], writes=["rstd"])
                    B.op("dve", lambda e: e.reciprocal(out=rstd[:], in_=rstd[:]), reads=["rstd"], writes=["rstd"])
                    B.op("dve", lambda e: e.tensor_tensor(out=kkB[hp][:, sl], in0=tmpA[:], in1=rstd[:], op=ALU.mult), reads=["tmpA", "rstd"], writes=[sk(4)])
            for cc in range(2):
                inproj(l, [(OFF_A + 896 + cc * 128, 128)], 0)
                B.op("act", lambda e: e.activation(out=obuf[:, cc, :], in_=PS(0, NSEG), func=AF.Silu), reads=psk(0, NSEG), writes=["obuf"])
            B.barrier()
            o = 0
            Fb = []
            for i in range(5):
                Fb.append(AF32(o, SA)); o += SA
            H0 = AF32(o, SA); o += SA
            H1 = AF32(o, SA); o += SA
            H2 = ABF(o, SA); o += SA // 2
            BK = ABF(o, 2 * SA); o += SA
            ARb = ABF(o, 2 * SA); o += SA
            HK = ABF(o, 2 * SA); o += SA
            Gm = ABF(o, 4 * SA); o += 2 * SA
            GT = ABF(o, SA); o += SA // 2
            Xs, XTs, Ps_ = [], [], []
            for i in range(2):
                Xs.append(ABF(o, SA)); o += SA // 2
                XTs.append(ABF(o, SA)); o += SA // 2
                Ps_.append(ABF(o, SA)); o += SA // 2
            assert o <= 8192, o
            s7 = sbf(7)
            VT, BhT, KhT = s7[:, 0:SA], s7[:, SA:2 * SA], s7[:, 2 * SA:3 * SA]
            Q1 = scr[7][:, 768:768 + SA]
            Qb = s7[:, 2560:2624]
            UTb = [s7[:, 2624:2688], s7[:, 2688:2752]]
            STf = [[scr[7][:, 1408 + (hp * 2 + i) * 64:1408 + (hp * 2 + i + 1) * 64] for i in range(2)] for hp in range(2)]
            STb = [[s7[:, 3328 + (hp * 2 + i) * 64:3328 + (hp * 2 + i + 1) * 64] for i in range(2)] for hp in range(2)]
            id64 = s7[:, 3600:3664]
            blkf = scr[7][:, 1856:1984]
            F0, F1, F2, F3, F4 = Fb
            B.op("dve", lambda e: e.tensor_tensor(out=id64, in0=cb("mfwd", 64, 64), in1=cb("mfwd", 64, 0), op=ALU.subtract), reads=["cbf"], writes=["id64"])
            B.op("dve", lambda e: e.tensor_copy(out=blkf, in_=cb("blk64")), reads=["cbf"], writes=["blkf"])
            BK4 = BK.rearrange("p (c two j) -> p c two j", two=2, j=64)
            AR4 = ARb.rearrange("p (c two j) -> p c two j", two=2, j=64)
            HK4 = HK.rearrange("p (c two j) -> p c two j", two=2, j=64)
            Gm3 = Gm.rearrange("p (c x) -> p c x", x=256)

            def v3(ap):
                return ap.rearrange("p (c j) -> p c j", j=64)

            def hs(hd):
                return slice(hd * 64, hd * 64 + 64)

            for d in range(2):
                mname = "mfwd" if d == 0 else "mbwd"
                mTname = "mfwdT" if d == 0 else "mbwdT"
                endc = 63 if d == 0 else 0
                cur = [0, 0]
                for hp in range(2):
                    B.op("dve", lambda e: e.memset(STf[hp][0], 0.0), writes=[("STf", hp, 0)])
                    B.op("dve", lambda e: e.memset(STb[hp][0], 0.0), writes=[("STb", hp, 0)])
                seg_order = range(NS) if d == 0 else range(NS - 1, -1, -1)
                for sg in seg_order:
                    ssl = slice(sg * SA, (sg + 1) * SA)
                    for hp in range(2):
                        wc = d * 256 + hp * 128
                        B.op("pe", lambda e: e.matmul(PS(0)[:, 0:SA], lhsT=upw_b[0:64, wc:wc + 128], rhs=waB[0:64, ssl], start=True, stop=True),
                             reads=["upw_b", sk(3)], writes=psk(0))
                        B.op("pe", lambda e: e.matmul(PS(1)[:, 0:SA], lhsT=upw_b[64:128, wc:wc + 128], rhs=waB[64:128, ssl], start=True, stop=True),
                             reads=["upw_b", sk(3)], writes=psk(1))
                        B.op("act", lambda e: e.activation(out=F0, in_=PS(0)[:, 0:SA], func=AF.Sigmoid, bias=pvc("w0", l, d * 2 + hp)), reads=psk(0) + ["pvt"], writes=["F0"])
                        B.op("act", lambda e: e.activation(out=F1, in_=PS(1)[:, 0:SA], func=AF.Sigmoid, bias=pvc("a0", l, d * 2 + hp)), reads=psk(1) + ["pvt"], writes=["F1"])
                        rst = cb("rst", SA)
                        if d == 0:
                            B.op("dve", lambda e: e.tensor_tensor_scan(out=F2, data0=rst, data1=F0, initial=0.0, op0=ALU.mult, op1=ALU.add), reads=["cbf", "F0"], writes=["F2"])
                        else:
                            B.op("dve", lambda e: e.tensor_tensor_scan(out=F2[:, ::-1], data0=rst, data1=F0[:, ::-1], initial=0.0, op0=ALU.mult, op1=ALU.add),
                                 reads=["cbf", "F0"], writes=["F2"])
                        B.op("dve", lambda e: e.tensor_tensor(out=F3, in0=F2, in1=F0, op=ALU.subtract), reads=["F2", "F0"], writes=["F3"])
                        endb = v3(F2)[:, :, endc:endc + 1].broadcast_to([128, nch, 64])
                        B.op("dve", lambda e: e.tensor_tensor(out=v3(F0), in0=endb, in1=v3(F2), op=ALU.subtract), reads=["F2"], writes=["F0"])
                        B.op("act", lambda e: e.activation(out=F4, in_=F2, func=AF.Exp, scale=LAMW), reads=["F2"], writes=["F4"])
                        B.op("act", lambda e: e.activation(out=F2, in_=F2, func=AF.Exp, scale=-LAMW), reads=["F2"], writes=["F2"])
                        B.op("act", lambda e: e.activation(out=F3, in_=F3, func=AF.Exp, scale=-LAMW), reads=["F3"], writes=["F3"])
                        B.op("act", lambda e: e.activation(out=F0, in_=F0, func=AF.Exp, scale=-LAMW), reads=["F0"], writes=["F0"])
                        kk_, r_, k_, v_ = kkB[hp][:, ssl], rB[hp][:, ssl], kB[hp][:, ssl], vB[hp][:, ssl]
                        B.op("dve", lambda e: e.tensor_tensor(out=H0, in0=kk_, in1=F1, op=ALU.mult), reads=[sk(4), "F1"], writes=["H0"])
                        B.op("dve", lambda e: e.tensor_scalar(out=F1, in0=F1, scalar1=pvc("k_a", l, hp), scalar2=dvc("omka", l, hp), op0=ALU.mult, op1=ALU.add),
                             reads=["F1", "pvt", "dvt"], writes=["F1"])
                        B.op("dve", lambda e: e.tensor_tensor(out=H1, in0=F1, in1=k_, op=ALU.mult), reads=["F1", sk(1)], writes=["H1"])
                        B.op("dve", lambda e: e.tensor_tensor(out=BK4[:, :, 0, :], in0=v3(H0), in1=v3(F4), op=ALU.mult), reads=["H0", "F4"], writes=["BK"])
                        B.op("dve", lambda e: e.tensor_tensor(out=BK4[:, :, 1, :], in0=v3(H1), in1=v3(F4), op=ALU.mult), reads=["H1", "F4"], writes=["BK"])
                        B.op("dve", lambda e: e.scalar_tensor_tensor(out=AR4[:, :, 0, :], in0=v3(kk_), scalar=-1.0, in1=v3(F3), op0=ALU.mult, op1=ALU.mult),
                             reads=[sk(4), "F3"], writes=["AR"])
                        B.op("dve", lambda e: e.tensor_tensor(out=AR4[:, :, 1, :], in0=v3(r_), in1=v3(F2), op=ALU.mult), reads=[sk(0), "F2"], writes=["AR"])
                        B.op("dve", lambda e: e.tensor_tensor(out=HK4[:, :, 0, :], in0=v3(H0), in1=v3(F0), op=ALU.mult), reads=["H0", "F0"], writes=["HK"])
                        B.op("dve", lambda e: e.tensor_tensor(out=HK4[:, :, 1, :], in0=v3(H1), in1=v3(F0), op=ALU.mult), reads=["H1", "F0"], writes=["HK"])
                        B.op("dve", lambda e: e.scalar_tensor_tensor(out=H2, in0=r_, scalar=pvc("r_k", l, hp), in1=H1, op0=ALU.mult, op1=ALU.mult),
                             reads=[sk(0), "pvt", "H1"], writes=["H2"])
                        B.op("pe", lambda e: e.matmul(PS(2)[:, 0:SA], lhsT=cb("blk64"), rhs=H2, start=True, stop=True), reads=["cbf", "H2"], writes=psk(2))
                        if d == 0:
                            B.op("act", lambda e: e.activation(out=bcB[hp][:, ssl], in_=PS(2)[:, 0:SA], func=AF.Copy), reads=psk(2), writes=[sk(6)])
                        else:
                            B.op("dve", lambda e: e.tensor_tensor(out=bcB[hp][:, ssl], in0=PS(2)[:, 0:SA], in1=bcB[hp][:, ssl], op=ALU.add), reads=psk(2) + [sk(6)], writes=[sk(6)])
                        Gps = PS(4, 4)
                        n_items = 2 * nch
                        it = 0
                        for c in range(nch):
                            for hd in range(2):
                                it += 1
                                last = (it == n_items)
                                B.op("pe", lambda e: e.matmul(Gps[hs(hd), c * 256:c * 256 + 128], lhsT=BK4[hs(hd), c, 0, :], rhs=AR4[hs(hd), c, :, :], start=True, stop=True),
                                     reads=["BK", "AR"], writes=psk(4, 4), inc=False)
                                B.op("pe", lambda e: e.matmul(Gps[hs(hd), c * 256 + 128:c * 256 + 256], lhsT=BK4[hs(hd), c, 1, :], rhs=AR4[hs(hd), c, :, :], start=True, stop=True),
                                     reads=["BK", "AR"], writes=psk(4, 4), inc=False)
                                B.op("pe", lambda e: e.matmul(PS(3)[hs(hd), c * 64:(c + 1) * 64], lhsT=AR4[hs(hd), c, 0, :], rhs=BK4[hs(hd), c, 0, :], start=True, stop=True),
                                     reads=["BK", "AR"], writes=psk(3), inc=last)
                        mk = cb(mname).rearrange("p (o x) -> p o x", o=1).broadcast_to([128, nch, 256])
                        B.op("dve", lambda e: e.tensor_tensor(out=Gm3, in0=Gps[:, 0:nch * 256].rearrange("p (c x) -> p c x", x=256), in1=mk, op=ALU.mult),
                             reads=psk(4, 4) + ["cbf"], writes=["Gm"])
                        mkT = cb(mTname).rearrange("p (o x) -> p o x", o=1).broadcast_to([128, nch, 64])
                        B.op("dve", lambda e: e.tensor_tensor(out=v3(GT), in0=v3(PS(3)[:, 0:SA]), in1=mkT, op=ALU.mult), reads=psk(3) + ["cbf"], writes=["GT"])
                        idb = id64.rearrange("p (o x) -> p o x", o=1).broadcast_to([128, nch, 64])
                        B.op("dve", lambda e: e.tensor_tensor(out=v3(Ps_[0]), in0=Gm3[:, :, 0:64], in1=idb, op=ALU.add), reads=["Gm", "id64"], writes=[("P", 0)])

                        def Xap(level, buf, hd, c):
                            if level == 0:
                                return Gm[hs(hd), c * 256:c * 256 + 64]
                            return Xs[buf][hs(hd), c * 64:(c + 1) * 64]

                        def XTap(level, buf, hd, c):
                            if level == 0:
                                return GT[hs(hd), c * 64:(c + 1) * 64]
                            return XTs[buf][hs(hd), c * 64:(c + 1) * 64]
                        pcur = 0
                        for lev in range(1, 6):
                            src_b, dst_b = (lev - 1) % 2, lev % 2
                            xk_src = ["Gm"] if lev == 1 else [("X", src_b)]
                            xtk_src = ["GT"] if lev == 1 else [("XT", src_b)]
                            it = 0
                            for c in range(nch):
                                for hd in range(2):
                                    it += 1
                                    last = (it == n_items)
                                    if lev < 5:
                                        B.op("pe", lambda e: e.matmul(PS(0)[hs(hd), c * 64:(c + 1) * 64], lhsT=XTap(lev - 1, src_b, hd, c), rhs=Xap(lev - 1, src_b, hd, c), start=True, stop=True),
                                             reads=xk_src + xtk_src, writes=psk(0), inc=False)
                                    B.op("pe", lambda e: e.matmul(PS(1)[hs(hd), c * 64:(c + 1) * 64], lhsT=Xap(lev - 1, src_b, hd, c), rhs=XTap(lev - 1, src_b, hd, c), start=True, stop=True),
                                         reads=xk_src + xtk_src, writes=psk(1) + (psk(0) if lev < 5 else []), inc=last)
                            if lev < 5:
                                B.op("act", lambda e: e.activation(out=Xs[dst_b], in_=PS(0)[:, 0:SA], func=AF.Copy), reads=psk(0), writes=[("X", dst_b)])
                            B.op("dve", lambda e: e.tensor_copy(out=XTs[dst_b], in_=PS(1)[:, 0:SA]), reads=psk(1), writes=[("XT", dst_b)])
                            it = 0
                            for c in range(nch):
                                for hd in range(2):
                                    it += 1
                                    last = (it == n_items)
                                    B.op("pe", lambda e: e.matmul(PS(2)[hs(hd), c * 64:(c + 1) * 64], lhsT=XTs[dst_b][hs(hd), c * 64:(c + 1) * 64], rhs=Ps_[pcur][hs(hd), c * 64:(c + 1) * 64], start=True, stop=True),
                                         reads=[("XT", dst_b), ("P", pcur)], writes=psk(2), inc=last)
                            B.op("dve", lambda e: e.tensor_tensor(out=Ps_[1 - pcur], in0=PS(2)[:, 0:SA], in1=Ps_[pcur], op=ALU.add), reads=psk(2) + [("P", pcur)], writes=[("P", 1 - pcur)])
                            pcur = 1 - pcur
                        Tm = Ps_[pcur]
                        tkey = ("P", pcur)
                        for (srcf, dstb, dkey, rk_, bank) in ((lambda c, hd: v_[hs(hd), c * 64:(c + 1) * 64], VT, "VT", [sk(2)], 0),
                                                               (lambda c, hd: HK4[hs(hd), c, 0, :], BhT, "BhT", ["HK"], 1),
                                                               (lambda c, hd: HK4[hs(hd), c, 1, :], KhT, "KhT", ["HK"], 3)):
                            pb = PS(bank).bitcast(BF16)
                            it = 0
                            for c in range(nch):
                                for hd in range(2):
                                    it += 1
                                    B.op("pe", lambda e: e.transpose(pb[hs(hd), c * 64:(c + 1) * 64], srcf(c, hd), cb("ident")[hs(hd), hd * 64:hd * 64 + 64]),
                                         reads=rk_ + ["cbf"], writes=psk(bank), inc=(it == n_items))
                            B.op("act", lambda e: e.activation(out=dstb, in_=pb[:, 0:SA], func=AF.Copy), reads=psk(bank), writes=[dkey])
                        it = 0
                        for c in range(nch):
                            for hd in range(2):
                                it += 1
                                B.op("pe", lambda e: e.matmul(PS(2)[hs(hd), c * 64:(c + 1) * 64], lhsT=Gm[hs(hd), c * 256 + 128:c * 256 + 192], rhs=VT[hs(hd), c * 64:(c + 1) * 64], start=True, stop=True),
                                     reads=["Gm", "VT"], writes=psk(2), inc=(it == n_items))
                        B.op("dve", lambda e: e.tensor_copy(out=Q1, in_=PS(2)[:, 0:SA]), reads=psk(2), writes=["Q1"])
                        corder = range(nch) if d == 0 else range(nch - 1, -1, -1)
                        for ci_, c in enumerate(corder):
                            cu = cur[hp]
                            nx = 1 - cu
                            cs_ = slice(c * 64, (c + 1) * 64)
                            ub = ci_ % 2
                            for hd in range(2):
                                B.op("pe", lambda e: e.matmul(PS(0)[hs(hd), 0:64], lhsT=AR4[hs(hd), c, 0, :], rhs=STb[hp][cu][hs(hd), :], start=True, stop=True),
                                     reads=["AR", ("STb", hp, cu)], writes=psk(0), inc=(hd == 1))
                            B.op("dve", lambda e: e.tensor_tensor(out=Qb, in0=PS(0)[:, 0:64], in1=Q1[:, cs_], op=ALU.add), reads=psk(0) + ["Q1"], writes=["Qb"])
                            for hd in range(2):
                                B.op("pe", lambda e: e.matmul(PS(1)[hs(hd), 0:64], lhsT=Tm[hs(hd), cs_], rhs=Qb[hs(hd), :], start=True, stop=True),
                                     reads=[tkey, "Qb"], writes=psk(1), inc=(hd == 1))
                            B.op("act", lambda e: e.activation(out=UTb[ub], in_=PS(1)[:, 0:64], func=AF.Copy), reads=psk(1), writes=[("UTb", ub)])
                            for hd in range(2):
                                B.op("pe", lambda e: e.matmul(PS(2)[hs(hd), 0:64], lhsT=BhT[hs(hd), cs_], rhs=UTb[ub][hs(hd), :], start=True, stop=False),
                                     reads=["BhT", ("UTb", ub)], writes=psk(2), inc=False)
                                B.op("pe", lambda e: e.matmul(PS(2)[hs(hd), 0:64], lhsT=KhT[hs(hd), cs_], rhs=VT[hs(hd), cs_], start=False, stop=True),
                                     reads=["KhT", "VT"], writes=psk(2), inc=(hd == 1))
                            gcol = F2[:, c * 64 + endc:c * 64 + endc + 1]
                            B.op("dve", lambda e: e.scalar_tensor_tensor(out=STf[hp][nx], in0=STf[hp][cu], scalar=gcol, in1=PS(2)[:, 0:64], op0=ALU.mult, op1=ALU.add),
                                 reads=[("STf", hp, cu), "F2"] + psk(2), writes=[("STf", hp, nx)])
                            B.op("act", lambda e: e.activation(out=STb[hp][nx], in_=STf[hp][nx], func=AF.Copy), reads=[("STf", hp, nx)], writes=[("STb", hp, nx)])
                            for hd in range(2):
                                B.op("pe", lambda e: e.matmul(PS(3)[hs(hd), cs_], lhsT=STb[hp][cu][hs(hd), :], rhs=AR4[hs(hd), c, 1, :], start=True, stop=False),
                                     reads=[("STb", hp, cu), "AR"], writes=psk(3), inc=False)
                                B.op("pe", lambda e: e.matmul(PS(3)[hs(hd), cs_], lhsT=UTb[ub][hs(hd), :], rhs=Gm[hs(hd), c * 256 + 64:c * 256 + 128], start=False, stop=False),
                                     reads=[("UTb", ub), "Gm"], writes=psk(3), inc=False)
                                B.op("pe", lambda e: e.matmul(PS(3)[hs(hd), cs_], lhsT=VT[hs(hd), cs_], rhs=Gm[hs(hd), c * 256 + 192:c * 256 + 256], start=False, stop=True),
                                     reads=["VT", "Gm"], writes=psk(3), inc=(hd == 1))
                            cur[hp] = nx
                        if d == 0:
                            B.op("act", lambda e: e.activation(out=yfB[hp][:, ssl], in_=PS(3)[:, 0:SA], func=AF.Copy), reads=psk(3), writes=[sk(5)])
                        else:
                            B.op("dve", lambda e: e.tensor_tensor(out=F0, in0=PS(3)[:, 0:SA], in1=yfB[hp][:, ssl], op=ALU.add), reads=psk(3) + [sk(5)], writes=["F0"])
                            B.op("pe", lambda e: e.matmul(PS(0)[:, 0:SA], lhsT=blkf, rhs=F0, start=True, stop=True), reads=["blkf", "F0"], writes=psk(0))
                            B.op("dve", lambda e: e.scalar_tensor_tensor(out=F1, in0=PS(0)[:, 0:SA], scalar=-1.0 / 64, in1=F0, op0=ALU.mult, op1=ALU.add), reads=psk(0) + ["F0"], writes=["F1"])
                            B.op("act", lambda e: e.activation(out=F3, in_=F1, func=AF.Square), reads=["F1"], writes=["F3"])
                            B.op("pe", lambda e: e.matmul(PS(1)[:, 0:SA], lhsT=blkf, rhs=F3, start=True, stop=True), reads=["blkf", "F3"], writes=psk(1))
                            B.op("act", lambda e: e.activation(out=F4, in_=PS(1)[:, 0:SA], func=AF.Sqrt, scale=1.0 / 64, bias=epsb[:, 2:3]), reads=psk(1) + ["epsb"], writes=["F4"])
                            B.op("dve", lambda e: e.reciprocal(out=F4, in_=F4), reads=["F4"], writes=["F4"])
                            B.op("dve", lambda e: e.tensor_tensor(out=F1, in0=F1, in1=F4, op=ALU.mult), reads=["F1", "F4"], writes=["F1"])
                            B.op("dve", lambda e: e.tensor_scalar(out=F1, in0=F1, scalar1=pvc("ln_g", l, hp), scalar2=pvc("ln_b", l, hp), op0=ALU.mult, op1=ALU.add), reads=["F1", "pvt"], writes=["F1"])
                            B.op("dve", lambda e: e.tensor_tensor(out=F3, in0=bcB[hp][:, ssl], in1=v_, op=ALU.mult), reads=[sk(6), sk(2)], writes=["F3"])
                            B.op("dve", lambda e: e.tensor_tensor(out=F1, in0=F1, in1=F3, op=ALU.add), reads=["F1", "F3"], writes=["F1"])
                            B.op("dve", lambda e: e.tensor_tensor(out=obuf[:, hp, ssl], in0=F1, in1=obuf[:, hp, ssl], op=ALU.mult), reads=["F1", "obuf"], writes=["obuf"])
            B.barrier()

        for s in range(NSEQ):
            for c in range(8):
                B.dma(xT[:, c, :], x_d[s * 1024 + c * 128:s * 1024 + (c + 1) * 128, :], writes=[("xT", c)], q="pool", semkey=f"xin{c}")
            for l in range(L):
                rmsnorm_to(l, "h")
                B.dma(scr[7][:, 0:1024], gw_d[:, l * 1024:(l + 1) * 1024], writes=[sk(7)], q="pool", semkey="gw")
                B.op("pool", lambda e: e.tensor_copy(out=gw_b[:], in_=scr[7][:, 0:1024]), reads=[sk(7)], writes=["gw_b"])
                B.dma(upw_f[:], upw_d[:, l * 512:(l + 1) * 512], writes=["upw_f"], q="pool", semkey="upw")
                B.op("pool", lambda e: e.tensor_copy(out=upw_b[:], in_=upw_f[:]), reads=["upw_f"], writes=["upw_b"])
                B.barrier()
                if "C" in cfg.MIX:
                    mixer_C(l)
                    outproj(l, 2)
                if "D" in cfg.MIX:
                    B.barrier()
                    mixer_D(l)
                    outproj(l, 3)
                if "B" in cfg.MIX:
                    B.barrier()
                    mixer_B(l)
                    outproj(l, 1)
                if "A" in cfg.MIX:
                    B.barrier()
                    mixer_A(l)
                    outproj(l, 0)
                B.barrier()
            rmsnorm_to(None, "x")
            for c in range(8):
                B.dma(y_d[s * 1024 + c * 128:s * 1024 + (c + 1) * 128, :], xT[:, c, :], reads=[("xT", c)], q="pool", semkey=f"yout{c}")
        toks = []
        for c in range(8):
            toks.extend(B.readers.get(("xT", c), {}).values())
        for tname in tap_d:
            t = B.last_w.get(("tap", tname))
            if t is not None:
                toks.append(t)
        B._wait("pool", toks)
        print(f"[build] instructions: {B.n_ins}")
    return nc


_NC_CACHE = {}


def run_trunk(cfg, xs, inp):
    T, L = cfg.T, cfg.DEPTH
    ncore = cfg.NCORES
    assert xs.shape[0] == ncore * cfg.NSEQ
    key = (cfg.T, cfg.NSEQ, cfg.DEPTH, cfg.MIX, cfg.taps)
    if key not in _NC_CACHE:
        _NC_CACHE[key] = build(cfg)
    nc = _NC_CACHE[key]
    cst, alibi, C, S = make_consts(T)
    pv, upw, gw = pack_params(inp, L)
    w_in = np.ascontiguousarray(np.asarray(inp["w_in"], np.float32)[:L].reshape(L * 1024, D_IN))
    w_out = np.ascontiguousarray(np.asarray(inp["w_out"], np.float32)[:L].reshape(L * 1024, 1024))
    in_maps = []
    for c in range(ncore):
        xc = xs[c * cfg.NSEQ:(c + 1) * cfg.NSEQ]
        xt = np.ascontiguousarray(xc.transpose(0, 2, 1)).reshape(cfg.NSEQ * 1024, T)
        in_maps.append({"x": xt, "w_in": w_in, "w_out": w_out, "pvec": pv, "upw": upw, "gw": gw, "cst": cst,
                        "alibi": alibi, "ropeC": C, "ropeS": S})
    res = run_bass_kernel_spmd(nc, in_maps, core_ids=list(range(ncore)))
    outs = []
    for c in range(ncore):
        yt = np.asarray(res.results[c]["y"]).reshape(cfg.NSEQ, 1024, T)
        outs.append(yt.transpose(0, 2, 1))
    return np.ascontiguousarray(np.concatenate(outs, 0)).astype(np.float32), res


def kernel(**inputs):
    inp = {k: np.asarray(v) for k, v in inputs.items()}
    xp, xs_ = inp["x_prompt"], inp["x_sample"]
    xs = np.concatenate([xp, xs_], 0).astype(np.float32)
    cfg = Cfg(T=xs.shape[1], NSEQ=xs.shape[0] // 8, DEPTH=inp["w_in"].shape[0], MIX="ABCD", NCORES=8)
    y, _ = run_trunk(cfg, xs, inp)
    return (y[:xp.shape[0]], y[xp.shape[0]:])
```

```python
import math
from contextlib import ExitStack
import numpy as np
import ml_dtypes
import concourse.bass as bass
import concourse.mybir as mybir
from concourse.bass_utils import run_bass_kernel_spmd

F32 = mybir.dt.float32
BF16 = mybir.dt.bfloat16
AF = mybir.ActivationFunctionType
ALU = mybir.AluOpType

D_MODEL = 1024
GRID_W = 64
D_IN = 3200
A_W, B_W, C_W, D_W = 1152, 768, 512, 768
OFF_A, OFF_B, OFF_C, OFF_D = 0, 1152, 1920, 2432
NORM_EPS = 1e-6
GN_EPS = 64e-5
LAMW = math.exp(-0.5)
SEM_LIMIT = 30000


class Cfg:
    def __init__(self, T=2048, NSEQ=5, DEPTH=4, MIX="ABCD", NCORES=8, taps=()):
        self.T, self.NSEQ, self.DEPTH, self.MIX, self.NCORES = T, NSEQ, DEPTH, MIX, NCORES
        self.taps = tuple(taps)


PV_FIELDS = [("norm_g", 8), ("sh0", 7), ("sh1", 7), ("w0", 4), ("a0", 4), ("k_k", 2), ("k_a", 2), ("r_k", 2),
             ("ln_g", 2), ("ln_b", 2), ("qn", 1), ("kn", 1), ("cw", 8), ("cb", 2), ("gb", 8), ("lam", 4),
             ("sink", 2), ("sinksw", 2)]
DV_FIELDS = [("c0", 7), ("omka", 2), ("cneg", 4), ("esinksw", 2), ("gbh", 8)]


def _offsets(fields):
    off, d = 0, {}
    for n, w in fields:
        d[n] = off
        off += w
    return d, off


PV_OFF, PV_W = _offsets(PV_FIELDS)
DV_OFF, DV_W = _offsets(DV_FIELDS)


def pv_col(L, name, l, i=0):
    if name == "final_g":
        return L * PV_W + i
    return l * PV_W + PV_OFF[name] + i


def dv_col(name, l, i=0):
    return l * DV_W + DV_OFF[name] + i


CST_FIELDS = [("ident", 128), ("blk64", 128), ("ones", 128), ("swap", 128), ("perm", 128), ("mfwd", 256),
              ("mbwd", 256), ("mfwdT", 64), ("mbwdT", 64), ("rst", 512)]
CST_OFF, CST_W = _offsets(CST_FIELDS)


def make_consts(T):
    c = np.zeros((128, CST_W), np.float32)
    alibi = np.zeros((128, 4 * 384), np.float32)
    p = np.arange(128)
    o = CST_OFF
    c[:, o["ident"]:o["ident"] + 128] = np.eye(128)
    c[:, o["blk64"]:o["blk64"] + 128] = (p[:, None] // 64 == p[None, :] // 64)
    c[:, o["ones"]:o["ones"] + 128] = 1.0
    c[:, o["swap"]:o["swap"] + 128] = (p[:, None] == (p[None, :] + 64) % 128)
    d = p % 64
    partner = np.where((d % 32) < 16, p + 16, p - 16)
    c[:, o["perm"]:o["perm"] + 128] = (p[:, None] == partner[None, :])
    s = (p % 64)[:, None]
    t = np.arange(64)[None, :]
    strict_f, incl_f = (s < t), (s <= t)
    strict_b, incl_b = (s > t), (s >= t)
    c[:, o["mfwd"]:o["mfwd"] + 256] = np.concatenate([strict_f, incl_f, strict_f, incl_f], 1)
    c[:, o["mbwd"]:o["mbwd"] + 256] = np.concatenate([strict_b, incl_b, strict_b, incl_b], 1)
    c[:, o["mfwdT"]:o["mfwdT"] + 64] = (t < s)
    c[:, o["mbwdT"]:o["mbwdT"] + 64] = (t > s)
    c[:, o["rst"]:o["rst"] + 512] = (np.arange(512)[None, :] % 64 != 0)
    cc = np.arange(384)[None, :]
    dist = np.abs(cc - 128 - p[:, None]).astype(np.float64)
    for h in range(4):
        slope = 2.0 ** (-8.0 * (h + 1) / 4)
        e = np.where(dist <= 128, np.exp(-slope * dist), 0.0)
        alibi[:, h * 384:(h + 1) * 384] = e
    row = (np.arange(T) // GRID_W).astype(np.float64)
    col = (np.arange(T) % GRID_W).astype(np.float64)
    inv = 10000.0 ** (-np.arange(0, 32, 2, dtype=np.float64) / 32)
    C = np.zeros((128, T), np.float32)
    S = np.zeros((128, T), np.float32)
    for pp in range(128):
        dd = pp % 64
        pos = row if dd < 32 else col
        f = inv[dd % 16]
        ang = pos * f
        C[pp] = np.cos(ang)
        S[pp] = (-np.sin(ang)) if (dd % 32) < 16 else np.sin(ang)
    return c, alibi, C, S


def pack_params(inp, L):
    pv = np.zeros((128, L * PV_W + 8), np.float32)

    def put(name, l, i, vec128):
        pv[:, pv_col(L, name, l, i)] = vec128

    for l in range(L):
        for i in range(8):
            put("norm_g", l, i, inp["norm_g"][l, i * 128:(i + 1) * 128])
        for i in range(7):
            put("sh0", l, i, inp["rwkv_shift"][l, 0, i * 128:(i + 1) * 128])
            put("sh1", l, i, inp["rwkv_shift"][l, 1, i * 128:(i + 1) * 128])
        for d in range(2):
            for hp in range(2):
                put("w0", l, d * 2 + hp, inp["rwkv_w0"][l, d, hp * 128:(hp + 1) * 128])
                put("a0", l, d * 2 + hp, inp["rwkv_a0"][l, d, hp * 128:(hp + 1) * 128])
        rk = inp["rwkv_r_k"][l].reshape(256)
        for hp in range(2):
            sl = slice(hp * 128, (hp + 1) * 128)
            put("k_k", l, hp, inp["rwkv_k_k"][l, sl])
            put("k_a", l, hp, inp["rwkv_k_a"][l, sl])
            put("r_k", l, hp, rk[sl])
            put("ln_g", l, hp, inp["rwkv_ln_g"][l, sl])
            put("ln_b", l, hp, inp["rwkv_ln_b"][l, sl])
        put("qn", l, 0, np.tile(inp["attn_q_norm"][l], 2))
        put("kn", l, 0, np.tile(inp["attn_k_norm"][l], 2))
        for j in range(4):
            for cc in range(2):
                put("cw", l, j * 2 + cc, inp["lru_conv_w"][l, j, cc * 128:(cc + 1) * 128])
        for cc in range(2):
            put("cb", l, cc, inp["lru_conv_b"][l, cc * 128:(cc + 1) * 128])
        for d in range(2):
            for k in range(2):
                for cc in range(2):
                    put("gb", l, (d * 2 + k) * 2 + cc, inp["lru_gate_b"][l, d, k, cc * 128:(cc + 1) * 128])
            for cc in range(2):
                put("lam", l, d * 2 + cc, inp["lru_lambda"][l, d, cc * 128:(cc + 1) * 128])
        sk = inp["swa_sink"][l]
        for cc in range(2):
            put("sink", l, cc, np.repeat(sk[2 * cc:2 * cc + 2], 64))
            put("sinksw", l, cc, np.repeat(sk[2 * cc:2 * cc + 2][::-1], 64))
    for i in range(8):
        pv[:, L * PV_W + i] = inp["final_g"][i * 128:(i + 1) * 128]
    upw = np.zeros((128, L * 2 * 256), np.float32)
    for l in range(L):
        for d in range(2):
            upw[0:64, (l * 2 + d) * 256:(l * 2 + d + 1) * 256] = inp["rwkv_w_up"][l, d]
            upw[64:128, (l * 2 + d) * 256:(l * 2 + d + 1) * 256] = inp["rwkv_a_up"][l, d]
    gw = np.zeros((128, L * 8 * 128), np.float32)
    for l in range(L):
        for d in range(2):
            for k in range(2):
                for cc in range(2):
                    base = (((l * 2 + d) * 2 + k) * 2 + cc) * 128
                    for b in range(2):
                        gw[b * 64:(b + 1) * 64, base + b * 64:base + (b + 1) * 64] = inp["lru_gate_w"][l, d, k, 2 * cc + b]
    return pv, upw, gw


class SemCounter:
    def __init__(self, bld, name, step):
        self.bld, self.name, self.step = bld, name, step
        self.n = 0
        self._new()

    def _new(self):
        self.sem = self.bld.es.enter_context(self.bld.nc.semaphore(f"{self.name}_{self.n}"))
        self.sname = f"{self.name}_{self.n}"
        self.n += 1
        self.val = 0

    def next(self):
        if self.val + self.step > SEM_LIMIT:
            self._new()
        self.val += self.step
        return (self.sname, self.sem, self.val)


class Builder:
    def __init__(self, nc, es):
        self.nc, self.es = nc, es
        self.eng = {"pe": nc.tensor, "act": nc.scalar, "dve": nc.vector, "pool": nc.gpsimd, "sp": nc.sync}
        self.cnt = {e: SemCounter(self, e, 1) for e in ("pe", "act", "dve", "pool")}
        self.dcnt = {}
        self.waited = {e: {} for e in self.eng}
        self.last_w = {}
        self.readers = {}
        self.pending = {e: ([], []) for e in self.eng}
        self.n_ins = 0

    def _deps(self, reads, writes):
        toks = []
        for k in reads:
            t = self.last_w.get(k)
            if t is not None:
                toks.append(t)
        for k in writes:
            t = self.last_w.get(k)
            if t is not None:
                toks.append(t)
            r = self.readers.get(k)
            if r:
                toks.extend(r.values())
        return toks

    def _wait(self, e, toks):
        w = self.waited[e]
        need = {}
        for (sn, sem, val) in toks:
            if w.get(sn, 0) < val and need.get(sn, (None, 0))[1] < val:
                need[sn] = (sem, val)
        for sn, (sem, val) in need.items():
            self.eng[e].wait_ge(sem, val)
            w[sn] = val

    def _commit(self, tok, reads, writes):
        for k in writes:
            self.last_w[k] = tok
            self.readers[k] = {}
        for k in reads:
            self.readers.setdefault(k, {})[tok[0]] = tok

    def op(self, e, fn, reads=(), writes=(), inc=True):
        self._wait(e, self._deps(reads, writes))
        ins = fn(self.eng[e])
        self.n_ins += 1
        pr, pw = self.pending[e]
        if not inc:
            pr.extend(reads)
            pw.extend(writes)
            return
        tok = self.cnt[e].next()
        ins.then_inc(tok[1], 1)
        self._commit(tok, list(reads) + pr, list(writes) + pw)
        self.pending[e] = ([], [])

    def dma(self, out, in_, reads=(), writes=(), q="sp", semkey=None):
        self._wait(q, self._deps(reads, writes))
        ins = self.eng[q].dma_start(out=out, in_=in_)
        self.n_ins += 1
        if semkey not in self.dcnt:
            self.dcnt[semkey] = SemCounter(self, "d" + semkey, 16)
        tok = self.dcnt[semkey].next()
        ins.then_inc(tok[1], 16)
        self._commit(tok, reads, writes)
        return tok

    def barrier(self):
        toks = []
        for c in list(self.cnt.values()) + list(self.dcnt.values()):
            if c.val > 0:
                toks.append((c.sname, c.sem, c.val))
        for e in self.eng:
            self._wait(e, toks)

    def final_wait(self, e, keys):
        toks = []
        for k in keys:
            t = self.last_w.get(k)
            if t is not None:
                toks.append(t)
        self._wait(e, toks)


class NullBuilder:
    def __init__(self):
        self.readers, self.last_w, self.n_ins = {}, {}, 0

    def op(self, *a, **k):
        pass

    def dma(self, *a, **k):
        pass

    def barrier(self):
        pass

    def _wait(self, *a, **k):
        pass


SCRW = 2048
NSCR = 8


def build(cfg):
    T, NSEQ, L = cfg.T, cfg.NSEQ, cfg.DEPTH
    NSEG, NT, NCH = T // 512, T // 128, T // 64
    assert T % 512 == 0 and T <= 2048
    nc = bass.Bass("TRN2", target_bir_lowering=False)

    def din(name, shape, dt=F32):
        return nc.dram_tensor(name, list(shape), dt, kind="ExternalInput").ap()

    x_d = din("x", [NSEQ * 1024, T])
    win_d = din("w_in", [L * 1024, D_IN])
    wout_d = din("w_out", [L * 1024, 1024])
    pv_d = din("pvec", [128, L * PV_W + 8])
    upw_d = din("upw", [128, L * 512])
    gw_d = din("gw", [128, L * 1024])
    cst_d = din("cst", [128, CST_W])
    alibi_d = din("alibi", [128, 1536])
    ropeC_d = din("ropeC", [128, T])
    ropeS_d = din("ropeS", [128, T])
    y_d = nc.dram_tensor("y", [NSEQ * 1024, T], F32, kind="ExternalOutput").ap()
    tap_d = {}
    for (tname, tshape) in cfg.taps:
        tap_d[tname] = nc.dram_tensor("tap_" + tname, list(tshape), F32, kind="ExternalOutput").ap()

    es = ExitStack()
    with es:
        B = Builder(nc, es)

        def sb(name, shape, dt):
            return es.enter_context(nc.sbuf_tensor(name, list(shape), dt))

        xT = sb("xT", [128, 8, T], F32)
        hT = sb("hT", [128, 8, max(T, 2048)], BF16)
        pvt = sb("pvt", [128, L * PV_W + 8], F32)
        dvt = sb("dvt", [128, L * DV_W], F32)
        CBW = CST_W
        cbf = sb("cbf", [128, CBW], BF16)
        swapf = sb("swapf", [128, 128], F32)
        upw_f = sb("upw_f", [128, 512], F32)
        upw_b = sb("upw_b", [128, 512], BF16)
        gw_b = sb("gw_b", [128, 1024], BF16)
        NWS = 3
        wst = [sb(f"wst{i}", [128, 8, 128], F32) for i in range(2)]
        wbf = [sb(f"wbf{i}", [128, 8, 128], BF16) for i in range(NWS)]
        wost = [sb(f"wost{i}", [128, 256], F32) for i in range(2)]
        wobf = [sb(f"wobf{i}", [128, 256], BF16) for i in range(2)]
        obuf = sb("obuf", [128, 2, T], BF16)
        sqb = sb("sqb", [128, 2, 512], BF16)
        rstd = sb("rstd", [128, 512], F32)
        epsb = sb("epsb", [128, 4], F32)
        tmpA = sb("tmpA", [128, 512], F32)
        tmpB = sb("tmpB", [128, 512], F32)
        gam = sb("gam", [128, 2, 8], F32)
        scr = [sb(f"scr{i}", [128, SCRW], F32) for i in range(NSCR)]
        psum = es.enter_context(nc.psum_tensor("psum", [128, 8 * 512], F32))

        def sk(i):
            return ("scr", i)

        def sbf(i):
            return scr[i][:].bitcast(BF16)

        def PS(b, n=1):
            return psum[:, b * 512:(b + n) * 512]

        def psk(b, n=1):
            return [("ps", b + i) for i in range(n)]

        def cb(name, w=None, off=0):
            o = CST_OFF[name] + off
            return cbf[:, o:o + (w if w is not None else dict(CST_FIELDS)[name])]

        def pvc(name, l, i=0):
            c = pv_col(L, name, l, i)
            return pvt[:, c:c + 1]

        def dvc(name, l, i=0):
            c = dv_col(name, l, i)
            return dvt[:, c:c + 1]

        def tap(name, ap_sb, rkeys, rows=128):
            if name in tap_d:
                B.dma(tap_d[name], ap_sb, reads=rkeys, writes=[("tap", name)], q="pool", semkey="tap" + name)

        wepoch = [0]
        wplan = []
        wstate = {"rec": True, "next": 0, "issued": 0}

        def _issue_w(i):
            l, ranges, ep = wplan[i]
            si, bi = i % 2, i % NWS
            off = 0
            src = win_d[l * 1024:(l + 1) * 1024, :].rearrange("(k p) c -> p k c", p=128)
            for (c0, n) in ranges:
                B.dma(wst[si][:, :, off:off + n], src[:, :, c0:c0 + n], writes=[f"wst{si}"], q="sp", semkey=f"wst{si}")
                off += n
            B.op("pool", lambda e: e.tensor_copy(out=wbf[bi][:, :, 0:off], in_=wst[si][:, :, 0:off]),
                 reads=[f"wst{si}"], writes=[f"wbf{bi}"])

        def load_win(l, ranges):
            m = sum(n for _, n in ranges)
            if wstate["rec"]:
                wplan.append((l, tuple(ranges), wepoch[0]))
                return 0, m
            i = wstate["next"]
            wstate["next"] += 1
            assert wplan[i][0] == l and wplan[i][1] == tuple(ranges), (i, wplan[i], l, ranges)
            hi = i
            while hi + 1 < len(wplan) and hi + 1 <= i + 2 and wplan[hi + 1][2] == wplan[i][2]:
                hi += 1
            while wstate["issued"] <= hi:
                _issue_w(wstate["issued"])
                wstate["issued"] += 1
            return i % NWS, m

        def inproj(l, ranges, bank0):
            bi, m = load_win(l, ranges)
            for sg in range(NSEG):
                for k in range(8):
                    B.op("pe", lambda e, k=k, sg=sg: e.matmul(PS(bank0 + sg)[0:m, :], lhsT=wbf[bi][:, k, 0:m], rhs=hT[:, k, sg * 512:(sg + 1) * 512],
                                                              start=(k == 0), stop=(k == 7)),
                         reads=[f"wbf{bi}", "hT"], writes=psk(bank0 + sg), inc=(k == 7))
            return m

        def inproj_tok(l, ranges, dst_fn, post):
            bi, m = load_win(l, ranges)
            assert m == 64
            for tb in range((NT + 7) // 8):
                bank = 4 + tb % 2
                ntt = min(8, NT - tb * 8)
                for q in range(ntt):
                    tt = tb * 8 + q
                    for k in range(8):
                        B.op("pe", lambda e, k=k, tt=tt, q=q, bank=bank: e.matmul(PS(bank)[:, q * 64:(q + 1) * 64], lhsT=hT[:, k, tt * 128:(tt + 1) * 128],
                                                                                  rhs=wbf[bi][:, k, 0:64], start=(k == 0), stop=(k == 7)),
                             reads=[f"wbf{bi}", "hT"], writes=psk(bank), inc=(k == 7))
                post(tb, bank, ntt)

        wo_slot = [0]

        def outproj(l, g):
            for co in range(8):
                si = wo_slot[0] % 2
                wo_slot[0] += 1
                for kc in range(2):
                    r0 = l * 1024 + g * 256 + kc * 128
                    B.dma(wost[si][:, kc * 128:(kc + 1) * 128], wout_d[r0:r0 + 128, co * 128:(co + 1) * 128],
                          writes=[f"wost{si}"], q="sp", semkey=f"wost{si}")
                B.op("pool", lambda e, si=si: e.tensor_copy(out=wobf[si][:], in_=wost[si][:]),
                     reads=[f"wost{si}"], writes=[f"wobf{si}"])
                nb = min(2, NSEG)
                for half in range(NSEG // nb):
                    pb = 4 + 2 * ((co * (NSEG // nb) + half) % 2)
                    for j in range(nb):
                        sg = half * nb + j
                        for kc in range(2):
                            B.op("pe", lambda e, si=si, kc=kc, sg=sg, j=j, pb=pb: e.matmul(
                                PS(pb + j), lhsT=wobf[si][:, kc * 128:(kc + 1) * 128], rhs=obuf[:, kc, sg * 512:(sg + 1) * 512],
                                start=(kc == 0), stop=(kc == 1)),
                                reads=[f"wobf{si}", "obuf"], writes=psk(pb + j), inc=(kc == 1))
                    t0 = half * nb * 512
                    w = nb * 512
                    B.op("dve", lambda e, co=co, pb=pb, t0=t0, w=w, nb=nb: e.tensor_tensor(
                        out=xT[:, co, t0:t0 + w], in0=PS(pb, nb), in1=xT[:, co, t0:t0 + w], op=ALU.add),
                        reads=psk(pb, nb) + [("xT", co)], writes=[("xT", co)])

        def rmsnorm_to(l, dest_kind):
            for sg in range(NSEG):
                sl = slice(sg * 512, (sg + 1) * 512)
                for c in range(8):
                    B.op("act", lambda e, c=c: e.activation(out=sqb[:, c % 2, :], in_=xT[:, c, sl], func=AF.Square),
                         reads=[("xT", c)], writes=[("sqb", c % 2)])
                    B.op("pe", lambda e, c=c: e.matmul(PS(0), lhsT=cb("ones"), rhs=sqb[:, c % 2, :], start=(c == 0), stop=(c == 7)),
                         reads=[("sqb", c % 2), "cbf"], writes=psk(0), inc=True)
                B.op("act", lambda e: e.activation(out=rstd[:], in_=PS(0), func=AF.Ln, scale=1.0 / 1024, bias=epsb[:, 0:1]),
                     reads=psk(0) + ["epsb"], writes=["rstd"])
                B.op("act", lambda e: e.activation(out=rstd[:], in_=rstd[:], func=AF.Exp, scale=-0.5), reads=["rstd"], writes=["rstd"])
                for c in range(8):
                    if dest_kind == "h":
                        gcol = pvc("norm_g", l, c)
                        B.op("dve", lambda e, c=c, gcol=gcol: e.scalar_tensor_tensor(out=hT[:, c, sl], in0=xT[:, c, sl], scalar=gcol, in1=rstd[:],
                                                                                      op0=ALU.mult, op1=ALU.mult),
                             reads=[("xT", c), "rstd", "pvt"], writes=["hT"])
                    else:
                        gcol = pvt[:, L * PV_W + c:L * PV_W + c + 1]
                        B.op("dve", lambda e, c=c, gcol=gcol: e.scalar_tensor_tensor(out=xT[:, c, sl], in0=xT[:, c, sl], scalar=gcol, in1=rstd[:],
                                                                                      op0=ALU.mult, op1=ALU.mult),
                             reads=[("xT", c), "rstd", "pvt"], writes=[("xT", c)])

        def mixer_C(l):
            for cc in range(2):
                inproj(l, [(OFF_C + cc * 128, 128)], 0)
                Z = PS(0, NSEG)
                zk = psk(0, NSEG)
                xc = scr[0][:, 0:T]
                B.op("act", lambda e: e.activation(out=xc, in_=Z, func=AF.Identity, scale=pvc("cw", l, 2 * 2 + cc), bias=pvc("cb", l, cc)),
                     reads=zk + ["pvt"], writes=[sk(0)])
                for (j, sh) in ((0, -2), (1, -1), (3, 1)):
                    if sh < 0:
                        o_ap, i_ap = xc[:, -sh:T], Z[:, 0:T + sh]
                    else:
                        o_ap, i_ap = xc[:, 0:T - sh], Z[:, sh:T]
                    B.op("dve", lambda e, o_ap=o_ap, i_ap=i_ap, j=j: e.scalar_tensor_tensor(out=o_ap, in0=i_ap, scalar=pvc("cw", l, j * 2 + cc), in1=o_ap,
                                                                                            op0=ALU.mult, op1=ALU.add),
                         reads=zk + [sk(0), "pvt"], writes=[sk(0)])
                xcb = sbf(1)[:, 0:T]
                B.op("act", lambda e: e.activation(out=xcb, in_=xc, func=AF.Copy), reads=[sk(0)], writes=[sk(1)])
                for d in range(2):
                    rb, ib, sbuf_, hb = scr[2][:, 0:T], scr[3][:, 0:T], scr[4][:, 0:T], scr[5 + d][:, 0:T]
                    for sg in range(NSEG):
                        sl = slice(sg * 512, (sg + 1) * 512)
                        for k in range(2):
                            bank = 4 + 2 * k + sg % 2
                            wcol = (((d * 2 + k) * 2 + cc)) * 128
                            B.op("pe", lambda e, bank=bank, wcol=wcol, sl=sl: e.matmul(PS(bank), lhsT=gw_b[:, wcol:wcol + 128], rhs=xcb[:, sl], start=True, stop=True),
                                 reads=["gw_b", sk(1)], writes=psk(bank))
                            dst = rb if k == 0 else ib
                            B.op("act", lambda e, bank=bank, dst=dst, sl=sl, k=k: e.activation(out=dst[:, sl], in_=PS(bank), func=AF.Sigmoid,
                                                                                               bias=pvc("gb", l, (d * 2 + k) * 2 + cc)),
                                 reads=psk(bank) + ["pvt"], writes=[sk(2 + k)])
                    B.op("act", lambda e: e.activation(out=rb, in_=rb, func=AF.Exp, scale=dvc("cneg", l, d * 2 + cc)), reads=[sk(2), "dvt"], writes=[sk(2)])
                    B.op("act", lambda e: e.activation(out=sbuf_, in_=rb, func=AF.Square), reads=[sk(2)], writes=[sk(4)])
                    B.op("act", lambda e: e.activation(out=sbuf_, in_=sbuf_, func=AF.Ln, scale=-1.0, bias=epsb[:, 1:2]), reads=[sk(4), "epsb"], writes=[sk(4)])
                    B.op("act", lambda e: e.activation(out=sbuf_, in_=sbuf_, func=AF.Exp, scale=0.5), reads=[sk(4)], writes=[sk(4)])
                    B.op("dve", lambda e: e.tensor_tensor(out=ib, in0=ib, in1=xc, op=ALU.mult), reads=[sk(3), sk(0)], writes=[sk(3)])
                    B.op("dve", lambda e: e.tensor_tensor(out=ib, in0=ib, in1=sbuf_, op=ALU.mult), reads=[sk(3), sk(4)], writes=[sk(3)])
                    if d == 0:
                        B.op("dve", lambda e: e.tensor_tensor_scan(out=hb, data0=rb, data1=ib, initial=0.0, op0=ALU.mult, op1=ALU.add),
                             reads=[sk(2), sk(3)], writes=[sk(5 + d)])
                    else:
                        B.op("dve", lambda e: e.tensor_tensor_scan(out=hb[:, ::-1], data0=rb[:, ::-1], data1=ib[:, ::-1], initial=0.0, op0=ALU.mult, op1=ALU.add),
                             reads=[sk(2), sk(3)], writes=[sk(5 + d)])
                hf, hbk = scr[5][:, 0:T], scr[6][:, 0:T]
                B.op("dve", lambda e: e.tensor_tensor(out=hf, in0=hf, in1=hbk, op=ALU.add), reads=[sk(5), sk(6)], writes=[sk(5)])
                inproj(l, [(OFF_C + 256 + cc * 128, 128)], 0)
                sgt = scr[2][:, 0:T]
                B.op("act", lambda e: e.activation(out=sgt, in_=PS(0, NSEG), func=AF.Silu), reads=psk(0, NSEG), writes=[sk(2)])
                B.op("dve", lambda e: e.tensor_tensor(out=obuf[:, cc, :], in0=hf, in1=sgt, op=ALU.mult), reads=[sk(5), sk(2)], writes=["obuf"])

        def load_qk(l, off_q, off_k, cc, prep):
            qT = sbf(0)[:, 0:T]
            kT = sbf(0)[:, SCRW:SCRW + T]
            prep([(off_q + cc * 128, 128)], qT, True)
            prep([(off_k + cc * 64, 64), (off_k + cc * 64, 64)], kT, False)
            return qT, kT

        def load_vaug(l, off_v, cc):
            va = sbf(1)
            vav = va[:, 0:NT * 192].rearrange("p (t c) -> p t c", c=192)
            B.op("pool", lambda e: e.memset(vav[:, :, 64:128], 1.0), writes=[sk(1)])

            def post(tb, bank, ntt):
                src = PS(bank)[:, 0:ntt * 64].rearrange("p (t c) -> p t c", c=64)
                B.op("act", lambda e: e.activation(out=vav[:, tb * 8:tb * 8 + ntt, 0:64], in_=src, func=AF.Copy), reads=psk(bank), writes=[sk(1)])
                B.op("dve", lambda e: e.tensor_copy(out=vav[:, tb * 8:tb * 8 + ntt, 128:192], in_=src), reads=psk(bank), writes=[sk(1)])
            inproj_tok(l, [(off_v + cc * 64, 64)], None, post)
            return vav

        def load_gate(l, off_g, cc):
            inproj(l, [(off_g + cc * 128, 128)], 0)
            sgt = scr[2][:, 0:T]
            B.op("act", lambda e: e.activation(out=sgt, in_=PS(0, NSEG), func=AF.Silu), reads=psk(0, NSEG), writes=[sk(2)])
            return sgt

        def attn_post(l, cc, qg, bx, by, sgt, sink):
            X, Y = PS(bx), PS(by)
            gsl = slice(qg * 512, (qg + 1) * 512)
            if sink:
                B.op("dve", lambda e: e.tensor_scalar(out=tmpA[0:64, :], in0=Y[0:64, :], scalar1=dvc("esinksw", l, cc)[0:64, :], scalar2=None, op0=ALU.add),
                     reads=psk(by) + ["dvt"], writes=["tmpA"])
                B.op("dve", lambda e: e.tensor_scalar(out=tmpA[64:128, :], in0=X[64:128, :], scalar1=dvc("esinksw", l, cc)[64:128, :], scalar2=None, op0=ALU.add),
                     reads=psk(bx) + ["dvt"], writes=["tmpA"])
                B.op("act", lambda e: e.activation(out=tmpA[:], in_=tmpA[:], func=AF.Ln), reads=["tmpA"], writes=["tmpA"])
            else:
                B.op("act", lambda e: e.activation(out=tmpA[0:64, :], in_=Y[0:64, :], func=AF.Ln), reads=psk(by), writes=["tmpA"])
                B.op("act", lambda e: e.activation(out=tmpA[64:128, :], in_=X[64:128, :], func=AF.Ln), reads=psk(bx), writes=["tmpA"])
            B.op("act", lambda e: e.activation(out=tmpA[:], in_=tmpA[:], func=AF.Exp, scale=-1.0), reads=["tmpA"], writes=["tmpA"])
            B.op("pe", lambda e: e.matmul(PS(0), lhsT=swapf[:], rhs=tmpA[:], start=True, stop=True), reads=["swapf", "tmpA"], writes=psk(0))
            B.op("dve", lambda e: e.tensor_tensor(out=tmpB[:], in0=PS(0), in1=sgt[:, gsl], op=ALU.mult), reads=psk(0) + [sk(2)], writes=["tmpB"])
            B.op("dve", lambda e: e.tensor_tensor(out=obuf[0:64, cc, gsl], in0=X[0:64, :], in1=tmpB[0:64, :], op=ALU.mult),
                 reads=psk(bx) + ["tmpB"], writes=["obuf"])
            B.op("dve", lambda e: e.tensor_tensor(out=obuf[64:128, cc, gsl], in0=Y[64:128, :], in1=tmpB[64:128, :], op=ALU.mult),
                 reads=psk(by) + ["tmpB"], writes=["obuf"])

        def mixer_D(l):
            B.dma(scr[3][:, 0:1536], alibi_d[:, :], writes=[sk(3)], q="pool", semkey="alibi")

            def prep(ranges, dst, isq):
                inproj(l, ranges, 0)
                B.op("act", lambda e: e.activation(out=dst, in_=PS(0, NSEG), func=AF.Copy), reads=psk(0, NSEG), writes=[sk(0)])
            for cc in range(2):
                qT, kT = load_qk(l, OFF_D, OFF_D + 256, cc, prep)
                vav = load_vaug(l, OFF_D + 384, cc)
                sgt = load_gate(l, OFF_D + 512, cc)
                ptv = sbf(4)
                exv = sbf(5)
                atab = scr[3][:, 2 * cc * 384:(2 * cc + 2) * 384].rearrange("p (h c) -> p h c", h=2)

                def qrange(j):
                    return max(j - 1, 0), min(j + 1, NT - 1)

                def qk(j):
                    b0, b1 = qrange(j)
                    ncol = (b1 - b0 + 1) * 128
                    for hh in range(2):
                        ph = slice(hh * 64, hh * 64 + 64)
                        bank = 2 * (j % 2) + hh
                        B.op("pe", lambda e: e.matmul(PS(bank)[:, 0:ncol], lhsT=kT[ph, j * 128:(j + 1) * 128], rhs=qT[ph, b0 * 128:b0 * 128 + ncol], start=True, stop=True),
                             reads=[sk(0)], writes=psk(bank), inc=(hh == 1))

                def ex(j):
                    b0, b1 = qrange(j)
                    ncol = (b1 - b0 + 1) * 128
                    tcol0 = (b0 - (j - 1)) * 128
                    src = PS(2 * (j % 2), 2).rearrange("p (b c) -> p b c", b=2)[:, :, 0:ncol]
                    exs = exv[:, (j % 2) * 768:(j % 2 + 1) * 768].rearrange("p (h c) -> p h c", h=2)[:, :, 0:ncol]
                    pts = ptv[:, (j % 4) * 768:(j % 4 + 1) * 768].rearrange("p (h c) -> p h c", h=2)[:, :, 0:ncol]
                    B.op("act", lambda e: e.activation(out=exs, in_=src, func=AF.Exp, scale=0.125), reads=psk(2 * (j % 2), 2), writes=[("ex", j % 2)])
                    B.op("dve", lambda e: e.tensor_tensor(out=pts, in0=exs, in1=atab[:, :, tcol0:tcol0 + ncol], op=ALU.mult),
                         reads=[("ex", j % 2), sk(3)], writes=[("pt", j % 4)])

                def pv_block(i):
                    qg = i // 4
                    js = [j for j in (i - 1, i, i + 1) if 0 <= j < NT]
                    for hh in range(2):
                        bank = 4 + 2 * (qg % 2) + hh
                        for n, j in enumerate(js):
                            b0, _ = qrange(j)
                            c0 = (j % 4) * 768 + hh * 384 + (i - b0) * 128
                            B.op("pe", lambda e: e.matmul(PS(bank)[:, (i % 4) * 128:(i % 4 + 1) * 128], lhsT=vav[:, j, hh * 64:hh * 64 + 128], rhs=ptv[:, c0:c0 + 128],
                                                         start=(n == 0), stop=(n == len(js) - 1)),
                                 reads=[sk(1), ("pt", j % 4)], writes=psk(bank), inc=(n == len(js) - 1))

                qk(0)
                for j in range(NT):
                    if j + 1 < NT:
                        qk(j + 1)
                    ex(j)
                    blocks = []
                    if j >= 1:
                        blocks.append(j - 1)
                    if j == NT - 1:
                        blocks.append(j)
                    for i in blocks:
                        pv_block(i)
                        if i % 4 == 3:
                            qg = i // 4
                            attn_post(l, cc, qg, 4 + 2 * (qg % 2), 4 + 2 * (qg % 2) + 1, sgt, True)

        def mixer_B(l):
            B.dma(scr[6][:, 0:T], ropeC_d[:, :], writes=[sk(6)], q="pool", semkey="ropeC")
            B.dma(scr[7][:, 0:T], ropeS_d[:, :], writes=[sk(7)], q="pool", semkey="ropeS")
            t1b, t2b = scr[3][:, 0:512], scr[3][:, 512:1024]
            qgb = sbf(3)[:, 2048:2560]

            def prep(ranges, dst, isq):
                inproj(l, ranges, 0)
                gcol = pvc("qn" if isq else "kn", l, 0)
                for sg in range(NSEG):
                    sl = slice(sg * 512, (sg + 1) * 512)
                    Z = PS(sg)
                    B.op("act", lambda e: e.activation(out=qgb, in_=Z, func=AF.Copy, scale=gcol), reads=psk(sg) + ["pvt"], writes=[("qgb",)])
                    B.op("act", lambda e: e.activation(out=sqb[:, 0, :], in_=Z, func=AF.Square), reads=psk(sg), writes=[("sqb", 0)])
                    B.op("pe", lambda e: e.matmul(PS(4), lhsT=cb("blk64"), rhs=sqb[:, 0, :], start=True, stop=True), reads=[("sqb", 0), "cbf"], writes=psk(4))
                    B.op("pe", lambda e: e.matmul(PS(5), lhsT=cb("perm"), rhs=qgb, start=True, stop=True), reads=[("qgb",), "cbf"], writes=psk(5))
                    B.op("act", lambda e: e.activation(out=rstd[:], in_=PS(4), func=AF.Ln, scale=1.0 / 64, bias=epsb[:, 0:1]), reads=psk(4) + ["epsb"], writes=["rstd"])
                    B.op("act", lambda e: e.activation(out=rstd[:], in_=rstd[:], func=AF.Exp, scale=-0.5), reads=["rstd"], writes=["rstd"])
                    B.op("dve", lambda e: e.tensor_tensor(out=t1b, in0=qgb, in1=scr[6][:, sl], op=ALU.mult), reads=[("qgb",), sk(6)], writes=[("t1b",)])
                    B.op("dve", lambda e: e.tensor_tensor(out=t2b, in0=PS(5), in1=scr[7][:, sl], op=ALU.mult), reads=psk(5) + [sk(7)], writes=[("t2b",)])
                    B.op("dve", lambda e: e.tensor_tensor(out=t1b, in0=t1b, in1=t2b, op=ALU.add), reads=[("t1b",), ("t2b",)], writes=[("t1b",)])
                    B.op("dve", lambda e: e.tensor_tensor(out=dst[:, sl], in0=t1b, in1=rstd[:], op=ALU.mult), reads=[("t1b",), "rstd"], writes=[sk(0)])
            for cc in range(2):
                qT, kT = load_qk(l, OFF_B, OFF_B + 256, cc, prep)
                vav = load_vaug(l, OFF_B + 384, cc)
                sgt = load_gate(l, OFF_B + 512, cc)
                ptv = sbf(4)
                for qg in range(NSEG):
                    gsl = slice(qg * 512, (qg + 1) * 512)
                    ab = 4 + 2 * (qg % 2)

                    def qk(j):
                        for hh in range(2):
                            ph = slice(hh * 64, hh * 64 + 64)
                            bank = 2 * (j % 2) + hh
                            B.op("pe", lambda e: e.matmul(PS(bank), lhsT=kT[ph, j * 128:(j + 1) * 128], rhs=qT[ph, gsl], start=True, stop=True),
                                 reads=[sk(0)], writes=psk(bank), inc=(hh == 1))

                    def ex(j):
                        slot = j % 3
                        B.op("act", lambda e: e.activation(out=ptv[:, slot * 1024:(slot + 1) * 1024], in_=PS(2 * (j % 2), 2), func=AF.Exp, scale=0.125),
                             reads=psk(2 * (j % 2), 2), writes=[("pt", slot)])

                    def pv(j):
                        slot = j % 3
                        for hh in range(2):
                            B.op("pe", lambda e: e.matmul(PS(ab + hh), lhsT=vav[:, j, hh * 64:hh * 64 + 128], rhs=ptv[:, slot * 1024 + hh * 512:slot * 1024 + (hh + 1) * 512],
                                                         start=(j == 0), stop=(j == NT - 1)),
                                 reads=[sk(1), ("pt", slot)], writes=psk(ab + hh), inc=(hh == 1))
                    qk(0)
                    for j in range(NT):
                        if j + 1 < NT:
                            qk(j + 1)
                        ex(j)
                        pv(j)
                    attn_post(l, cc, qg, ab, ab + 1, sgt, False)

        def mixer_A(l):
            SA = min(512, T)
            NS = T // SA
            nch = SA // 64
            arena_b = hT[:].rearrange("p k t -> p (k t)")

            def AF32(off, w):
                return arena_b[:, 2 * off:2 * (off + w)].bitcast(F32)

            def ABF(off, w):
                return arena_b[:, 2 * off:2 * off + w]

            rB = [sbf(0)[:, 0:T], sbf(0)[:, SCRW:SCRW + T]]
            kB = [sbf(1)[:, 0:T], sbf(1)[:, SCRW:SCRW + T]]
            vB = [sbf(2)[:, 0:T], sbf(2)[:, SCRW:SCRW + T]]
            waB = sbf(3)[:, 0:T]
            kkB = [sbf(4)[:, 0:T], sbf(4)[:, SCRW:SCRW + T]]
            yfB = [sbf(5)[:, 0:T], sbf(5)[:, SCRW:SCRW + T]]
            bcB = [sbf(6)[:, 0:T], sbf(6)[:, SCRW:SCRW + T]]
            for ci in range(7):
                pb0 = 4 * (ci % 2)
                xi = 7 if ci % 2 == 0 else 5
                Xt = scr[xi][:, 0:T]
                inproj(l, [(OFF_A + ci * 128, 128)], pb0)
                Z = PS(pb0, NSEG)
                zk = psk(pb0, NSEG)
                B.op("act", lambda e: e.activation(out=Xt, in_=Z, func=AF.Copy, scale=dvc("c0", l, ci)), reads=zk + ["dvt"], writes=[sk(xi)])
                B.op("dve", lambda e: e.scalar_tensor_tensor(out=Xt[:, 1:T], in0=Z[:, 0:T - 1], scalar=pvc("sh0", l, ci), in1=Xt[:, 1:T], op0=ALU.mult, op1=ALU.add),
                     reads=zk + [sk(xi), "pvt"], writes=[sk(xi)])
                if ci < 6:
                    dst = (rB, kB, vB)[ci // 2][ci % 2]
                    dk_ = sk(ci // 2)
                    B.op("dve", lambda e: e.scalar_tensor_tensor(out=dst[:, 0:T - 1], in0=Z[:, 1:T], scalar=pvc("sh1", l, ci), in1=Xt[:, 0:T - 1], op0=ALU.mult, op1=ALU.add),
                         reads=zk + [sk(xi), "pvt"], writes=[dk_])
                    B.op("dve", lambda e: e.tensor_copy(out=dst[:, T - 1:T], in_=Xt[:, T - 1:T]), reads=[sk(xi)], writes=[dk_])
                else:
                    B.op("dve", lambda e: e.scalar_tensor_tensor(out=Xt[:, 0:T - 1], in0=Z[:, 1:T], scalar=pvc("sh1", l, ci), in1=Xt[:, 0:T - 1], op0=ALU.mult, op1=ALU.add),
                         reads=zk + [sk(xi), "pvt"], writes=[sk(xi)])
                    B.op("act", lambda e: e.activation(out=waB[0:64, :], in_=Xt[0:64, :], func=AF.Tanh), reads=[sk(xi)], writes=[sk(3)])
                    B.op("act", lambda e: e.activation(out=waB[64:128, :], in_=Xt[64:128, :], func=AF.Copy), reads=[sk(xi)], writes=[sk(3)])
            for hp in range(2):
                for sg in range(NSEG):
                    sl = slice(sg * 512, (sg + 1) * 512)
                    B.op("dve", lambda e: e.tensor_scalar(out=tmpA[:], in0=kB[hp][:, sl], scalar1=pvc("k_k", l, hp), scalar2=None, op0=ALU.mult),
                         reads=[sk(1), "pvt"], writes=["tmpA"])
                    B.op("act", lambda e: e.activation(out=sqb[:, 0, :], in_=tmpA[:], func=AF.Square), reads=["tmpA"], writes=[("sqb", 0)])
                    B.op("pe", lambda e: e.matmul(PS(4), lhsT=cb("blk64"), rhs=sqb[:, 0, :], start=True, stop=True), reads=[("sqb", 0), "cbf"], writes=psk(4))
                    B.op("act", lambda e: e.activation(out=rstd[:], in_=PS(4), func=AF.Ln, bias=epsb[:, 3:4]), reads=psk(4) + ["epsb"], writes=["rstd"])
                    B.op("act", lambda e: e.activation(out=rstd[:], in_=rstd[:], func=AF.Exp, scale=-0.5), reads=["rstd"], writes=["rstd"])
                    B.op("dve", lambda e: e.tensor_tensor(out=kkB[hp][:, sl], in0=tmpA[:], in1=rstd[:], op=ALU.mult), reads=["tmpA", "rstd"], writes=[sk(4)])
            for cc in range(2):
                inproj(l, [(OFF_A + 896 + cc * 128, 128)], 0)
                B.op("act", lambda e: e.activation(out=obuf[:, cc, :], in_=PS(0, NSEG), func=AF.Silu), reads=psk(0, NSEG), writes=["obuf"])
            wepoch[0] += 1
            B.barrier()
            o = 0
            Fb = []
            for i in range(5):
                Fb.append(AF32(o, SA)); o += SA
            H0 = AF32(o, SA); o += SA
            H1 = AF32(o, SA); o += SA
            H2 = ABF(o, SA); o += SA // 2
            BK = ABF(o, 2 * SA); o += SA
            ARb = ABF(o, 2 * SA); o += SA
            HK = ABF(o, 2 * SA); o += SA
            Gm = ABF(o, 4 * SA); o += 2 * SA
            GT = ABF(o, SA); o += SA // 2
            Xs, XTs, Ps_ = [], [], []
            for i in range(2):
                Xs.append(ABF(o, SA)); o += SA // 2
                XTs.append(ABF(o, SA)); o += SA // 2
                Ps_.append(ABF(o, SA)); o += SA // 2
            assert o <= 8192, o
            s7 = sbf(7)
            VT, BhT, KhT = s7[:, 0:SA], s7[:, SA:2 * SA], s7[:, 2 * SA:3 * SA]
            Q1 = scr[7][:, 768:768 + SA]
            Qb = s7[:, 2560:2624]
            UTb = [s7[:, 2624:2688], s7[:, 2688:2752]]
            STf = [[scr[7][:, 1408 + (hp * 2 + i) * 64:1408 + (hp * 2 + i + 1) * 64] for i in range(2)] for hp in range(2)]
            STb = [[s7[:, 3328 + (hp * 2 + i) * 64:3328 + (hp * 2 + i + 1) * 64] for i in range(2)] for hp in range(2)]
            id64 = s7[:, 3600:3664]
            blkf = scr[7][:, 1856:1984]
            F0, F1, F2, F3, F4 = Fb
            B.op("dve", lambda e: e.tensor_tensor(out=id64, in0=cb("mfwd", 64, 64), in1=cb("mfwd", 64, 0), op=ALU.subtract), reads=["cbf"], writes=["id64"])
            B.op("dve", lambda e: e.tensor_copy(out=blkf, in_=cb("blk64")), reads=["cbf"], writes=["blkf"])
            BK4 = BK.rearrange("p (c two j) -> p c two j", two=2, j=64)
            AR4 = ARb.rearrange("p (c two j) -> p c two j", two=2, j=64)
            HK4 = HK.rearrange("p (c two j) -> p c two j", two=2, j=64)
            Gm3 = Gm.rearrange("p (c x) -> p c x", x=256)

            def v3(ap):
                return ap.rearrange("p (c j) -> p c j", j=64)

            def hs(hd):
                return slice(hd * 64, hd * 64 + 64)

            Wf = [wst[i][:].rearrange("p k c -> p (k c)") for i in range(2)]
            fin = [Wf[0][:, 0:SA], Wf[0][:, SA:2 * SA], Wf[1][:, 0:SA], Wf[1][:, SA:2 * SA]]
            fink = ["wst0", "wst0", "wst1", "wst1"]
            sets = []
            sets.append(dict(BK=BK, AR=ARb, HK=HK, kBK="BK", kAR="AR", kHK="HK", gi=0))
            wfl = [wbf[i][:].rearrange("p k c -> p (k c)") for i in range(3)]
            sets.append(dict(BK=wfl[0][:, 0:2 * SA], AR=wfl[1][:, 0:2 * SA], HK=wfl[2][:, 0:2 * SA], kBK="wbf0", kAR="wbf1", kHK="wbf2", gi=1))
            for S_ in sets:
                for nm in ("BK", "AR", "HK"):
                    S_[nm + "4"] = S_[nm].rearrange("p (c two j) -> p c two j", two=2, j=64)
            iters = []
            for d in range(2):
                for sg in (range(NS) if d == 0 else range(NS - 1, -1, -1)):
                    for hp in range(2):
                        iters.append((d, sg, hp))
            cur = [0, 0]

            def pre_gen(ii):
                d, sg, hp = iters[ii]
                S_ = sets[ii % 2]
                BK4, AR4, HK4 = S_["BK4"], S_["AR4"], S_["HK4"]
                endc = 63 if d == 0 else 0
                ssl = slice(sg * SA, (sg + 1) * SA)
                first_of_dir = (sg == (0 if d == 0 else NS - 1))
                if first_of_dir:
                    cur[hp] = 0
                    B.op("dve", lambda e: e.memset(STf[hp][0], 0.0), writes=[("STf", hp, 0)])
                    B.op("dve", lambda e: e.memset(STb[hp][0], 0.0), writes=[("STb", hp, 0)])
                    yield
                wc = d * 256 + hp * 128
                B.op("pe", lambda e: e.matmul(PS(4)[:, 0:SA], lhsT=upw_b[0:64, wc:wc + 128], rhs=waB[0:64, ssl], start=True, stop=True),
                     reads=["upw_b", sk(3)], writes=psk(4))
                B.op("pe", lambda e: e.matmul(PS(5)[:, 0:SA], lhsT=upw_b[64:128, wc:wc + 128], rhs=waB[64:128, ssl], start=True, stop=True),
                     reads=["upw_b", sk(3)], writes=psk(5))
                yield
                B.op("act", lambda e: e.activation(out=F0, in_=PS(4)[:, 0:SA], func=AF.Sigmoid, bias=pvc("w0", l, d * 2 + hp)), reads=psk(4) + ["pvt"], writes=["F0"])
                B.op("act", lambda e: e.activation(out=F1, in_=PS(5)[:, 0:SA], func=AF.Sigmoid, bias=pvc("a0", l, d * 2 + hp)), reads=psk(5) + ["pvt"], writes=["F1"])
                yield
                rst = cb("rst", SA)
                if d == 0:
                    B.op("dve", lambda e: e.tensor_tensor_scan(out=F2, data0=rst, data1=F0, initial=0.0, op0=ALU.mult, op1=ALU.add), reads=["cbf", "F0"], writes=["F2"])
                else:
                    B.op("dve", lambda e: e.tensor_tensor_scan(out=F2[:, ::-1], data0=rst, data1=F0[:, ::-1], initial=0.0, op0=ALU.mult, op1=ALU.add),
                         reads=["cbf", "F0"], writes=["F2"])
                yield
                B.op("dve", lambda e: e.tensor_tensor(out=F3, in0=F2, in1=F0, op=ALU.subtract), reads=["F2", "F0"], writes=["F3"])
                yield
                endb = v3(F2)[:, :, endc:endc + 1].broadcast_to([128, nch, 64])
                B.op("dve", lambda e: e.tensor_tensor(out=v3(F0), in0=endb, in1=v3(F2), op=ALU.subtract), reads=["F2"], writes=["F0"])
                yield
                B.op("act", lambda e: e.activation(out=F4, in_=F2, func=AF.Exp, scale=LAMW), reads=["F2"], writes=["F4"])
                B.op("act", lambda e: e.activation(out=F2, in_=F2, func=AF.Exp, scale=-LAMW), reads=["F2"], writes=["F2"])
                yield
                B.op("act", lambda e: e.activation(out=F3, in_=F3, func=AF.Exp, scale=-LAMW), reads=["F3"], writes=["F3"])
                B.op("act", lambda e: e.activation(out=F0, in_=F0, func=AF.Exp, scale=-LAMW), reads=["F0"], writes=["F0"])
                yield
                gv = gam[:, S_["gi"], 0:nch]
                B.op("dve", lambda e: e.tensor_copy(out=gv, in_=v3(F2)[:, :, endc]), reads=["F2"], writes=[("gam", S_["gi"])])
                kk_, r_, k_, v_ = kkB[hp][:, ssl], rB[hp][:, ssl], kB[hp][:, ssl], vB[hp][:, ssl]
                B.op("dve", lambda e: e.tensor_tensor(out=H0, in0=kk_, in1=F1, op=ALU.mult), reads=[sk(4), "F1"], writes=["H0"])
                yield
                B.op("dve", lambda e: e.tensor_scalar(out=F1, in0=F1, scalar1=pvc("k_a", l, hp), scalar2=dvc("omka", l, hp), op0=ALU.mult, op1=ALU.add),
                     reads=["F1", "pvt", "dvt"], writes=["F1"])
                yield
                B.op("dve", lambda e: e.tensor_tensor(out=H1, in0=F1, in1=k_, op=ALU.mult), reads=["F1", sk(1)], writes=["H1"])
                yield
                B.op("dve", lambda e: e.tensor_tensor(out=BK4[:, :, 0, :], in0=v3(H0), in1=v3(F4), op=ALU.mult), reads=["H0", "F4"], writes=[S_["kBK"]])
                yield
                B.op("dve", lambda e: e.tensor_tensor(out=BK4[:, :, 1, :], in0=v3(H1), in1=v3(F4), op=ALU.mult), reads=["H1", "F4"], writes=[S_["kBK"]])
                yield
                B.op("dve", lambda e: e.scalar_tensor_tensor(out=AR4[:, :, 0, :], in0=v3(kk_), scalar=-1.0, in1=v3(F3), op0=ALU.mult, op1=ALU.mult),
                     reads=[sk(4), "F3"], writes=[S_["kAR"]])
                yield
                B.op("dve", lambda e: e.tensor_tensor(out=AR4[:, :, 1, :], in0=v3(r_), in1=v3(F2), op=ALU.mult), reads=[sk(0), "F2"], writes=[S_["kAR"]])
                yield
                B.op("dve", lambda e: e.tensor_tensor(out=HK4[:, :, 0, :], in0=v3(H0), in1=v3(F0), op=ALU.mult), reads=["H0", "F0"], writes=[S_["kHK"]])
                yield
                B.op("dve", lambda e: e.tensor_tensor(out=HK4[:, :, 1, :], in0=v3(H1), in1=v3(F0), op=ALU.mult), reads=["H1", "F0"], writes=[S_["kHK"]])
                yield
                B.op("dve", lambda e: e.scalar_tensor_tensor(out=H2, in0=r_, scalar=pvc("r_k", l, hp), in1=H1, op0=ALU.mult, op1=ALU.mult),
                     reads=[sk(0), "pvt", "H1"], writes=["H2"])
                B.op("pe", lambda e: e.matmul(PS(6)[:, 0:SA], lhsT=cb("blk64"), rhs=H2, start=True, stop=True), reads=["cbf", "H2"], writes=psk(6))
                yield
                if d == 0:
                    B.op("act", lambda e: e.activation(out=bcB[hp][:, ssl], in_=PS(6)[:, 0:SA], func=AF.Copy), reads=psk(6), writes=[("bc", hp)])
                else:
                    B.op("dve", lambda e: e.tensor_tensor(out=bcB[hp][:, ssl], in0=PS(6)[:, 0:SA], in1=bcB[hp][:, ssl], op=ALU.add), reads=psk(6) + [("bc", hp)], writes=[("bc", hp)])
                yield

            def post(ii, pump):
                d, sg, hp = iters[ii]
                S_ = sets[ii % 2]
                BK4, AR4, HK4 = S_["BK4"], S_["AR4"], S_["HK4"]
                kBK, kAR, kHK = S_["kBK"], S_["kAR"], S_["kHK"]
                mname = "mfwd" if d == 0 else "mbwd"
                mTname = "mfwdT" if d == 0 else "mbwdT"
                ssl = slice(sg * SA, (sg + 1) * SA)
                v_ = vB[hp][:, ssl]
                Gps = PS(4, 4)
                n_items = 2 * nch
                it = 0
                for c in range(nch):
                    for hd in range(2):
                        it += 1
                        last = (it == n_items)
                        B.op("pe", lambda e: e.matmul(Gps[hs(hd), c * 256:c * 256 + 128], lhsT=BK4[hs(hd), c, 0, :], rhs=AR4[hs(hd), c, :, :], start=True, stop=True),
                             reads=[kBK, kAR], writes=psk(4, 4), inc=False)
                        B.op("pe", lambda e: e.matmul(Gps[hs(hd), c * 256 + 128:c * 256 + 256], lhsT=BK4[hs(hd), c, 1, :], rhs=AR4[hs(hd), c, :, :], start=True, stop=True),
                             reads=[kBK, kAR], writes=psk(4, 4), inc=False)
                        B.op("pe", lambda e: e.matmul(PS(3)[hs(hd), c * 64:(c + 1) * 64], lhsT=AR4[hs(hd), c, 0, :], rhs=BK4[hs(hd), c, 0, :], start=True, stop=True),
                             reads=[kBK, kAR], writes=psk(3), inc=last)
                mk = cb(mname).rearrange("p (o x) -> p o x", o=1).broadcast_to([128, nch, 256])
                B.op("dve", lambda e: e.tensor_tensor(out=Gm3, in0=Gps[:, 0:nch * 256].rearrange("p (c x) -> p c x", x=256), in1=mk, op=ALU.mult),
                     reads=psk(4, 4) + ["cbf"], writes=["Gm"])
                mkT = cb(mTname).rearrange("p (o x) -> p o x", o=1).broadcast_to([128, nch, 64])
                B.op("dve", lambda e: e.tensor_tensor(out=v3(GT), in0=v3(PS(3)[:, 0:SA]), in1=mkT, op=ALU.mult), reads=psk(3) + ["cbf"], writes=["GT"])
                pump(2)
                idb = id64.rearrange("p (o x) -> p o x", o=1).broadcast_to([128, nch, 64])
                B.op("dve", lambda e: e.tensor_tensor(out=v3(Ps_[0]), in0=Gm3[:, :, 0:64], in1=idb, op=ALU.add), reads=["Gm", "id64"], writes=[("P", 0)])

                def Xap(level, buf, hd, c):
                    if level == 0:
                        return Gm[hs(hd), c * 256:c * 256 + 64]
                    return Xs[buf][hs(hd), c * 64:(c + 1) * 64]

                def XTap(level, buf, hd, c):
                    if level == 0:
                        return GT[hs(hd), c * 64:(c + 1) * 64]
                    return XTs[buf][hs(hd), c * 64:(c + 1) * 64]
                for r in range(1, 7):
                    src_b, dst_b = (r - 1) % 2, r % 2
                    xk_src = ["Gm"] if r == 1 else [("X", src_b)]
                    xtk_src = ["GT"] if r == 1 else [("XT", src_b)]
                    do_x, do_xt, do_p = (r <= 4), (r <= 5), (r >= 2)
                    psrc, pdst = (r - 2) % 2, (r - 1) % 2
                    it = 0
                    for c in range(nch):
                        for hd in range(2):
                            it += 1
                            last = (it == n_items)
                            if do_x:
                                B.op("pe", lambda e: e.matmul(PS(0)[hs(hd), c * 64:(c + 1) * 64], lhsT=XTap(r - 1, src_b, hd, c), rhs=Xap(r - 1, src_b, hd, c), start=True, stop=True),
                                     reads=xk_src + xtk_src, writes=psk(0), inc=False)
                            if do_xt:
                                B.op("pe", lambda e: e.matmul(PS(1)[hs(hd), c * 64:(c + 1) * 64], lhsT=Xap(r - 1, src_b, hd, c), rhs=XTap(r - 1, src_b, hd, c), start=True, stop=True),
                                     reads=xk_src + xtk_src, writes=psk(1), inc=(last and not do_p))
                            if do_p:
                                B.op("pe", lambda e: e.matmul(PS(2)[hs(hd), c * 64:(c + 1) * 64], lhsT=XTap(r - 1, src_b, hd, c), rhs=Ps_[psrc][hs(hd), c * 64:(c + 1) * 64], start=True, stop=True),
                                     reads=xtk_src + [("P", psrc)], writes=psk(2), inc=last)
                    if do_x:
                        B.op("act", lambda e: e.activation(out=Xs[dst_b], in_=PS(0)[:, 0:SA], func=AF.Copy), reads=psk(0), writes=[("X", dst_b)])
                    if do_xt:
                        B.op("act" if not do_x else "dve", (lambda e: e.activation(out=XTs[dst_b], in_=PS(1)[:, 0:SA], func=AF.Copy)) if not do_x else (lambda e: e.tensor_copy(out=XTs[dst_b], in_=PS(1)[:, 0:SA])),
                             reads=psk(1), writes=[("XT", dst_b)])
                    if do_p:
                        B.op("dve", lambda e: e.tensor_tensor(out=Ps_[pdst], in0=PS(2)[:, 0:SA], in1=Ps_[psrc], op=ALU.add), reads=psk(2) + [("P", psrc)], writes=[("P", pdst)])
                    pump(3)
                pcur = 5 % 2
                Tm = Ps_[pcur]
                tkey = ("P", pcur)
                for (srcf, dstb, dkey, rk_, bank) in ((lambda c, hd: v_[hs(hd), c * 64:(c + 1) * 64], VT, "VT", [sk(2)], 0),
                                                       (lambda c, hd: HK4[hs(hd), c, 0, :], BhT, "BhT", [kHK], 1),
                                                       (lambda c, hd: HK4[hs(hd), c, 1, :], KhT, "KhT", [kHK], 3)):
                    pb = PS(bank).bitcast(BF16)
                    it = 0
                    for c in range(nch):
                        for hd in range(2):
                            it += 1
                            B.op("pe", lambda e: e.transpose(pb[hs(hd), c * 64:(c + 1) * 64], srcf(c, hd), cb("ident")[hs(hd), hd * 64:hd * 64 + 64]),
                                 reads=rk_ + ["cbf"], writes=psk(bank), inc=(it == n_items))
                    B.op("act", lambda e: e.activation(out=dstb, in_=pb[:, 0:SA], func=AF.Copy), reads=psk(bank), writes=[dkey])
                it = 0
                for c in range(nch):
                    for hd in range(2):
                        it += 1
                        B.op("pe", lambda e: e.matmul(PS(2)[hs(hd), c * 64:(c + 1) * 64], lhsT=Gm[hs(hd), c * 256 + 128:c * 256 + 192], rhs=VT[hs(hd), c * 64:(c + 1) * 64], start=True, stop=True),
                             reads=["Gm", "VT"], writes=psk(2), inc=(it == n_items))
                B.op("dve", lambda e: e.tensor_copy(out=Q1, in_=PS(2)[:, 0:SA]), reads=psk(2), writes=["Q1"])
                corder = range(nch) if d == 0 else range(nch - 1, -1, -1)
                for ci_, c in enumerate(corder):
                    cu = cur[hp]
                    nx = 1 - cu
                    cs_ = slice(c * 64, (c + 1) * 64)
                    ub = ci_ % 2
                    for hd in range(2):
                        B.op("pe", lambda e: e.matmul(PS(0)[hs(hd), 0:64], lhsT=AR4[hs(hd), c, 0, :], rhs=STb[hp][cu][hs(hd), :], start=True, stop=True),
                             reads=[kAR, ("STb", hp, cu)], writes=psk(0), inc=(hd == 1))
                    B.op("dve", lambda e: e.tensor_tensor(out=Qb, in0=PS(0)[:, 0:64], in1=Q1[:, cs_], op=ALU.add), reads=psk(0) + ["Q1"], writes=["Qb"])
                    for hd in range(2):
                        B.op("pe", lambda e: e.matmul(PS(1)[hs(hd), 0:64], lhsT=Tm[hs(hd), cs_], rhs=Qb[hs(hd), :], start=True, stop=True),
                             reads=[tkey, "Qb"], writes=psk(1), inc=(hd == 1))
                    B.op("act", lambda e: e.activation(out=UTb[ub], in_=PS(1)[:, 0:64], func=AF.Copy), reads=psk(1), writes=[("UTb", ub)])
                    for hd in range(2):
                        B.op("pe", lambda e: e.matmul(PS(2)[hs(hd), 0:64], lhsT=BhT[hs(hd), cs_], rhs=UTb[ub][hs(hd), :], start=True, stop=False),
                             reads=["BhT", ("UTb", ub)], writes=psk(2), inc=False)
                        B.op("pe", lambda e: e.matmul(PS(2)[hs(hd), 0:64], lhsT=KhT[hs(hd), cs_], rhs=VT[hs(hd), cs_], start=False, stop=True),
                             reads=["KhT", "VT"], writes=psk(2), inc=(hd == 1))
                    gcol = gam[:, S_["gi"], c:c + 1]
                    B.op("dve", lambda e: e.scalar_tensor_tensor(out=STf[hp][nx], in0=STf[hp][cu], scalar=gcol, in1=PS(2)[:, 0:64], op0=ALU.mult, op1=ALU.add),
                         reads=[("STf", hp, cu), ("gam", S_["gi"])] + psk(2), writes=[("STf", hp, nx)])
                    B.op("act", lambda e: e.activation(out=STb[hp][nx], in_=STf[hp][nx], func=AF.Copy), reads=[("STf", hp, nx)], writes=[("STb", hp, nx)])
                    for hd in range(2):
                        B.op("pe", lambda e: e.matmul(PS(3)[hs(hd), cs_], lhsT=STb[hp][cu][hs(hd), :], rhs=AR4[hs(hd), c, 1, :], start=True, stop=False),
                             reads=[("STb", hp, cu), kAR], writes=psk(3), inc=False)
                        B.op("pe", lambda e: e.matmul(PS(3)[hs(hd), cs_], lhsT=UTb[ub][hs(hd), :], rhs=Gm[hs(hd), c * 256 + 64:c * 256 + 128], start=False, stop=False),
                             reads=[("UTb", ub), "Gm"], writes=psk(3), inc=False)
                        B.op("pe", lambda e: e.matmul(PS(3)[hs(hd), cs_], lhsT=VT[hs(hd), cs_], rhs=Gm[hs(hd), c * 256 + 192:c * 256 + 256], start=False, stop=True),
                             reads=["VT", "Gm"], writes=psk(3), inc=(hd == 1))
                    cur[hp] = nx
                    pump(1)
                if d == 0:
                    B.op("act", lambda e: e.activation(out=yfB[hp][:, ssl], in_=PS(3)[:, 0:SA], func=AF.Copy), reads=psk(3), writes=[("yf", hp)])
                else:
                    B.op("dve", lambda e: e.tensor_tensor(out=fin[0], in0=PS(3)[:, 0:SA], in1=yfB[hp][:, ssl], op=ALU.add), reads=psk(3) + [("yf", hp)], writes=[fink[0]])
                    B.op("pe", lambda e: e.matmul(PS(0)[:, 0:SA], lhsT=blkf, rhs=fin[0], start=True, stop=True), reads=["blkf", fink[0]], writes=psk(0))
                    B.op("dve", lambda e: e.scalar_tensor_tensor(out=fin[1], in0=PS(0)[:, 0:SA], scalar=-1.0 / 64, in1=fin[0], op0=ALU.mult, op1=ALU.add), reads=psk(0) + [fink[0]], writes=[fink[1]])
                    B.op("act", lambda e: e.activation(out=fin[2], in_=fin[1], func=AF.Square), reads=[fink[1]], writes=[fink[2]])
                    B.op("pe", lambda e: e.matmul(PS(1)[:, 0:SA], lhsT=blkf, rhs=fin[2], start=True, stop=True), reads=["blkf", fink[2]], writes=psk(1))
                    B.op("act", lambda e: e.activation(out=fin[3], in_=PS(1)[:, 0:SA], func=AF.Ln, scale=1.0 / 64, bias=epsb[:, 2:3]), reads=psk(1) + ["epsb"], writes=[fink[3]])
                    B.op("act", lambda e: e.activation(out=fin[3], in_=fin[3], func=AF.Exp, scale=-0.5), reads=[fink[3]], writes=[fink[3]])
                    B.op("dve", lambda e: e.tensor_tensor(out=fin[1], in0=fin[1], in1=fin[3], op=ALU.mult), reads=[fink[1], fink[3]], writes=[fink[1]])
                    B.op("dve", lambda e: e.tensor_scalar(out=fin[1], in0=fin[1], scalar1=pvc("ln_g", l, hp), scalar2=pvc("ln_b", l, hp), op0=ALU.mult, op1=ALU.add), reads=[fink[1], "pvt"], writes=[fink[1]])
                    B.op("dve", lambda e: e.tensor_tensor(out=fin[2], in0=bcB[hp][:, ssl], in1=v_, op=ALU.mult), reads=[("bc", hp), sk(2)], writes=[fink[2]])
                    B.op("dve", lambda e: e.tensor_tensor(out=fin[1], in0=fin[1], in1=fin[2], op=ALU.add), reads=[fink[1], fink[2]], writes=[fink[1]])
                    B.op("dve", lambda e: e.tensor_tensor(out=obuf[:, hp, ssl], in0=fin[1], in1=obuf[:, hp, ssl], op=ALU.mult), reads=[fink[1], "obuf"], writes=["obuf"])

            def run_all():
                g = pre_gen(0)
                for _ in g:
                    pass
                for ii in range(len(iters)):
                    g = pre_gen(ii + 1) if ii + 1 < len(iters) else iter(())

                    def pump(n, g=g):
                        for _ in range(n):
                            try:
                                next(g)
                            except StopIteration:
                                return
                    post(ii, pump)
                    for _ in g:
                        pass
            run_all()
            B.barrier()
            wepoch[0] += 1

        def program():
            B.dma(scr[0][:, 0:CST_W], cst_d[:, :], writes=[sk(0)], q="pool", semkey="cst")
            B.dma(pvt[:], pv_d[:, :], writes=["pvt"], q="pool", semkey="pvt")
            B.op("dve", lambda e: e.tensor_copy(out=cbf[:], in_=scr[0][:, 0:CST_W]), reads=[sk(0)], writes=["cbf"])
            B.op("dve", lambda e: e.tensor_copy(out=swapf[:], in_=scr[0][:, CST_OFF["swap"]:CST_OFF["swap"] + 128]), reads=[sk(0)], writes=["swapf"])
            B.op("dve", lambda e: e.memset(epsb[:, 0:1], NORM_EPS), writes=["epsb"])
            B.op("dve", lambda e: e.memset(epsb[:, 1:2], 1.0), writes=["epsb"])
            B.op("dve", lambda e: e.memset(epsb[:, 2:3], GN_EPS), writes=["epsb"])
            B.op("dve", lambda e: e.memset(epsb[:, 3:4], 1e-18), writes=["epsb"])
            for l in range(L):
                for i in range(7):
                    B.op("dve", lambda e, l=l, i=i: e.tensor_tensor(out=dvc("c0", l, i), in0=pvc("sh0", l, i), in1=pvc("sh1", l, i), op=ALU.add),
                         reads=["pvt"], writes=["dvt"])
                    B.op("dve", lambda e, l=l, i=i: e.tensor_scalar(out=dvc("c0", l, i), in0=dvc("c0", l, i), scalar1=-1.0, scalar2=1.0, op0=ALU.mult, op1=ALU.add),
                         reads=["dvt"], writes=["dvt"])
                for i in range(2):
                    B.op("dve", lambda e, l=l, i=i: e.tensor_scalar(out=dvc("omka", l, i), in0=pvc("k_a", l, i), scalar1=-1.0, scalar2=1.0, op0=ALU.mult, op1=ALU.add),
                         reads=["pvt"], writes=["dvt"])
                    B.op("act", lambda e, l=l, i=i: e.activation(out=dvc("esinksw", l, i), in_=pvc("sinksw", l, i), func=AF.Exp),
                         reads=["pvt"], writes=["dvt"])
                for i in range(4):
                    B.op("act", lambda e, l=l, i=i: e.activation(out=dvc("cneg", l, i), in_=pvc("lam", l, i), func=AF.Exp, scale=-1.0),
                         reads=["pvt"], writes=["dvt"])
                    B.op("act", lambda e, l=l, i=i: e.activation(out=dvc("cneg", l, i), in_=dvc("cneg", l, i), func=AF.Ln, bias=epsb[:, 1:2]),
                         reads=["dvt", "epsb"], writes=["dvt"])
                    B.op("dve", lambda e, l=l, i=i: e.tensor_scalar(out=dvc("cneg", l, i), in0=dvc("cneg", l, i), scalar1=-8.0, scalar2=None, op0=ALU.mult),
                         reads=["dvt"], writes=["dvt"])


            for s in range(NSEQ):
                for c in range(8):
                    B.dma(xT[:, c, :], x_d[s * 1024 + c * 128:s * 1024 + (c + 1) * 128, :], writes=[("xT", c)], q="pool", semkey=f"xin{c}")
                for l in range(L):
                    rmsnorm_to(l, "h")
                    B.dma(scr[7][:, 0:1024], gw_d[:, l * 1024:(l + 1) * 1024], writes=[sk(7)], q="pool", semkey="gw")
                    B.op("pool", lambda e: e.tensor_copy(out=gw_b[:], in_=scr[7][:, 0:1024]), reads=[sk(7)], writes=["gw_b"])
                    B.dma(upw_f[:], upw_d[:, l * 512:(l + 1) * 512], writes=["upw_f"], q="pool", semkey="upw")
                    B.op("pool", lambda e: e.tensor_copy(out=upw_b[:], in_=upw_f[:]), reads=["upw_f"], writes=["upw_b"])
                    if "C" in cfg.MIX:
                        mixer_C(l)
                        outproj(l, 2)
                    if "D" in cfg.MIX:
                        mixer_D(l)
                        outproj(l, 3)
                    if "B" in cfg.MIX:
                        mixer_B(l)
                        outproj(l, 1)
                    if "A" in cfg.MIX:
                        mixer_A(l)
                        outproj(l, 0)
                B.barrier()
                rmsnorm_to(None, "x")
                for c in range(8):
                    B.dma(y_d[s * 1024 + c * 128:s * 1024 + (c + 1) * 128, :], xT[:, c, :], reads=[("xT", c)], q="pool", semkey=f"yout{c}")
            toks = []
            for c in range(8):
                toks.extend(B.readers.get(("xT", c), {}).values())
            for tname in tap_d:
                t = B.last_w.get(("tap", tname))
                if t is not None:
                    toks.append(t)
            B._wait("pool", toks)

        realB = B
        B = NullBuilder()
        wstate["rec"] = True
        program()
        B = realB
        wstate["rec"] = False
        wepoch[0] = 0
        wo_slot[0] = 0
        program()
        print(f"[build] instructions: {B.n_ins}")
    return nc


_NC_CACHE = {}


def run_trunk(cfg, xs, inp):
    T, L = cfg.T, cfg.DEPTH
    ncore = cfg.NCORES
    assert xs.shape[0] == ncore * cfg.NSEQ
    key = (cfg.T, cfg.NSEQ, cfg.DEPTH, cfg.MIX, cfg.taps)
    if key not in _NC_CACHE:
        _NC_CACHE[key] = build(cfg)
    nc = _NC_CACHE[key]
    cst, alibi, C, S = make_consts(T)
    pv, upw, gw = pack_params(inp, L)
    w_in = np.ascontiguousarray(np.asarray(inp["w_in"], np.float32)[:L].reshape(L * 1024, D_IN))
    w_out = np.ascontiguousarray(np.asarray(inp["w_out"], np.float32)[:L].reshape(L * 1024, 1024))
    in_maps = []
    for c in range(ncore):
        xc = xs[c * cfg.NSEQ:(c + 1) * cfg.NSEQ]
        xt = np.ascontiguousarray(xc.transpose(0, 2, 1)).reshape(cfg.NSEQ * 1024, T)
        in_maps.append({"x": xt, "w_in": w_in, "w_out": w_out, "pvec": pv, "upw": upw, "gw": gw, "cst": cst,
                        "alibi": alibi, "ropeC": C, "ropeS": S})
    res = run_bass_kernel_spmd(nc, in_maps, core_ids=list(range(ncore)))
    outs = []
    for c in range(ncore):
        yt = np.asarray(res.results[c]["y"]).reshape(cfg.NSEQ, 1024, T)
        outs.append(yt.transpose(0, 2, 1))
    return np.ascontiguousarray(np.concatenate(outs, 0)).astype(np.float32), res


def kernel(**inputs):
    inp = {k: np.asarray(v) for k, v in inputs.items()}
    xp, xs_ = inp["x_prompt"], inp["x_sample"]
    xs = np.concatenate([xp, xs_], 0).astype(np.float32)
    cfg = Cfg(T=xs.shape[1], NSEQ=xs.shape[0] // 8, DEPTH=inp["w_in"].shape[0], MIX="ABCD", NCORES=8)
    y, _ = run_trunk(cfg, xs, inp)
    return (y[:xp.shape[0]], y[xp.shape[0]:])
```

```python
import math
from contextlib import ExitStack
import numpy as np
import ml_dtypes
import concourse.bass as bass
import concourse.mybir as mybir
from concourse.bass_utils import run_bass_kernel_spmd

F32 = mybir.dt.float32
BF16 = mybir.dt.bfloat16
AF = mybir.ActivationFunctionType
ALU = mybir.AluOpType

D_MODEL = 1024
GRID_W = 64
D_IN = 3200
A_W, B_W, C_W, D_W = 1152, 768, 512, 768
OFF_A, OFF_B, OFF_C, OFF_D = 0, 1152, 1920, 2432
NORM_EPS = 1e-6
GN_EPS = 64e-5
LAMW = math.exp(-0.5)
SEM_LIMIT = 30000


class Cfg:
    def __init__(self, T=2048, NSEQ=5, DEPTH=4, MIX="ABCD", NCORES=8, taps=()):
        self.T, self.NSEQ, self.DEPTH, self.MIX, self.NCORES = T, NSEQ, DEPTH, MIX, NCORES
        self.taps = tuple(taps)


PV_FIELDS = [("norm_g", 8), ("sh0", 7), ("sh1", 7), ("w0", 4), ("a0", 4), ("k_k", 2), ("k_a", 2), ("r_k", 2),
             ("ln_g", 2), ("ln_b", 2), ("qn", 1), ("kn", 1), ("cw", 8), ("cb", 2), ("gb", 8), ("lam", 4),
             ("sink", 2), ("sinksw", 2)]
DV_FIELDS = [("c0", 7), ("omka", 2), ("cneg", 4), ("esinksw", 2), ("gbh", 8)]


def _offsets(fields):
    off, d = 0, {}
    for n, w in fields:
        d[n] = off
        off += w
    return d, off


PV_OFF, PV_W = _offsets(PV_FIELDS)
DV_OFF, DV_W = _offsets(DV_FIELDS)


def pv_col(L, name, l, i=0):
    if name == "final_g":
        return L * PV_W + i
    return l * PV_W + PV_OFF[name] + i


def dv_col(name, l, i=0):
    return l * DV_W + DV_OFF[name] + i


CST_FIELDS = [("ident", 128), ("blk64", 128), ("ones", 128), ("swap", 128), ("perm", 128), ("mfwd", 256),
              ("mbwd", 256), ("mfwdT", 64), ("mbwdT", 64), ("rst", 512)]
CST_OFF, CST_W = _offsets(CST_FIELDS)


def make_consts(T):
    c = np.zeros((128, CST_W), np.float32)
    alibi = np.zeros((128, 4 * 384), np.float32)
    p = np.arange(128)
    o = CST_OFF
    c[:, o["ident"]:o["ident"] + 128] = np.eye(128)
    c[:, o["blk64"]:o["blk64"] + 128] = (p[:, None] // 64 == p[None, :] // 64)
    c[:, o["ones"]:o["ones"] + 128] = 1.0
    c[:, o["swap"]:o["swap"] + 128] = (p[:, None] == (p[None, :] + 64) % 128)
    d = p % 64
    partner = np.where((d % 32) < 16, p + 16, p - 16)
    c[:, o["perm"]:o["perm"] + 128] = (p[:, None] == partner[None, :])
    s = (p % 64)[:, None]
    t = np.arange(64)[None, :]
    strict_f, incl_f = (s < t), (s <= t)
    strict_b, incl_b = (s > t), (s >= t)
    c[:, o["mfwd"]:o["mfwd"] + 256] = np.concatenate([strict_f, incl_f, strict_f, incl_f], 1)
    c[:, o["mbwd"]:o["mbwd"] + 256] = np.concatenate([strict_b, incl_b, strict_b, incl_b], 1)
    c[:, o["mfwdT"]:o["mfwdT"] + 64] = (t < s)
    c[:, o["mbwdT"]:o["mbwdT"] + 64] = (t > s)
    c[:, o["rst"]:o["rst"] + 512] = (np.arange(512)[None, :] % 64 != 0)
    cc = np.arange(384)[None, :]
    dist = np.abs(cc - 128 - p[:, None]).astype(np.float64)
    for h in range(4):
        slope = 2.0 ** (-8.0 * (h + 1) / 4)
        e = np.where(dist <= 128, np.exp(-slope * dist), 0.0)
        alibi[:, h * 384:(h + 1) * 384] = e
    row = (np.arange(T) // GRID_W).astype(np.float64)
    col = (np.arange(T) % GRID_W).astype(np.float64)
    inv = 10000.0 ** (-np.arange(0, 32, 2, dtype=np.float64) / 32)
    C = np.zeros((128, T), np.float32)
    S = np.zeros((128, T), np.float32)
    for pp in range(128):
        dd = pp % 64
        pos = row if dd < 32 else col
        f = inv[dd % 16]
        ang = pos * f
        C[pp] = np.cos(ang)
        S[pp] = (-np.sin(ang)) if (dd % 32) < 16 else np.sin(ang)
    return c, alibi, C, S


def pack_params(inp, L):
    pv = np.zeros((128, L * PV_W + 8), np.float32)

    def put(name, l, i, vec128):
        pv[:, pv_col(L, name, l, i)] = vec128

    for l in range(L):
        for i in range(8):
            put("norm_g", l, i, inp["norm_g"][l, i * 128:(i + 1) * 128])
        for i in range(7):
            put("sh0", l, i, inp["rwkv_shift"][l, 0, i * 128:(i + 1) * 128])
            put("sh1", l, i, inp["rwkv_shift"][l, 1, i * 128:(i + 1) * 128])
        for d in range(2):
            for hp in range(2):
                put("w0", l, d * 2 + hp, inp["rwkv_w0"][l, d, hp * 128:(hp + 1) * 128])
                put("a0", l, d * 2 + hp, inp["rwkv_a0"][l, d, hp * 128:(hp + 1) * 128])
        rk = inp["rwkv_r_k"][l].reshape(256)
        for hp in range(2):
            sl = slice(hp * 128, (hp + 1) * 128)
            put("k_k", l, hp, inp["rwkv_k_k"][l, sl])
            put("k_a", l, hp, inp["rwkv_k_a"][l, sl])
            put("r_k", l, hp, rk[sl])
            put("ln_g", l, hp, inp["rwkv_ln_g"][l, sl])
            put("ln_b", l, hp, inp["rwkv_ln_b"][l, sl])
        put("qn", l, 0, np.tile(inp["attn_q_norm"][l], 2))
        put("kn", l, 0, np.tile(inp["attn_k_norm"][l], 2))
        for j in range(4):
            for cc in range(2):
                put("cw", l, j * 2 + cc, inp["lru_conv_w"][l, j, cc * 128:(cc + 1) * 128])
        for cc in range(2):
            put("cb", l, cc, inp["lru_conv_b"][l, cc * 128:(cc + 1) * 128])
        for d in range(2):
            for k in range(2):
                for cc in range(2):
                    put("gb", l, (d * 2 + k) * 2 + cc, inp["lru_gate_b"][l, d, k, cc * 128:(cc + 1) * 128])
            for cc in range(2):
                put("lam", l, d * 2 + cc, inp["lru_lambda"][l, d, cc * 128:(cc + 1) * 128])
        sk = inp["swa_sink"][l]
        for cc in range(2):
            put("sink", l, cc, np.repeat(sk[2 * cc:2 * cc + 2], 64))
            put("sinksw", l, cc, np.repeat(sk[2 * cc:2 * cc + 2][::-1], 64))
    for i in range(8):
        pv[:, L * PV_W + i] = inp["final_g"][i * 128:(i + 1) * 128]
    upw = np.zeros((128, L * 2 * 256), np.float32)
    for l in range(L):
        for d in range(2):
            upw[0:64, (l * 2 + d) * 256:(l * 2 + d + 1) * 256] = inp["rwkv_w_up"][l, d]
            upw[64:128, (l * 2 + d) * 256:(l * 2 + d + 1) * 256] = inp["rwkv_a_up"][l, d]
    gw = np.zeros((128, L * 8 * 128), np.float32)
    for l in range(L):
        for d in range(2):
            for k in range(2):
                for cc in range(2):
                    base = (((l * 2 + d) * 2 + k) * 2 + cc) * 128
                    for b in range(2):
                        gw[b * 64:(b + 1) * 64, base + b * 64:base + (b + 1) * 64] = inp["lru_gate_w"][l, d, k, 2 * cc + b]
    return pv, upw, gw


class SemCounter:
    def __init__(self, bld, name, step):
        self.bld, self.name, self.step = bld, name, step
        self.n = 0
        self._new()

    def _new(self):
        self.sem = self.bld.es.enter_context(self.bld.nc.semaphore(f"{self.name}_{self.n}"))
        self.sname = f"{self.name}_{self.n}"
        self.n += 1
        self.val = 0

    def next(self):
        if self.val + self.step > SEM_LIMIT:
            self._new()
        self.val += self.step
        return (self.sname, self.sem, self.val)


class Builder:
    def __init__(self, nc, es):
        self.nc, self.es = nc, es
        self.eng = {"pe": nc.tensor, "act": nc.scalar, "dve": nc.vector, "pool": nc.gpsimd, "sp": nc.sync}
        self.cnt = {e: SemCounter(self, e, 1) for e in ("pe", "act", "dve", "pool")}
        self.dcnt = {}
        self.waited = {e: {} for e in self.eng}
        self.last_w = {}
        self.readers = {}
        self.pending = {e: ([], []) for e in self.eng}
        self.n_ins = 0

    def _deps(self, reads, writes):
        toks = []
        for k in reads:
            t = self.last_w.get(k)
            if t is not None:
                toks.append(t)
        for k in writes:
            t = self.last_w.get(k)
            if t is not None:
                toks.append(t)
            r = self.readers.get(k)
            if r:
                toks.extend(r.values())
        return toks

    def _wait(self, e, toks):
        w = self.waited[e]
        need = {}
        for (sn, sem, val) in toks:
            if w.get(sn, 0) < val and need.get(sn, (None, 0))[1] < val:
                need[sn] = (sem, val)
        for sn, (sem, val) in need.items():
            self.eng[e].wait_ge(sem, val)
            w[sn] = val

    def _commit(self, tok, reads, writes):
        for k in writes:
            self.last_w[k] = tok
            self.readers[k] = {}
        for k in reads:
            self.readers.setdefault(k, {})[tok[0]] = tok

    def op(self, e, fn, reads=(), writes=(), inc=True):
        self._wait(e, self._deps(reads, writes))
        ins = fn(self.eng[e])
        self.n_ins += 1
        pr, pw = self.pending[e]
        if not inc:
            pr.extend(reads)
            pw.extend(writes)
            return
        tok = self.cnt[e].next()
        ins.then_inc(tok[1], 1)
        self._commit(tok, list(reads) + pr, list(writes) + pw)
        self.pending[e] = ([], [])

    def dma(self, out, in_, reads=(), writes=(), q="sp", semkey=None):
        self._wait(q, self._deps(reads, writes))
        ins = self.eng[q].dma_start(out=out, in_=in_)
        self.n_ins += 1
        if semkey not in self.dcnt:
            self.dcnt[semkey] = SemCounter(self, "d" + semkey, 16)
        tok = self.dcnt[semkey].next()
        ins.then_inc(tok[1], 16)
        self._commit(tok, reads, writes)
        return tok

    def barrier(self):
        toks = []
        for c in list(self.cnt.values()) + list(self.dcnt.values()):
            if c.val > 0:
                toks.append((c.sname, c.sem, c.val))
        for e in self.eng:
            self._wait(e, toks)

    def final_wait(self, e, keys):
        toks = []
        for k in keys:
            t = self.last_w.get(k)
            if t is not None:
                toks.append(t)
        self._wait(e, toks)


class NullBuilder:
    def __init__(self):
        self.readers, self.last_w, self.n_ins = {}, {}, 0

    def op(self, *a, **k):
        pass

    def dma(self, *a, **k):
        pass

    def barrier(self):
        pass

    def _wait(self, *a, **k):
        pass


SCRW = 2048
NSCR = 8


def build(cfg):
    T, NSEQ, L = cfg.T, cfg.NSEQ, cfg.DEPTH
    NSEG, NT, NCH = T // 512, T // 128, T // 64
    assert T % 512 == 0 and T <= 2048
    nc = bass.Bass("TRN2", target_bir_lowering=False)

    def din(name, shape, dt=F32):
        return nc.dram_tensor(name, list(shape), dt, kind="ExternalInput").ap()

    x_d = din("x", [NSEQ * 1024, T])
    win_d = din("w_in", [L * 1024, D_IN])
    wout_d = din("w_out", [L * 1024, 1024])
    pv_d = din("pvec", [128, L * PV_W + 8])
    upw_d = din("upw", [128, L * 512])
    gw_d = din("gw", [128, L * 1024])
    cst_d = din("cst", [128, CST_W])
    alibi_d = din("alibi", [128, 1536])
    ropeC_d = din("ropeC", [128, T])
    ropeS_d = din("ropeS", [128, T])
    y_d = nc.dram_tensor("y", [NSEQ * 1024, T], F32, kind="ExternalOutput").ap()
    tap_d = {}
    for (tname, tshape) in cfg.taps:
        tap_d[tname] = nc.dram_tensor("tap_" + tname, list(tshape), F32, kind="ExternalOutput").ap()

    es = ExitStack()
    with es:
        B = Builder(nc, es)

        def sb(name, shape, dt):
            return es.enter_context(nc.sbuf_tensor(name, list(shape), dt))

        xT = sb("xT", [128, 8, T], F32)
        hT = sb("hT", [128, 8, max(T, 2048)], BF16)
        pvt = sb("pvt", [128, L * PV_W + 8], F32)
        dvt = sb("dvt", [128, L * DV_W], F32)
        CBW = CST_W
        cbf = sb("cbf", [128, CBW], BF16)
        swapf = sb("swapf", [128, 128], F32)
        upw_f = sb("upw_f", [128, 512], F32)
        upw_b = sb("upw_b", [128, 512], BF16)
        gw_b = sb("gw_b", [128, 1024], BF16)
        NWS = 3
        wst = [sb(f"wst{i}", [128, 8, 128], F32) for i in range(2)]
        wbf = [sb(f"wbf{i}", [128, 8, 128], BF16) for i in range(NWS)]
        wost = [sb(f"wost{i}", [128, 256], F32) for i in range(2)]
        wobf = [sb(f"wobf{i}", [128, 256], BF16) for i in range(2)]
        obuf = sb("obuf", [128, 2, T], BF16)
        sqb = sb("sqb", [128, 2, 512], BF16)
        rstd = sb("rstd", [128, 512], F32)
        epsb = sb("epsb", [128, 4], F32)
        tmpA = sb("tmpA", [128, 512], F32)
        tmpB = sb("tmpB", [128, 512], F32)
        gam = sb("gam", [128, 3, 8], F32)
        scr = [sb(f"scr{i}", [128, SCRW], F32) for i in range(NSCR)]
        psum = es.enter_context(nc.psum_tensor("psum", [128, 8 * 512], F32))

        def sk(i):
            return ("scr", i)

        def sbf(i):
            return scr[i][:].bitcast(BF16)

        def PS(b, n=1):
            return psum[:, b * 512:(b + n) * 512]

        def psk(b, n=1):
            return [("ps", b + i) for i in range(n)]

        def cb(name, w=None, off=0):
            o = CST_OFF[name] + off
            return cbf[:, o:o + (w if w is not None else dict(CST_FIELDS)[name])]

        def pvc(name, l, i=0):
            c = pv_col(L, name, l, i)
            return pvt[:, c:c + 1]

        def dvc(name, l, i=0):
            c = dv_col(name, l, i)
            return dvt[:, c:c + 1]

        def tap(name, ap_sb, rkeys, rows=128):
            if name in tap_d:
                B.dma(tap_d[name], ap_sb, reads=rkeys, writes=[("tap", name)], q="pool", semkey="tap" + name)

        wepoch = [0]
        wplan = []
        wstate = {"rec": True, "next": 0, "issued": 0}

        def _issue_w(i):
            l, ranges, ep = wplan[i]
            si, bi = i % 2, i % NWS
            off = 0
            src = win_d[l * 1024:(l + 1) * 1024, :].rearrange("(k p) c -> p k c", p=128)
            for (c0, n) in ranges:
                B.dma(wst[si][:, :, off:off + n], src[:, :, c0:c0 + n], writes=[f"wst{si}"], q="sp", semkey=f"wst{si}")
                off += n
            B.op("pool", lambda e: e.tensor_copy(out=wbf[bi][:, :, 0:off], in_=wst[si][:, :, 0:off]),
                 reads=[f"wst{si}"], writes=[f"wbf{bi}"])

        def load_win(l, ranges):
            m = sum(n for _, n in ranges)
            if wstate["rec"]:
                wplan.append((l, tuple(ranges), wepoch[0]))
                return 0, m
            i = wstate["next"]
            wstate["next"] += 1
            assert wplan[i][0] == l and wplan[i][1] == tuple(ranges), (i, wplan[i], l, ranges)
            hi = i
            while hi + 1 < len(wplan) and hi + 1 <= i + 2 and wplan[hi + 1][2] == wplan[i][2]:
                hi += 1
            while wstate["issued"] <= hi:
                _issue_w(wstate["issued"])
                wstate["issued"] += 1
            return i % NWS, m

        def inproj(l, ranges, bank0):
            bi, m = load_win(l, ranges)
            for sg in range(NSEG):
                for k in range(8):
                    B.op("pe", lambda e, k=k, sg=sg: e.matmul(PS(bank0 + sg)[0:m, :], lhsT=wbf[bi][:, k, 0:m], rhs=hT[:, k, sg * 512:(sg + 1) * 512],
                                                              start=(k == 0), stop=(k == 7)),
                         reads=[f"wbf{bi}", "hT"], writes=psk(bank0 + sg), inc=(k == 7))
            return m

        def inproj_tok(l, ranges, dst_fn, post):
            bi, m = load_win(l, ranges)
            assert m == 64
            for tb in range((NT + 7) // 8):
                bank = 4 + tb % 2
                ntt = min(8, NT - tb * 8)
                for q in range(ntt):
                    tt = tb * 8 + q
                    for k in range(8):
                        B.op("pe", lambda e, k=k, tt=tt, q=q, bank=bank: e.matmul(PS(bank)[:, q * 64:(q + 1) * 64], lhsT=hT[:, k, tt * 128:(tt + 1) * 128],
                                                                                  rhs=wbf[bi][:, k, 0:64], start=(k == 0), stop=(k == 7)),
                             reads=[f"wbf{bi}", "hT"], writes=psk(bank), inc=(k == 7))
                post(tb, bank, ntt)

        wo_slot = [0]

        def outproj(l, g):
            for co in range(8):
                si = wo_slot[0] % 2
                wo_slot[0] += 1
                for kc in range(2):
                    r0 = l * 1024 + g * 256 + kc * 128
                    B.dma(wost[si][:, kc * 128:(kc + 1) * 128], wout_d[r0:r0 + 128, co * 128:(co + 1) * 128],
                          writes=[f"wost{si}"], q="sp", semkey=f"wost{si}")
                B.op("pool", lambda e, si=si: e.tensor_copy(out=wobf[si][:], in_=wost[si][:]),
                     reads=[f"wost{si}"], writes=[f"wobf{si}"])
                nb = min(2, NSEG)
                for half in range(NSEG // nb):
                    pb = 4 + 2 * ((co * (NSEG // nb) + half) % 2)
                    for j in range(nb):
                        sg = half * nb + j
                        for kc in range(2):
                            B.op("pe", lambda e, si=si, kc=kc, sg=sg, j=j, pb=pb: e.matmul(
                                PS(pb + j), lhsT=wobf[si][:, kc * 128:(kc + 1) * 128], rhs=obuf[:, kc, sg * 512:(sg + 1) * 512],
                                start=(kc == 0), stop=(kc == 1)),
                                reads=[f"wobf{si}", "obuf"], writes=psk(pb + j), inc=(kc == 1))
                    t0 = half * nb * 512
                    w = nb * 512
                    B.op("dve", lambda e, co=co, pb=pb, t0=t0, w=w, nb=nb: e.tensor_tensor(
                        out=xT[:, co, t0:t0 + w], in0=PS(pb, nb), in1=xT[:, co, t0:t0 + w], op=ALU.add),
                        reads=psk(pb, nb) + [("xT", co)], writes=[("xT", co)])

        def rmsnorm_to(l, dest_kind):
            for sg in range(NSEG):
                sl = slice(sg * 512, (sg + 1) * 512)
                for c in range(8):
                    B.op("act", lambda e, c=c: e.activation(out=sqb[:, c % 2, :], in_=xT[:, c, sl], func=AF.Square),
                         reads=[("xT", c)], writes=[("sqb", c % 2)])
                    B.op("pe", lambda e, c=c: e.matmul(PS(0), lhsT=cb("ones"), rhs=sqb[:, c % 2, :], start=(c == 0), stop=(c == 7)),
                         reads=[("sqb", c % 2), "cbf"], writes=psk(0), inc=True)
                B.op("act", lambda e: e.activation(out=rstd[:], in_=PS(0), func=AF.Ln, scale=1.0 / 1024, bias=epsb[:, 0:1]),
                     reads=psk(0) + ["epsb"], writes=["rstd"])
                B.op("act", lambda e: e.activation(out=rstd[:], in_=rstd[:], func=AF.Exp, scale=-0.5), reads=["rstd"], writes=["rstd"])
                for c in range(8):
                    if dest_kind == "h":
                        gcol = pvc("norm_g", l, c)
                        B.op("dve", lambda e, c=c, gcol=gcol: e.scalar_tensor_tensor(out=hT[:, c, sl], in0=xT[:, c, sl], scalar=gcol, in1=rstd[:],
                                                                                      op0=ALU.mult, op1=ALU.mult),
                             reads=[("xT", c), "rstd", "pvt"], writes=["hT"])
                    else:
                        gcol = pvt[:, L * PV_W + c:L * PV_W + c + 1]
                        B.op("dve", lambda e, c=c, gcol=gcol: e.scalar_tensor_tensor(out=xT[:, c, sl], in0=xT[:, c, sl], scalar=gcol, in1=rstd[:],
                                                                                      op0=ALU.mult, op1=ALU.mult),
                             reads=[("xT", c), "rstd", "pvt"], writes=[("xT", c)])

        def mixer_C(l):
            for cc in range(2):
                inproj(l, [(OFF_C + cc * 128, 128)], 0)
                Z = PS(0, NSEG)
                zk = psk(0, NSEG)
                xc = scr[0][:, 0:T]
                B.op("act", lambda e: e.activation(out=xc, in_=Z, func=AF.Identity, scale=pvc("cw", l, 2 * 2 + cc), bias=pvc("cb", l, cc)),
                     reads=zk + ["pvt"], writes=[sk(0)])
                for (j, sh) in ((0, -2), (1, -1), (3, 1)):
                    if sh < 0:
                        o_ap, i_ap = xc[:, -sh:T], Z[:, 0:T + sh]
                    else:
                        o_ap, i_ap = xc[:, 0:T - sh], Z[:, sh:T]
                    B.op("dve", lambda e, o_ap=o_ap, i_ap=i_ap, j=j: e.scalar_tensor_tensor(out=o_ap, in0=i_ap, scalar=pvc("cw", l, j * 2 + cc), in1=o_ap,
                                                                                            op0=ALU.mult, op1=ALU.add),
                         reads=zk + [sk(0), "pvt"], writes=[sk(0)])
                xcb = sbf(1)[:, 0:T]
                B.op("act", lambda e: e.activation(out=xcb, in_=xc, func=AF.Copy), reads=[sk(0)], writes=[sk(1)])
                for d in range(2):
                    rb, ib, sbuf_, hb = scr[2][:, 0:T], scr[3][:, 0:T], scr[4][:, 0:T], scr[5 + d][:, 0:T]
                    for sg in range(NSEG):
                        sl = slice(sg * 512, (sg + 1) * 512)
                        for k in range(2):
                            bank = 4 + 2 * k + sg % 2
                            wcol = (((d * 2 + k) * 2 + cc)) * 128
                            B.op("pe", lambda e, bank=bank, wcol=wcol, sl=sl: e.matmul(PS(bank), lhsT=gw_b[:, wcol:wcol + 128], rhs=xcb[:, sl], start=True, stop=True),
                                 reads=["gw_b", sk(1)], writes=psk(bank))
                            dst = rb if k == 0 else ib
                            B.op("act", lambda e, bank=bank, dst=dst, sl=sl, k=k: e.activation(out=dst[:, sl], in_=PS(bank), func=AF.Sigmoid,
                                                                                               bias=pvc("gb", l, (d * 2 + k) * 2 + cc)),
                                 reads=psk(bank) + ["pvt"], writes=[sk(2 + k)])
                    B.op("act", lambda e: e.activation(out=rb, in_=rb, func=AF.Exp, scale=dvc("cneg", l, d * 2 + cc)), reads=[sk(2), "dvt"], writes=[sk(2)])
                    B.op("act", lambda e: e.activation(out=sbuf_, in_=rb, func=AF.Square), reads=[sk(2)], writes=[sk(4)])
                    B.op("act", lambda e: e.activation(out=sbuf_, in_=sbuf_, func=AF.Ln, scale=-1.0, bias=epsb[:, 1:2]), reads=[sk(4), "epsb"], writes=[sk(4)])
                    B.op("act", lambda e: e.activation(out=sbuf_, in_=sbuf_, func=AF.Exp, scale=0.5), reads=[sk(4)], writes=[sk(4)])
                    B.op("dve", lambda e: e.tensor_tensor(out=ib, in0=ib, in1=xc, op=ALU.mult), reads=[sk(3), sk(0)], writes=[sk(3)])
                    B.op("dve", lambda e: e.tensor_tensor(out=ib, in0=ib, in1=sbuf_, op=ALU.mult), reads=[sk(3), sk(4)], writes=[sk(3)])
                    if d == 0:
                        B.op("dve", lambda e: e.tensor_tensor_scan(out=hb, data0=rb, data1=ib, initial=0.0, op0=ALU.mult, op1=ALU.add),
                             reads=[sk(2), sk(3)], writes=[sk(5 + d)])
                    else:
                        B.op("dve", lambda e: e.tensor_tensor_scan(out=hb[:, ::-1], data0=rb[:, ::-1], data1=ib[:, ::-1], initial=0.0, op0=ALU.mult, op1=ALU.add),
                             reads=[sk(2), sk(3)], writes=[sk(5 + d)])
                hf, hbk = scr[5][:, 0:T], scr[6][:, 0:T]
                B.op("dve", lambda e: e.tensor_tensor(out=hf, in0=hf, in1=hbk, op=ALU.add), reads=[sk(5), sk(6)], writes=[sk(5)])
                inproj(l, [(OFF_C + 256 + cc * 128, 128)], 0)
                sgt = scr[2][:, 0:T]
                B.op("act", lambda e: e.activation(out=sgt, in_=PS(0, NSEG), func=AF.Silu), reads=psk(0, NSEG), writes=[sk(2)])
                B.op("dve", lambda e: e.tensor_tensor(out=obuf[:, cc, :], in0=hf, in1=sgt, op=ALU.mult), reads=[sk(5), sk(2)], writes=["obuf"])

        def load_qk(l, off_q, off_k, cc, prep):
            qT = sbf(0)[:, 0:T]
            kT = sbf(0)[:, SCRW:SCRW + T]
            prep([(off_q + cc * 128, 128)], qT, True)
            prep([(off_k + cc * 64, 64), (off_k + cc * 64, 64)], kT, False)
            return qT, kT

        def load_vaug(l, off_v, cc):
            va = sbf(1)
            vav = va[:, 0:NT * 192].rearrange("p (t c) -> p t c", c=192)
            B.op("pool", lambda e: e.memset(vav[:, :, 64:128], 1.0), writes=[sk(1)])

            def post(tb, bank, ntt):
                src = PS(bank)[:, 0:ntt * 64].rearrange("p (t c) -> p t c", c=64)
                B.op("act", lambda e: e.activation(out=vav[:, tb * 8:tb * 8 + ntt, 0:64], in_=src, func=AF.Copy), reads=psk(bank), writes=[sk(1)])
                B.op("dve", lambda e: e.tensor_copy(out=vav[:, tb * 8:tb * 8 + ntt, 128:192], in_=src), reads=psk(bank), writes=[sk(1)])
            inproj_tok(l, [(off_v + cc * 64, 64)], None, post)
            return vav

        def load_gate(l, off_g, cc):
            inproj(l, [(off_g + cc * 128, 128)], 0)
            sgt = scr[2][:, 0:T]
            B.op("act", lambda e: e.activation(out=sgt, in_=PS(0, NSEG), func=AF.Silu), reads=psk(0, NSEG), writes=[sk(2)])
            return sgt

        def attn_post(l, cc, qg, bx, by, sgt, sink):
            X, Y = PS(bx), PS(by)
            gsl = slice(qg * 512, (qg + 1) * 512)
            if sink:
                B.op("dve", lambda e: e.tensor_scalar(out=tmpA[0:64, :], in0=Y[0:64, :], scalar1=dvc("esinksw", l, cc)[0:64, :], scalar2=None, op0=ALU.add),
                     reads=psk(by) + ["dvt"], writes=["tmpA"])
                B.op("dve", lambda e: e.tensor_scalar(out=tmpA[64:128, :], in0=X[64:128, :], scalar1=dvc("esinksw", l, cc)[64:128, :], scalar2=None, op0=ALU.add),
                     reads=psk(bx) + ["dvt"], writes=["tmpA"])
                B.op("act", lambda e: e.activation(out=tmpA[:], in_=tmpA[:], func=AF.Ln), reads=["tmpA"], writes=["tmpA"])
            else:
                B.op("act", lambda e: e.activation(out=tmpA[0:64, :], in_=Y[0:64, :], func=AF.Ln), reads=psk(by), writes=["tmpA"])
                B.op("act", lambda e: e.activation(out=tmpA[64:128, :], in_=X[64:128, :], func=AF.Ln), reads=psk(bx), writes=["tmpA"])
            B.op("act", lambda e: e.activation(out=tmpA[:], in_=tmpA[:], func=AF.Exp, scale=-1.0), reads=["tmpA"], writes=["tmpA"])
            B.op("pe", lambda e: e.matmul(PS(0), lhsT=swapf[:], rhs=tmpA[:], start=True, stop=True), reads=["swapf", "tmpA"], writes=psk(0))
            B.op("dve", lambda e: e.tensor_tensor(out=tmpB[:], in0=PS(0), in1=sgt[:, gsl], op=ALU.mult), reads=psk(0) + [sk(2)], writes=["tmpB"])
            B.op("dve", lambda e: e.tensor_tensor(out=obuf[0:64, cc, gsl], in0=X[0:64, :], in1=tmpB[0:64, :], op=ALU.mult),
                 reads=psk(bx) + ["tmpB"], writes=["obuf"])
            B.op("dve", lambda e: e.tensor_tensor(out=obuf[64:128, cc, gsl], in0=Y[64:128, :], in1=tmpB[64:128, :], op=ALU.mult),
                 reads=psk(by) + ["tmpB"], writes=["obuf"])

        def mixer_D(l):
            B.dma(scr[3][:, 0:1536], alibi_d[:, :], writes=[sk(3)], q="pool", semkey="alibi")

            def prep(ranges, dst, isq):
                inproj(l, ranges, 0)
                B.op("act", lambda e: e.activation(out=dst, in_=PS(0, NSEG), func=AF.Copy), reads=psk(0, NSEG), writes=[sk(0)])
            for cc in range(2):
                qT, kT = load_qk(l, OFF_D, OFF_D + 256, cc, prep)
                vav = load_vaug(l, OFF_D + 384, cc)
                sgt = load_gate(l, OFF_D + 512, cc)
                ptv = sbf(4)
                exv = sbf(5)
                atab = scr[3][:, 2 * cc * 384:(2 * cc + 2) * 384].rearrange("p (h c) -> p h c", h=2)

                def qrange(j):
                    return max(j - 1, 0), min(j + 1, NT - 1)

                def qk(j):
                    b0, b1 = qrange(j)
                    ncol = (b1 - b0 + 1) * 128
                    for hh in range(2):
                        ph = slice(hh * 64, hh * 64 + 64)
                        bank = 2 * (j % 2) + hh
                        B.op("pe", lambda e: e.matmul(PS(bank)[:, 0:ncol], lhsT=kT[ph, j * 128:(j + 1) * 128], rhs=qT[ph, b0 * 128:b0 * 128 + ncol], start=True, stop=True),
                             reads=[sk(0)], writes=psk(bank), inc=(hh == 1))

                def ex(j):
                    b0, b1 = qrange(j)
                    ncol = (b1 - b0 + 1) * 128
                    tcol0 = (b0 - (j - 1)) * 128
                    src = PS(2 * (j % 2), 2).rearrange("p (b c) -> p b c", b=2)[:, :, 0:ncol]
                    exs = exv[:, (j % 2) * 768:(j % 2 + 1) * 768].rearrange("p (h c) -> p h c", h=2)[:, :, 0:ncol]
                    pts = ptv[:, (j % 4) * 768:(j % 4 + 1) * 768].rearrange("p (h c) -> p h c", h=2)[:, :, 0:ncol]
                    B.op("act", lambda e: e.activation(out=exs, in_=src, func=AF.Exp, scale=0.125), reads=psk(2 * (j % 2), 2), writes=[("ex", j % 2)])
                    B.op("dve", lambda e: e.tensor_tensor(out=pts, in0=exs, in1=atab[:, :, tcol0:tcol0 + ncol], op=ALU.mult),
                         reads=[("ex", j % 2), sk(3)], writes=[("pt", j % 4)])

                def pv_block(i):
                    qg = i // 4
                    js = [j for j in (i - 1, i, i + 1) if 0 <= j < NT]
                    for hh in range(2):
                        bank = 4 + 2 * (qg % 2) + hh
                        for n, j in enumerate(js):
                            b0, _ = qrange(j)
                            c0 = (j % 4) * 768 + hh * 384 + (i - b0) * 128
                            B.op("pe", lambda e: e.matmul(PS(bank)[:, (i % 4) * 128:(i % 4 + 1) * 128], lhsT=vav[:, j, hh * 64:hh * 64 + 128], rhs=ptv[:, c0:c0 + 128],
                                                         start=(n == 0), stop=(n == len(js) - 1)),
                                 reads=[sk(1), ("pt", j % 4)], writes=psk(bank), inc=(n == len(js) - 1))

                qk(0)
                for j in range(NT):
                    if j + 1 < NT:
                        qk(j + 1)
                    ex(j)
                    blocks = []
                    if j >= 1:
                        blocks.append(j - 1)
                    if j == NT - 1:
                        blocks.append(j)
                    for i in blocks:
                        pv_block(i)
                        if i % 4 == 3:
                            qg = i // 4
                            attn_post(l, cc, qg, 4 + 2 * (qg % 2), 4 + 2 * (qg % 2) + 1, sgt, True)

        def mixer_B(l):
            B.dma(scr[6][:, 0:T], ropeC_d[:, :], writes=[sk(6)], q="pool", semkey="ropeC")
            B.dma(scr[7][:, 0:T], ropeS_d[:, :], writes=[sk(7)], q="pool", semkey="ropeS")
            t1b, t2b = scr[3][:, 0:512], scr[3][:, 512:1024]
            qgb = sbf(3)[:, 2048:2560]

            def prep(ranges, dst, isq):
                inproj(l, ranges, 0)
                gcol = pvc("qn" if isq else "kn", l, 0)
                for sg in range(NSEG):
                    sl = slice(sg * 512, (sg + 1) * 512)
                    Z = PS(sg)
                    B.op("act", lambda e: e.activation(out=qgb, in_=Z, func=AF.Copy, scale=gcol), reads=psk(sg) + ["pvt"], writes=[("qgb",)])
                    B.op("act", lambda e: e.activation(out=sqb[:, 0, :], in_=Z, func=AF.Square), reads=psk(sg), writes=[("sqb", 0)])
                    B.op("pe", lambda e: e.matmul(PS(4), lhsT=cb("blk64"), rhs=sqb[:, 0, :], start=True, stop=True), reads=[("sqb", 0), "cbf"], writes=psk(4))
                    B.op("pe", lambda e: e.matmul(PS(5), lhsT=cb("perm"), rhs=qgb, start=True, stop=True), reads=[("qgb",), "cbf"], writes=psk(5))
                    B.op("act", lambda e: e.activation(out=rstd[:], in_=PS(4), func=AF.Ln, scale=1.0 / 64, bias=epsb[:, 0:1]), reads=psk(4) + ["epsb"], writes=["rstd"])
                    B.op("act", lambda e: e.activation(out=rstd[:], in_=rstd[:], func=AF.Exp, scale=-0.5), reads=["rstd"], writes=["rstd"])
                    B.op("dve", lambda e: e.tensor_tensor(out=t1b, in0=qgb, in1=scr[6][:, sl], op=ALU.mult), reads=[("qgb",), sk(6)], writes=[("t1b",)])
                    B.op("dve", lambda e: e.tensor_tensor(out=t2b, in0=PS(5), in1=scr[7][:, sl], op=ALU.mult), reads=psk(5) + [sk(7)], writes=[("t2b",)])
                    B.op("dve", lambda e: e.tensor_tensor(out=t1b, in0=t1b, in1=t2b, op=ALU.add), reads=[("t1b",), ("t2b",)], writes=[("t1b",)])
                    B.op("dve", lambda e: e.tensor_tensor(out=dst[:, sl], in0=t1b, in1=rstd[:], op=ALU.mult), reads=[("t1b",), "rstd"], writes=[sk(0)])
            for cc in range(2):
                qT, kT = load_qk(l, OFF_B, OFF_B + 256, cc, prep)
                vav = load_vaug(l, OFF_B + 384, cc)
                sgt = load_gate(l, OFF_B + 512, cc)
                ptv = sbf(4)
                for qg in range(NSEG):
                    gsl = slice(qg * 512, (qg + 1) * 512)
                    ab = 4 + 2 * (qg % 2)

                    def qk(j):
                        for hh in range(2):
                            ph = slice(hh * 64, hh * 64 + 64)
                            bank = 2 * (j % 2) + hh
                            B.op("pe", lambda e: e.matmul(PS(bank), lhsT=kT[ph, j * 128:(j + 1) * 128], rhs=qT[ph, gsl], start=True, stop=True),
                                 reads=[sk(0)], writes=psk(bank), inc=(hh == 1))

                    def ex(j):
                        slot = j % 3
                        B.op("act", lambda e: e.activation(out=ptv[:, slot * 1024:(slot + 1) * 1024], in_=PS(2 * (j % 2), 2), func=AF.Exp, scale=0.125),
                             reads=psk(2 * (j % 2), 2), writes=[("pt", slot)])

                    def pv(j):
                        slot = j % 3
                        for hh in range(2):
                            B.op("pe", lambda e: e.matmul(PS(ab + hh), lhsT=vav[:, j, hh * 64:hh * 64 + 128], rhs=ptv[:, slot * 1024 + hh * 512:slot * 1024 + (hh + 1) * 512],
                                                         start=(j == 0), stop=(j == NT - 1)),
                                 reads=[sk(1), ("pt", slot)], writes=psk(ab + hh), inc=(hh == 1))
                    qk(0)
                    for j in range(NT):
                        if j + 1 < NT:
                            qk(j + 1)
                        ex(j)
                        pv(j)
                    attn_post(l, cc, qg, ab, ab + 1, sgt, False)

        def mixer_A(l):
            SA = min(512, T)
            NS = T // SA
            nch = SA // 64
            arena_b = hT[:].rearrange("p k t -> p (k t)")

            def AF32(off, w):
                return arena_b[:, 2 * off:2 * (off + w)].bitcast(F32)

            def ABF(off, w):
                return arena_b[:, 2 * off:2 * off + w]

            rB = [sbf(0)[:, 0:T], sbf(0)[:, SCRW:SCRW + T]]
            kB = [sbf(1)[:, 0:T], sbf(1)[:, SCRW:SCRW + T]]
            vB = [sbf(2)[:, 0:T], sbf(2)[:, SCRW:SCRW + T]]
            waB = sbf(3)[:, 0:T]
            kkB = [sbf(4)[:, 0:T], sbf(4)[:, SCRW:SCRW + T]]
            yfB = [sbf(5)[:, 0:T], sbf(5)[:, SCRW:SCRW + T]]
            bcB = [sbf(6)[:, 0:T], sbf(6)[:, SCRW:SCRW + T]]
            for ci in range(7):
                pb0 = 4 * (ci % 2)
                xi = 7 if ci % 2 == 0 else 5
                Xt = scr[xi][:, 0:T]
                inproj(l, [(OFF_A + ci * 128, 128)], pb0)
                Z = PS(pb0, NSEG)
                zk = psk(pb0, NSEG)
                B.op("act", lambda e: e.activation(out=Xt, in_=Z, func=AF.Copy, scale=dvc("c0", l, ci)), reads=zk + ["dvt"], writes=[sk(xi)])
                B.op("dve", lambda e: e.scalar_tensor_tensor(out=Xt[:, 1:T], in0=Z[:, 0:T - 1], scalar=pvc("sh0", l, ci), in1=Xt[:, 1:T], op0=ALU.mult, op1=ALU.add),
                     reads=zk + [sk(xi), "pvt"], writes=[sk(xi)])
                if ci < 6:
                    dst = (rB, kB, vB)[ci // 2][ci % 2]
                    dk_ = sk(ci // 2)
                    B.op("dve", lambda e: e.scalar_tensor_tensor(out=dst[:, 0:T - 1], in0=Z[:, 1:T], scalar=pvc("sh1", l, ci), in1=Xt[:, 0:T - 1], op0=ALU.mult, op1=ALU.add),
                         reads=zk + [sk(xi), "pvt"], writes=[dk_])
                    B.op("dve", lambda e: e.tensor_copy(out=dst[:, T - 1:T], in_=Xt[:, T - 1:T]), reads=[sk(xi)], writes=[dk_])
                else:
                    B.op("dve", lambda e: e.scalar_tensor_tensor(out=Xt[:, 0:T - 1], in0=Z[:, 1:T], scalar=pvc("sh1", l, ci), in1=Xt[:, 0:T - 1], op0=ALU.mult, op1=ALU.add),
                         reads=zk + [sk(xi), "pvt"], writes=[sk(xi)])
                    B.op("act", lambda e: e.activation(out=waB[0:64, :], in_=Xt[0:64, :], func=AF.Tanh), reads=[sk(xi)], writes=[sk(3)])
                    B.op("act", lambda e: e.activation(out=waB[64:128, :], in_=Xt[64:128, :], func=AF.Copy), reads=[sk(xi)], writes=[sk(3)])
            for hp in range(2):
                for sg in range(NSEG):
                    sl = slice(sg * 512, (sg + 1) * 512)
                    B.op("dve", lambda e: e.tensor_scalar(out=tmpA[:], in0=kB[hp][:, sl], scalar1=pvc("k_k", l, hp), scalar2=None, op0=ALU.mult),
                         reads=[sk(1), "pvt"], writes=["tmpA"])
                    B.op("act", lambda e: e.activation(out=sqb[:, 0, :], in_=tmpA[:], func=AF.Square), reads=["tmpA"], writes=[("sqb", 0)])
                    B.op("pe", lambda e: e.matmul(PS(4), lhsT=cb("blk64"), rhs=sqb[:, 0, :], start=True, stop=True), reads=[("sqb", 0), "cbf"], writes=psk(4))
                    B.op("act", lambda e: e.activation(out=rstd[:], in_=PS(4), func=AF.Ln, bias=epsb[:, 3:4]), reads=psk(4) + ["epsb"], writes=["rstd"])
                    B.op("act", lambda e: e.activation(out=rstd[:], in_=rstd[:], func=AF.Exp, scale=-0.5), reads=["rstd"], writes=["rstd"])
                    B.op("dve", lambda e: e.tensor_tensor(out=kkB[hp][:, sl], in0=tmpA[:], in1=rstd[:], op=ALU.mult), reads=["tmpA", "rstd"], writes=[sk(4)])
            for cc in range(2):
                inproj(l, [(OFF_A + 896 + cc * 128, 128)], 0)
                B.op("act", lambda e: e.activation(out=obuf[:, cc, :], in_=PS(0, NSEG), func=AF.Silu), reads=psk(0, NSEG), writes=["obuf"])
            wepoch[0] += 1
            B.barrier()
            o = 0
            Fb = []
            for i in range(5):
                Fb.append(AF32(o, SA)); o += SA
            H0 = AF32(o, SA); o += SA
            H1 = AF32(o, SA); o += SA
            H2 = ABF(o, SA); o += SA // 2
            BK = ABF(o, 2 * SA); o += SA
            ARb = ABF(o, 2 * SA); o += SA
            HK = ABF(o, 2 * SA); o += SA
            Gm = ABF(o, 4 * SA); o += 2 * SA
            GT = ABF(o, SA); o += SA // 2
            Xs, XTs, Ps_ = [], [], []
            for i in range(2):
                Xs.append(ABF(o, SA)); o += SA // 2
                XTs.append(ABF(o, SA)); o += SA // 2
                Ps_.append(ABF(o, SA)); o += SA // 2
            assert o <= 8192, o
            s7 = sbf(7)
            VT, BhT, KhT = s7[:, 0:SA], s7[:, SA:2 * SA], s7[:, 2 * SA:3 * SA]
            Q1 = scr[7][:, 768:768 + SA]
            Qb = s7[:, 2560:2624]
            UTb = [s7[:, 2624:2688], s7[:, 2688:2752]]
            STf = [[scr[7][:, 1408 + (hp * 2 + i) * 64:1408 + (hp * 2 + i + 1) * 64] for i in range(2)] for hp in range(2)]
            STb = [[s7[:, 3328 + (hp * 2 + i) * 64:3328 + (hp * 2 + i + 1) * 64] for i in range(2)] for hp in range(2)]
            id64 = s7[:, 3600:3664]
            blkf = scr[7][:, 1856:1984]
            F0, F1, F2, F3, F4 = Fb
            B.op("dve", lambda e: e.tensor_tensor(out=id64, in0=cb("mfwd", 64, 64), in1=cb("mfwd", 64, 0), op=ALU.subtract), reads=["cbf"], writes=["id64"])
            B.op("dve", lambda e: e.tensor_copy(out=blkf, in_=cb("blk64")), reads=["cbf"], writes=["blkf"])
            BK4 = BK.rearrange("p (c two j) -> p c two j", two=2, j=64)
            AR4 = ARb.rearrange("p (c two j) -> p c two j", two=2, j=64)
            HK4 = HK.rearrange("p (c two j) -> p c two j", two=2, j=64)
            Gm3 = Gm.rearrange("p (c x) -> p c x", x=256)

            def v3(ap):
                return ap.rearrange("p (c j) -> p c j", j=64)

            def hs(hd):
                return slice(hd * 64, hd * 64 + 64)

            Wf = [wst[i][:].rearrange("p k c -> p (k c)") for i in range(2)]
            fin = [Wf[0][:, 0:SA], Wf[0][:, SA:2 * SA], Wf[1][:, 0:SA], Wf[1][:, SA:2 * SA]]
            fink = ["wst0", "wst0", "wst1", "wst1"]
            sets = []
            sets.append(dict(BK=BK, AR=ARb, HK=HK, kBK="BK", kAR="AR", kHK="HK", gi=0))
            wfl = [wbf[i][:].rearrange("p k c -> p (k c)") for i in range(3)]
            sets.append(dict(BK=wfl[0][:, 0:2 * SA], AR=wfl[1][:, 0:2 * SA], HK=wfl[2][:, 0:2 * SA], kBK="wbf0", kAR="wbf1", kHK="wbf2", gi=1))
            for S_ in sets:
                for nm in ("BK", "AR", "HK"):
                    S_[nm + "4"] = S_[nm].rearrange("p (c two j) -> p c two j", two=2, j=64)
            upwv = upw_f[:].bitcast(BF16)
            AR4s = [ARb.rearrange("p (c two j) -> p c two j", two=2, j=64), wfl[1][:, 0:2 * SA].rearrange("p (c two j) -> p c two j", two=2, j=64),
                    upwv[:, 0:2 * SA].rearrange("p (c two j) -> p c two j", two=2, j=64)]
            kARs = ["AR", "wbf1", "upw_f"]
            Gms = [Gm, sbf(3)[:, 2 * SCRW:2 * SCRW + 4 * SA] if False else sbf(3)[:, SCRW:SCRW + 4 * SA]]
            kGms = ["Gm", "scr3hi"]
            tAv, tBv = tmpA[:].bitcast(BF16), tmpB[:].bitcast(BF16)
            VTs, BhTs, KhTs = [VT, tAv[:, 0:SA]], [BhT, tAv[:, SA:2 * SA]], [KhT, tBv[:, 0:SA]]
            kVTs, kBhTs, kKhTs = ["VT", "tmpA_lo"], ["BhT", "tmpA_hi"], ["KhT", "tmpB"]
            Q1s, kQ1s = [Q1, rstd[:, 0:SA]], ["Q1", "rstd"]
            Tms, kTms = [sqb[:, 0, 0:SA], sqb[:, 1, 0:SA]], [("sqb", 0), ("sqb", 1)]
            iters = []
            for d in range(2):
                for sg in (range(NS) if d == 0 else range(NS - 1, -1, -1)):
                    for hp in range(2):
                        iters.append((d, sg, hp))
            cur = [0, 0]

            def pre_gen(ii):
                d, sg, hp = iters[ii]
                S_ = sets[ii % 2]
                BK4, HK4 = S_["BK4"], S_["HK4"]
                AR4 = AR4s[ii % 3]
                kAR_ = kARs[ii % 3]
                gi = ii % 3
                endc = 63 if d == 0 else 0
                ssl = slice(sg * SA, (sg + 1) * SA)
                first_of_dir = (sg == (0 if d == 0 else NS - 1))
                wc = d * 256 + hp * 128
                B.op("pe", lambda e: e.matmul(PS(2)[:, 0:SA], lhsT=upw_b[0:64, wc:wc + 128], rhs=waB[0:64, ssl], start=True, stop=True),
                     reads=["upw_b", sk(3)], writes=psk(2))
                B.op("pe", lambda e: e.matmul(PS(3)[:, 0:SA], lhsT=upw_b[64:128, wc:wc + 128], rhs=waB[64:128, ssl], start=True, stop=True),
                     reads=["upw_b", sk(3)], writes=psk(3))
                yield
                B.op("act", lambda e: e.activation(out=F0, in_=PS(2)[:, 0:SA], func=AF.Sigmoid, bias=pvc("w0", l, d * 2 + hp)), reads=psk(2) + ["pvt"], writes=["F0"])
                B.op("act", lambda e: e.activation(out=F1, in_=PS(3)[:, 0:SA], func=AF.Sigmoid, bias=pvc("a0", l, d * 2 + hp)), reads=psk(3) + ["pvt"], writes=["F1"])
                yield
                rst = cb("rst", SA)
                if d == 0:
                    B.op("dve", lambda e: e.tensor_tensor_scan(out=F2, data0=rst, data1=F0, initial=0.0, op0=ALU.mult, op1=ALU.add), reads=["cbf", "F0"], writes=["F2"])
                else:
                    B.op("dve", lambda e: e.tensor_tensor_scan(out=F2[:, ::-1], data0=rst, data1=F0[:, ::-1], initial=0.0, op0=ALU.mult, op1=ALU.add),
                         reads=["cbf", "F0"], writes=["F2"])
                yield
                B.op("dve", lambda e: e.tensor_tensor(out=F3, in0=F2, in1=F0, op=ALU.subtract), reads=["F2", "F0"], writes=["F3"])
                yield
                endb = v3(F2)[:, :, endc:endc + 1].broadcast_to([128, nch, 64])
                B.op("dve", lambda e: e.tensor_tensor(out=v3(F0), in0=endb, in1=v3(F2), op=ALU.subtract), reads=["F2"], writes=["F0"])
                yield
                B.op("act", lambda e: e.activation(out=F4, in_=F2, func=AF.Exp, scale=LAMW), reads=["F2"], writes=["F4"])
                B.op("act", lambda e: e.activation(out=F2, in_=F2, func=AF.Exp, scale=-LAMW), reads=["F2"], writes=["F2"])
                yield
                B.op("act", lambda e: e.activation(out=F3, in_=F3, func=AF.Exp, scale=-LAMW), reads=["F3"], writes=["F3"])
                B.op("act", lambda e: e.activation(out=F0, in_=F0, func=AF.Exp, scale=-LAMW), reads=["F0"], writes=["F0"])
                yield
                gv = gam[:, gi, 0:nch]
                B.op("dve", lambda e: e.tensor_copy(out=gv, in_=v3(F2)[:, :, endc]), reads=["F2"], writes=[("gam", gi)])
                kk_, r_, k_, v_ = kkB[hp][:, ssl], rB[hp][:, ssl], kB[hp][:, ssl], vB[hp][:, ssl]
                B.op("pool", lambda e: e.tensor_tensor(out=H0, in0=kk_, in1=F1, op=ALU.mult), reads=[sk(4), "F1"], writes=["H0"])
                yield
                B.op("pool", lambda e: e.tensor_scalar(out=F1, in0=F1, scalar1=pvc("k_a", l, hp), scalar2=dvc("omka", l, hp), op0=ALU.mult, op1=ALU.add),
                     reads=["F1", "pvt", "dvt"], writes=["F1"])
                yield
                B.op("pool", lambda e: e.tensor_tensor(out=H1, in0=F1, in1=k_, op=ALU.mult), reads=["F1", sk(1)], writes=["H1"])
                yield
                B.op("pool", lambda e: e.tensor_tensor(out=BK4[:, :, 0, :], in0=v3(H0), in1=v3(F4), op=ALU.mult), reads=["H0", "F4"], writes=[S_["kBK"]])
                yield
                B.op("pool", lambda e: e.tensor_tensor(out=BK4[:, :, 1, :], in0=v3(H1), in1=v3(F4), op=ALU.mult), reads=["H1", "F4"], writes=[S_["kBK"]])
                yield
                B.op("dve", lambda e: e.scalar_tensor_tensor(out=AR4[:, :, 0, :], in0=v3(kk_), scalar=-1.0, in1=v3(F3), op0=ALU.mult, op1=ALU.mult),
                     reads=[sk(4), "F3"], writes=[kAR_])
                yield
                B.op("pool", lambda e: e.tensor_tensor(out=AR4[:, :, 1, :], in0=v3(r_), in1=v3(F2), op=ALU.mult), reads=[sk(0), "F2"], writes=[kAR_])
                yield
                B.op("pool", lambda e: e.tensor_tensor(out=HK4[:, :, 0, :], in0=v3(H0), in1=v3(F0), op=ALU.mult), reads=["H0", "F0"], writes=[S_["kHK"]])
                yield
                B.op("pool", lambda e: e.tensor_tensor(out=HK4[:, :, 1, :], in0=v3(H1), in1=v3(F0), op=ALU.mult), reads=["H1", "F0"], writes=[S_["kHK"]])
                yield
                B.op("dve", lambda e: e.scalar_tensor_tensor(out=H2, in0=r_, scalar=pvc("r_k", l, hp), in1=H1, op0=ALU.mult, op1=ALU.mult),
                     reads=[sk(0), "pvt", "H1"], writes=["H2"])
                B.op("pe", lambda e: e.matmul(PS(2)[:, 0:SA], lhsT=cb("blk64"), rhs=H2, start=True, stop=True), reads=["cbf", "H2"], writes=psk(2))
                yield
                if d == 0:
                    B.op("act", lambda e: e.activation(out=bcB[hp][:, ssl], in_=PS(2)[:, 0:SA], func=AF.Copy), reads=psk(2), writes=[("bc", hp)])
                else:
                    B.op("dve", lambda e: e.tensor_tensor(out=bcB[hp][:, ssl], in0=PS(2)[:, 0:SA], in1=bcB[hp][:, ssl], op=ALU.add), reads=psk(2) + [("bc", hp)], writes=[("bc", hp)])
                yield

            def inv_gen(ii):
                d, sg, hp = iters[ii]
                S_ = sets[ii % 2]
                BK4, HK4 = S_["BK4"], S_["HK4"]
                kBK, kHK = S_["kBK"], S_["kHK"]
                AR4, kAR = AR4s[ii % 3], kARs[ii % 3]
                gi = ii % 3
                Gm, kGm = Gms[ii % 2], kGms[ii % 2]
                Gm3 = Gm.rearrange("p (c x) -> p c x", x=256)
                VT, BhT, KhT = VTs[ii % 2], BhTs[ii % 2], KhTs[ii % 2]
                kVT, kBhT, kKhT = kVTs[ii % 2], kBhTs[ii % 2], kKhTs[ii % 2]
                Q1, kQ1 = Q1s[ii % 2], kQ1s[ii % 2]
                Tm, tkey = Tms[ii % 2], kTms[ii % 2]
                mname = "mfwd" if d == 0 else "mbwd"
                mTname = "mfwdT" if d == 0 else "mbwdT"
                ssl = slice(sg * SA, (sg + 1) * SA)
                v_ = vB[hp][:, ssl]
                n_items = 2 * nch
                Gps = PS(4, 4)
                n_items = 2 * nch
                it = 0
                for c in range(nch):
                    for hd in range(2):
                        it += 1
                        last = (it == n_items)
                        B.op("pe", lambda e: e.matmul(Gps[hs(hd), c * 256:c * 256 + 128], lhsT=BK4[hs(hd), c, 0, :], rhs=AR4[hs(hd), c, :, :], start=True, stop=True),
                             reads=[kBK, kAR], writes=psk(4, 4), inc=False)
                        B.op("pe", lambda e: e.matmul(Gps[hs(hd), c * 256 + 128:c * 256 + 256], lhsT=BK4[hs(hd), c, 1, :], rhs=AR4[hs(hd), c, :, :], start=True, stop=True),
                             reads=[kBK, kAR], writes=psk(4, 4), inc=last)
                mk = cb(mname).rearrange("p (o x) -> p o x", o=1).broadcast_to([128, nch, 256])
                B.op("dve", lambda e: e.tensor_tensor(out=Gm3, in0=Gps[:, 0:nch * 256].rearrange("p (c x) -> p c x", x=256), in1=mk, op=ALU.mult),
                     reads=psk(4, 4) + ["cbf"], writes=[kGm])
                it = 0
                for c in range(nch):
                    for hd in range(2):
                        it += 1
                        B.op("pe", lambda e: e.matmul(PS(4)[hs(hd), c * 64:(c + 1) * 64], lhsT=AR4[hs(hd), c, 0, :], rhs=BK4[hs(hd), c, 0, :], start=True, stop=True),
                             reads=[kBK, kAR], writes=psk(4), inc=(it == n_items))
                mkT = cb(mTname).rearrange("p (o x) -> p o x", o=1).broadcast_to([128, nch, 64])
                B.op("dve", lambda e: e.tensor_tensor(out=v3(GT), in0=v3(PS(4)[:, 0:SA]), in1=mkT, op=ALU.mult), reads=psk(4) + ["cbf"], writes=["GT"])
                yield
                idb = id64.rearrange("p (o x) -> p o x", o=1).broadcast_to([128, nch, 64])
                B.op("dve", lambda e: e.tensor_tensor(out=v3(Ps_[0]), in0=Gm3[:, :, 0:64], in1=idb, op=ALU.add), reads=[kGm, "id64"], writes=[("P", 0)])

                def Xap(level, buf, hd, c):
                    if level == 0:
                        return Gm[hs(hd), c * 256:c * 256 + 64]
                    return Xs[buf][hs(hd), c * 64:(c + 1) * 64]

                def XTap(level, buf, hd, c):
                    if level == 0:
                        return GT[hs(hd), c * 64:(c + 1) * 64]
                    return XTs[buf][hs(hd), c * 64:(c + 1) * 64]
                for r in range(1, 7):
                    src_b, dst_b = (r - 1) % 2, r % 2
                    xk_src = [kGm] if r == 1 else [("X", src_b)]
                    xtk_src = ["GT"] if r == 1 else [("XT", src_b)]
                    do_x, do_xt, do_p = (r <= 4), (r <= 5), (r >= 2)
                    psrc, pdst = (r - 2) % 2, (r - 1) % 2
                    it = 0
                    for c in range(nch):
                        for hd in range(2):
                            it += 1
                            last = (it == n_items)
                            if do_x:
                                B.op("pe", lambda e: e.matmul(PS(5)[hs(hd), c * 64:(c + 1) * 64], lhsT=XTap(r - 1, src_b, hd, c), rhs=Xap(r - 1, src_b, hd, c), start=True, stop=True),
                                     reads=xk_src + xtk_src, writes=psk(5), inc=False)
                            if do_xt:
                                B.op("pe", lambda e: e.matmul(PS(6)[hs(hd), c * 64:(c + 1) * 64], lhsT=Xap(r - 1, src_b, hd, c), rhs=XTap(r - 1, src_b, hd, c), start=True, stop=True),
                                     reads=xk_src + xtk_src, writes=psk(6), inc=(last and not do_p))
                            if do_p:
                                B.op("pe", lambda e: e.matmul(PS(7)[hs(hd), c * 64:(c + 1) * 64], lhsT=XTap(r - 1, src_b, hd, c), rhs=Ps_[psrc][hs(hd), c * 64:(c + 1) * 64], start=True, stop=True),
                                     reads=xtk_src + [("P", psrc)], writes=psk(7), inc=last)
                            if it % 4 == 0 and not last:
                                yield
                    if do_x:
                        B.op("act", lambda e: e.activation(out=Xs[dst_b], in_=PS(5)[:, 0:SA], func=AF.Copy), reads=psk(5), writes=[("X", dst_b)])
                    if do_xt:
                        B.op("act", lambda e: e.activation(out=XTs[dst_b], in_=PS(6)[:, 0:SA], func=AF.Copy),
                             reads=psk(6), writes=[("XT", dst_b)])
                    if do_p:
                        B.op("dve", lambda e: e.tensor_tensor(out=(Tm if r == 6 else Ps_[pdst]), in0=PS(7)[:, 0:SA], in1=Ps_[psrc], op=ALU.add), reads=psk(7) + [("P", psrc)], writes=[(tkey if r == 6 else ("P", pdst))])
                    yield
                for (srcf, dstb, dkey, rk_, bank) in ((lambda c, hd: v_[hs(hd), c * 64:(c + 1) * 64], VT, kVT, [sk(2)], 4),
                                                       (lambda c, hd: HK4[hs(hd), c, 0, :], BhT, kBhT, [kHK], 5),
                                                       (lambda c, hd: HK4[hs(hd), c, 1, :], KhT, kKhT, [kHK], 6)):
                    pb = PS(bank).bitcast(BF16)
                    it = 0
                    for c in range(nch):
                        for hd in range(2):
                            it += 1
                            B.op("pe", lambda e: e.transpose(pb[hs(hd), c * 64:(c + 1) * 64], srcf(c, hd), cb("ident")[hs(hd), hd * 64:hd * 64 + 64]),
                                 reads=rk_ + ["cbf"], writes=psk(bank), inc=(it == n_items))
                    B.op("act", lambda e: e.activation(out=dstb, in_=pb[:, 0:SA], func=AF.Copy), reads=psk(bank), writes=[dkey])
                    yield
                it = 0
                for c in range(nch):
                    for hd in range(2):
                        it += 1
                        B.op("pe", lambda e: e.matmul(PS(7)[hs(hd), c * 64:(c + 1) * 64], lhsT=Gm[hs(hd), c * 256 + 128:c * 256 + 192], rhs=VT[hs(hd), c * 64:(c + 1) * 64], start=True, stop=True),
                             reads=[kGm, kVT], writes=psk(7), inc=(it == n_items))
                B.op("dve", lambda e: e.tensor_copy(out=Q1, in_=PS(7)[:, 0:SA]), reads=psk(7), writes=[kQ1])
                yield

            def chain_gen(ii):
                d, sg, hp = iters[ii]
                S_ = sets[ii % 2]
                BK4, HK4 = S_["BK4"], S_["HK4"]
                kBK, kHK = S_["kBK"], S_["kHK"]
                AR4, kAR = AR4s[ii % 3], kARs[ii % 3]
                gi = ii % 3
                Gm, kGm = Gms[ii % 2], kGms[ii % 2]
                Gm3 = Gm.rearrange("p (c x) -> p c x", x=256)
                VT, BhT, KhT = VTs[ii % 2], BhTs[ii % 2], KhTs[ii % 2]
                kVT, kBhT, kKhT = kVTs[ii % 2], kBhTs[ii % 2], kKhTs[ii % 2]
                Q1, kQ1 = Q1s[ii % 2], kQ1s[ii % 2]
                Tm, tkey = Tms[ii % 2], kTms[ii % 2]
                mname = "mfwd" if d == 0 else "mbwd"
                mTname = "mfwdT" if d == 0 else "mbwdT"
                ssl = slice(sg * SA, (sg + 1) * SA)
                v_ = vB[hp][:, ssl]
                n_items = 2 * nch
                if sg == (0 if d == 0 else NS - 1):
                    cur[hp] = 0
                    B.op("dve", lambda e: e.memset(STf[hp][0], 0.0), writes=[("STf", hp, 0)])
                    B.op("dve", lambda e: e.memset(STb[hp][0], 0.0), writes=[("STb", hp, 0)])
                corder = range(nch) if d == 0 else range(nch - 1, -1, -1)
                for ci_, c in enumerate(corder):
                    cu = cur[hp]
                    nx = 1 - cu
                    cs_ = slice(c * 64, (c + 1) * 64)
                    ub = ci_ % 2
                    for hd in range(2):
                        B.op("pe", lambda e: e.matmul(PS(0)[hs(hd), 0:64], lhsT=AR4[hs(hd), c, 0, :], rhs=STb[hp][cu][hs(hd), :], start=True, stop=True),
                             reads=[kAR, ("STb", hp, cu)], writes=psk(0), inc=(hd == 1))
                    yield
                    B.op("dve", lambda e: e.tensor_tensor(out=Qb, in0=PS(0)[:, 0:64], in1=Q1[:, cs_], op=ALU.add), reads=psk(0) + [kQ1], writes=["Qb"])
                    for hd in range(2):
                        B.op("pe", lambda e: e.matmul(PS(0)[hs(hd), 64:128], lhsT=Tm[hs(hd), cs_], rhs=Qb[hs(hd), :], start=True, stop=True),
                             reads=[tkey, "Qb"], writes=psk(0), inc=(hd == 1))
                    yield
                    B.op("act", lambda e: e.activation(out=UTb[ub], in_=PS(0)[:, 64:128], func=AF.Copy), reads=psk(0), writes=[("UTb", ub)])
                    for hd in range(2):
                        B.op("pe", lambda e: e.matmul(PS(0)[hs(hd), 128:192], lhsT=BhT[hs(hd), cs_], rhs=UTb[ub][hs(hd), :], start=True, stop=False),
                             reads=[kBhT, ("UTb", ub)], writes=psk(0), inc=False)
                        B.op("pe", lambda e: e.matmul(PS(0)[hs(hd), 128:192], lhsT=KhT[hs(hd), cs_], rhs=VT[hs(hd), cs_], start=False, stop=True),
                             reads=[kKhT, kVT], writes=psk(0), inc=(hd == 1))
                    yield
                    gcol = gam[:, gi, c:c + 1]
                    B.op("dve", lambda e: e.scalar_tensor_tensor(out=STf[hp][nx], in0=STf[hp][cu], scalar=gcol, in1=PS(0)[:, 128:192], op0=ALU.mult, op1=ALU.add),
                         reads=[("STf", hp, cu), ("gam", gi)] + psk(0), writes=[("STf", hp, nx)])
                    B.op("act", lambda e: e.activation(out=STb[hp][nx], in_=STf[hp][nx], func=AF.Copy), reads=[("STf", hp, nx)], writes=[("STb", hp, nx)])
                    for hd in range(2):
                        B.op("pe", lambda e: e.matmul(PS(1)[hs(hd), cs_], lhsT=STb[hp][cu][hs(hd), :], rhs=AR4[hs(hd), c, 1, :], start=True, stop=False),
                             reads=[("STb", hp, cu), kAR], writes=psk(1), inc=False)
                        B.op("pe", lambda e: e.matmul(PS(1)[hs(hd), cs_], lhsT=UTb[ub][hs(hd), :], rhs=Gm[hs(hd), c * 256 + 64:c * 256 + 128], start=False, stop=False),
                             reads=[("UTb", ub), kGm], writes=psk(1), inc=False)
                        B.op("pe", lambda e: e.matmul(PS(1)[hs(hd), cs_], lhsT=VT[hs(hd), cs_], rhs=Gm[hs(hd), c * 256 + 192:c * 256 + 256], start=False, stop=True),
                             reads=[kVT, kGm], writes=psk(1), inc=(hd == 1))
                    cur[hp] = nx
                    yield
                if d == 0:
                    B.op("act", lambda e: e.activation(out=yfB[hp][:, ssl], in_=PS(1)[:, 0:SA], func=AF.Copy), reads=psk(1), writes=[("yf", hp)])
                else:
                    B.op("dve", lambda e: e.tensor_tensor(out=fin[0], in0=PS(1)[:, 0:SA], in1=yfB[hp][:, ssl], op=ALU.add), reads=psk(1) + [("yf", hp)], writes=[fink[0]])
                    B.op("pe", lambda e: e.matmul(PS(0)[:, 0:SA], lhsT=blkf, rhs=fin[0], start=True, stop=True), reads=["blkf", fink[0]], writes=psk(0))
                    B.op("dve", lambda e: e.scalar_tensor_tensor(out=fin[1], in0=PS(0)[:, 0:SA], scalar=-1.0 / 64, in1=fin[0], op0=ALU.mult, op1=ALU.add), reads=psk(0) + [fink[0]], writes=[fink[1]])
                    B.op("act", lambda e: e.activation(out=fin[2], in_=fin[1], func=AF.Square), reads=[fink[1]], writes=[fink[2]])
                    B.op("pe", lambda e: e.matmul(PS(0)[:, 0:SA], lhsT=blkf, rhs=fin[2], start=True, stop=True), reads=["blkf", fink[2]], writes=psk(0))
                    B.op("act", lambda e: e.activation(out=fin[3], in_=PS(0)[:, 0:SA], func=AF.Ln, scale=1.0 / 64, bias=epsb[:, 2:3]), reads=psk(0) + ["epsb"], writes=[fink[3]])
                    B.op("act", lambda e: e.activation(out=fin[3], in_=fin[3], func=AF.Exp, scale=-0.5), reads=[fink[3]], writes=[fink[3]])
                    B.op("dve", lambda e: e.tensor_tensor(out=fin[1], in0=fin[1], in1=fin[3], op=ALU.mult), reads=[fink[1], fink[3]], writes=[fink[1]])
                    B.op("dve", lambda e: e.tensor_scalar(out=fin[1], in0=fin[1], scalar1=pvc("ln_g", l, hp), scalar2=pvc("ln_b", l, hp), op0=ALU.mult, op1=ALU.add), reads=[fink[1], "pvt"], writes=[fink[1]])
                    B.op("dve", lambda e: e.tensor_tensor(out=fin[2], in0=bcB[hp][:, ssl], in1=v_, op=ALU.mult), reads=[("bc", hp), sk(2)], writes=[fink[2]])
                    B.op("dve", lambda e: e.tensor_tensor(out=fin[1], in0=fin[1], in1=fin[2], op=ALU.add), reads=[fink[1], fink[2]], writes=[fink[1]])
                    B.op("dve", lambda e: e.tensor_tensor(out=obuf[:, hp, ssl], in0=fin[1], in1=obuf[:, hp, ssl], op=ALU.mult), reads=[fink[1], "obuf"], writes=["obuf"])


                yield

            def run_all():
                n = len(iters)
                for _ in pre_gen(0):
                    pass
                if n > 1:
                    gens0 = [pre_gen(1), inv_gen(0)]
                else:
                    gens0 = [inv_gen(0)]
                _rr(gens0)
                for s_ in range(n):
                    gens = [chain_gen(s_)]
                    if s_ + 1 < n:
                        gens.append(inv_gen(s_ + 1))
                    if s_ + 2 < n:
                        gens.append(pre_gen(s_ + 2))
                    _rr(gens)

            def _rr(gens):
                import os
                gens = list(gens)
                mode = os.environ.get("A_MODE", "cip")
                if os.environ.get("A_SEQ"):
                    mode = ""
                names = {"chain_gen": "c", "inv_gen": "i", "pre_gen": "p"}
                conc = [g for g in gens if names[g.gi_code.co_name] in mode]
                seq = [g for g in gens if names[g.gi_code.co_name] not in mode]
                while conc:
                    for g in list(conc):
                        try:
                            next(g)
                        except StopIteration:
                            conc.remove(g)
                for g in seq:
                    for _ in g:
                        pass
            run_all()
            B.barrier()
            wepoch[0] += 1

        def program():
            B.dma(scr[0][:, 0:CST_W], cst_d[:, :], writes=[sk(0)], q="pool", semkey="cst")
            B.dma(pvt[:], pv_d[:, :], writes=["pvt"], q="pool", semkey="pvt")
            B.op("dve", lambda e: e.tensor_copy(out=cbf[:], in_=scr[0][:, 0:CST_W]), reads=[sk(0)], writes=["cbf"])
            B.op("dve", lambda e: e.tensor_copy(out=swapf[:], in_=scr[0][:, CST_OFF["swap"]:CST_OFF["swap"] + 128]), reads=[sk(0)], writes=["swapf"])
            B.op("dve", lambda e: e.memset(epsb[:, 0:1], NORM_EPS), writes=["epsb"])
            B.op("dve", lambda e: e.memset(epsb[:, 1:2], 1.0), writes=["epsb"])
            B.op("dve", lambda e: e.memset(epsb[:, 2:3], GN_EPS), writes=["epsb"])
            B.op("dve", lambda e: e.memset(epsb[:, 3:4], 1e-18), writes=["epsb"])
            for l in range(L):
                for i in range(7):
                    B.op("dve", lambda e, l=l, i=i: e.tensor_tensor(out=dvc("c0", l, i), in0=pvc("sh0", l, i), in1=pvc("sh1", l, i), op=ALU.add),
                         reads=["pvt"], writes=["dvt"])
                    B.op("dve", lambda e, l=l, i=i: e.tensor_scalar(out=dvc("c0", l, i), in0=dvc("c0", l, i), scalar1=-1.0, scalar2=1.0, op0=ALU.mult, op1=ALU.add),
                         reads=["dvt"], writes=["dvt"])
                for i in range(2):
                    B.op("dve", lambda e, l=l, i=i: e.tensor_scalar(out=dvc("omka", l, i), in0=pvc("k_a", l, i), scalar1=-1.0, scalar2=1.0, op0=ALU.mult, op1=ALU.add),
                         reads=["pvt"], writes=["dvt"])
                    B.op("act", lambda e, l=l, i=i: e.activation(out=dvc("esinksw", l, i), in_=pvc("sinksw", l, i), func=AF.Exp),
                         reads=["pvt"], writes=["dvt"])
                for i in range(4):
                    B.op("act", lambda e, l=l, i=i: e.activation(out=dvc("cneg", l, i), in_=pvc("lam", l, i), func=AF.Exp, scale=-1.0),
                         reads=["pvt"], writes=["dvt"])
                    B.op("act", lambda e, l=l, i=i: e.activation(out=dvc("cneg", l, i), in_=dvc("cneg", l, i), func=AF.Ln, bias=epsb[:, 1:2]),
                         reads=["dvt", "epsb"], writes=["dvt"])
                    B.op("dve", lambda e, l=l, i=i: e.tensor_scalar(out=dvc("cneg", l, i), in0=dvc("cneg", l, i), scalar1=-8.0, scalar2=None, op0=ALU.mult),
                         reads=["dvt"], writes=["dvt"])


            for s in range(NSEQ):
                for c in range(8):
                    B.dma(xT[:, c, :], x_d[s * 1024 + c * 128:s * 1024 + (c + 1) * 128, :], writes=[("xT", c)], q="pool", semkey=f"xin{c}")
                for l in range(L):
                    rmsnorm_to(l, "h")
                    B.dma(scr[7][:, 0:1024], gw_d[:, l * 1024:(l + 1) * 1024], writes=[sk(7)], q="pool", semkey="gw")
                    B.op("pool", lambda e: e.tensor_copy(out=gw_b[:], in_=scr[7][:, 0:1024]), reads=[sk(7)], writes=["gw_b"])
                    B.dma(upw_f[:], upw_d[:, l * 512:(l + 1) * 512], writes=["upw_f"], q="pool", semkey="upw")
                    B.op("pool", lambda e: e.tensor_copy(out=upw_b[:], in_=upw_f[:]), reads=["upw_f"], writes=["upw_b"])
                    if "C" in cfg.MIX:
                        mixer_C(l)
                        outproj(l, 2)
                    if "D" in cfg.MIX:
                        mixer_D(l)
                        outproj(l, 3)
                    if "B" in cfg.MIX:
                        mixer_B(l)
                        outproj(l, 1)
                    if "A" in cfg.MIX:
                        mixer_A(l)
                        outproj(l, 0)
                B.barrier()
                rmsnorm_to(None, "x")
                for c in range(8):
                    B.dma(y_d[s * 1024 + c * 128:s * 1024 + (c + 1) * 128, :], xT[:, c, :], reads=[("xT", c)], q="pool", semkey=f"yout{c}")
            toks = []
            for c in range(8):
                toks.extend(B.readers.get(("xT", c), {}).values())
            for tname in tap_d:
                t = B.last_w.get(("tap", tname))
                if t is not None:
                    toks.append(t)
            B._wait("pool", toks)

        realB = B
        B = NullBuilder()
        wstate["rec"] = True
        program()
        B = realB
        wstate["rec"] = False
        wepoch[0] = 0
        wo_slot[0] = 0
        program()
        print(f"[build] instructions: {B.n_ins}")
    return nc


_NC_CACHE = {}


def run_trunk(cfg, xs, inp):
    T, L = cfg.T, cfg.DEPTH
    ncore = cfg.NCORES
    assert xs.shape[0] == ncore * cfg.NSEQ
    key = (cfg.T, cfg.NSEQ, cfg.DEPTH, cfg.MIX, cfg.taps)
    if key not in _NC_CACHE:
        _NC_CACHE[key] = build(cfg)
    nc = _NC_CACHE[key]
    cst, alibi, C, S = make_consts(T)
    pv, upw, gw = pack_params(inp, L)
    w_in = np.ascontiguousarray(np.asarray(inp["w_in"], np.float32)[:L].reshape(L * 1024, D_IN))
    w_out = np.ascontiguousarray(np.asarray(inp["w_out"], np.float32)[:L].reshape(L * 1024, 1024))
    in_maps = []
    for c in range(ncore):
        xc = xs[c * cfg.NSEQ:(c + 1) * cfg.NSEQ]
        xt = np.ascontiguousarray(xc.transpose(0, 2, 1)).reshape(cfg.NSEQ * 1024, T)
        in_maps.append({"x": xt, "w_in": w_in, "w_out": w_out, "pvec": pv, "upw": upw, "gw": gw, "cst": cst,
                        "alibi": alibi, "ropeC": C, "ropeS": S})
    res = run_bass_kernel_spmd(nc, in_maps, core_ids=list(range(ncore)))
    outs = []
    for c in range(ncore):
        yt = np.asarray(res.results[c]["y"]).reshape(cfg.NSEQ, 1024, T)
        outs.append(yt.transpose(0, 2, 1))
    return np.ascontiguousarray(np.concatenate(outs, 0)).astype(np.float32), res


def kernel(**inputs):
    inp = {k: np.asarray(v) for k, v in inputs.items()}
    xp, xs_ = inp["x_prompt"], inp["x_sample"]
    xs = np.concatenate([xp, xs_], 0).astype(np.float32)
    cfg = Cfg(T=xs.shape[1], NSEQ=xs.shape[0] // 8, DEPTH=inp["w_in"].shape[0], MIX="ABCD", NCORES=8)
    y, _ = run_trunk(cfg, xs, inp)
    return (y[:xp.shape[0]], y[xp.shape[0]:])
```

```python
import math
from contextlib import ExitStack
import numpy as np
import ml_dtypes
import concourse.bass as bass
import concourse.mybir as mybir
from concourse.bass_utils import run_bass_kernel_spmd

F32 = mybir.dt.float32
BF16 = mybir.dt.bfloat16
AF = mybir.ActivationFunctionType
ALU = mybir.AluOpType

D_MODEL = 1024
GRID_W = 64
D_IN = 3200
A_W, B_W, C_W, D_W = 1152, 768, 512, 768
OFF_A, OFF_B, OFF_C, OFF_D = 0, 1152, 1920, 2432
NORM_EPS = 1e-6
GN_EPS = 64e-5
LAMW = math.exp(-0.5)
SEM_LIMIT = 30000


class Cfg:
    def __init__(self, T=2048, NSEQ=5, DEPTH=4, MIX="ABCD", NCORES=8, taps=()):
        self.T, self.NSEQ, self.DEPTH, self.MIX, self.NCORES = T, NSEQ, DEPTH, MIX, NCORES
        self.taps = tuple(taps)


PV_FIELDS = [("norm_g", 8), ("sh0", 7), ("sh1", 7), ("w0", 4), ("a0", 4), ("k_k", 2), ("k_a", 2), ("r_k", 2),
             ("ln_g", 2), ("ln_b", 2), ("qn", 1), ("kn", 1), ("cw", 8), ("cb", 2), ("gb", 8), ("lam", 4),
             ("sink", 2), ("sinksw", 2)]
DV_FIELDS = [("c0", 7), ("omka", 2), ("cneg", 4), ("esinksw", 2), ("gbh", 8), ("w0h", 4), ("a0h", 4)]


def _offsets(fields):
    off, d = 0, {}
    for n, w in fields:
        d[n] = off
        off += w
    return d, off


PV_OFF, PV_W = _offsets(PV_FIELDS)
DV_OFF, DV_W = _offsets(DV_FIELDS)


def pv_col(L, name, l, i=0):
    if name == "final_g":
        return L * PV_W + i
    return l * PV_W + PV_OFF[name] + i


def dv_col(name, l, i=0):
    return l * DV_W + DV_OFF[name] + i


CST_FIELDS = [("ident", 128), ("blk64", 128), ("ones", 128), ("swap", 128), ("perm", 128), ("mfwd", 256),
              ("mbwd", 256), ("mfwdT", 64), ("mbwdT", 64), ("rst", 512)]
CST_OFF, CST_W = _offsets(CST_FIELDS)


def make_consts(T):
    c = np.zeros((128, CST_W), np.float32)
    alibi = np.zeros((128, 4 * 384), np.float32)
    p = np.arange(128)
    o = CST_OFF
    c[:, o["ident"]:o["ident"] + 128] = np.eye(128)
    c[:, o["blk64"]:o["blk64"] + 128] = (p[:, None] // 64 == p[None, :] // 64)
    c[:, o["ones"]:o["ones"] + 128] = 1.0
    c[:, o["swap"]:o["swap"] + 128] = (p[:, None] == (p[None, :] + 64) % 128)
    d = p % 64
    partner = np.where((d % 32) < 16, p + 16, p - 16)
    c[:, o["perm"]:o["perm"] + 128] = (p[:, None] == partner[None, :])
    s = (p % 64)[:, None]
    t = np.arange(64)[None, :]
    strict_f, incl_f = (s < t), (s <= t)
    strict_b, incl_b = (s > t), (s >= t)
    c[:, o["mfwd"]:o["mfwd"] + 256] = np.concatenate([strict_f, incl_f, strict_f, incl_f], 1)
    c[:, o["mbwd"]:o["mbwd"] + 256] = np.concatenate([strict_b, incl_b, strict_b, incl_b], 1)
    c[:, o["mfwdT"]:o["mfwdT"] + 64] = (t < s)
    c[:, o["mbwdT"]:o["mbwdT"] + 64] = (t > s)
    c[:, o["rst"]:o["rst"] + 512] = (np.arange(512)[None, :] % 64 != 0)
    cc = np.arange(384)[None, :]
    dist = np.abs(cc - 128 - p[:, None]).astype(np.float64)
    for h in range(4):
        slope = 2.0 ** (-8.0 * (h + 1) / 4)
        e = np.where(dist <= 128, np.exp(-slope * dist), 0.0)
        alibi[:, h * 384:(h + 1) * 384] = e
    row = (np.arange(T) // GRID_W).astype(np.float64)
    col = (np.arange(T) % GRID_W).astype(np.float64)
    inv = 10000.0 ** (-np.arange(0, 32, 2, dtype=np.float64) / 32)
    C = np.zeros((128, T), np.float32)
    S = np.zeros((128, T), np.float32)
    for pp in range(128):
        dd = pp % 64
        pos = row if dd < 32 else col
        f = inv[dd % 16]
        ang = pos * f
        C[pp] = np.cos(ang)
        S[pp] = (-np.sin(ang)) if (dd % 32) < 16 else np.sin(ang)
    return c, alibi, C, S


def pack_params(inp, L):
    pv = np.zeros((128, L * PV_W + 8), np.float32)

    def put(name, l, i, vec128):
        pv[:, pv_col(L, name, l, i)] = vec128

    for l in range(L):
        for i in range(8):
            put("norm_g", l, i, inp["norm_g"][l, i * 128:(i + 1) * 128])
        for i in range(7):
            put("sh0", l, i, inp["rwkv_shift"][l, 0, i * 128:(i + 1) * 128])
            put("sh1", l, i, inp["rwkv_shift"][l, 1, i * 128:(i + 1) * 128])
        for d in range(2):
            for hp in range(2):
                put("w0", l, d * 2 + hp, inp["rwkv_w0"][l, d, hp * 128:(hp + 1) * 128])
                put("a0", l, d * 2 + hp, inp["rwkv_a0"][l, d, hp * 128:(hp + 1) * 128])
        rk = inp["rwkv_r_k"][l].reshape(256)
        for hp in range(2):
            sl = slice(hp * 128, (hp + 1) * 128)
            put("k_k", l, hp, inp["rwkv_k_k"][l, sl])
            put("k_a", l, hp, inp["rwkv_k_a"][l, sl])
            put("r_k", l, hp, rk[sl])
            put("ln_g", l, hp, inp["rwkv_ln_g"][l, sl])
            put("ln_b", l, hp, inp["rwkv_ln_b"][l, sl])
        put("qn", l, 0, np.tile(inp["attn_q_norm"][l], 2))
        put("kn", l, 0, np.tile(inp["attn_k_norm"][l], 2))
        for j in range(4):
            for cc in range(2):
                put("cw", l, j * 2 + cc, inp["lru_conv_w"][l, j, cc * 128:(cc + 1) * 128])
        for cc in range(2):
            put("cb", l, cc, inp["lru_conv_b"][l, cc * 128:(cc + 1) * 128])
        for d in range(2):
            for k in range(2):
                for cc in range(2):
                    put("gb", l, (d * 2 + k) * 2 + cc, inp["lru_gate_b"][l, d, k, cc * 128:(cc + 1) * 128])
            for cc in range(2):
                put("lam", l, d * 2 + cc, inp["lru_lambda"][l, d, cc * 128:(cc + 1) * 128])
        sk = inp["swa_sink"][l]
        for cc in range(2):
            put("sink", l, cc, np.repeat(sk[2 * cc:2 * cc + 2], 64))
            put("sinksw", l, cc, np.repeat(sk[2 * cc:2 * cc + 2][::-1], 64))
    for i in range(8):
        pv[:, L * PV_W + i] = inp["final_g"][i * 128:(i + 1) * 128]
    upw = np.zeros((128, L * 2 * 256), np.float32)
    for l in range(L):
        for d in range(2):
            upw[0:64, (l * 2 + d) * 256:(l * 2 + d + 1) * 256] = inp["rwkv_w_up"][l, d]
            upw[64:128, (l * 2 + d) * 256:(l * 2 + d + 1) * 256] = inp["rwkv_a_up"][l, d]
    gw = np.zeros((128, L * 8 * 128), np.float32)
    for l in range(L):
        for d in range(2):
            for k in range(2):
                for cc in range(2):
                    base = (((l * 2 + d) * 2 + k) * 2 + cc) * 128
                    for b in range(2):
                        gw[b * 64:(b + 1) * 64, base + b * 64:base + (b + 1) * 64] = inp["lru_gate_w"][l, d, k, 2 * cc + b]
    return pv, upw, gw


class SemCounter:
    def __init__(self, bld, name, step):
        self.bld, self.name, self.step = bld, name, step
        self.n = 0
        self._new()

    def _new(self):
        self.sem = self.bld.es.enter_context(self.bld.nc.semaphore(f"{self.name}_{self.n}"))
        self.sname = f"{self.name}_{self.n}"
        self.n += 1
        self.val = 0

    def next(self):
        if self.val + self.step > SEM_LIMIT:
            self._new()
        self.val += self.step
        return (self.sname, self.sem, self.val)


class Builder:
    def __init__(self, nc, es):
        self.nc, self.es = nc, es
        self.eng = {"pe": nc.tensor, "act": nc.scalar, "dve": nc.vector, "pool": nc.gpsimd, "sp": nc.sync}
        self.cnt = {e: SemCounter(self, e, 1) for e in ("pe", "act", "dve", "pool")}
        self.dcnt = {}
        self.waited = {e: {} for e in self.eng}
        self.last_w = {}
        self.readers = {}
        self.pending = {e: ([], []) for e in self.eng}
        self.n_ins = 0

    def _deps(self, reads, writes):
        toks = []
        for k in reads:
            t = self.last_w.get(k)
            if t is not None:
                toks.append(t)
        for k in writes:
            t = self.last_w.get(k)
            if t is not None:
                toks.append(t)
            r = self.readers.get(k)
            if r:
                toks.extend(r.values())
        return toks

    def _wait(self, e, toks):
        w = self.waited[e]
        need = {}
        for (sn, sem, val) in toks:
            if w.get(sn, 0) < val and need.get(sn, (None, 0))[1] < val:
                need[sn] = (sem, val)
        for sn, (sem, val) in need.items():
            self.eng[e].wait_ge(sem, val)
            w[sn] = val

    def _commit(self, tok, reads, writes):
        for k in writes:
            self.last_w[k] = tok
            self.readers[k] = {}
        for k in reads:
            self.readers.setdefault(k, {})[tok[0]] = tok

    def op(self, e, fn, reads=(), writes=(), inc=True):
        self._wait(e, self._deps(reads, writes))
        ins = fn(self.eng[e])
        self.n_ins += 1
        pr, pw = self.pending[e]
        if not inc:
            pr.extend(reads)
            pw.extend(writes)
            return
        tok = self.cnt[e].next()
        ins.then_inc(tok[1], 1)
        self._commit(tok, list(reads) + pr, list(writes) + pw)
        self.pending[e] = ([], [])

    def dma(self, out, in_, reads=(), writes=(), q="sp", semkey=None):
        self._wait(q, self._deps(reads, writes))
        ins = self.eng[q].dma_start(out=out, in_=in_)
        self.n_ins += 1
        if semkey not in self.dcnt:
            self.dcnt[semkey] = SemCounter(self, "d" + semkey, 16)
        tok = self.dcnt[semkey].next()
        ins.then_inc(tok[1], 16)
        self._commit(tok, reads, writes)
        return tok

    def barrier(self):
        toks = []
        for c in list(self.cnt.values()) + list(self.dcnt.values()):
            if c.val > 0:
                toks.append((c.sname, c.sem, c.val))
        for e in self.eng:
            self._wait(e, toks)

    def final_wait(self, e, keys):
        toks = []
        for k in keys:
            t = self.last_w.get(k)
            if t is not None:
                toks.append(t)
        self._wait(e, toks)


class NullBuilder:
    def __init__(self):
        self.readers, self.last_w, self.n_ins = {}, {}, 0

    def op(self, *a, **k):
        pass

    def dma(self, *a, **k):
        pass

    def barrier(self):
        pass

    def _wait(self, *a, **k):
        pass


SCRW = 2048
NSCR = 8


def build(cfg):
    T, NSEQ, L = cfg.T, cfg.NSEQ, cfg.DEPTH
    NSEG, NT, NCH = T // 512, T // 128, T // 64
    assert T % 512 == 0 and T <= 2048
    nc = bass.Bass("TRN2", target_bir_lowering=False)

    def din(name, shape, dt=F32):
        return nc.dram_tensor(name, list(shape), dt, kind="ExternalInput").ap()

    x_d = din("x", [NSEQ * 1024, T])
    win_d = din("w_in", [L * 1024, D_IN])
    wout_d = din("w_out", [L * 1024, 1024])
    pv_d = din("pvec", [128, L * PV_W + 8])
    upw_d = din("upw", [128, L * 512])
    gw_d = din("gw", [128, L * 1024])
    cst_d = din("cst", [128, CST_W])
    alibi_d = din("alibi", [128, 1536])
    ropeC_d = din("ropeC", [128, T])
    ropeS_d = din("ropeS", [128, T])
    y_d = nc.dram_tensor("y", [NSEQ * 1024, T], F32, kind="ExternalOutput").ap()
    tap_d = {}
    for (tname, tshape) in cfg.taps:
        tap_d[tname] = nc.dram_tensor("tap_" + tname, list(tshape), F32, kind="ExternalOutput").ap()

    es = ExitStack()
    with es:
        B = Builder(nc, es)

        def sb(name, shape, dt):
            return es.enter_context(nc.sbuf_tensor(name, list(shape), dt))

        xT = sb("xT", [128, 8, T], F32)
        hT = sb("hT", [128, 8, max(T, 2048)], BF16)
        pvt = sb("pvt", [128, L * PV_W + 8], F32)
        dvt = sb("dvt", [128, L * DV_W], F32)
        CBW = CST_W
        cbf = sb("cbf", [128, CBW], BF16)
        swapf = sb("swapf", [128, 128], F32)
        upw_f = sb("upw_f", [128, 512], F32)
        upw_b = sb("upw_b", [128, 512], BF16)
        gw_b = sb("gw_b", [128, 1024], BF16)
        NWS = 3
        wst = [sb(f"wst{i}", [128, 8, 128], F32) for i in range(2)]
        wbf = [sb(f"wbf{i}", [128, 8, 128], BF16) for i in range(NWS)]
        wost = [sb(f"wost{i}", [128, 256], F32) for i in range(2)]
        wobf = [sb(f"wobf{i}", [128, 256], BF16) for i in range(2)]
        obuf = sb("obuf", [128, 2, T], BF16)
        sqb = sb("sqb", [128, 2, 512], BF16)
        rstd = sb("rstd", [128, 512], F32)
        epsb = sb("epsb", [128, 4], F32)
        tmpA = sb("tmpA", [128, 512], F32)
        tmpB = sb("tmpB", [128, 512], F32)
        gam = sb("gam", [128, 3, 8], F32)
        scr = [sb(f"scr{i}", [128, SCRW], F32) for i in range(NSCR)]
        psum = es.enter_context(nc.psum_tensor("psum", [128, 8 * 512], F32))

        def sk(i):
            return ("scr", i)

        def sbf(i):
            return scr[i][:].bitcast(BF16)

        def PS(b, n=1):
            return psum[:, b * 512:(b + n) * 512]

        def psk(b, n=1):
            return [("ps", b + i) for i in range(n)]

        def cb(name, w=None, off=0):
            o = CST_OFF[name] + off
            return cbf[:, o:o + (w if w is not None else dict(CST_FIELDS)[name])]

        def pvc(name, l, i=0):
            c = pv_col(L, name, l, i)
            return pvt[:, c:c + 1]

        def dvc(name, l, i=0):
            c = dv_col(name, l, i)
            return dvt[:, c:c + 1]

        def tap(name, ap_sb, rkeys, rows=128):
            if name in tap_d:
                B.dma(tap_d[name], ap_sb, reads=rkeys, writes=[("tap", name)], q="pool", semkey="tap" + name)

        wepoch = [0]
        wplan = []
        wstate = {"rec": True, "next": 0, "issued": 0}

        def _issue_w(i):
            l, ranges, ep = wplan[i]
            si, bi = i % 2, i % NWS
            off = 0
            src = win_d[l * 1024:(l + 1) * 1024, :].rearrange("(k p) c -> p k c", p=128)
            for (c0, n) in ranges:
                B.dma(wst[si][:, :, off:off + n], src[:, :, c0:c0 + n], writes=[f"wst{si}"], q="sp", semkey=f"wst{si}")
                off += n
            B.op("pool", lambda e: e.tensor_copy(out=wbf[bi][:, :, 0:off], in_=wst[si][:, :, 0:off]),
                 reads=[f"wst{si}"], writes=[f"wbf{bi}"])

        def load_win(l, ranges):
            m = sum(n for _, n in ranges)
            if wstate["rec"]:
                wplan.append((l, tuple(ranges), wepoch[0]))
                return 0, m
            i = wstate["next"]
            wstate["next"] += 1
            assert wplan[i][0] == l and wplan[i][1] == tuple(ranges), (i, wplan[i], l, ranges)
            hi = i
            while hi + 1 < len(wplan) and hi + 1 <= i + 2 and wplan[hi + 1][2] == wplan[i][2]:
                hi += 1
            while wstate["issued"] <= hi:
                _issue_w(wstate["issued"])
                wstate["issued"] += 1
            return i % NWS, m

        ipb = [0]

        def next_bank():
            b = ipb[0]
            ipb[0] = 4 - b
            return b

        def inproj(l, ranges, bank0):
            bi, m = load_win(l, ranges)
            for sg in range(NSEG):
                for k in range(8):
                    B.op("pe", lambda e, k=k, sg=sg: e.matmul(PS(bank0 + sg)[0:m, :], lhsT=wbf[bi][:, k, 0:m], rhs=hT[:, k, sg * 512:(sg + 1) * 512],
                                                              start=(k == 0), stop=(k == 7)),
                         reads=[f"wbf{bi}", "hT"], writes=psk(bank0 + sg), inc=(k == 7))
            return m

        def inproj_tok(l, ranges, dst_fn, post):
            bi, m = load_win(l, ranges)
            assert m == 64
            tb0 = next_bank()
            for tb in range((NT + 7) // 8):
                bank = tb0 + tb % 2
                ntt = min(8, NT - tb * 8)
                for q in range(ntt):
                    tt = tb * 8 + q
                    for k in range(8):
                        B.op("pe", lambda e, k=k, tt=tt, q=q, bank=bank: e.matmul(PS(bank)[:, q * 64:(q + 1) * 64], lhsT=hT[:, k, tt * 128:(tt + 1) * 128],
                                                                                  rhs=wbf[bi][:, k, 0:64], start=(k == 0), stop=(k == 7)),
                             reads=[f"wbf{bi}", "hT"], writes=psk(bank), inc=(k == 7))
                post(tb, bank, ntt)

        wo_slot = [0]

        def outproj(l, g):
            for co in range(8):
                si = wo_slot[0] % 2
                wo_slot[0] += 1
                for kc in range(2):
                    r0 = l * 1024 + g * 256 + kc * 128
                    B.dma(wost[si][:, kc * 128:(kc + 1) * 128], wout_d[r0:r0 + 128, co * 128:(co + 1) * 128],
                          writes=[f"wost{si}"], q="sp", semkey=f"wost{si}")
                B.op("pool", lambda e, si=si: e.tensor_copy(out=wobf[si][:], in_=wost[si][:]),
                     reads=[f"wost{si}"], writes=[f"wobf{si}"])
                nb = min(2, NSEG)
                for half in range(NSEG // nb):
                    pb = 4 + 2 * ((co * (NSEG // nb) + half) % 2)
                    for j in range(nb):
                        sg = half * nb + j
                        for kc in range(2):
                            B.op("pe", lambda e, si=si, kc=kc, sg=sg, j=j, pb=pb: e.matmul(
                                PS(pb + j), lhsT=wobf[si][:, kc * 128:(kc + 1) * 128], rhs=obuf[:, kc, sg * 512:(sg + 1) * 512],
                                start=(kc == 0), stop=(kc == 1)),
                                reads=[f"wobf{si}", "obuf"], writes=psk(pb + j), inc=(kc == 1))
                    t0 = half * nb * 512
                    w = nb * 512
                    B.op("dve", lambda e, co=co, pb=pb, t0=t0, w=w, nb=nb: e.tensor_tensor(
                        out=xT[:, co, t0:t0 + w], in0=PS(pb, nb), in1=xT[:, co, t0:t0 + w], op=ALU.add),
                        reads=psk(pb, nb) + [("xT", co)], writes=[("xT", co)])

        def rmsnorm_to(l, dest_kind):
            for sg in range(NSEG):
                sl = slice(sg * 512, (sg + 1) * 512)
                for c in range(8):
                    B.op("act", lambda e, c=c: e.activation(out=sqb[:, c % 2, :], in_=xT[:, c, sl], func=AF.Square),
                         reads=[("xT", c)], writes=[("sqb", c % 2)])
                    B.op("pe", lambda e, c=c: e.matmul(PS(0), lhsT=cb("ones"), rhs=sqb[:, c % 2, :], start=(c == 0), stop=(c == 7)),
                         reads=[("sqb", c % 2), "cbf"], writes=psk(0), inc=True)
                B.op("act", lambda e: e.activation(out=rstd[:], in_=PS(0), func=AF.Ln, scale=1.0 / 1024, bias=epsb[:, 0:1]),
                     reads=psk(0) + ["epsb"], writes=["rstd"])
                B.op("act", lambda e: e.activation(out=rstd[:], in_=rstd[:], func=AF.Exp, scale=-0.5), reads=["rstd"], writes=["rstd"])
                for c in range(8):
                    if dest_kind == "h":
                        gcol = pvc("norm_g", l, c)
                        B.op("dve", lambda e, c=c, gcol=gcol: e.scalar_tensor_tensor(out=hT[:, c, sl], in0=xT[:, c, sl], scalar=gcol, in1=rstd[:],
                                                                                      op0=ALU.mult, op1=ALU.mult),
                             reads=[("xT", c), "rstd", "pvt"], writes=["hT"])
                    else:
                        gcol = pvt[:, L * PV_W + c:L * PV_W + c + 1]
                        B.op("dve", lambda e, c=c, gcol=gcol: e.scalar_tensor_tensor(out=xT[:, c, sl], in0=xT[:, c, sl], scalar=gcol, in1=rstd[:],
                                                                                      op0=ALU.mult, op1=ALU.mult),
                             reads=[("xT", c), "rstd", "pvt"], writes=[("xT", c)])

        def mixer_C(l):
            for cc in range(2):
                zb = next_bank()
                inproj(l, [(OFF_C + cc * 128, 128)], zb)
                Z = PS(zb, NSEG)
                zk = psk(zb, NSEG)
                xc = scr[0][:, 0:T]
                B.op("act", lambda e: e.activation(out=xc, in_=Z, func=AF.Identity, scale=pvc("cw", l, 2 * 2 + cc), bias=pvc("cb", l, cc)),
                     reads=zk + ["pvt"], writes=[sk(0)])
                for (j, sh) in ((0, -2), (1, -1), (3, 1)):
                    if sh < 0:
                        o_ap, i_ap = xc[:, -sh:T], Z[:, 0:T + sh]
                    else:
                        o_ap, i_ap = xc[:, 0:T - sh], Z[:, sh:T]
                    B.op("dve", lambda e, o_ap=o_ap, i_ap=i_ap, j=j: e.scalar_tensor_tensor(out=o_ap, in0=i_ap, scalar=pvc("cw", l, j * 2 + cc), in1=o_ap,
                                                                                            op0=ALU.mult, op1=ALU.add),
                         reads=zk + [sk(0), "pvt"], writes=[sk(0)])
                xcb = sbf(1)[:, 0:T]
                B.op("act", lambda e: e.activation(out=xcb, in_=xc, func=AF.Copy), reads=[sk(0)], writes=[sk(1)])
                for d in range(2):
                    rb, ib, sbuf_, hb = scr[2][:, 0:T], scr[3][:, 0:T], scr[4][:, 0:T], scr[5 + d][:, 0:T]
                    for sg in range(NSEG):
                        sl = slice(sg * 512, (sg + 1) * 512)
                        for k in range(2):
                            bank = 4 + 2 * k + sg % 2
                            wcol = (((d * 2 + k) * 2 + cc)) * 128
                            B.op("pe", lambda e, bank=bank, wcol=wcol, sl=sl: e.matmul(PS(bank), lhsT=gw_b[:, wcol:wcol + 128], rhs=xcb[:, sl], start=True, stop=True),
                                 reads=["gw_b", sk(1)], writes=psk(bank))
                            dst = rb if k == 0 else ib
                            B.op("act", lambda e, bank=bank, dst=dst, sl=sl, k=k: e.activation(out=dst[:, sl], in_=PS(bank), func=AF.Sigmoid,
                                                                                               bias=pvc("gb", l, (d * 2 + k) * 2 + cc)),
                                 reads=psk(bank) + ["pvt"], writes=[sk(2 + k)])
                    B.op("act", lambda e: e.activation(out=rb, in_=rb, func=AF.Exp, scale=dvc("cneg", l, d * 2 + cc)), reads=[sk(2), "dvt"], writes=[sk(2)])
                    B.op("act", lambda e: e.activation(out=sbuf_, in_=rb, func=AF.Square), reads=[sk(2)], writes=[sk(4)])
                    B.op("act", lambda e: e.activation(out=sbuf_, in_=sbuf_, func=AF.Ln, scale=-1.0, bias=epsb[:, 1:2]), reads=[sk(4), "epsb"], writes=[sk(4)])
                    B.op("act", lambda e: e.activation(out=sbuf_, in_=sbuf_, func=AF.Exp, scale=0.5), reads=[sk(4)], writes=[sk(4)])
                    B.op("dve", lambda e: e.tensor_tensor(out=ib, in0=ib, in1=xc, op=ALU.mult), reads=[sk(3), sk(0)], writes=[sk(3)])
                    B.op("dve", lambda e: e.tensor_tensor(out=ib, in0=ib, in1=sbuf_, op=ALU.mult), reads=[sk(3), sk(4)], writes=[sk(3)])
                    if d == 0:
                        B.op("dve", lambda e: e.tensor_tensor_scan(out=hb, data0=rb, data1=ib, initial=0.0, op0=ALU.mult, op1=ALU.add),
                             reads=[sk(2), sk(3)], writes=[sk(5 + d)])
                    else:
                        B.op("dve", lambda e: e.tensor_tensor_scan(out=hb[:, ::-1], data0=rb[:, ::-1], data1=ib[:, ::-1], initial=0.0, op0=ALU.mult, op1=ALU.add),
                             reads=[sk(2), sk(3)], writes=[sk(5 + d)])
                hf, hbk = scr[5][:, 0:T], scr[6][:, 0:T]
                B.op("dve", lambda e: e.tensor_tensor(out=hf, in0=hf, in1=hbk, op=ALU.add), reads=[sk(5), sk(6)], writes=[sk(5)])
                gb_ = next_bank()
                inproj(l, [(OFF_C + 256 + cc * 128, 128)], gb_)
                sgt = scr[2][:, 0:T]
                B.op("act", lambda e: e.activation(out=sgt, in_=PS(gb_, NSEG), func=AF.Silu), reads=psk(gb_, NSEG), writes=[sk(2)])
                B.op("dve", lambda e: e.tensor_tensor(out=obuf[:, cc, :], in0=hf, in1=sgt, op=ALU.mult), reads=[sk(5), sk(2)], writes=["obuf"])

        def load_qk(l, off_q, off_k, cc, prep):
            qT = sbf(0)[:, 0:T]
            kT = sbf(0)[:, SCRW:SCRW + T]
            prep([(off_q + cc * 128, 128)], qT, True)
            prep([(off_k + cc * 64, 64), (off_k + cc * 64, 64)], kT, False)
            return qT, kT

        def load_vaug(l, off_v, cc):
            va = sbf(1)
            vav = va[:, 0:NT * 192].rearrange("p (t c) -> p t c", c=192)
            B.op("pool", lambda e: e.memset(vav[:, :, 64:128], 1.0), writes=[sk(1)])

            def post(tb, bank, ntt):
                src = PS(bank)[:, 0:ntt * 64].rearrange("p (t c) -> p t c", c=64)
                B.op("act", lambda e: e.activation(out=vav[:, tb * 8:tb * 8 + ntt, 0:64], in_=src, func=AF.Copy), reads=psk(bank), writes=[sk(1)])
                B.op("dve", lambda e: e.tensor_copy(out=vav[:, tb * 8:tb * 8 + ntt, 128:192], in_=src), reads=psk(bank), writes=[sk(1)])
            inproj_tok(l, [(off_v + cc * 64, 64)], None, post)
            return vav

        def load_gate(l, off_g, cc):
            gb_ = next_bank()
            inproj(l, [(off_g + cc * 128, 128)], gb_)
            sgt = scr[2][:, 0:T]
            B.op("act", lambda e: e.activation(out=sgt, in_=PS(gb_, NSEG), func=AF.Silu), reads=psk(gb_, NSEG), writes=[sk(2)])
            return sgt

        def attn_post(l, cc, qg, bx, by, sgt, sink):
            X, Y = PS(bx), PS(by)
            gsl = slice(qg * 512, (qg + 1) * 512)
            if sink:
                B.op("dve", lambda e: e.tensor_scalar(out=tmpA[0:64, :], in0=Y[0:64, :], scalar1=dvc("esinksw", l, cc)[0:64, :], scalar2=None, op0=ALU.add),
                     reads=psk(by) + ["dvt"], writes=["tmpA"])
                B.op("dve", lambda e: e.tensor_scalar(out=tmpA[64:128, :], in0=X[64:128, :], scalar1=dvc("esinksw", l, cc)[64:128, :], scalar2=None, op0=ALU.add),
                     reads=psk(bx) + ["dvt"], writes=["tmpA"])
                B.op("act", lambda e: e.activation(out=tmpA[:], in_=tmpA[:], func=AF.Ln), reads=["tmpA"], writes=["tmpA"])
            else:
                B.op("act", lambda e: e.activation(out=tmpA[0:64, :], in_=Y[0:64, :], func=AF.Ln), reads=psk(by), writes=["tmpA"])
                B.op("act", lambda e: e.activation(out=tmpA[64:128, :], in_=X[64:128, :], func=AF.Ln), reads=psk(bx), writes=["tmpA"])
            B.op("act", lambda e: e.activation(out=tmpA[:], in_=tmpA[:], func=AF.Exp, scale=-1.0), reads=["tmpA"], writes=["tmpA"])
            B.op("pe", lambda e: e.matmul(PS(0), lhsT=swapf[:], rhs=tmpA[:], start=True, stop=True), reads=["swapf", "tmpA"], writes=psk(0))
            B.op("dve", lambda e: e.tensor_tensor(out=tmpB[:], in0=PS(0), in1=sgt[:, gsl], op=ALU.mult), reads=psk(0) + [sk(2)], writes=["tmpB"])
            B.op("dve", lambda e: e.tensor_tensor(out=obuf[0:64, cc, gsl], in0=X[0:64, :], in1=tmpB[0:64, :], op=ALU.mult),
                 reads=psk(bx) + ["tmpB"], writes=["obuf"])
            B.op("dve", lambda e: e.tensor_tensor(out=obuf[64:128, cc, gsl], in0=Y[64:128, :], in1=tmpB[64:128, :], op=ALU.mult),
                 reads=psk(by) + ["tmpB"], writes=["obuf"])

        def mixer_D(l):
            B.dma(scr[3][:, 0:1536], alibi_d[:, :], writes=[sk(3)], q="pool", semkey="alibi")

            def prep(ranges, dst, isq):
                pb_ = next_bank()
                inproj(l, ranges, pb_)
                B.op("act", lambda e: e.activation(out=dst, in_=PS(pb_, NSEG), func=AF.Copy), reads=psk(pb_, NSEG), writes=[sk(0)])
            for cc in range(2):
                qT, kT = load_qk(l, OFF_D, OFF_D + 256, cc, prep)
                vav = load_vaug(l, OFF_D + 384, cc)
                sgt = load_gate(l, OFF_D + 512, cc)
                ptv = sbf(4)
                exv = sbf(5)
                atab = scr[3][:, 2 * cc * 384:(2 * cc + 2) * 384].rearrange("p (h c) -> p h c", h=2)

                def qrange(j):
                    return max(j - 1, 0), min(j + 1, NT - 1)

                def qk(j):
                    b0, b1 = qrange(j)
                    ncol = (b1 - b0 + 1) * 128
                    for hh in range(2):
                        ph = slice(hh * 64, hh * 64 + 64)
                        bank = 2 * (j % 2) + hh
                        B.op("pe", lambda e: e.matmul(PS(bank)[:, 0:ncol], lhsT=kT[ph, j * 128:(j + 1) * 128], rhs=qT[ph, b0 * 128:b0 * 128 + ncol], start=True, stop=True),
                             reads=[sk(0)], writes=psk(bank), inc=(hh == 1))

                def ex(j):
                    b0, b1 = qrange(j)
                    ncol = (b1 - b0 + 1) * 128
                    tcol0 = (b0 - (j - 1)) * 128
                    src = PS(2 * (j % 2), 2).rearrange("p (b c) -> p b c", b=2)[:, :, 0:ncol]
                    exs = exv[:, (j % 2) * 768:(j % 2 + 1) * 768].rearrange("p (h c) -> p h c", h=2)[:, :, 0:ncol]
                    pts = ptv[:, (j % 4) * 768:(j % 4 + 1) * 768].rearrange("p (h c) -> p h c", h=2)[:, :, 0:ncol]
                    B.op("act", lambda e: e.activation(out=exs, in_=src, func=AF.Exp, scale=0.125), reads=psk(2 * (j % 2), 2), writes=[("ex", j % 2)])
                    B.op("dve", lambda e: e.tensor_tensor(out=pts, in0=exs, in1=atab[:, :, tcol0:tcol0 + ncol], op=ALU.mult),
                         reads=[("ex", j % 2), sk(3)], writes=[("pt", j % 4)])

                def pv_block(i):
                    qg = i // 4
                    js = [j for j in (i - 1, i, i + 1) if 0 <= j < NT]
                    for hh in range(2):
                        bank = 4 + 2 * (qg % 2) + hh
                        for n, j in enumerate(js):
                            b0, _ = qrange(j)
                            c0 = (j % 4) * 768 + hh * 384 + (i - b0) * 128
                            B.op("pe", lambda e: e.matmul(PS(bank)[:, (i % 4) * 128:(i % 4 + 1) * 128], lhsT=vav[:, j, hh * 64:hh * 64 + 128], rhs=ptv[:, c0:c0 + 128],
                                                         start=(n == 0), stop=(n == len(js) - 1)),
                                 reads=[sk(1), ("pt", j % 4)], writes=psk(bank), inc=(n == len(js) - 1))

                qk(0)
                for j in range(NT):
                    if j + 1 < NT:
                        qk(j + 1)
                    ex(j)
                    blocks = []
                    if j >= 1:
                        blocks.append(j - 1)
                    if j == NT - 1:
                        blocks.append(j)
                    for i in blocks:
                        pv_block(i)
                        if i % 4 == 3:
                            qg = i // 4
                            attn_post(l, cc, qg, 4 + 2 * (qg % 2), 4 + 2 * (qg % 2) + 1, sgt, True)

        def mixer_B(l):
            B.dma(scr[6][:, 0:T], ropeC_d[:, :], writes=[sk(6)], q="pool", semkey="ropeC")
            B.dma(scr[7][:, 0:T], ropeS_d[:, :], writes=[sk(7)], q="pool", semkey="ropeS")
            t1s = [scr[3][:, 0:512], scr[3][:, 1536:2048]]
            t2s = [scr[3][:, 512:1024], scr[5][:, 0:512]]
            qgs = [sbf(3)[:, 2048:2560], sbf(3)[:, 2560:3072]]
            rss = [rstd, tmpA]
            rsk = ["rstd", "tmpA"]

            def prep(ranges, dst, isq):
                inproj(l, ranges, 0)
                gcol = pvc("qn" if isq else "kn", l, 0)
                for sg in range(NSEG):
                    sl = slice(sg * 512, (sg + 1) * 512)
                    Z = PS(sg)
                    u = sg % 2
                    qgb, t1b, t2b, rs_, rk_ = qgs[u], t1s[u], t2s[u], rss[u], rsk[u]
                    b4, b5 = 4 + 2 * u, 5 + 2 * u
                    B.op("act", lambda e: e.activation(out=qgb, in_=Z, func=AF.Copy, scale=gcol), reads=psk(sg) + ["pvt"], writes=[("qgb", u)])
                    B.op("act", lambda e: e.activation(out=sqb[:, u, :], in_=Z, func=AF.Square), reads=psk(sg), writes=[("sqb", u)])
                    B.op("pe", lambda e: e.matmul(PS(b4), lhsT=cb("blk64"), rhs=sqb[:, u, :], start=True, stop=True), reads=[("sqb", u), "cbf"], writes=psk(b4))
                    B.op("pe", lambda e: e.matmul(PS(b5), lhsT=cb("perm"), rhs=qgb, start=True, stop=True), reads=[("qgb", u), "cbf"], writes=psk(b5))
                    B.op("act", lambda e: e.activation(out=rs_[:], in_=PS(b4), func=AF.Ln, scale=1.0 / 64, bias=epsb[:, 0:1]), reads=psk(b4) + ["epsb"], writes=[rk_])
                    B.op("act", lambda e: e.activation(out=rs_[:], in_=rs_[:], func=AF.Exp, scale=-0.5), reads=[rk_], writes=[rk_])
                    B.op("pool", lambda e: e.tensor_tensor(out=t1b, in0=qgb, in1=scr[6][:, sl], op=ALU.mult), reads=[("qgb", u), sk(6)], writes=[("t1b", u)])
                    B.op("dve", lambda e: e.tensor_tensor(out=t2b, in0=PS(b5), in1=scr[7][:, sl], op=ALU.mult), reads=psk(b5) + [sk(7)], writes=[("t2b", u)])
                    B.op("dve", lambda e: e.tensor_tensor(out=t1b, in0=t1b, in1=t2b, op=ALU.add), reads=[("t1b", u), ("t2b", u)], writes=[("t1b", u)])
                    B.op("dve", lambda e: e.tensor_tensor(out=dst[:, sl], in0=t1b, in1=rs_[:], op=ALU.mult), reads=[("t1b", u), rk_], writes=[sk(0)])
            for cc in range(2):
                qT, kT = load_qk(l, OFF_B, OFF_B + 256, cc, prep)
                vav = load_vaug(l, OFF_B + 384, cc)
                sgt = load_gate(l, OFF_B + 512, cc)
                ptv = sbf(4)
                for qg in range(NSEG):
                    gsl = slice(qg * 512, (qg + 1) * 512)
                    ab = 4 + 2 * (qg % 2)

                    def qk(j):
                        for hh in range(2):
                            ph = slice(hh * 64, hh * 64 + 64)
                            bank = 2 * (j % 2) + hh
                            B.op("pe", lambda e: e.matmul(PS(bank), lhsT=kT[ph, j * 128:(j + 1) * 128], rhs=qT[ph, gsl], start=True, stop=True),
                                 reads=[sk(0)], writes=psk(bank), inc=(hh == 1))

                    def ex(j):
                        slot = j % 3
                        B.op("act", lambda e: e.activation(out=ptv[:, slot * 1024:(slot + 1) * 1024], in_=PS(2 * (j % 2), 2), func=AF.Exp, scale=0.125),
                             reads=psk(2 * (j % 2), 2), writes=[("pt", slot)])

                    def pv(j):
                        slot = j % 3
                        for hh in range(2):
                            B.op("pe", lambda e: e.matmul(PS(ab + hh), lhsT=vav[:, j, hh * 64:hh * 64 + 128], rhs=ptv[:, slot * 1024 + hh * 512:slot * 1024 + (hh + 1) * 512],
                                                         start=(j == 0), stop=(j == NT - 1)),
                                 reads=[sk(1), ("pt", slot)], writes=psk(ab + hh), inc=(hh == 1))
                    qk(0)
                    for j in range(NT):
                        if j + 1 < NT:
                            qk(j + 1)
                        ex(j)
                        pv(j)
                    attn_post(l, cc, qg, ab, ab + 1, sgt, False)

        def mixer_A(l):
            SA = min(512, T)
            NS = T // SA
            nch = SA // 64
            arena_b = hT[:].rearrange("p k t -> p (k t)")

            def AF32(off, w):
                return arena_b[:, 2 * off:2 * (off + w)].bitcast(F32)

            def ABF(off, w):
                return arena_b[:, 2 * off:2 * off + w]

            rB = [sbf(0)[:, 0:T], sbf(0)[:, SCRW:SCRW + T]]
            kB = [sbf(1)[:, 0:T], sbf(1)[:, SCRW:SCRW + T]]
            vB = [sbf(2)[:, 0:T], sbf(2)[:, SCRW:SCRW + T]]
            waB = sbf(3)[:, 0:T]
            kkB = [sbf(4)[:, 0:T], sbf(4)[:, SCRW:SCRW + T]]
            yfB = [sbf(5)[:, 0:T], sbf(5)[:, SCRW:SCRW + T]]
            bcB = [sbf(6)[:, 0:T], sbf(6)[:, SCRW:SCRW + T]]
            for ci in range(7):
                pb0 = 4 * (ci % 2)
                xi = 7 if ci % 2 == 0 else 5
                Xt = scr[xi][:, 0:T]
                inproj(l, [(OFF_A + ci * 128, 128)], pb0)
                Z = PS(pb0, NSEG)
                zk = psk(pb0, NSEG)
                B.op("act", lambda e: e.activation(out=Xt, in_=Z, func=AF.Copy, scale=dvc("c0", l, ci)), reads=zk + ["dvt"], writes=[sk(xi)])
                B.op("dve", lambda e: e.scalar_tensor_tensor(out=Xt[:, 1:T], in0=Z[:, 0:T - 1], scalar=pvc("sh0", l, ci), in1=Xt[:, 1:T], op0=ALU.mult, op1=ALU.add),
                     reads=zk + [sk(xi), "pvt"], writes=[sk(xi)])
                if ci < 6:
                    dst = (rB, kB, vB)[ci // 2][ci % 2]
                    dk_ = sk(ci // 2)
                    B.op("dve", lambda e: e.scalar_tensor_tensor(out=dst[:, 0:T - 1], in0=Z[:, 1:T], scalar=pvc("sh1", l, ci), in1=Xt[:, 0:T - 1], op0=ALU.mult, op1=ALU.add),
                         reads=zk + [sk(xi), "pvt"], writes=[dk_])
                    B.op("dve", lambda e: e.tensor_copy(out=dst[:, T - 1:T], in_=Xt[:, T - 1:T]), reads=[sk(xi)], writes=[dk_])
                else:
                    B.op("dve", lambda e: e.scalar_tensor_tensor(out=Xt[:, 0:T - 1], in0=Z[:, 1:T], scalar=pvc("sh1", l, ci), in1=Xt[:, 0:T - 1], op0=ALU.mult, op1=ALU.add),
                         reads=zk + [sk(xi), "pvt"], writes=[sk(xi)])
                    B.op("act", lambda e: e.activation(out=waB[0:64, :], in_=Xt[0:64, :], func=AF.Tanh), reads=[sk(xi)], writes=[sk(3)])
                    B.op("act", lambda e: e.activation(out=waB[64:128, :], in_=Xt[64:128, :], func=AF.Copy), reads=[sk(xi)], writes=[sk(3)])
            for hp in range(2):
                for sg in range(NSEG):
                    sl = slice(sg * 512, (sg + 1) * 512)
                    B.op("dve", lambda e: e.tensor_scalar(out=tmpA[:], in0=kB[hp][:, sl], scalar1=pvc("k_k", l, hp), scalar2=None, op0=ALU.mult),
                         reads=[sk(1), "pvt"], writes=["tmpA"])
                    B.op("act", lambda e: e.activation(out=sqb[:, 0, :], in_=tmpA[:], func=AF.Square), reads=["tmpA"], writes=[("sqb", 0)])
                    B.op("pe", lambda e: e.matmul(PS(4), lhsT=cb("blk64"), rhs=sqb[:, 0, :], start=True, stop=True), reads=[("sqb", 0), "cbf"], writes=psk(4))
                    B.op("act", lambda e: e.activation(out=rstd[:], in_=PS(4), func=AF.Ln, bias=epsb[:, 3:4]), reads=psk(4) + ["epsb"], writes=["rstd"])
                    B.op("act", lambda e: e.activation(out=rstd[:], in_=rstd[:], func=AF.Exp, scale=-0.5), reads=["rstd"], writes=["rstd"])
                    B.op("dve", lambda e: e.tensor_tensor(out=kkB[hp][:, sl], in0=tmpA[:], in1=rstd[:], op=ALU.mult), reads=["tmpA", "rstd"], writes=[sk(4)])
            for cc in range(2):
                gb_ = next_bank()
                inproj(l, [(OFF_A + 896 + cc * 128, 128)], gb_)
                B.op("act", lambda e: e.activation(out=obuf[:, cc, :], in_=PS(gb_, NSEG), func=AF.Silu), reads=psk(gb_, NSEG), writes=["obuf"])
            wepoch[0] += 1
            B.barrier()
            o = 0
            Fb = []
            for i in range(5):
                Fb.append(AF32(o, SA)); o += SA
            H0 = AF32(o, SA); o += SA
            H1 = AF32(o, SA); o += SA
            H2 = ABF(o, SA); o += SA // 2
            BK = ABF(o, 2 * SA); o += SA
            ARb = ABF(o, 2 * SA); o += SA
            HK = ABF(o, 2 * SA); o += SA
            Gm = ABF(o, 4 * SA); o += 2 * SA
            GT = ABF(o, SA); o += SA // 2
            Xs, XTs, Ps_ = [], [], []
            for i in range(2):
                Xs.append(ABF(o, SA)); o += SA // 2
                XTs.append(ABF(o, SA)); o += SA // 2
                Ps_.append(ABF(o, SA)); o += SA // 2
            assert o <= 8192, o
            s7 = sbf(7)
            VT, BhT, KhT = s7[:, 0:SA], s7[:, SA:2 * SA], s7[:, 2 * SA:3 * SA]
            Q1 = scr[7][:, 768:768 + SA]
            Qb = s7[:, 2560:2624]
            UTb = [s7[:, 2624:2688], s7[:, 2688:2752]]
            STf = [[scr[7][:, 1408 + (hp * 2 + i) * 64:1408 + (hp * 2 + i + 1) * 64] for i in range(2)] for hp in range(2)]
            STb = [[s7[:, 3328 + (hp * 2 + i) * 64:3328 + (hp * 2 + i + 1) * 64] for i in range(2)] for hp in range(2)]
            id64 = s7[:, 3600:3664]
            blkf = scr[7][:, 1856:1984]
            F0, F1, F2, F3, F4 = Fb
            B.op("dve", lambda e: e.tensor_tensor(out=id64, in0=cb("mfwd", 64, 64), in1=cb("mfwd", 64, 0), op=ALU.subtract), reads=["cbf"], writes=["id64"])
            B.op("dve", lambda e: e.tensor_copy(out=blkf, in_=cb("blk64")), reads=["cbf"], writes=["blkf"])
            BK4 = BK.rearrange("p (c two j) -> p c two j", two=2, j=64)
            AR4 = ARb.rearrange("p (c two j) -> p c two j", two=2, j=64)
            HK4 = HK.rearrange("p (c two j) -> p c two j", two=2, j=64)
            Gm3 = Gm.rearrange("p (c x) -> p c x", x=256)

            def v3(ap):
                return ap.rearrange("p (c j) -> p c j", j=64)

            def hs(hd):
                return slice(hd * 64, hd * 64 + 64)

            Wf = [wst[i][:].rearrange("p k c -> p (k c)") for i in range(2)]
            fin = [Wf[0][:, 0:SA], Wf[0][:, SA:2 * SA], Wf[1][:, 0:SA], Wf[1][:, SA:2 * SA]]
            fink = ["wst0", "wst0", "wst1", "wst1"]
            sets = []
            sets.append(dict(BK=BK, AR=ARb, HK=HK, kBK="BK", kAR="AR", kHK="HK", gi=0))
            wfl = [wbf[i][:].rearrange("p k c -> p (k c)") for i in range(3)]
            sets.append(dict(BK=wfl[0][:, 0:2 * SA], AR=wfl[1][:, 0:2 * SA], HK=wfl[2][:, 0:2 * SA], kBK="wbf0", kAR="wbf1", kHK="wbf2", gi=1))
            for S_ in sets:
                for nm in ("BK", "AR", "HK"):
                    S_[nm + "4"] = S_[nm].rearrange("p (c two j) -> p c two j", two=2, j=64)
            upwv = upw_f[:].bitcast(BF16)
            AR4s = [ARb.rearrange("p (c two j) -> p c two j", two=2, j=64), wfl[1][:, 0:2 * SA].rearrange("p (c two j) -> p c two j", two=2, j=64),
                    upwv[:, 0:2 * SA].rearrange("p (c two j) -> p c two j", two=2, j=64)]
            kARs = ["AR", "wbf1", "upw_f"]
            Gms = [Gm, sbf(3)[:, 2 * SCRW:2 * SCRW + 4 * SA] if False else sbf(3)[:, SCRW:SCRW + 4 * SA]]
            kGms = ["Gm", "scr3hi"]
            tAv, tBv = tmpA[:].bitcast(BF16), tmpB[:].bitcast(BF16)
            VTs, BhTs, KhTs = [VT, tAv[:, 0:SA]], [BhT, tAv[:, SA:2 * SA]], [KhT, tBv[:, 0:SA]]
            kVTs, kBhTs, kKhTs = ["VT", "tmpA_lo"], ["BhT", "tmpA_hi"], ["KhT", "tmpB"]
            Q1s, kQ1s = [Q1, rstd[:, 0:SA]], ["Q1", "rstd"]
            Tms, kTms = [sqb[:, 0, 0:SA], sqb[:, 1, 0:SA]], [("sqb", 0), ("sqb", 1)]
            iters = []
            for d in range(2):
                for sg in (range(NS) if d == 0 else range(NS - 1, -1, -1)):
                    for hp in range(2):
                        iters.append((d, sg, hp))
            cur = [0, 0]

            def pre_gen(ii):
                d, sg, hp = iters[ii]
                S_ = sets[ii % 2]
                BK4, HK4 = S_["BK4"], S_["HK4"]
                AR4 = AR4s[ii % 3]
                kAR_ = kARs[ii % 3]
                gi = ii % 3
                endc = 63 if d == 0 else 0
                ssl = slice(sg * SA, (sg + 1) * SA)
                first_of_dir = (sg == (0 if d == 0 else NS - 1))
                wc = d * 256 + hp * 128
                B.op("pe", lambda e: e.matmul(PS(2)[:, 0:SA], lhsT=upw_b[0:64, wc:wc + 128], rhs=waB[0:64, ssl], start=True, stop=True),
                     reads=["upw_b", sk(3)], writes=psk(2))
                B.op("pe", lambda e: e.matmul(PS(3)[:, 0:SA], lhsT=upw_b[64:128, wc:wc + 128], rhs=waB[64:128, ssl], start=True, stop=True),
                     reads=["upw_b", sk(3)], writes=psk(3))
                yield
                B.op("act", lambda e: e.activation(out=F0, in_=PS(2)[:, 0:SA], func=AF.Tanh, scale=0.5, bias=dvc("w0h", l, d * 2 + hp)), reads=psk(2) + ["dvt"], writes=["F0"])
                B.op("act", lambda e: e.activation(out=F1, in_=PS(3)[:, 0:SA], func=AF.Tanh, scale=0.5, bias=dvc("a0h", l, d * 2 + hp)), reads=psk(3) + ["dvt"], writes=["F1"])
                yield
                B.op("pool", lambda e: e.tensor_scalar(out=F0, in0=F0, scalar1=0.5, scalar2=0.5, op0=ALU.mult, op1=ALU.add), reads=["F0"], writes=["F0"])
                B.op("pool", lambda e: e.tensor_scalar(out=F1, in0=F1, scalar1=0.5, scalar2=0.5, op0=ALU.mult, op1=ALU.add), reads=["F1"], writes=["F1"])
                yield
                rst = cb("rst", SA)
                if d == 0:
                    B.op("dve", lambda e: e.tensor_tensor_scan(out=F2, data0=rst, data1=F0, initial=0.0, op0=ALU.mult, op1=ALU.add), reads=["cbf", "F0"], writes=["F2"])
                else:
                    B.op("dve", lambda e: e.tensor_tensor_scan(out=F2[:, ::-1], data0=rst, data1=F0[:, ::-1], initial=0.0, op0=ALU.mult, op1=ALU.add),
                         reads=["cbf", "F0"], writes=["F2"])
                yield
                B.op("dve", lambda e: e.tensor_tensor(out=F3, in0=F2, in1=F0, op=ALU.subtract), reads=["F2", "F0"], writes=["F3"])
                yield
                endb = v3(F2)[:, :, endc:endc + 1].broadcast_to([128, nch, 64])
                B.op("dve", lambda e: e.tensor_tensor(out=v3(F0), in0=endb, in1=v3(F2), op=ALU.subtract), reads=["F2"], writes=["F0"])
                yield
                B.op("act", lambda e: e.activation(out=F4, in_=F2, func=AF.Exp, scale=LAMW), reads=["F2"], writes=["F4"])
                B.op("act", lambda e: e.activation(out=F2, in_=F2, func=AF.Exp, scale=-LAMW), reads=["F2"], writes=["F2"])
                yield
                B.op("act", lambda e: e.activation(out=F3, in_=F3, func=AF.Exp, scale=-LAMW), reads=["F3"], writes=["F3"])
                B.op("act", lambda e: e.activation(out=F0, in_=F0, func=AF.Exp, scale=-LAMW), reads=["F0"], writes=["F0"])
                yield
                gv = gam[:, gi, 0:nch]
                B.op("dve", lambda e: e.tensor_copy(out=gv, in_=v3(F2)[:, :, endc]), reads=["F2"], writes=[("gam", gi)])
                kk_, r_, k_, v_ = kkB[hp][:, ssl], rB[hp][:, ssl], kB[hp][:, ssl], vB[hp][:, ssl]
                B.op("pool", lambda e: e.tensor_tensor(out=H0, in0=kk_, in1=F1, op=ALU.mult), reads=[sk(4), "F1"], writes=["H0"])
                yield
                B.op("pool", lambda e: e.tensor_scalar(out=F1, in0=F1, scalar1=pvc("k_a", l, hp), scalar2=dvc("omka", l, hp), op0=ALU.mult, op1=ALU.add),
                     reads=["F1", "pvt", "dvt"], writes=["F1"])
                yield
                B.op("pool", lambda e: e.tensor_tensor(out=H1, in0=F1, in1=k_, op=ALU.mult), reads=["F1", sk(1)], writes=["H1"])
                yield
                B.op("pool", lambda e: e.tensor_tensor(out=BK4[:, :, 0, :], in0=v3(H0), in1=v3(F4), op=ALU.mult), reads=["H0", "F4"], writes=[S_["kBK"]])
                yield
                B.op("pool", lambda e: e.tensor_tensor(out=BK4[:, :, 1, :], in0=v3(H1), in1=v3(F4), op=ALU.mult), reads=["H1", "F4"], writes=[S_["kBK"]])
                yield
                B.op("dve", lambda e: e.scalar_tensor_tensor(out=AR4[:, :, 0, :], in0=v3(kk_), scalar=-1.0, in1=v3(F3), op0=ALU.mult, op1=ALU.mult),
                     reads=[sk(4), "F3"], writes=[kAR_])
                yield
                B.op("pool", lambda e: e.tensor_tensor(out=AR4[:, :, 1, :], in0=v3(r_), in1=v3(F2), op=ALU.mult), reads=[sk(0), "F2"], writes=[kAR_])
                yield
                B.op("pool", lambda e: e.tensor_tensor(out=HK4[:, :, 0, :], in0=v3(H0), in1=v3(F0), op=ALU.mult), reads=["H0", "F0"], writes=[S_["kHK"]])
                yield
                B.op("pool", lambda e: e.tensor_tensor(out=HK4[:, :, 1, :], in0=v3(H1), in1=v3(F0), op=ALU.mult), reads=["H1", "F0"], writes=[S_["kHK"]])
                yield
                B.op("dve", lambda e: e.scalar_tensor_tensor(out=H2, in0=r_, scalar=pvc("r_k", l, hp), in1=H1, op0=ALU.mult, op1=ALU.mult),
                     reads=[sk(0), "pvt", "H1"], writes=["H2"])
                B.op("pe", lambda e: e.matmul(PS(2)[:, 0:SA], lhsT=cb("blk64"), rhs=H2, start=True, stop=True), reads=["cbf", "H2"], writes=psk(2))
                yield
                if d == 0:
                    B.op("act", lambda e: e.activation(out=bcB[hp][:, ssl], in_=PS(2)[:, 0:SA], func=AF.Copy), reads=psk(2), writes=[("bc", hp)])
                else:
                    B.op("dve", lambda e: e.tensor_tensor(out=bcB[hp][:, ssl], in0=PS(2)[:, 0:SA], in1=bcB[hp][:, ssl], op=ALU.add), reads=psk(2) + [("bc", hp)], writes=[("bc", hp)])
                yield

            def inv_gen(ii):
                d, sg, hp = iters[ii]
                S_ = sets[ii % 2]
                BK4, HK4 = S_["BK4"], S_["HK4"]
                kBK, kHK = S_["kBK"], S_["kHK"]
                AR4, kAR = AR4s[ii % 3], kARs[ii % 3]
                gi = ii % 3
                Gm, kGm = Gms[ii % 2], kGms[ii % 2]
                Gm3 = Gm.rearrange("p (c x) -> p c x", x=256)
                VT, BhT, KhT = VTs[ii % 2], BhTs[ii % 2], KhTs[ii % 2]
                kVT, kBhT, kKhT = kVTs[ii % 2], kBhTs[ii % 2], kKhTs[ii % 2]
                Q1, kQ1 = Q1s[ii % 2], kQ1s[ii % 2]
                Tm, tkey = Tms[ii % 2], kTms[ii % 2]
                mname = "mfwd" if d == 0 else "mbwd"
                mTname = "mfwdT" if d == 0 else "mbwdT"
                ssl = slice(sg * SA, (sg + 1) * SA)
                v_ = vB[hp][:, ssl]
                n_items = 2 * nch
                Gps = PS(4, 4)
                n_items = 2 * nch
                it = 0
                for c in range(nch):
                    for hd in range(2):
                        it += 1
                        last = (it == n_items)
                        B.op("pe", lambda e: e.matmul(Gps[hs(hd), c * 256:c * 256 + 128], lhsT=BK4[hs(hd), c, 0, :], rhs=AR4[hs(hd), c, :, :], start=True, stop=True),
                             reads=[kBK, kAR], writes=psk(4, 4), inc=False)
                        B.op("pe", lambda e: e.matmul(Gps[hs(hd), c * 256 + 128:c * 256 + 256], lhsT=BK4[hs(hd), c, 1, :], rhs=AR4[hs(hd), c, :, :], start=True, stop=True),
                             reads=[kBK, kAR], writes=psk(4, 4), inc=last)
                mk = cb(mname).rearrange("p (o x) -> p o x", o=1).broadcast_to([128, nch, 256])
                B.op("dve", lambda e: e.tensor_tensor(out=Gm3, in0=Gps[:, 0:nch * 256].rearrange("p (c x) -> p c x", x=256), in1=mk, op=ALU.mult),
                     reads=psk(4, 4) + ["cbf"], writes=[kGm])
                it = 0
                for c in range(nch):
                    for hd in range(2):
                        it += 1
                        B.op("pe", lambda e: e.matmul(PS(4)[hs(hd), c * 64:(c + 1) * 64], lhsT=AR4[hs(hd), c, 0, :], rhs=BK4[hs(hd), c, 0, :], start=True, stop=True),
                             reads=[kBK, kAR], writes=psk(4), inc=(it == n_items))
                mkT = cb(mTname).rearrange("p (o x) -> p o x", o=1).broadcast_to([128, nch, 64])
                B.op("dve", lambda e: e.tensor_tensor(out=v3(GT), in0=v3(PS(4)[:, 0:SA]), in1=mkT, op=ALU.mult), reads=psk(4) + ["cbf"], writes=["GT"])
                yield
                idb = id64.rearrange("p (o x) -> p o x", o=1).broadcast_to([128, nch, 64])
                B.op("dve", lambda e: e.tensor_tensor(out=v3(Ps_[0]), in0=Gm3[:, :, 0:64], in1=idb, op=ALU.add), reads=[kGm, "id64"], writes=[("P", 0)])

                def Xap(level, buf, hd, c):
                    if level == 0:
                        return Gm[hs(hd), c * 256:c * 256 + 64]
                    return Xs[buf][hs(hd), c * 64:(c + 1) * 64]

                def XTap(level, buf, hd, c):
                    if level == 0:
                        return GT[hs(hd), c * 64:(c + 1) * 64]
                    return XTs[buf][hs(hd), c * 64:(c + 1) * 64]
                for r in range(1, 7):
                    src_b, dst_b = (r - 1) % 2, r % 2
                    xk_src = [kGm] if r == 1 else [("X", src_b)]
                    xtk_src = ["GT"] if r == 1 else [("XT", src_b)]
                    do_x, do_xt, do_p = (r <= 4), (r <= 5), (r >= 2)
                    psrc, pdst = (r - 2) % 2, (r - 1) % 2
                    it = 0
                    for c in range(nch):
                        for hd in range(2):
                            it += 1
                            last = (it == n_items)
                            if do_x:
                                B.op("pe", lambda e: e.matmul(PS(5)[hs(hd), c * 64:(c + 1) * 64], lhsT=XTap(r - 1, src_b, hd, c), rhs=Xap(r - 1, src_b, hd, c), start=True, stop=True),
                                     reads=xk_src + xtk_src, writes=psk(5), inc=False)
                            if do_xt:
                                B.op("pe", lambda e: e.matmul(PS(6)[hs(hd), c * 64:(c + 1) * 64], lhsT=Xap(r - 1, src_b, hd, c), rhs=XTap(r - 1, src_b, hd, c), start=True, stop=True),
                                     reads=xk_src + xtk_src, writes=psk(6), inc=(last and not do_p))
                            if do_p:
                                B.op("pe", lambda e: e.matmul(PS(7)[hs(hd), c * 64:(c + 1) * 64], lhsT=XTap(r - 1, src_b, hd, c), rhs=Ps_[psrc][hs(hd), c * 64:(c + 1) * 64], start=True, stop=True),
                                     reads=xtk_src + [("P", psrc)], writes=psk(7), inc=last)
                            if it % 4 == 0 and not last:
                                yield
                    if do_x:
                        B.op("act", lambda e: e.activation(out=Xs[dst_b], in_=PS(5)[:, 0:SA], func=AF.Copy), reads=psk(5), writes=[("X", dst_b)])
                    if do_xt:
                        B.op("act", lambda e: e.activation(out=XTs[dst_b], in_=PS(6)[:, 0:SA], func=AF.Copy),
                             reads=psk(6), writes=[("XT", dst_b)])
                    if do_p:
                        B.op("dve", lambda e: e.tensor_tensor(out=(Tm if r == 6 else Ps_[pdst]), in0=PS(7)[:, 0:SA], in1=Ps_[psrc], op=ALU.add), reads=psk(7) + [("P", psrc)], writes=[(tkey if r == 6 else ("P", pdst))])
                    yield
                for (srcf, dstb, dkey, rk_, bank) in ((lambda c, hd: v_[hs(hd), c * 64:(c + 1) * 64], VT, kVT, [sk(2)], 4),
                                                       (lambda c, hd: HK4[hs(hd), c, 0, :], BhT, kBhT, [kHK], 5),
                                                       (lambda c, hd: HK4[hs(hd), c, 1, :], KhT, kKhT, [kHK], 6)):
                    pb = PS(bank).bitcast(BF16)
                    it = 0
                    for c in range(nch):
                        for hd in range(2):
                            it += 1
                            B.op("pe", lambda e: e.transpose(pb[hs(hd), c * 64:(c + 1) * 64], srcf(c, hd), cb("ident")[hs(hd), hd * 64:hd * 64 + 64]),
                                 reads=rk_ + ["cbf"], writes=psk(bank), inc=(it == n_items))
                    B.op("act", lambda e: e.activation(out=dstb, in_=pb[:, 0:SA], func=AF.Copy), reads=psk(bank), writes=[dkey])
                    yield
                it = 0
                for c in range(nch):
                    for hd in range(2):
                        it += 1
                        B.op("pe", lambda e: e.matmul(PS(7)[hs(hd), c * 64:(c + 1) * 64], lhsT=Gm[hs(hd), c * 256 + 128:c * 256 + 192], rhs=VT[hs(hd), c * 64:(c + 1) * 64], start=True, stop=True),
                             reads=[kGm, kVT], writes=psk(7), inc=(it == n_items))
                B.op("dve", lambda e: e.tensor_copy(out=Q1, in_=PS(7)[:, 0:SA]), reads=psk(7), writes=[kQ1])
                yield

            def chain_gen(ii):
                d, sg, hp = iters[ii]
                S_ = sets[ii % 2]
                BK4, HK4 = S_["BK4"], S_["HK4"]
                kBK, kHK = S_["kBK"], S_["kHK"]
                AR4, kAR = AR4s[ii % 3], kARs[ii % 3]
                gi = ii % 3
                Gm, kGm = Gms[ii % 2], kGms[ii % 2]
                Gm3 = Gm.rearrange("p (c x) -> p c x", x=256)
                VT, BhT, KhT = VTs[ii % 2], BhTs[ii % 2], KhTs[ii % 2]
                kVT, kBhT, kKhT = kVTs[ii % 2], kBhTs[ii % 2], kKhTs[ii % 2]
                Q1, kQ1 = Q1s[ii % 2], kQ1s[ii % 2]
                Tm, tkey = Tms[ii % 2], kTms[ii % 2]
                mname = "mfwd" if d == 0 else "mbwd"
                mTname = "mfwdT" if d == 0 else "mbwdT"
                ssl = slice(sg * SA, (sg + 1) * SA)
                v_ = vB[hp][:, ssl]
                n_items = 2 * nch
                if sg == (0 if d == 0 else NS - 1):
                    cur[hp] = 0
                    B.op("dve", lambda e: e.memset(STf[hp][0], 0.0), writes=[("STf", hp, 0)])
                    B.op("dve", lambda e: e.memset(STb[hp][0], 0.0), writes=[("STb", hp, 0)])
                corder = range(nch) if d == 0 else range(nch - 1, -1, -1)
                for ci_, c in enumerate(corder):
                    cu = cur[hp]
                    nx = 1 - cu
                    cs_ = slice(c * 64, (c + 1) * 64)
                    ub = ci_ % 2
                    for hd in range(2):
                        B.op("pe", lambda e: e.matmul(PS(0)[hs(hd), 0:64], lhsT=AR4[hs(hd), c, 0, :], rhs=STb[hp][cu][hs(hd), :], start=True, stop=True),
                             reads=[kAR, ("STb", hp, cu)], writes=psk(0), inc=(hd == 1))
                    yield
                    B.op("dve", lambda e: e.tensor_tensor(out=Qb, in0=PS(0)[:, 0:64], in1=Q1[:, cs_], op=ALU.add), reads=psk(0) + [kQ1], writes=["Qb"])
                    for hd in range(2):
                        B.op("pe", lambda e: e.matmul(PS(0)[hs(hd), 64:128], lhsT=Tm[hs(hd), cs_], rhs=Qb[hs(hd), :], start=True, stop=True),
                             reads=[tkey, "Qb"], writes=psk(0), inc=(hd == 1))
                    yield
                    B.op("act", lambda e: e.activation(out=UTb[ub], in_=PS(0)[:, 64:128], func=AF.Copy), reads=psk(0), writes=[("UTb", ub)])
                    for hd in range(2):
                        B.op("pe", lambda e: e.matmul(PS(0)[hs(hd), 128:192], lhsT=BhT[hs(hd), cs_], rhs=UTb[ub][hs(hd), :], start=True, stop=False),
                             reads=[kBhT, ("UTb", ub)], writes=psk(0), inc=False)
                        B.op("pe", lambda e: e.matmul(PS(0)[hs(hd), 128:192], lhsT=KhT[hs(hd), cs_], rhs=VT[hs(hd), cs_], start=False, stop=True),
                             reads=[kKhT, kVT], writes=psk(0), inc=(hd == 1))
                    yield
                    gcol = gam[:, gi, c:c + 1]
                    B.op("dve", lambda e: e.scalar_tensor_tensor(out=STf[hp][nx], in0=STf[hp][cu], scalar=gcol, in1=PS(0)[:, 128:192], op0=ALU.mult, op1=ALU.add),
                         reads=[("STf", hp, cu), ("gam", gi)] + psk(0), writes=[("STf", hp, nx)])
                    B.op("act", lambda e: e.activation(out=STb[hp][nx], in_=STf[hp][nx], func=AF.Copy), reads=[("STf", hp, nx)], writes=[("STb", hp, nx)])
                    for hd in range(2):
                        B.op("pe", lambda e: e.matmul(PS(1)[hs(hd), cs_], lhsT=STb[hp][cu][hs(hd), :], rhs=AR4[hs(hd), c, 1, :], start=True, stop=False),
                             reads=[("STb", hp, cu), kAR], writes=psk(1), inc=False)
                        B.op("pe", lambda e: e.matmul(PS(1)[hs(hd), cs_], lhsT=UTb[ub][hs(hd), :], rhs=Gm[hs(hd), c * 256 + 64:c * 256 + 128], start=False, stop=False),
                             reads=[("UTb", ub), kGm], writes=psk(1), inc=False)
                        B.op("pe", lambda e: e.matmul(PS(1)[hs(hd), cs_], lhsT=VT[hs(hd), cs_], rhs=Gm[hs(hd), c * 256 + 192:c * 256 + 256], start=False, stop=True),
                             reads=[kVT, kGm], writes=psk(1), inc=(hd == 1))
                    cur[hp] = nx
                    yield
                if d == 0:
                    B.op("act", lambda e: e.activation(out=yfB[hp][:, ssl], in_=PS(1)[:, 0:SA], func=AF.Copy), reads=psk(1), writes=[("yf", hp)])
                else:
                    B.op("dve", lambda e: e.tensor_tensor(out=fin[0], in0=PS(1)[:, 0:SA], in1=yfB[hp][:, ssl], op=ALU.add), reads=psk(1) + [("yf", hp)], writes=[fink[0]])
                    B.op("pe", lambda e: e.matmul(PS(0)[:, 0:SA], lhsT=blkf, rhs=fin[0], start=True, stop=True), reads=["blkf", fink[0]], writes=psk(0))
                    B.op("dve", lambda e: e.scalar_tensor_tensor(out=fin[1], in0=PS(0)[:, 0:SA], scalar=-1.0 / 64, in1=fin[0], op0=ALU.mult, op1=ALU.add), reads=psk(0) + [fink[0]], writes=[fink[1]])
                    B.op("act", lambda e: e.activation(out=fin[2], in_=fin[1], func=AF.Square), reads=[fink[1]], writes=[fink[2]])
                    B.op("pe", lambda e: e.matmul(PS(0)[:, 0:SA], lhsT=blkf, rhs=fin[2], start=True, stop=True), reads=["blkf", fink[2]], writes=psk(0))
                    B.op("act", lambda e: e.activation(out=fin[3], in_=PS(0)[:, 0:SA], func=AF.Ln, scale=1.0 / 64, bias=epsb[:, 2:3]), reads=psk(0) + ["epsb"], writes=[fink[3]])
                    B.op("act", lambda e: e.activation(out=fin[3], in_=fin[3], func=AF.Exp, scale=-0.5), reads=[fink[3]], writes=[fink[3]])
                    B.op("dve", lambda e: e.tensor_tensor(out=fin[1], in0=fin[1], in1=fin[3], op=ALU.mult), reads=[fink[1], fink[3]], writes=[fink[1]])
                    B.op("dve", lambda e: e.tensor_scalar(out=fin[1], in0=fin[1], scalar1=pvc("ln_g", l, hp), scalar2=pvc("ln_b", l, hp), op0=ALU.mult, op1=ALU.add), reads=[fink[1], "pvt"], writes=[fink[1]])
                    B.op("dve", lambda e: e.tensor_tensor(out=fin[2], in0=bcB[hp][:, ssl], in1=v_, op=ALU.mult), reads=[("bc", hp), sk(2)], writes=[fink[2]])
                    B.op("dve", lambda e: e.tensor_tensor(out=fin[1], in0=fin[1], in1=fin[2], op=ALU.add), reads=[fink[1], fink[2]], writes=[fink[1]])
                    B.op("dve", lambda e: e.tensor_tensor(out=obuf[:, hp, ssl], in0=fin[1], in1=obuf[:, hp, ssl], op=ALU.mult), reads=[fink[1], "obuf"], writes=["obuf"])


                yield

            def run_all():
                n = len(iters)
                for _ in pre_gen(0):
                    pass
                if n > 1:
                    gens0 = [pre_gen(1), inv_gen(0)]
                else:
                    gens0 = [inv_gen(0)]
                _rr(gens0)
                for s_ in range(n):
                    gens = [chain_gen(s_)]
                    if s_ + 1 < n:
                        gens.append(inv_gen(s_ + 1))
                    if s_ + 2 < n:
                        gens.append(pre_gen(s_ + 2))
                    _rr(gens)

            def _rr(gens):
                import os
                gens = list(gens)
                mode = os.environ.get("A_MODE", "cip")
                if os.environ.get("A_SEQ"):
                    mode = ""
                names = {"chain_gen": "c", "inv_gen": "i", "pre_gen": "p"}
                conc = [g for g in gens if names[g.gi_code.co_name] in mode]
                seq = [g for g in gens if names[g.gi_code.co_name] not in mode]
                while conc:
                    for g in list(conc):
                        try:
                            next(g)
                        except StopIteration:
                            conc.remove(g)
                for g in seq:
                    for _ in g:
                        pass
            run_all()
            B.barrier()
            wepoch[0] += 1

        def program():
            B.dma(scr[0][:, 0:CST_W], cst_d[:, :], writes=[sk(0)], q="pool", semkey="cst")
            B.dma(pvt[:], pv_d[:, :], writes=["pvt"], q="pool", semkey="pvt")
            B.op("dve", lambda e: e.tensor_copy(out=cbf[:], in_=scr[0][:, 0:CST_W]), reads=[sk(0)], writes=["cbf"])
            B.op("dve", lambda e: e.tensor_copy(out=swapf[:], in_=scr[0][:, CST_OFF["swap"]:CST_OFF["swap"] + 128]), reads=[sk(0)], writes=["swapf"])
            B.op("dve", lambda e: e.memset(epsb[:, 0:1], NORM_EPS), writes=["epsb"])
            B.op("dve", lambda e: e.memset(epsb[:, 1:2], 1.0), writes=["epsb"])
            B.op("dve", lambda e: e.memset(epsb[:, 2:3], GN_EPS), writes=["epsb"])
            B.op("dve", lambda e: e.memset(epsb[:, 3:4], 1e-18), writes=["epsb"])
            for l in range(L):
                for i in range(7):
                    B.op("dve", lambda e, l=l, i=i: e.tensor_tensor(out=dvc("c0", l, i), in0=pvc("sh0", l, i), in1=pvc("sh1", l, i), op=ALU.add),
                         reads=["pvt"], writes=["dvt"])
                    B.op("dve", lambda e, l=l, i=i: e.tensor_scalar(out=dvc("c0", l, i), in0=dvc("c0", l, i), scalar1=-1.0, scalar2=1.0, op0=ALU.mult, op1=ALU.add),
                         reads=["dvt"], writes=["dvt"])
                for i in range(2):
                    B.op("dve", lambda e, l=l, i=i: e.tensor_scalar(out=dvc("omka", l, i), in0=pvc("k_a", l, i), scalar1=-1.0, scalar2=1.0, op0=ALU.mult, op1=ALU.add),
                         reads=["pvt"], writes=["dvt"])
                    B.op("act", lambda e, l=l, i=i: e.activation(out=dvc("esinksw", l, i), in_=pvc("sinksw", l, i), func=AF.Exp),
                         reads=["pvt"], writes=["dvt"])
                for i in range(4):
                    B.op("dve", lambda e, l=l, i=i: e.tensor_scalar(out=dvc("w0h", l, i), in0=pvc("w0", l, i), scalar1=0.5, scalar2=None, op0=ALU.mult),
                         reads=["pvt"], writes=["dvt"])
                    B.op("dve", lambda e, l=l, i=i: e.tensor_scalar(out=dvc("a0h", l, i), in0=pvc("a0", l, i), scalar1=0.5, scalar2=None, op0=ALU.mult),
                         reads=["pvt"], writes=["dvt"])
                    B.op("act", lambda e, l=l, i=i: e.activation(out=dvc("cneg", l, i), in_=pvc("lam", l, i), func=AF.Exp, scale=-1.0),
                         reads=["pvt"], writes=["dvt"])
                    B.op("act", lambda e, l=l, i=i: e.activation(out=dvc("cneg", l, i), in_=dvc("cneg", l, i), func=AF.Ln, bias=epsb[:, 1:2]),
                         reads=["dvt", "epsb"], writes=["dvt"])
                    B.op("dve", lambda e, l=l, i=i: e.tensor_scalar(out=dvc("cneg", l, i), in0=dvc("cneg", l, i), scalar1=-8.0, scalar2=None, op0=ALU.mult),
                         reads=["dvt"], writes=["dvt"])


            for s in range(NSEQ):
                for c in range(8):
                    B.dma(xT[:, c, :], x_d[s * 1024 + c * 128:s * 1024 + (c + 1) * 128, :], writes=[("xT", c)], q="pool", semkey=f"xin{c}")
                for l in range(L):
                    rmsnorm_to(l, "h")
                    B.dma(scr[7][:, 0:1024], gw_d[:, l * 1024:(l + 1) * 1024], writes=[sk(7)], q="pool", semkey="gw")
                    B.op("pool", lambda e: e.tensor_copy(out=gw_b[:], in_=scr[7][:, 0:1024]), reads=[sk(7)], writes=["gw_b"])
                    B.dma(upw_f[:], upw_d[:, l * 512:(l + 1) * 512], writes=["upw_f"], q="pool", semkey="upw")
                    B.op("pool", lambda e: e.tensor_copy(out=upw_b[:], in_=upw_f[:]), reads=["upw_f"], writes=["upw_b"])
                    if "C" in cfg.MIX:
                        mixer_C(l)
                        B.barrier()
                        outproj(l, 2)
                    if "D" in cfg.MIX:
                        mixer_D(l)
                        B.barrier()
                        outproj(l, 3)
                    if "B" in cfg.MIX:
                        mixer_B(l)
                        B.barrier()
                        outproj(l, 1)
                    if "A" in cfg.MIX:
                        mixer_A(l)
                        B.barrier()
                        outproj(l, 0)
                rmsnorm_to(None, "x")
                for c in range(8):
                    B.dma(y_d[s * 1024 + c * 128:s * 1024 + (c + 1) * 128, :], xT[:, c, :], reads=[("xT", c)], q="pool", semkey=f"yout{c}")
            toks = []
            for c in range(8):
                toks.extend(B.readers.get(("xT", c), {}).values())
            for tname in tap_d:
                t = B.last_w.get(("tap", tname))
                if t is not None:
                    toks.append(t)
            B._wait("pool", toks)

        realB = B
        B = NullBuilder()
        wstate["rec"] = True
        program()
        B = realB
        wstate["rec"] = False
        wepoch[0] = 0
        wo_slot[0] = 0
        program()
        print(f"[build] instructions: {B.n_ins}")
    return nc


_NC_CACHE = {}


def run_trunk(cfg, xs, inp):
    T, L = cfg.T, cfg.DEPTH
    ncore = cfg.NCORES
    assert xs.shape[0] == ncore * cfg.NSEQ
    key = (cfg.T, cfg.NSEQ, cfg.DEPTH, cfg.MIX, cfg.taps)
    if key not in _NC_CACHE:
        _NC_CACHE[key] = build(cfg)
    nc = _NC_CACHE[key]
    cst, alibi, C, S = make_consts(T)
    pv, upw, gw = pack_params(inp, L)
    w_in = np.ascontiguousarray(np.asarray(inp["w_in"], np.float32)[:L].reshape(L * 1024, D_IN))
    w_out = np.ascontiguousarray(np.asarray(inp["w_out"], np.float32)[:L].reshape(L * 1024, 1024))
    in_maps = []
    for c in range(ncore):
        xc = xs[c * cfg.NSEQ:(c + 1) * cfg.NSEQ]
        xt = np.ascontiguousarray(xc.transpose(0, 2, 1)).reshape(cfg.NSEQ * 1024, T)
        in_maps.append({"x": xt, "w_in": w_in, "w_out": w_out, "pvec": pv, "upw": upw, "gw": gw, "cst": cst,
                        "alibi": alibi, "ropeC": C, "ropeS": S})
    res = run_bass_kernel_spmd(nc, in_maps, core_ids=list(range(ncore)))
    outs = []
    for c in range(ncore):
        yt = np.asarray(res.results[c]["y"]).reshape(cfg.NSEQ, 1024, T)
        outs.append(yt.transpose(0, 2, 1))
    return np.ascontiguousarray(np.concatenate(outs, 0)).astype(np.float32), res


def kernel(**inputs):
    inp = {k: np.asarray(v) for k, v in inputs.items()}
    xp, xs_ = inp["x_prompt"], inp["x_sample"]
    xs = np.concatenate([xp, xs_], 0).astype(np.float32)
    cfg = Cfg(T=xs.shape[1], NSEQ=xs.shape[0] // 8, DEPTH=inp["w_in"].shape[0], MIX="ABCD", NCORES=8)
    y, _ = run_trunk(cfg, xs, inp)
    return (y[:xp.shape[0]], y[xp.shape[0]:])
```

```python
import math
from contextlib import ExitStack
import numpy as np
import ml_dtypes
import concourse.bass as bass
import concourse.mybir as mybir
from concourse.bass_utils import run_bass_kernel_spmd

F32 = mybir.dt.float32
BF16 = mybir.dt.bfloat16
AF = mybir.ActivationFunctionType
ALU = mybir.AluOpType

D_MODEL = 1024
GRID_W = 64
D_IN = 3200
A_W, B_W, C_W, D_W = 1152, 768, 512, 768
OFF_A, OFF_B, OFF_C, OFF_D = 0, 1152, 1920, 2432
NORM_EPS = 1e-6
GN_EPS = 64e-5
LAMW = math.exp(-0.5)
SEM_LIMIT = 30000


class Cfg:
    def __init__(self, T=2048, NSEQ=5, DEPTH=4, MIX="ABCD", NCORES=8, taps=()):
        self.T, self.NSEQ, self.DEPTH, self.MIX, self.NCORES = T, NSEQ, DEPTH, MIX, NCORES
        self.taps = tuple(taps)


PV_FIELDS = [("norm_g", 8), ("sh0", 7), ("sh1", 7), ("w0", 4), ("a0", 4), ("k_k", 2), ("k_a", 2), ("r_k", 2),
             ("ln_g", 2), ("ln_b", 2), ("qn", 1), ("kn", 1), ("cw", 8), ("cb", 2), ("gb", 8), ("lam", 4),
             ("sink", 2), ("sinksw", 2)]
DV_FIELDS = [("c0", 7), ("omka", 2), ("cneg", 4), ("esinksw", 2), ("gbh", 8), ("w0h", 4), ("a0h", 4)]


def _offsets(fields):
    off, d = 0, {}
    for n, w in fields:
        d[n] = off
        off += w
    return d, off


PV_OFF, PV_W = _offsets(PV_FIELDS)
DV_OFF, DV_W = _offsets(DV_FIELDS)


def pv_col(L, name, l, i=0):
    if name == "final_g":
        return L * PV_W + i
    return l * PV_W + PV_OFF[name] + i


def dv_col(name, l, i=0):
    return l * DV_W + DV_OFF[name] + i


CST_FIELDS = [("ident", 128), ("blk64", 128), ("ones", 128), ("swap", 128), ("perm", 128), ("mfwd", 256),
              ("mbwd", 256), ("mfwdT", 64), ("mbwdT", 64), ("rst", 512)]
CST_OFF, CST_W = _offsets(CST_FIELDS)


def make_consts(T):
    c = np.zeros((128, CST_W), np.float32)
    alibi = np.zeros((128, 4 * 384), np.float32)
    p = np.arange(128)
    o = CST_OFF
    c[:, o["ident"]:o["ident"] + 128] = np.eye(128)
    c[:, o["blk64"]:o["blk64"] + 128] = (p[:, None] // 64 == p[None, :] // 64)
    c[:, o["ones"]:o["ones"] + 128] = 1.0
    c[:, o["swap"]:o["swap"] + 128] = (p[:, None] == (p[None, :] + 64) % 128)
    d = p % 64
    partner = np.where((d % 32) < 16, p + 16, p - 16)
    c[:, o["perm"]:o["perm"] + 128] = (p[:, None] == partner[None, :])
    s = (p % 64)[:, None]
    t = np.arange(64)[None, :]
    strict_f, incl_f = (s < t), (s <= t)
    strict_b, incl_b = (s > t), (s >= t)
    c[:, o["mfwd"]:o["mfwd"] + 256] = np.concatenate([strict_f, incl_f, strict_f, incl_f], 1)
    c[:, o["mbwd"]:o["mbwd"] + 256] = np.concatenate([strict_b, incl_b, strict_b, incl_b], 1)
    c[:, o["mfwdT"]:o["mfwdT"] + 64] = (t < s)
    c[:, o["mbwdT"]:o["mbwdT"] + 64] = (t > s)
    c[:, o["rst"]:o["rst"] + 512] = (np.arange(512)[None, :] % 64 != 0)
    cc = np.arange(384)[None, :]
    dist = np.abs(cc - 128 - p[:, None]).astype(np.float64)
    for h in range(4):
        slope = 2.0 ** (-8.0 * (h + 1) / 4)
        e = np.where(dist <= 128, np.exp(-slope * dist), 0.0)
        alibi[:, h * 384:(h + 1) * 384] = e
    row = (np.arange(T) // GRID_W).astype(np.float64)
    col = (np.arange(T) % GRID_W).astype(np.float64)
    inv = 10000.0 ** (-np.arange(0, 32, 2, dtype=np.float64) / 32)
    C = np.zeros((128, T), np.float32)
    S = np.zeros((128, T), np.float32)
    for pp in range(128):
        dd = pp % 64
        pos = row if dd < 32 else col
        f = inv[dd % 16]
        ang = pos * f
        C[pp] = np.cos(ang)
        S[pp] = (-np.sin(ang)) if (dd % 32) < 16 else np.sin(ang)
    return c, alibi, C, S


def pack_params(inp, L):
    pv = np.zeros((128, L * PV_W + 8), np.float32)

    def put(name, l, i, vec128):
        pv[:, pv_col(L, name, l, i)] = vec128

    for l in range(L):
        for i in range(8):
            put("norm_g", l, i, inp["norm_g"][l, i * 128:(i + 1) * 128])
        for i in range(7):
            put("sh0", l, i, inp["rwkv_shift"][l, 0, i * 128:(i + 1) * 128])
            put("sh1", l, i, inp["rwkv_shift"][l, 1, i * 128:(i + 1) * 128])
        for d in range(2):
            for hp in range(2):
                put("w0", l, d * 2 + hp, inp["rwkv_w0"][l, d, hp * 128:(hp + 1) * 128])
                put("a0", l, d * 2 + hp, inp["rwkv_a0"][l, d, hp * 128:(hp + 1) * 128])
        rk = inp["rwkv_r_k"][l].reshape(256)
        for hp in range(2):
            sl = slice(hp * 128, (hp + 1) * 128)
            put("k_k", l, hp, inp["rwkv_k_k"][l, sl])
            put("k_a", l, hp, inp["rwkv_k_a"][l, sl])
            put("r_k", l, hp, rk[sl])
            put("ln_g", l, hp, inp["rwkv_ln_g"][l, sl])
            put("ln_b", l, hp, inp["rwkv_ln_b"][l, sl])
        put("qn", l, 0, np.tile(inp["attn_q_norm"][l], 2))
        put("kn", l, 0, np.tile(inp["attn_k_norm"][l], 2))
        for j in range(4):
            for cc in range(2):
                put("cw", l, j * 2 + cc, inp["lru_conv_w"][l, j, cc * 128:(cc + 1) * 128])
        for cc in range(2):
            put("cb", l, cc, inp["lru_conv_b"][l, cc * 128:(cc + 1) * 128])
        for d in range(2):
            for k in range(2):
                for cc in range(2):
                    put("gb", l, (d * 2 + k) * 2 + cc, inp["lru_gate_b"][l, d, k, cc * 128:(cc + 1) * 128])
            for cc in range(2):
                put("lam", l, d * 2 + cc, inp["lru_lambda"][l, d, cc * 128:(cc + 1) * 128])
        sk = inp["swa_sink"][l]
        for cc in range(2):
            put("sink", l, cc, np.repeat(sk[2 * cc:2 * cc + 2], 64))
            put("sinksw", l, cc, np.repeat(sk[2 * cc:2 * cc + 2][::-1], 64))
    for i in range(8):
        pv[:, L * PV_W + i] = inp["final_g"][i * 128:(i + 1) * 128]
    upw = np.zeros((128, L * 2 * 256), np.float32)
    for l in range(L):
        for d in range(2):
            upw[0:64, (l * 2 + d) * 256:(l * 2 + d + 1) * 256] = inp["rwkv_w_up"][l, d]
            upw[64:128, (l * 2 + d) * 256:(l * 2 + d + 1) * 256] = inp["rwkv_a_up"][l, d]
    gw = np.zeros((128, L * 8 * 128), np.float32)
    for l in range(L):
        for d in range(2):
            for k in range(2):
                for cc in range(2):
                    base = (((l * 2 + d) * 2 + k) * 2 + cc) * 128
                    for b in range(2):
                        gw[b * 64:(b + 1) * 64, base + b * 64:base + (b + 1) * 64] = inp["lru_gate_w"][l, d, k, 2 * cc + b]
    return pv, upw, gw


class SemCounter:
    def __init__(self, bld, name, step):
        self.bld, self.name, self.step = bld, name, step
        self.n = 0
        self._new()

    def _new(self):
        self.sem = self.bld.es.enter_context(self.bld.nc.semaphore(f"{self.name}_{self.n}"))
        self.sname = f"{self.name}_{self.n}"
        self.n += 1
        self.val = 0

    def next(self):
        if self.val + self.step > SEM_LIMIT:
            self._new()
        self.val += self.step
        return (self.sname, self.sem, self.val)


class Builder:
    def __init__(self, nc, es):
        self.nc, self.es = nc, es
        self.eng = {"pe": nc.tensor, "act": nc.scalar, "dve": nc.vector, "pool": nc.gpsimd, "sp": nc.sync}
        self.cnt = {e: SemCounter(self, e, 1) for e in ("pe", "act", "dve", "pool")}
        self.dcnt = {}
        self.waited = {e: {} for e in self.eng}
        self.last_w = {}
        self.readers = {}
        self.pending = {e: ([], []) for e in self.eng}
        self.n_ins = 0

    def _deps(self, reads, writes):
        toks = []
        for k in reads:
            t = self.last_w.get(k)
            if t is not None:
                toks.append(t)
        for k in writes:
            t = self.last_w.get(k)
            if t is not None:
                toks.append(t)
            r = self.readers.get(k)
            if r:
                toks.extend(r.values())
        return toks

    def _wait(self, e, toks):
        w = self.waited[e]
        need = {}
        for (sn, sem, val) in toks:
            if w.get(sn, 0) < val and need.get(sn, (None, 0))[1] < val:
                need[sn] = (sem, val)
        for sn, (sem, val) in need.items():
            self.eng[e].wait_ge(sem, val)
            w[sn] = val

    def _commit(self, tok, reads, writes):
        for k in writes:
            self.last_w[k] = tok
            self.readers[k] = {}
        for k in reads:
            self.readers.setdefault(k, {})[tok[0]] = tok

    def op(self, e, fn, reads=(), writes=(), inc=True):
        self._wait(e, self._deps(reads, writes))
        ins = fn(self.eng[e])
        self.n_ins += 1
        pr, pw = self.pending[e]
        if not inc:
            pr.extend(reads)
            pw.extend(writes)
            return
        tok = self.cnt[e].next()
        ins.then_inc(tok[1], 1)
        self._commit(tok, list(reads) + pr, list(writes) + pw)
        self.pending[e] = ([], [])

    def dma(self, out, in_, reads=(), writes=(), q="sp", semkey=None):
        self._wait(q, self._deps(reads, writes))
        ins = self.eng[q].dma_start(out=out, in_=in_)
        self.n_ins += 1
        if semkey not in self.dcnt:
            self.dcnt[semkey] = SemCounter(self, "d" + semkey, 16)
        tok = self.dcnt[semkey].next()
        ins.then_inc(tok[1], 16)
        self._commit(tok, reads, writes)
        return tok

    def barrier(self):
        toks = []
        for c in list(self.cnt.values()) + list(self.dcnt.values()):
            if c.val > 0:
                toks.append((c.sname, c.sem, c.val))
        for e in self.eng:
            self._wait(e, toks)

    def final_wait(self, e, keys):
        toks = []
        for k in keys:
            t = self.last_w.get(k)
            if t is not None:
                toks.append(t)
        self._wait(e, toks)


class NullBuilder:
    def __init__(self):
        self.readers, self.last_w, self.n_ins = {}, {}, 0

    def op(self, *a, **k):
        pass

    def dma(self, *a, **k):
        pass

    def barrier(self):
        pass

    def _wait(self, *a, **k):
        pass


SCRW = 2048
NSCR = 8


def build(cfg):
    T, NSEQ, L = cfg.T, cfg.NSEQ, cfg.DEPTH
    NSEG, NT, NCH = T // 512, T // 128, T // 64
    assert T % 512 == 0 and T <= 2048
    nc = bass.Bass("TRN2", target_bir_lowering=False)

    def din(name, shape, dt=F32):
        return nc.dram_tensor(name, list(shape), dt, kind="ExternalInput").ap()

    x_d = din("x", [NSEQ * 1024, T])
    win_d = din("w_in", [L * 1024, D_IN])
    wout_d = din("w_out", [L * 1024, 1024])
    pv_d = din("pvec", [128, L * PV_W + 8])
    upw_d = din("upw", [128, L * 512])
    gw_d = din("gw", [128, L * 1024])
    cst_d = din("cst", [128, CST_W])
    alibi_d = din("alibi", [128, 1536])
    ropeC_d = din("ropeC", [128, T])
    ropeS_d = din("ropeS", [128, T])
    y_d = nc.dram_tensor("y", [NSEQ * 1024, T], F32, kind="ExternalOutput").ap()
    tap_d = {}
    for (tname, tshape) in cfg.taps:
        tap_d[tname] = nc.dram_tensor("tap_" + tname, list(tshape), F32, kind="ExternalOutput").ap()

    es = ExitStack()
    with es:
        B = Builder(nc, es)

        def sb(name, shape, dt):
            return es.enter_context(nc.sbuf_tensor(name, list(shape), dt))

        xT = sb("xT", [128, 8, T], F32)
        hT = sb("hT", [128, 8, max(T, 2048)], BF16)
        pvt = sb("pvt", [128, L * PV_W + 8], F32)
        dvt = sb("dvt", [128, L * DV_W], F32)
        CBW = CST_W
        cbf = sb("cbf", [128, CBW], BF16)
        swapf = sb("swapf", [128, 128], F32)
        upw_f = sb("upw_f", [128, 512], F32)
        upw_b = sb("upw_b", [128, 512], BF16)
        gw_b = sb("gw_b", [128, 1024], BF16)
        NWS = 3
        wst = [sb(f"wst{i}", [128, 8, 128], F32) for i in range(2)]
        wbf = [sb(f"wbf{i}", [128, 8, 128], BF16) for i in range(NWS)]
        wost = [sb(f"wost{i}", [128, 256], F32) for i in range(2)]
        wobf = [sb(f"wobf{i}", [128, 256], BF16) for i in range(2)]
        obuf = sb("obuf", [128, 2, T], BF16)
        sqb = sb("sqb", [128, 2, 512], BF16)
        rstd = sb("rstd", [128, 512], F32)
        epsb = sb("epsb", [128, 4], F32)
        tmpA = sb("tmpA", [128, 512], F32)
        tmpB = sb("tmpB", [128, 512], F32)
        gam = sb("gam", [128, 3, 8], F32)
        scr = [sb(f"scr{i}", [128, SCRW], F32) for i in range(NSCR)]
        psum = es.enter_context(nc.psum_tensor("psum", [128, 8 * 512], F32))

        def sk(i):
            return ("scr", i)

        def sbf(i):
            return scr[i][:].bitcast(BF16)

        def PS(b, n=1):
            return psum[:, b * 512:(b + n) * 512]

        def psk(b, n=1):
            return [("ps", b + i) for i in range(n)]

        def cb(name, w=None, off=0):
            o = CST_OFF[name] + off
            return cbf[:, o:o + (w if w is not None else dict(CST_FIELDS)[name])]

        def pvc(name, l, i=0):
            c = pv_col(L, name, l, i)
            return pvt[:, c:c + 1]

        def dvc(name, l, i=0):
            c = dv_col(name, l, i)
            return dvt[:, c:c + 1]

        def tap(name, ap_sb, rkeys, rows=128):
            if name in tap_d:
                B.dma(tap_d[name], ap_sb, reads=rkeys, writes=[("tap", name)], q="pool", semkey="tap" + name)

        wepoch = [0]
        wplan = []
        wstate = {"rec": True, "next": 0, "issued": 0}

        def _issue_w(i):
            l, ranges, ep = wplan[i]
            si, bi = i % 2, i % NWS
            off = 0
            src = win_d[l * 1024:(l + 1) * 1024, :].rearrange("(k p) c -> p k c", p=128)
            for (c0, n) in ranges:
                B.dma(wst[si][:, :, off:off + n], src[:, :, c0:c0 + n], writes=[f"wst{si}"], q="sp", semkey=f"wst{si}")
                off += n
            B.op("pool", lambda e: e.tensor_copy(out=wbf[bi][:, :, 0:off], in_=wst[si][:, :, 0:off]),
                 reads=[f"wst{si}"], writes=[f"wbf{bi}"])

        def load_win(l, ranges):
            m = sum(n for _, n in ranges)
            if wstate["rec"]:
                wplan.append((l, tuple(ranges), wepoch[0]))
                return 0, m
            i = wstate["next"]
            wstate["next"] += 1
            assert wplan[i][0] == l and wplan[i][1] == tuple(ranges), (i, wplan[i], l, ranges)
            hi = i
            while hi + 1 < len(wplan) and hi + 1 <= i + 2 and wplan[hi + 1][2] == wplan[i][2]:
                hi += 1
            while wstate["issued"] <= hi:
                _issue_w(wstate["issued"])
                wstate["issued"] += 1
            return i % NWS, m

        ipb = [0]

        def next_bank():
            b = ipb[0]
            ipb[0] = 4 - b
            return b

        def inproj(l, ranges, bank0):
            bi, m = load_win(l, ranges)
            for sg in range(NSEG):
                for k in range(8):
                    B.op("pe", lambda e, k=k, sg=sg: e.matmul(PS(bank0 + sg)[0:m, :], lhsT=wbf[bi][:, k, 0:m], rhs=hT[:, k, sg * 512:(sg + 1) * 512],
                                                              start=(k == 0), stop=(k == 7)),
                         reads=[f"wbf{bi}", "hT"], writes=psk(bank0 + sg), inc=(k == 7))
            return m

        def inproj_tok(l, ranges, dst_fn, post):
            bi, m = load_win(l, ranges)
            assert m == 64
            tb0 = next_bank()
            for tb in range((NT + 7) // 8):
                bank = tb0 + tb % 2
                ntt = min(8, NT - tb * 8)
                for q in range(ntt):
                    tt = tb * 8 + q
                    for k in range(8):
                        B.op("pe", lambda e, k=k, tt=tt, q=q, bank=bank: e.matmul(PS(bank)[:, q * 64:(q + 1) * 64], lhsT=hT[:, k, tt * 128:(tt + 1) * 128],
                                                                                  rhs=wbf[bi][:, k, 0:64], start=(k == 0), stop=(k == 7)),
                             reads=[f"wbf{bi}", "hT"], writes=psk(bank), inc=(k == 7))
                post(tb, bank, ntt)

        wo_slot = [0]

        def outproj(l, g):
            for co in range(8):
                si = wo_slot[0] % 2
                wo_slot[0] += 1
                for kc in range(2):
                    r0 = l * 1024 + g * 256 + kc * 128
                    B.dma(wost[si][:, kc * 128:(kc + 1) * 128], wout_d[r0:r0 + 128, co * 128:(co + 1) * 128],
                          writes=[f"wost{si}"], q="sp", semkey=f"wost{si}")
                B.op("pool", lambda e, si=si: e.tensor_copy(out=wobf[si][:], in_=wost[si][:]),
                     reads=[f"wost{si}"], writes=[f"wobf{si}"])
                nb = min(2, NSEG)
                for half in range(NSEG // nb):
                    pb = 4 + 2 * ((co * (NSEG // nb) + half) % 2)
                    for j in range(nb):
                        sg = half * nb + j
                        for kc in range(2):
                            B.op("pe", lambda e, si=si, kc=kc, sg=sg, j=j, pb=pb: e.matmul(
                                PS(pb + j), lhsT=wobf[si][:, kc * 128:(kc + 1) * 128], rhs=obuf[:, kc, sg * 512:(sg + 1) * 512],
                                start=(kc == 0), stop=(kc == 1)),
                                reads=[f"wobf{si}", "obuf"], writes=psk(pb + j), inc=(kc == 1))
                    t0 = half * nb * 512
                    w = nb * 512
                    B.op("dve", lambda e, co=co, pb=pb, t0=t0, w=w, nb=nb: e.tensor_tensor(
                        out=xT[:, co, t0:t0 + w], in0=PS(pb, nb), in1=xT[:, co, t0:t0 + w], op=ALU.add),
                        reads=psk(pb, nb) + [("xT", co)], writes=[("xT", co)])

        def rmsnorm_to(l, dest_kind):
            for sg in range(NSEG):
                sl = slice(sg * 512, (sg + 1) * 512)
                for c in range(8):
                    B.op("act", lambda e, c=c: e.activation(out=sqb[:, c % 2, :], in_=xT[:, c, sl], func=AF.Square),
                         reads=[("xT", c)], writes=[("sqb", c % 2)])
                    B.op("pe", lambda e, c=c: e.matmul(PS(0), lhsT=cb("ones"), rhs=sqb[:, c % 2, :], start=(c == 0), stop=(c == 7)),
                         reads=[("sqb", c % 2), "cbf"], writes=psk(0), inc=True)
                B.op("act", lambda e: e.activation(out=rstd[:], in_=PS(0), func=AF.Ln, scale=1.0 / 1024, bias=epsb[:, 0:1]),
                     reads=psk(0) + ["epsb"], writes=["rstd"])
                B.op("act", lambda e: e.activation(out=rstd[:], in_=rstd[:], func=AF.Exp, scale=-0.5), reads=["rstd"], writes=["rstd"])
                for c in range(8):
                    if dest_kind == "h":
                        gcol = pvc("norm_g", l, c)
                        B.op("dve", lambda e, c=c, gcol=gcol: e.scalar_tensor_tensor(out=hT[:, c, sl], in0=xT[:, c, sl], scalar=gcol, in1=rstd[:],
                                                                                      op0=ALU.mult, op1=ALU.mult),
                             reads=[("xT", c), "rstd", "pvt"], writes=["hT"])
                    else:
                        gcol = pvt[:, L * PV_W + c:L * PV_W + c + 1]
                        B.op("dve", lambda e, c=c, gcol=gcol: e.scalar_tensor_tensor(out=xT[:, c, sl], in0=xT[:, c, sl], scalar=gcol, in1=rstd[:],
                                                                                      op0=ALU.mult, op1=ALU.mult),
                             reads=[("xT", c), "rstd", "pvt"], writes=[("xT", c)])

        def mixer_C(l):
            for cc in range(2):
                zb = next_bank()
                inproj(l, [(OFF_C + cc * 128, 128)], zb)
                Z = PS(zb, NSEG)
                zk = psk(zb, NSEG)
                xc = scr[0][:, 0:T]
                B.op("act", lambda e: e.activation(out=xc, in_=Z, func=AF.Identity, scale=pvc("cw", l, 2 * 2 + cc), bias=pvc("cb", l, cc)),
                     reads=zk + ["pvt"], writes=[sk(0)])
                for (j, sh) in ((0, -2), (1, -1), (3, 1)):
                    if sh < 0:
                        o_ap, i_ap = xc[:, -sh:T], Z[:, 0:T + sh]
                    else:
                        o_ap, i_ap = xc[:, 0:T - sh], Z[:, sh:T]
                    B.op("dve", lambda e, o_ap=o_ap, i_ap=i_ap, j=j: e.scalar_tensor_tensor(out=o_ap, in0=i_ap, scalar=pvc("cw", l, j * 2 + cc), in1=o_ap,
                                                                                            op0=ALU.mult, op1=ALU.add),
                         reads=zk + [sk(0), "pvt"], writes=[sk(0)])
                xcb = sbf(1)[:, 0:T]
                B.op("act", lambda e: e.activation(out=xcb, in_=xc, func=AF.Copy), reads=[sk(0)], writes=[sk(1)])
                for d in range(2):
                    rb, ib, sbuf_, hb = scr[2][:, 0:T], scr[3][:, 0:T], scr[4][:, 0:T], scr[5 + d][:, 0:T]
                    for sg in range(NSEG):
                        sl = slice(sg * 512, (sg + 1) * 512)
                        for k in range(2):
                            bank = 4 + 2 * k + sg % 2
                            wcol = (((d * 2 + k) * 2 + cc)) * 128
                            B.op("pe", lambda e, bank=bank, wcol=wcol, sl=sl: e.matmul(PS(bank), lhsT=gw_b[:, wcol:wcol + 128], rhs=xcb[:, sl], start=True, stop=True),
                                 reads=["gw_b", sk(1)], writes=psk(bank))
                            dst = rb if k == 0 else ib
                            B.op("act", lambda e, bank=bank, dst=dst, sl=sl, k=k: e.activation(out=dst[:, sl], in_=PS(bank), func=AF.Sigmoid,
                                                                                               bias=pvc("gb", l, (d * 2 + k) * 2 + cc)),
                                 reads=psk(bank) + ["pvt"], writes=[sk(2 + k)])
                    B.op("act", lambda e: e.activation(out=rb, in_=rb, func=AF.Exp, scale=dvc("cneg", l, d * 2 + cc)), reads=[sk(2), "dvt"], writes=[sk(2)])
                    B.op("act", lambda e: e.activation(out=sbuf_, in_=rb, func=AF.Square), reads=[sk(2)], writes=[sk(4)])
                    B.op("act", lambda e: e.activation(out=sbuf_, in_=sbuf_, func=AF.Ln, scale=-1.0, bias=epsb[:, 1:2]), reads=[sk(4), "epsb"], writes=[sk(4)])
                    B.op("act", lambda e: e.activation(out=sbuf_, in_=sbuf_, func=AF.Exp, scale=0.5), reads=[sk(4)], writes=[sk(4)])
                    B.op("dve", lambda e: e.tensor_tensor(out=ib, in0=ib, in1=xc, op=ALU.mult), reads=[sk(3), sk(0)], writes=[sk(3)])
                    B.op("dve", lambda e: e.tensor_tensor(out=ib, in0=ib, in1=sbuf_, op=ALU.mult), reads=[sk(3), sk(4)], writes=[sk(3)])
                    if d == 0:
                        B.op("dve", lambda e: e.tensor_tensor_scan(out=hb, data0=rb, data1=ib, initial=0.0, op0=ALU.mult, op1=ALU.add),
                             reads=[sk(2), sk(3)], writes=[sk(5 + d)])
                    else:
                        B.op("dve", lambda e: e.tensor_tensor_scan(out=hb[:, ::-1], data0=rb[:, ::-1], data1=ib[:, ::-1], initial=0.0, op0=ALU.mult, op1=ALU.add),
                             reads=[sk(2), sk(3)], writes=[sk(5 + d)])
                hf, hbk = scr[5][:, 0:T], scr[6][:, 0:T]
                B.op("dve", lambda e: e.tensor_tensor(out=hf, in0=hf, in1=hbk, op=ALU.add), reads=[sk(5), sk(6)], writes=[sk(5)])
                gb_ = next_bank()
                inproj(l, [(OFF_C + 256 + cc * 128, 128)], gb_)
                sgt = scr[2][:, 0:T]
                B.op("act", lambda e: e.activation(out=sgt, in_=PS(gb_, NSEG), func=AF.Silu), reads=psk(gb_, NSEG), writes=[sk(2)])
                B.op("dve", lambda e: e.tensor_tensor(out=obuf[:, cc, :], in0=hf, in1=sgt, op=ALU.mult), reads=[sk(5), sk(2)], writes=["obuf"])

        def kT_bufs():
            return [sbf(0)[:, SCRW:SCRW + T], sbf(5)[:, SCRW:SCRW + T]]

        def load_k_all(l, off_k, prep):
            kTs = kT_bufs()
            prep([(off_k, 128)], None, False)
            B.dma(kTs[0][64:128, :], kTs[0][0:64, :], reads=[("kT", 0)], writes=[("kT", 0)], q="sp", semkey="kdup0")
            B.dma(kTs[1][0:64, :], kTs[1][64:128, :], reads=[("kT", 1)], writes=[("kT", 1)], q="sp", semkey="kdup1")
            return kTs

        def load_q(l, off_q, cc, prep):
            qT = sbf(0)[:, 0:T]
            prep([(off_q + cc * 128, 128)], qT, True)
            return qT

        def load_vaug(l, off_v, cc):
            va = sbf(1)
            vav = va[:, 0:NT * 192].rearrange("p (t c) -> p t c", c=192)
            B.op("pool", lambda e: e.memset(vav[:, :, 64:128], 1.0), writes=[sk(1)])

            def post(tb, bank, ntt):
                src = PS(bank)[:, 0:ntt * 64].rearrange("p (t c) -> p t c", c=64)
                B.op("act", lambda e: e.activation(out=vav[:, tb * 8:tb * 8 + ntt, 0:64], in_=src, func=AF.Copy), reads=psk(bank), writes=[sk(1)])
                B.op("dve", lambda e: e.tensor_copy(out=vav[:, tb * 8:tb * 8 + ntt, 128:192], in_=src), reads=psk(bank), writes=[sk(1)])
            inproj_tok(l, [(off_v + cc * 64, 64)], None, post)
            return vav

        def load_gate(l, off_g, cc):
            gb_ = next_bank()
            inproj(l, [(off_g + cc * 128, 128)], gb_)
            sgt = scr[2][:, 0:T]
            B.op("act", lambda e: e.activation(out=sgt, in_=PS(gb_, NSEG), func=AF.Silu), reads=psk(gb_, NSEG), writes=[sk(2)])
            return sgt

        def attn_post(l, cc, qg, bx, by, sgt, sink):
            X, Y = PS(bx), PS(by)
            gsl = slice(qg * 512, (qg + 1) * 512)
            if sink:
                B.op("dve", lambda e: e.tensor_scalar(out=tmpA[0:64, :], in0=Y[0:64, :], scalar1=dvc("esinksw", l, cc)[0:64, :], scalar2=None, op0=ALU.add),
                     reads=psk(by) + ["dvt"], writes=["tmpA"])
                B.op("dve", lambda e: e.tensor_scalar(out=tmpA[64:128, :], in0=X[64:128, :], scalar1=dvc("esinksw", l, cc)[64:128, :], scalar2=None, op0=ALU.add),
                     reads=psk(bx) + ["dvt"], writes=["tmpA"])
                B.op("act", lambda e: e.activation(out=tmpA[:], in_=tmpA[:], func=AF.Ln), reads=["tmpA"], writes=["tmpA"])
            else:
                B.op("act", lambda e: e.activation(out=tmpA[0:64, :], in_=Y[0:64, :], func=AF.Ln), reads=psk(by), writes=["tmpA"])
                B.op("act", lambda e: e.activation(out=tmpA[64:128, :], in_=X[64:128, :], func=AF.Ln), reads=psk(bx), writes=["tmpA"])
            B.op("act", lambda e: e.activation(out=tmpA[:], in_=tmpA[:], func=AF.Exp, scale=-1.0), reads=["tmpA"], writes=["tmpA"])
            B.op("pe", lambda e: e.matmul(PS(0), lhsT=swapf[:], rhs=tmpA[:], start=True, stop=True), reads=["swapf", "tmpA"], writes=psk(0))
            B.op("dve", lambda e: e.tensor_tensor(out=tmpB[:], in0=PS(0), in1=sgt[:, gsl], op=ALU.mult), reads=psk(0) + [sk(2)], writes=["tmpB"])
            B.op("dve", lambda e: e.tensor_tensor(out=obuf[0:64, cc, gsl], in0=X[0:64, :], in1=tmpB[0:64, :], op=ALU.mult),
                 reads=psk(bx) + ["tmpB"], writes=["obuf"])
            B.op("dve", lambda e: e.tensor_tensor(out=obuf[64:128, cc, gsl], in0=Y[64:128, :], in1=tmpB[64:128, :], op=ALU.mult),
                 reads=psk(by) + ["tmpB"], writes=["obuf"])

        def mixer_D(l):
            B.dma(scr[3][:, 0:1536], alibi_d[:, :], writes=[sk(3)], q="pool", semkey="alibi")

            def prep(ranges, dst, isq):
                pb_ = next_bank()
                inproj(l, ranges, pb_)
                if isq:
                    B.op("act", lambda e: e.activation(out=dst, in_=PS(pb_, NSEG), func=AF.Copy), reads=psk(pb_, NSEG), writes=[sk(0)])
                else:
                    kTs_ = kT_bufs()
                    B.op("act", lambda e: e.activation(out=kTs_[0][0:64, :], in_=PS(pb_, NSEG)[0:64, :], func=AF.Copy), reads=psk(pb_, NSEG), writes=[("kT", 0)])
                    B.op("act", lambda e: e.activation(out=kTs_[1][64:128, :], in_=PS(pb_, NSEG)[64:128, :], func=AF.Copy), reads=psk(pb_, NSEG), writes=[("kT", 1)])
            kTs = load_k_all(l, OFF_D + 256, prep)
            for cc in range(2):
                qT = load_q(l, OFF_D, cc, prep)
                kT = kTs[cc]
                kkey = ("kT", cc)
                vav = load_vaug(l, OFF_D + 384, cc)
                sgt = load_gate(l, OFF_D + 512, cc)
                ptv = sbf(4)
                exv = sbf(5)
                atab = scr[3][:, 2 * cc * 384:(2 * cc + 2) * 384].rearrange("p (h c) -> p h c", h=2)

                def qrange(j):
                    return max(j - 1, 0), min(j + 1, NT - 1)

                def qk(j):
                    b0, b1 = qrange(j)
                    ncol = (b1 - b0 + 1) * 128
                    for hh in range(2):
                        ph = slice(hh * 64, hh * 64 + 64)
                        bank = 2 * (j % 2) + hh
                        B.op("pe", lambda e: e.matmul(PS(bank)[:, 0:ncol], lhsT=kT[ph, j * 128:(j + 1) * 128], rhs=qT[ph, b0 * 128:b0 * 128 + ncol], start=True, stop=True),
                             reads=[sk(0), kkey], writes=psk(bank), inc=(hh == 1))

                def ex(j):
                    b0, b1 = qrange(j)
                    ncol = (b1 - b0 + 1) * 128
                    tcol0 = (b0 - (j - 1)) * 128
                    src = PS(2 * (j % 2), 2).rearrange("p (b c) -> p b c", b=2)[:, :, 0:ncol]
                    exs = exv[:, (j % 2) * 768:(j % 2 + 1) * 768].rearrange("p (h c) -> p h c", h=2)[:, :, 0:ncol]
                    pts = ptv[:, (j % 4) * 768:(j % 4 + 1) * 768].rearrange("p (h c) -> p h c", h=2)[:, :, 0:ncol]
                    B.op("act", lambda e: e.activation(out=exs, in_=src, func=AF.Exp, scale=0.125), reads=psk(2 * (j % 2), 2), writes=[("ex", j % 2)])
                    B.op("dve", lambda e: e.tensor_tensor(out=pts, in0=exs, in1=atab[:, :, tcol0:tcol0 + ncol], op=ALU.mult),
                         reads=[("ex", j % 2), sk(3)], writes=[("pt", j % 4)])

                def pv_block(i):
                    qg = i // 4
                    js = [j for j in (i - 1, i, i + 1) if 0 <= j < NT]
                    for hh in range(2):
                        bank = 4 + 2 * (qg % 2) + hh
                        for n, j in enumerate(js):
                            b0, _ = qrange(j)
                            c0 = (j % 4) * 768 + hh * 384 + (i - b0) * 128
                            B.op("pe", lambda e: e.matmul(PS(bank)[:, (i % 4) * 128:(i % 4 + 1) * 128], lhsT=vav[:, j, hh * 64:hh * 64 + 128], rhs=ptv[:, c0:c0 + 128],
                                                         start=(n == 0), stop=(n == len(js) - 1)),
                                 reads=[sk(1), ("pt", j % 4)], writes=psk(bank), inc=(n == len(js) - 1))

                qk(0)
                for j in range(NT):
                    if j + 1 < NT:
                        qk(j + 1)
                    ex(j)
                    blocks = []
                    if j >= 1:
                        blocks.append(j - 1)
                    if j == NT - 1:
                        blocks.append(j)
                    for i in blocks:
                        pv_block(i)
                        if i % 4 == 3:
                            qg = i // 4
                            attn_post(l, cc, qg, 4 + 2 * (qg % 2), 4 + 2 * (qg % 2) + 1, sgt, True)

        def mixer_B(l):
            B.dma(scr[6][:, 0:T], ropeC_d[:, :], writes=[sk(6)], q="pool", semkey="ropeC")
            B.dma(scr[7][:, 0:T], ropeS_d[:, :], writes=[sk(7)], q="pool", semkey="ropeS")
            t1s = [scr[3][:, 0:512], scr[3][:, 1536:2048]]
            t2s = [scr[3][:, 512:1024], scr[5][:, 0:512]]
            qgs = [sbf(3)[:, 2048:2560], sbf(3)[:, 2560:3072]]
            rss = [rstd, tmpA]
            rsk = ["rstd", "tmpA"]

            def prep(ranges, dst, isq):
                inproj(l, ranges, 0)
                gcol = pvc("qn" if isq else "kn", l, 0)
                for sg in range(NSEG):
                    sl = slice(sg * 512, (sg + 1) * 512)
                    Z = PS(sg)
                    u = sg % 2
                    qgb, t1b, t2b, rs_, rk_ = qgs[u], t1s[u], t2s[u], rss[u], rsk[u]
                    b4, b5 = 4 + 2 * u, 5 + 2 * u
                    B.op("act", lambda e: e.activation(out=qgb, in_=Z, func=AF.Copy, scale=gcol), reads=psk(sg) + ["pvt"], writes=[("qgb", u)])
                    B.op("act", lambda e: e.activation(out=sqb[:, u, :], in_=Z, func=AF.Square), reads=psk(sg), writes=[("sqb", u)])
                    B.op("pe", lambda e: e.matmul(PS(b4), lhsT=cb("blk64"), rhs=sqb[:, u, :], start=True, stop=True), reads=[("sqb", u), "cbf"], writes=psk(b4))
                    B.op("pe", lambda e: e.matmul(PS(b5), lhsT=cb("perm"), rhs=qgb, start=True, stop=True), reads=[("qgb", u), "cbf"], writes=psk(b5))
                    B.op("act", lambda e: e.activation(out=rs_[:], in_=PS(b4), func=AF.Ln, scale=1.0 / 64, bias=epsb[:, 0:1]), reads=psk(b4) + ["epsb"], writes=[rk_])
                    B.op("act", lambda e: e.activation(out=rs_[:], in_=rs_[:], func=AF.Exp, scale=-0.5), reads=[rk_], writes=[rk_])
                    B.op("pool", lambda e: e.tensor_tensor(out=t1b, in0=qgb, in1=scr[6][:, sl], op=ALU.mult), reads=[("qgb", u), sk(6)], writes=[("t1b", u)])
                    B.op("dve", lambda e: e.tensor_tensor(out=t2b, in0=PS(b5), in1=scr[7][:, sl], op=ALU.mult), reads=psk(b5) + [sk(7)], writes=[("t2b", u)])
                    B.op("dve", lambda e: e.tensor_tensor(out=t1b, in0=t1b, in1=t2b, op=ALU.add), reads=[("t1b", u), ("t2b", u)], writes=[("t1b", u)])
                    if isq:
                        B.op("dve", lambda e: e.tensor_tensor(out=dst[:, sl], in0=t1b, in1=rs_[:], op=ALU.mult), reads=[("t1b", u), rk_], writes=[sk(0)])
                    else:
                        kTs_ = kT_bufs()
                        B.op("dve", lambda e: e.tensor_tensor(out=kTs_[0][0:64, sl], in0=t1b[0:64, :], in1=rs_[0:64, :], op=ALU.mult), reads=[("t1b", u), rk_], writes=[("kT", 0)])
                        B.op("dve", lambda e: e.tensor_tensor(out=kTs_[1][64:128, sl], in0=t1b[64:128, :], in1=rs_[64:128, :], op=ALU.mult), reads=[("t1b", u), rk_], writes=[("kT", 1)])
            kTs = load_k_all(l, OFF_B + 256, prep)
            for cc in range(2):
                qT = load_q(l, OFF_B, cc, prep)
                kT = kTs[cc]
                kkey = ("kT", cc)
                vav = load_vaug(l, OFF_B + 384, cc)
                sgt = load_gate(l, OFF_B + 512, cc)
                ptv = sbf(4)
                for qg in range(NSEG):
                    gsl = slice(qg * 512, (qg + 1) * 512)
                    ab = 4 + 2 * (qg % 2)

                    def qk(j):
                        for hh in range(2):
                            ph = slice(hh * 64, hh * 64 + 64)
                            bank = 2 * (j % 2) + hh
                            B.op("pe", lambda e: e.matmul(PS(bank), lhsT=kT[ph, j * 128:(j + 1) * 128], rhs=qT[ph, gsl], start=True, stop=True),
                                 reads=[sk(0), kkey], writes=psk(bank), inc=(hh == 1))

                    def ex(j):
                        slot = j % 3
                        B.op("act", lambda e: e.activation(out=ptv[:, slot * 1024:(slot + 1) * 1024], in_=PS(2 * (j % 2), 2), func=AF.Exp, scale=0.125),
                             reads=psk(2 * (j % 2), 2), writes=[("pt", slot)])

                    def pv(j):
                        slot = j % 3
                        for hh in range(2):
                            B.op("pe", lambda e: e.matmul(PS(ab + hh), lhsT=vav[:, j, hh * 64:hh * 64 + 128], rhs=ptv[:, slot * 1024 + hh * 512:slot * 1024 + (hh + 1) * 512],
                                                         start=(j == 0), stop=(j == NT - 1)),
                                 reads=[sk(1), ("pt", slot)], writes=psk(ab + hh), inc=(hh == 1))
                    qk(0)
                    for j in range(NT):
                        if j + 1 < NT:
                            qk(j + 1)
                        ex(j)
                        pv(j)
                    attn_post(l, cc, qg, ab, ab + 1, sgt, False)

        def mixer_A(l):
            SA = min(512, T)
            NS = T // SA
            nch = SA // 64
            arena_b = hT[:].rearrange("p k t -> p (k t)")

            def AF32(off, w):
                return arena_b[:, 2 * off:2 * (off + w)].bitcast(F32)

            def ABF(off, w):
                return arena_b[:, 2 * off:2 * off + w]

            rB = [sbf(0)[:, 0:T], sbf(0)[:, SCRW:SCRW + T]]
            kB = [sbf(1)[:, 0:T], sbf(1)[:, SCRW:SCRW + T]]
            vB = [sbf(2)[:, 0:T], sbf(2)[:, SCRW:SCRW + T]]
            waB = sbf(3)[:, 0:T]
            kkB = [sbf(4)[:, 0:T], sbf(4)[:, SCRW:SCRW + T]]
            yfB = [sbf(5)[:, 0:T], sbf(5)[:, SCRW:SCRW + T]]
            bcB = [sbf(6)[:, 0:T], sbf(6)[:, SCRW:SCRW + T]]
            for ci in range(7):
                pb0 = 4 * (ci % 2)
                xi = 7 if ci % 2 == 0 else 5
                Xt = scr[xi][:, 0:T]
                inproj(l, [(OFF_A + ci * 128, 128)], pb0)
                Z = PS(pb0, NSEG)
                zk = psk(pb0, NSEG)
                B.op("act", lambda e: e.activation(out=Xt, in_=Z, func=AF.Copy, scale=dvc("c0", l, ci)), reads=zk + ["dvt"], writes=[sk(xi)])
                B.op("dve", lambda e: e.scalar_tensor_tensor(out=Xt[:, 1:T], in0=Z[:, 0:T - 1], scalar=pvc("sh0", l, ci), in1=Xt[:, 1:T], op0=ALU.mult, op1=ALU.add),
                     reads=zk + [sk(xi), "pvt"], writes=[sk(xi)])
                if ci < 6:
                    dst = (rB, kB, vB)[ci // 2][ci % 2]
                    dk_ = sk(ci // 2)
                    B.op("dve", lambda e: e.scalar_tensor_tensor(out=dst[:, 0:T - 1], in0=Z[:, 1:T], scalar=pvc("sh1", l, ci), in1=Xt[:, 0:T - 1], op0=ALU.mult, op1=ALU.add),
                         reads=zk + [sk(xi), "pvt"], writes=[dk_])
                    B.op("dve", lambda e: e.tensor_copy(out=dst[:, T - 1:T], in_=Xt[:, T - 1:T]), reads=[sk(xi)], writes=[dk_])
                else:
                    B.op("dve", lambda e: e.scalar_tensor_tensor(out=Xt[:, 0:T - 1], in0=Z[:, 1:T], scalar=pvc("sh1", l, ci), in1=Xt[:, 0:T - 1], op0=ALU.mult, op1=ALU.add),
                         reads=zk + [sk(xi), "pvt"], writes=[sk(xi)])
                    B.op("act", lambda e: e.activation(out=waB[0:64, :], in_=Xt[0:64, :], func=AF.Tanh), reads=[sk(xi)], writes=[sk(3)])
                    B.op("act", lambda e: e.activation(out=waB[64:128, :], in_=Xt[64:128, :], func=AF.Copy), reads=[sk(xi)], writes=[sk(3)])
            for hp in range(2):
                for sg in range(NSEG):
                    sl = slice(sg * 512, (sg + 1) * 512)
                    B.op("dve", lambda e: e.tensor_scalar(out=tmpA[:], in0=kB[hp][:, sl], scalar1=pvc("k_k", l, hp), scalar2=None, op0=ALU.mult),
                         reads=[sk(1), "pvt"], writes=["tmpA"])
                    B.op("act", lambda e: e.activation(out=sqb[:, 0, :], in_=tmpA[:], func=AF.Square), reads=["tmpA"], writes=[("sqb", 0)])
                    B.op("pe", lambda e: e.matmul(PS(4), lhsT=cb("blk64"), rhs=sqb[:, 0, :], start=True, stop=True), reads=[("sqb", 0), "cbf"], writes=psk(4))
                    B.op("act", lambda e: e.activation(out=rstd[:], in_=PS(4), func=AF.Ln, bias=epsb[:, 3:4]), reads=psk(4) + ["epsb"], writes=["rstd"])
                    B.op("act", lambda e: e.activation(out=rstd[:], in_=rstd[:], func=AF.Exp, scale=-0.5), reads=["rstd"], writes=["rstd"])
                    B.op("dve", lambda e: e.tensor_tensor(out=kkB[hp][:, sl], in0=tmpA[:], in1=rstd[:], op=ALU.mult), reads=["tmpA", "rstd"], writes=[sk(4)])
            for cc in range(2):
                gb_ = next_bank()
                inproj(l, [(OFF_A + 896 + cc * 128, 128)], gb_)
                B.op("act", lambda e: e.activation(out=obuf[:, cc, :], in_=PS(gb_, NSEG), func=AF.Silu), reads=psk(gb_, NSEG), writes=["obuf"])
            wepoch[0] += 1
            B.barrier()
            o = 0
            Fb = []
            for i in range(5):
                Fb.append(AF32(o, SA)); o += SA
            H0 = AF32(o, SA); o += SA
            H1 = AF32(o, SA); o += SA
            H2 = ABF(o, SA); o += SA // 2
            BK = ABF(o, 2 * SA); o += SA
            ARb = ABF(o, 2 * SA); o += SA
            HK = ABF(o, 2 * SA); o += SA
            Gm = ABF(o, 4 * SA); o += 2 * SA
            GT = ABF(o, SA); o += SA // 2
            Xs, XTs, Ps_ = [], [], []
            for i in range(2):
                Xs.append(ABF(o, SA)); o += SA // 2
                XTs.append(ABF(o, SA)); o += SA // 2
                Ps_.append(ABF(o, SA)); o += SA // 2
            assert o <= 8192, o
            s7 = sbf(7)
            VT, BhT, KhT = s7[:, 0:SA], s7[:, SA:2 * SA], s7[:, 2 * SA:3 * SA]
            Q1 = scr[7][:, 768:768 + SA]
            Qb = s7[:, 2560:2624]
            UTb = [s7[:, 2624:2688], s7[:, 2688:2752]]
            STf = [[scr[7][:, 1408 + (hp * 2 + i) * 64:1408 + (hp * 2 + i + 1) * 64] for i in range(2)] for hp in range(2)]
            STb = [[s7[:, 3328 + (hp * 2 + i) * 64:3328 + (hp * 2 + i + 1) * 64] for i in range(2)] for hp in range(2)]
            id64 = s7[:, 3600:3664]
            blkf = scr[7][:, 1856:1984]
            F0, F1, F2, F3, F4 = Fb
            B.op("dve", lambda e: e.tensor_tensor(out=id64, in0=cb("mfwd", 64, 64), in1=cb("mfwd", 64, 0), op=ALU.subtract), reads=["cbf"], writes=["id64"])
            B.op("dve", lambda e: e.tensor_copy(out=blkf, in_=cb("blk64")), reads=["cbf"], writes=["blkf"])
            BK4 = BK.rearrange("p (c two j) -> p c two j", two=2, j=64)
            AR4 = ARb.rearrange("p (c two j) -> p c two j", two=2, j=64)
            HK4 = HK.rearrange("p (c two j) -> p c two j", two=2, j=64)
            Gm3 = Gm.rearrange("p (c x) -> p c x", x=256)

            def v3(ap):
                return ap.rearrange("p (c j) -> p c j", j=64)

            def hs(hd):
                return slice(hd * 64, hd * 64 + 64)

            Wf = [wst[i][:].rearrange("p k c -> p (k c)") for i in range(2)]
            fin = [Wf[0][:, 0:SA], Wf[0][:, SA:2 * SA], Wf[1][:, 0:SA], Wf[1][:, SA:2 * SA]]
            fink = ["wst0", "wst0", "wst1", "wst1"]
            sets = []
            sets.append(dict(BK=BK, AR=ARb, HK=HK, kBK="BK", kAR="AR", kHK="HK", gi=0))
            wfl = [wbf[i][:].rearrange("p k c -> p (k c)") for i in range(3)]
            sets.append(dict(BK=wfl[0][:, 0:2 * SA], AR=wfl[1][:, 0:2 * SA], HK=wfl[2][:, 0:2 * SA], kBK="wbf0", kAR="wbf1", kHK="wbf2", gi=1))
            for S_ in sets:
                for nm in ("BK", "AR", "HK"):
                    S_[nm + "4"] = S_[nm].rearrange("p (c two j) -> p c two j", two=2, j=64)
            upwv = upw_f[:].bitcast(BF16)
            AR4s = [ARb.rearrange("p (c two j) -> p c two j", two=2, j=64), wfl[1][:, 0:2 * SA].rearrange("p (c two j) -> p c two j", two=2, j=64),
                    upwv[:, 0:2 * SA].rearrange("p (c two j) -> p c two j", two=2, j=64)]
            kARs = ["AR", "wbf1", "upw_f"]
            Gms = [Gm, sbf(3)[:, 2 * SCRW:2 * SCRW + 4 * SA] if False else sbf(3)[:, SCRW:SCRW + 4 * SA]]
            kGms = ["Gm", "scr3hi"]
            tAv, tBv = tmpA[:].bitcast(BF16), tmpB[:].bitcast(BF16)
            VTs, BhTs, KhTs = [VT, tAv[:, 0:SA]], [BhT, tAv[:, SA:2 * SA]], [KhT, tBv[:, 0:SA]]
            kVTs, kBhTs, kKhTs = ["VT", "tmpA_lo"], ["BhT", "tmpA_hi"], ["KhT", "tmpB"]
            Q1s, kQ1s = [Q1, rstd[:, 0:SA]], ["Q1", "rstd"]
            Tms, kTms = [sqb[:, 0, 0:SA], sqb[:, 1, 0:SA]], [("sqb", 0), ("sqb", 1)]
            iters = []
            for d in range(2):
                for sg in (range(NS) if d == 0 else range(NS - 1, -1, -1)):
                    for hp in range(2):
                        iters.append((d, sg, hp))
            cur = [0, 0]

            def pre_gen(ii):
                d, sg, hp = iters[ii]
                S_ = sets[ii % 2]
                BK4, HK4 = S_["BK4"], S_["HK4"]
                AR4 = AR4s[ii % 3]
                kAR_ = kARs[ii % 3]
                gi = ii % 3
                endc = 63 if d == 0 else 0
                ssl = slice(sg * SA, (sg + 1) * SA)
                first_of_dir = (sg == (0 if d == 0 else NS - 1))
                wc = d * 256 + hp * 128
                B.op("pe", lambda e: e.matmul(PS(2)[:, 0:SA], lhsT=upw_b[0:64, wc:wc + 128], rhs=waB[0:64, ssl], start=True, stop=True),
                     reads=["upw_b", sk(3)], writes=psk(2))
                B.op("pe", lambda e: e.matmul(PS(3)[:, 0:SA], lhsT=upw_b[64:128, wc:wc + 128], rhs=waB[64:128, ssl], start=True, stop=True),
                     reads=["upw_b", sk(3)], writes=psk(3))
                yield
                B.op("act", lambda e: e.activation(out=F0, in_=PS(2)[:, 0:SA], func=AF.Tanh, scale=0.5, bias=dvc("w0h", l, d * 2 + hp)), reads=psk(2) + ["dvt"], writes=["F0"])
                B.op("act", lambda e: e.activation(out=F1, in_=PS(3)[:, 0:SA], func=AF.Tanh, scale=0.5, bias=dvc("a0h", l, d * 2 + hp)), reads=psk(3) + ["dvt"], writes=["F1"])
                yield
                B.op("pool", lambda e: e.tensor_scalar(out=F0, in0=F0, scalar1=0.5, scalar2=0.5, op0=ALU.mult, op1=ALU.add), reads=["F0"], writes=["F0"])
                B.op("pool", lambda e: e.tensor_scalar(out=F1, in0=F1, scalar1=0.5, scalar2=0.5, op0=ALU.mult, op1=ALU.add), reads=["F1"], writes=["F1"])
                yield
                rst = cb("rst", SA)
                if d == 0:
                    B.op("dve", lambda e: e.tensor_tensor_scan(out=F2, data0=rst, data1=F0, initial=0.0, op0=ALU.mult, op1=ALU.add), reads=["cbf", "F0"], writes=["F2"])
                else:
                    B.op("dve", lambda e: e.tensor_tensor_scan(out=F2[:, ::-1], data0=rst, data1=F0[:, ::-1], initial=0.0, op0=ALU.mult, op1=ALU.add),
                         reads=["cbf", "F0"], writes=["F2"])
                yield
                B.op("dve", lambda e: e.tensor_tensor(out=F3, in0=F2, in1=F0, op=ALU.subtract), reads=["F2", "F0"], writes=["F3"])
                yield
                endb = v3(F2)[:, :, endc:endc + 1].broadcast_to([128, nch, 64])
                B.op("dve", lambda e: e.tensor_tensor(out=v3(F0), in0=endb, in1=v3(F2), op=ALU.subtract), reads=["F2"], writes=["F0"])
                yield
                B.op("act", lambda e: e.activation(out=F4, in_=F2, func=AF.Exp, scale=LAMW), reads=["F2"], writes=["F4"])
                B.op("act", lambda e: e.activation(out=F2, in_=F2, func=AF.Exp, scale=-LAMW), reads=["F2"], writes=["F2"])
                yield
                B.op("act", lambda e: e.activation(out=F3, in_=F3, func=AF.Exp, scale=-LAMW), reads=["F3"], writes=["F3"])
                B.op("act", lambda e: e.activation(out=F0, in_=F0, func=AF.Exp, scale=-LAMW), reads=["F0"], writes=["F0"])
                yield
                gv = gam[:, gi, 0:nch]
                B.op("dve", lambda e: e.tensor_copy(out=gv, in_=v3(F2)[:, :, endc]), reads=["F2"], writes=[("gam", gi)])
                kk_, r_, k_, v_ = kkB[hp][:, ssl], rB[hp][:, ssl], kB[hp][:, ssl], vB[hp][:, ssl]
                B.op("pool", lambda e: e.tensor_tensor(out=H0, in0=kk_, in1=F1, op=ALU.mult), reads=[sk(4), "F1"], writes=["H0"])
                yield
                B.op("pool", lambda e: e.tensor_scalar(out=F1, in0=F1, scalar1=pvc("k_a", l, hp), scalar2=dvc("omka", l, hp), op0=ALU.mult, op1=ALU.add),
                     reads=["F1", "pvt", "dvt"], writes=["F1"])
                yield
                B.op("pool", lambda e: e.tensor_tensor(out=H1, in0=F1, in1=k_, op=ALU.mult), reads=["F1", sk(1)], writes=["H1"])
                yield
                B.op("pool", lambda e: e.tensor_tensor(out=BK4[:, :, 0, :], in0=v3(H0), in1=v3(F4), op=ALU.mult), reads=["H0", "F4"], writes=[S_["kBK"]])
                yield
                B.op("pool", lambda e: e.tensor_tensor(out=BK4[:, :, 1, :], in0=v3(H1), in1=v3(F4), op=ALU.mult), reads=["H1", "F4"], writes=[S_["kBK"]])
                yield
                B.op("dve", lambda e: e.scalar_tensor_tensor(out=AR4[:, :, 0, :], in0=v3(kk_), scalar=-1.0, in1=v3(F3), op0=ALU.mult, op1=ALU.mult),
                     reads=[sk(4), "F3"], writes=[kAR_])
                yield
                B.op("pool", lambda e: e.tensor_tensor(out=AR4[:, :, 1, :], in0=v3(r_), in1=v3(F2), op=ALU.mult), reads=[sk(0), "F2"], writes=[kAR_])
                yield
                B.op("pool", lambda e: e.tensor_tensor(out=HK4[:, :, 0, :], in0=v3(H0), in1=v3(F0), op=ALU.mult), reads=["H0", "F0"], writes=[S_["kHK"]])
                yield
                B.op("pool", lambda e: e.tensor_tensor(out=HK4[:, :, 1, :], in0=v3(H1), in1=v3(F0), op=ALU.mult), reads=["H1", "F0"], writes=[S_["kHK"]])
                yield
                B.op("dve", lambda e: e.scalar_tensor_tensor(out=H2, in0=r_, scalar=pvc("r_k", l, hp), in1=H1, op0=ALU.mult, op1=ALU.mult),
                     reads=[sk(0), "pvt", "H1"], writes=["H2"])
                B.op("pe", lambda e: e.matmul(PS(2)[:, 0:SA], lhsT=cb("blk64"), rhs=H2, start=True, stop=True), reads=["cbf", "H2"], writes=psk(2))
                yield
                if d == 0:
                    B.op("act", lambda e: e.activation(out=bcB[hp][:, ssl], in_=PS(2)[:, 0:SA], func=AF.Copy), reads=psk(2), writes=[("bc", hp)])
                else:
                    B.op("dve", lambda e: e.tensor_tensor(out=bcB[hp][:, ssl], in0=PS(2)[:, 0:SA], in1=bcB[hp][:, ssl], op=ALU.add), reads=psk(2) + [("bc", hp)], writes=[("bc", hp)])
                yield

            def inv_gen(ii):
                d, sg, hp = iters[ii]
                S_ = sets[ii % 2]
                BK4, HK4 = S_["BK4"], S_["HK4"]
                kBK, kHK = S_["kBK"], S_["kHK"]
                AR4, kAR = AR4s[ii % 3], kARs[ii % 3]
                gi = ii % 3
                Gm, kGm = Gms[ii % 2], kGms[ii % 2]
                Gm3 = Gm.rearrange("p (c x) -> p c x", x=256)
                VT, BhT, KhT = VTs[ii % 2], BhTs[ii % 2], KhTs[ii % 2]
                kVT, kBhT, kKhT = kVTs[ii % 2], kBhTs[ii % 2], kKhTs[ii % 2]
                Q1, kQ1 = Q1s[ii % 2], kQ1s[ii % 2]
                Tm, tkey = Tms[ii % 2], kTms[ii % 2]
                mname = "mfwd" if d == 0 else "mbwd"
                mTname = "mfwdT" if d == 0 else "mbwdT"
                ssl = slice(sg * SA, (sg + 1) * SA)
                v_ = vB[hp][:, ssl]
                n_items = 2 * nch
                Gps = PS(4, 4)
                n_items = 2 * nch
                it = 0
                for c in range(nch):
                    for hd in range(2):
                        it += 1
                        last = (it == n_items)
                        B.op("pe", lambda e: e.matmul(Gps[hs(hd), c * 256:c * 256 + 128], lhsT=BK4[hs(hd), c, 0, :], rhs=AR4[hs(hd), c, :, :], start=True, stop=True),
                             reads=[kBK, kAR], writes=psk(4, 4), inc=False)
                        B.op("pe", lambda e: e.matmul(Gps[hs(hd), c * 256 + 128:c * 256 + 256], lhsT=BK4[hs(hd), c, 1, :], rhs=AR4[hs(hd), c, :, :], start=True, stop=True),
                             reads=[kBK, kAR], writes=psk(4, 4), inc=last)
                mk = cb(mname).rearrange("p (o x) -> p o x", o=1).broadcast_to([128, nch, 256])
                B.op("dve", lambda e: e.tensor_tensor(out=Gm3, in0=Gps[:, 0:nch * 256].rearrange("p (c x) -> p c x", x=256), in1=mk, op=ALU.mult),
                     reads=psk(4, 4) + ["cbf"], writes=[kGm])
                it = 0
                for c in range(nch):
                    for hd in range(2):
                        it += 1
                        B.op("pe", lambda e: e.matmul(PS(4)[hs(hd), c * 64:(c + 1) * 64], lhsT=AR4[hs(hd), c, 0, :], rhs=BK4[hs(hd), c, 0, :], start=True, stop=True),
                             reads=[kBK, kAR], writes=psk(4), inc=(it == n_items))
                mkT = cb(mTname).rearrange("p (o x) -> p o x", o=1).broadcast_to([128, nch, 64])
                B.op("dve", lambda e: e.tensor_tensor(out=v3(GT), in0=v3(PS(4)[:, 0:SA]), in1=mkT, op=ALU.mult), reads=psk(4) + ["cbf"], writes=["GT"])
                yield
                idb = id64.rearrange("p (o x) -> p o x", o=1).broadcast_to([128, nch, 64])
                B.op("dve", lambda e: e.tensor_tensor(out=v3(Ps_[0]), in0=Gm3[:, :, 0:64], in1=idb, op=ALU.add), reads=[kGm, "id64"], writes=[("P", 0)])

                def Xap(level, buf, hd, c):
                    if level == 0:
                        return Gm[hs(hd), c * 256:c * 256 + 64]
                    return Xs[buf][hs(hd), c * 64:(c + 1) * 64]

                def XTap(level, buf, hd, c):
                    if level == 0:
                        return GT[hs(hd), c * 64:(c + 1) * 64]
                    return XTs[buf][hs(hd), c * 64:(c + 1) * 64]
                for r in range(1, 7):
                    src_b, dst_b = (r - 1) % 2, r % 2
                    xk_src = [kGm] if r == 1 else [("X", src_b)]
                    xtk_src = ["GT"] if r == 1 else [("XT", src_b)]
                    do_x, do_xt, do_p = (r <= 4), (r <= 5), (r >= 2)
                    psrc, pdst = (r - 2) % 2, (r - 1) % 2
                    it = 0
                    for c in range(nch):
                        for hd in range(2):
                            it += 1
                            last = (it == n_items)
                            if do_x:
                                B.op("pe", lambda e: e.matmul(PS(5)[hs(hd), c * 64:(c + 1) * 64], lhsT=XTap(r - 1, src_b, hd, c), rhs=Xap(r - 1, src_b, hd, c), start=True, stop=True),
                                     reads=xk_src + xtk_src, writes=psk(5), inc=False)
                            if do_xt:
                                B.op("pe", lambda e: e.matmul(PS(6)[hs(hd), c * 64:(c + 1) * 64], lhsT=Xap(r - 1, src_b, hd, c), rhs=XTap(r - 1, src_b, hd, c), start=True, stop=True),
                                     reads=xk_src + xtk_src, writes=psk(6), inc=(last and not do_p))
                            if do_p:
                                B.op("pe", lambda e: e.matmul(PS(7)[hs(hd), c * 64:(c + 1) * 64], lhsT=XTap(r - 1, src_b, hd, c), rhs=Ps_[psrc][hs(hd), c * 64:(c + 1) * 64], start=True, stop=True),
                                     reads=xtk_src + [("P", psrc)], writes=psk(7), inc=last)
                            if it % 4 == 0 and not last:
                                yield
                    if do_x:
                        B.op("act", lambda e: e.activation(out=Xs[dst_b], in_=PS(5)[:, 0:SA], func=AF.Copy), reads=psk(5), writes=[("X", dst_b)])
                    if do_xt:
                        B.op("act", lambda e: e.activation(out=XTs[dst_b], in_=PS(6)[:, 0:SA], func=AF.Copy),
                             reads=psk(6), writes=[("XT", dst_b)])
                    if do_p:
                        B.op("dve", lambda e: e.tensor_tensor(out=(Tm if r == 6 else Ps_[pdst]), in0=PS(7)[:, 0:SA], in1=Ps_[psrc], op=ALU.add), reads=psk(7) + [("P", psrc)], writes=[(tkey if r == 6 else ("P", pdst))])
                    yield
                for (srcf, dstb, dkey, rk_, bank) in ((lambda c, hd: v_[hs(hd), c * 64:(c + 1) * 64], VT, kVT, [sk(2)], 4),
                                                       (lambda c, hd: HK4[hs(hd), c, 0, :], BhT, kBhT, [kHK], 5),
                                                       (lambda c, hd: HK4[hs(hd), c, 1, :], KhT, kKhT, [kHK], 6)):
                    pb = PS(bank).bitcast(BF16)
                    it = 0
                    for c in range(nch):
                        for hd in range(2):
                            it += 1
                            B.op("pe", lambda e: e.transpose(pb[hs(hd), c * 64:(c + 1) * 64], srcf(c, hd), cb("ident")[hs(hd), hd * 64:hd * 64 + 64]),
                                 reads=rk_ + ["cbf"], writes=psk(bank), inc=(it == n_items))
                    B.op("act", lambda e: e.activation(out=dstb, in_=pb[:, 0:SA], func=AF.Copy), reads=psk(bank), writes=[dkey])
                    yield
                it = 0
                for c in range(nch):
                    for hd in range(2):
                        it += 1
                        B.op("pe", lambda e: e.matmul(PS(7)[hs(hd), c * 64:(c + 1) * 64], lhsT=Gm[hs(hd), c * 256 + 128:c * 256 + 192], rhs=VT[hs(hd), c * 64:(c + 1) * 64], start=True, stop=True),
                             reads=[kGm, kVT], writes=psk(7), inc=(it == n_items))
                B.op("dve", lambda e: e.tensor_copy(out=Q1, in_=PS(7)[:, 0:SA]), reads=psk(7), writes=[kQ1])
                yield

            def chain_gen(ii):
                d, sg, hp = iters[ii]
                S_ = sets[ii % 2]
                BK4, HK4 = S_["BK4"], S_["HK4"]
                kBK, kHK = S_["kBK"], S_["kHK"]
                AR4, kAR = AR4s[ii % 3], kARs[ii % 3]
                gi = ii % 3
                Gm, kGm = Gms[ii % 2], kGms[ii % 2]
                Gm3 = Gm.rearrange("p (c x) -> p c x", x=256)
                VT, BhT, KhT = VTs[ii % 2], BhTs[ii % 2], KhTs[ii % 2]
                kVT, kBhT, kKhT = kVTs[ii % 2], kBhTs[ii % 2], kKhTs[ii % 2]
                Q1, kQ1 = Q1s[ii % 2], kQ1s[ii % 2]
                Tm, tkey = Tms[ii % 2], kTms[ii % 2]
                mname = "mfwd" if d == 0 else "mbwd"
                mTname = "mfwdT" if d == 0 else "mbwdT"
                ssl = slice(sg * SA, (sg + 1) * SA)
                v_ = vB[hp][:, ssl]
                n_items = 2 * nch
                if sg == (0 if d == 0 else NS - 1):
                    cur[hp] = 0
                    B.op("dve", lambda e: e.memset(STf[hp][0], 0.0), writes=[("STf", hp, 0)])
                    B.op("dve", lambda e: e.memset(STb[hp][0], 0.0), writes=[("STb", hp, 0)])
                corder = range(nch) if d == 0 else range(nch - 1, -1, -1)
                for ci_, c in enumerate(corder):
                    cu = cur[hp]
                    nx = 1 - cu
                    cs_ = slice(c * 64, (c + 1) * 64)
                    ub = ci_ % 2
                    for hd in range(2):
                        B.op("pe", lambda e: e.matmul(PS(0)[hs(hd), 0:64], lhsT=AR4[hs(hd), c, 0, :], rhs=STb[hp][cu][hs(hd), :], start=True, stop=True),
                             reads=[kAR, ("STb", hp, cu)], writes=psk(0), inc=(hd == 1))
                    yield
                    B.op("dve", lambda e: e.tensor_tensor(out=Qb, in0=PS(0)[:, 0:64], in1=Q1[:, cs_], op=ALU.add), reads=psk(0) + [kQ1], writes=["Qb"])
                    for hd in range(2):
                        B.op("pe", lambda e: e.matmul(PS(0)[hs(hd), 64:128], lhsT=Tm[hs(hd), cs_], rhs=Qb[hs(hd), :], start=True, stop=True),
                             reads=[tkey, "Qb"], writes=psk(0), inc=(hd == 1))
                    yield
                    B.op("act", lambda e: e.activation(out=UTb[ub], in_=PS(0)[:, 64:128], func=AF.Copy), reads=psk(0), writes=[("UTb", ub)])
                    for hd in range(2):
                        B.op("pe", lambda e: e.matmul(PS(0)[hs(hd), 128:192], lhsT=BhT[hs(hd), cs_], rhs=UTb[ub][hs(hd), :], start=True, stop=False),
                             reads=[kBhT, ("UTb", ub)], writes=psk(0), inc=False)
                        B.op("pe", lambda e: e.matmul(PS(0)[hs(hd), 128:192], lhsT=KhT[hs(hd), cs_], rhs=VT[hs(hd), cs_], start=False, stop=True),
                             reads=[kKhT, kVT], writes=psk(0), inc=(hd == 1))
                    yield
                    gcol = gam[:, gi, c:c + 1]
                    B.op("dve", lambda e: e.scalar_tensor_tensor(out=STf[hp][nx], in0=STf[hp][cu], scalar=gcol, in1=PS(0)[:, 128:192], op0=ALU.mult, op1=ALU.add),
                         reads=[("STf", hp, cu), ("gam", gi)] + psk(0), writes=[("STf", hp, nx)])
                    B.op("act", lambda e: e.activation(out=STb[hp][nx], in_=STf[hp][nx], func=AF.Copy), reads=[("STf", hp, nx)], writes=[("STb", hp, nx)])
                    for hd in range(2):
                        B.op("pe", lambda e: e.matmul(PS(1)[hs(hd), cs_], lhsT=STb[hp][cu][hs(hd), :], rhs=AR4[hs(hd), c, 1, :], start=True, stop=False),
                             reads=[("STb", hp, cu), kAR], writes=psk(1), inc=False)
                        B.op("pe", lambda e: e.matmul(PS(1)[hs(hd), cs_], lhsT=UTb[ub][hs(hd), :], rhs=Gm[hs(hd), c * 256 + 64:c * 256 + 128], start=False, stop=False),
                             reads=[("UTb", ub), kGm], writes=psk(1), inc=False)
                        B.op("pe", lambda e: e.matmul(PS(1)[hs(hd), cs_], lhsT=VT[hs(hd), cs_], rhs=Gm[hs(hd), c * 256 + 192:c * 256 + 256], start=False, stop=True),
                             reads=[kVT, kGm], writes=psk(1), inc=(hd == 1))
                    cur[hp] = nx
                    yield
                if d == 0:
                    B.op("act", lambda e: e.activation(out=yfB[hp][:, ssl], in_=PS(1)[:, 0:SA], func=AF.Copy), reads=psk(1), writes=[("yf", hp)])
                else:
                    B.op("dve", lambda e: e.tensor_tensor(out=fin[0], in0=PS(1)[:, 0:SA], in1=yfB[hp][:, ssl], op=ALU.add), reads=psk(1) + [("yf", hp)], writes=[fink[0]])
                    B.op("pe", lambda e: e.matmul(PS(0)[:, 0:SA], lhsT=blkf, rhs=fin[0], start=True, stop=True), reads=["blkf", fink[0]], writes=psk(0))
                    B.op("dve", lambda e: e.scalar_tensor_tensor(out=fin[1], in0=PS(0)[:, 0:SA], scalar=-1.0 / 64, in1=fin[0], op0=ALU.mult, op1=ALU.add), reads=psk(0) + [fink[0]], writes=[fink[1]])
                    B.op("act", lambda e: e.activation(out=fin[2], in_=fin[1], func=AF.Square), reads=[fink[1]], writes=[fink[2]])
                    B.op("pe", lambda e: e.matmul(PS(0)[:, 0:SA], lhsT=blkf, rhs=fin[2], start=True, stop=True), reads=["blkf", fink[2]], writes=psk(0))
                    B.op("act", lambda e: e.activation(out=fin[3], in_=PS(0)[:, 0:SA], func=AF.Ln, scale=1.0 / 64, bias=epsb[:, 2:3]), reads=psk(0) + ["epsb"], writes=[fink[3]])
                    B.op("act", lambda e: e.activation(out=fin[3], in_=fin[3], func=AF.Exp, scale=-0.5), reads=[fink[3]], writes=[fink[3]])
                    B.op("dve", lambda e: e.tensor_tensor(out=fin[1], in0=fin[1], in1=fin[3], op=ALU.mult), reads=[fink[1], fink[3]], writes=[fink[1]])
                    B.op("dve", lambda e: e.tensor_scalar(out=fin[1], in0=fin[1], scalar1=pvc("ln_g", l, hp), scalar2=pvc("ln_b", l, hp), op0=ALU.mult, op1=ALU.add), reads=[fink[1], "pvt"], writes=[fink[1]])
                    B.op("dve", lambda e: e.tensor_tensor(out=fin[2], in0=bcB[hp][:, ssl], in1=v_, op=ALU.mult), reads=[("bc", hp), sk(2)], writes=[fink[2]])
                    B.op("dve", lambda e: e.tensor_tensor(out=fin[1], in0=fin[1], in1=fin[2], op=ALU.add), reads=[fink[1], fink[2]], writes=[fink[1]])
                    B.op("dve", lambda e: e.tensor_tensor(out=obuf[:, hp, ssl], in0=fin[1], in1=obuf[:, hp, ssl], op=ALU.mult), reads=[fink[1], "obuf"], writes=["obuf"])


                yield

            def run_all():
                n = len(iters)
                for _ in pre_gen(0):
                    pass
                if n > 1:
                    gens0 = [pre_gen(1), inv_gen(0)]
                else:
                    gens0 = [inv_gen(0)]
                _rr(gens0)
                for s_ in range(n):
                    gens = [chain_gen(s_)]
                    if s_ + 1 < n:
                        gens.append(inv_gen(s_ + 1))
                    if s_ + 2 < n:
                        gens.append(pre_gen(s_ + 2))
                    _rr(gens)

            def _rr(gens):
                import os
                gens = list(gens)
                mode = os.environ.get("A_MODE", "cip")
                if os.environ.get("A_SEQ"):
                    mode = ""
                names = {"chain_gen": "c", "inv_gen": "i", "pre_gen": "p"}
                conc = [g for g in gens if names[g.gi_code.co_name] in mode]
                seq = [g for g in gens if names[g.gi_code.co_name] not in mode]
                while conc:
                    for g in list(conc):
                        try:
                            next(g)
                        except StopIteration:
                            conc.remove(g)
                for g in seq:
                    for _ in g:
                        pass
            run_all()
            B.barrier()
            wepoch[0] += 1

        def program():
            B.dma(scr[0][:, 0:CST_W], cst_d[:, :], writes=[sk(0)], q="pool", semkey="cst")
            B.dma(pvt[:], pv_d[:, :], writes=["pvt"], q="pool", semkey="pvt")
            B.op("dve", lambda e: e.tensor_copy(out=cbf[:], in_=scr[0][:, 0:CST_W]), reads=[sk(0)], writes=["cbf"])
            B.op("dve", lambda e: e.tensor_copy(out=swapf[:], in_=scr[0][:, CST_OFF["swap"]:CST_OFF["swap"] + 128]), reads=[sk(0)], writes=["swapf"])
            B.op("dve", lambda e: e.memset(epsb[:, 0:1], NORM_EPS), writes=["epsb"])
            B.op("dve", lambda e: e.memset(epsb[:, 1:2], 1.0), writes=["epsb"])
            B.op("dve", lambda e: e.memset(epsb[:, 2:3], GN_EPS), writes=["epsb"])
            B.op("dve", lambda e: e.memset(epsb[:, 3:4], 1e-18), writes=["epsb"])
            for l in range(L):
                for i in range(7):
                    B.op("dve", lambda e, l=l, i=i: e.tensor_tensor(out=dvc("c0", l, i), in0=pvc("sh0", l, i), in1=pvc("sh1", l, i), op=ALU.add),
                         reads=["pvt"], writes=["dvt"])
                    B.op("dve", lambda e, l=l, i=i: e.tensor_scalar(out=dvc("c0", l, i), in0=dvc("c0", l, i), scalar1=-1.0, scalar2=1.0, op0=ALU.mult, op1=ALU.add),
                         reads=["dvt"], writes=["dvt"])
                for i in range(2):
                    B.op("dve", lambda e, l=l, i=i: e.tensor_scalar(out=dvc("omka", l, i), in0=pvc("k_a", l, i), scalar1=-1.0, scalar2=1.0, op0=ALU.mult, op1=ALU.add),
                         reads=["pvt"], writes=["dvt"])
                    B.op("act", lambda e, l=l, i=i: e.activation(out=dvc("esinksw", l, i), in_=pvc("sinksw", l, i), func=AF.Exp),
                         reads=["pvt"], writes=["dvt"])
                for i in range(4):
                    B.op("dve", lambda e, l=l, i=i: e.tensor_scalar(out=dvc("w0h", l, i), in0=pvc("w0", l, i), scalar1=0.5, scalar2=None, op0=ALU.mult),
                         reads=["pvt"], writes=["dvt"])
                    B.op("dve", lambda e, l=l, i=i: e.tensor_scalar(out=dvc("a0h", l, i), in0=pvc("a0", l, i), scalar1=0.5, scalar2=None, op0=ALU.mult),
                         reads=["pvt"], writes=["dvt"])
                    B.op("act", lambda e, l=l, i=i: e.activation(out=dvc("cneg", l, i), in_=pvc("lam", l, i), func=AF.Exp, scale=-1.0),
                         reads=["pvt"], writes=["dvt"])
                    B.op("act", lambda e, l=l, i=i: e.activation(out=dvc("cneg", l, i), in_=dvc("cneg", l, i), func=AF.Ln, bias=epsb[:, 1:2]),
                         reads=["dvt", "epsb"], writes=["dvt"])
                    B.op("dve", lambda e, l=l, i=i: e.tensor_scalar(out=dvc("cneg", l, i), in0=dvc("cneg", l, i), scalar1=-8.0, scalar2=None, op0=ALU.mult),
                         reads=["dvt"], writes=["dvt"])


            for s in range(NSEQ):
                for c in range(8):
                    B.dma(xT[:, c, :], x_d[s * 1024 + c * 128:s * 1024 + (c + 1) * 128, :], writes=[("xT", c)], q="pool", semkey=f"xin{c}")
                for l in range(L):
                    rmsnorm_to(l, "h")
                    B.dma(scr[7][:, 0:1024], gw_d[:, l * 1024:(l + 1) * 1024], writes=[sk(7)], q="pool", semkey="gw")
                    B.op("pool", lambda e: e.tensor_copy(out=gw_b[:], in_=scr[7][:, 0:1024]), reads=[sk(7)], writes=["gw_b"])
                    B.dma(upw_f[:], upw_d[:, l * 512:(l + 1) * 512], writes=["upw_f"], q="pool", semkey="upw")
                    B.op("pool", lambda e: e.tensor_copy(out=upw_b[:], in_=upw_f[:]), reads=["upw_f"], writes=["upw_b"])
                    if "C" in cfg.MIX:
                        mixer_C(l)
                        B.barrier()
                        outproj(l, 2)
                    if "D" in cfg.MIX:
                        mixer_D(l)
                        B.barrier()
                        outproj(l, 3)
                    if "B" in cfg.MIX:
                        mixer_B(l)
                        B.barrier()
                        outproj(l, 1)
                    if "A" in cfg.MIX:
                        mixer_A(l)
                        B.barrier()
                        outproj(l, 0)
                rmsnorm_to(None, "x")
                for c in range(8):
                    B.dma(y_d[s * 1024 + c * 128:s * 1024 + (c + 1) * 128, :], xT[:, c, :], reads=[("xT", c)], q="pool", semkey=f"yout{c}")
            toks = []
            for c in range(8):
                toks.extend(B.readers.get(("xT", c), {}).values())
            for tname in tap_d:
                t = B.last_w.get(("tap", tname))
                if t is not None:
                    toks.append(t)
            B._wait("pool", toks)

        realB = B
        B = NullBuilder()
        wstate["rec"] = True
        program()
        B = realB
        wstate["rec"] = False
        wepoch[0] = 0
        wo_slot[0] = 0
        program()
        print(f"[build] instructions: {B.n_ins}")
    return nc


_NC_CACHE = {}


def run_trunk(cfg, xs, inp):
    T, L = cfg.T, cfg.DEPTH
    ncore = cfg.NCORES
    assert xs.shape[0] == ncore * cfg.NSEQ
    key = (cfg.T, cfg.NSEQ, cfg.DEPTH, cfg.MIX, cfg.taps)
    if key not in _NC_CACHE:
        _NC_CACHE[key] = build(cfg)
    nc = _NC_CACHE[key]
    cst, alibi, C, S = make_consts(T)
    pv, upw, gw = pack_params(inp, L)
    w_in = np.ascontiguousarray(np.asarray(inp["w_in"], np.float32)[:L].reshape(L * 1024, D_IN))
    w_out = np.ascontiguousarray(np.asarray(inp["w_out"], np.float32)[:L].reshape(L * 1024, 1024))
    in_maps = []
    for c in range(ncore):
        xc = xs[c * cfg.NSEQ:(c + 1) * cfg.NSEQ]
        xt = np.ascontiguousarray(xc.transpose(0, 2, 1)).reshape(cfg.NSEQ * 1024, T)
        in_maps.append({"x": xt, "w_in": w_in, "w_out": w_out, "pvec": pv, "upw": upw, "gw": gw, "cst": cst,
                        "alibi": alibi, "ropeC": C, "ropeS": S})
    res = run_bass_kernel_spmd(nc, in_maps, core_ids=list(range(ncore)))
    outs = []
    for c in range(ncore):
        yt = np.asarray(res.results[c]["y"]).reshape(cfg.NSEQ, 1024, T)
        outs.append(yt.transpose(0, 2, 1))
    return np.ascontiguousarray(np.concatenate(outs, 0)).astype(np.float32), res


def kernel(**inputs):
    inp = {k: np.asarray(v) for k, v in inputs.items()}
    xp, xs_ = inp["x_prompt"], inp["x_sample"]
    xs = np.concatenate([xp, xs_], 0).astype(np.float32)
    cfg = Cfg(T=xs.shape[1], NSEQ=xs.shape[0] // 8, DEPTH=inp["w_in"].shape[0], MIX="ABCD", NCORES=8)
    y, _ = run_trunk(cfg, xs, inp)
    return (y[:xp.shape[0]], y[xp.shape[0]:])
```

```python
import math
from contextlib import ExitStack
import numpy as np
import ml_dtypes
import concourse.bass as bass
import concourse.mybir as mybir
from concourse.bass_utils import run_bass_kernel_spmd

F32 = mybir.dt.float32
BF16 = mybir.dt.bfloat16
AF = mybir.ActivationFunctionType
ALU = mybir.AluOpType

D_MODEL = 1024
GRID_W = 64
D_IN = 3200
A_W, B_W, C_W, D_W = 1152, 768, 512, 768
OFF_A, OFF_B, OFF_C, OFF_D = 0, 1152, 1920, 2432
NORM_EPS = 1e-6
GN_EPS = 64e-5
LAMW = math.exp(-0.5)
SEM_LIMIT = 30000


class Cfg:
    def __init__(self, T=2048, NSEQ=5, DEPTH=4, MIX="ABCD", NCORES=8, taps=()):
        self.T, self.NSEQ, self.DEPTH, self.MIX, self.NCORES = T, NSEQ, DEPTH, MIX, NCORES
        self.taps = tuple(taps)


PV_FIELDS = [("norm_g", 8), ("sh0", 7), ("sh1", 7), ("w0", 4), ("a0", 4), ("k_k", 2), ("k_a", 2), ("r_k", 2),
             ("ln_g", 2), ("ln_b", 2), ("qn", 1), ("kn", 1), ("cw", 8), ("cb", 2), ("gb", 8), ("lam", 4),
             ("sink", 2), ("sinksw", 2)]
DV_FIELDS = [("c0", 7), ("omka", 2), ("cneg", 4), ("esinksw", 2), ("gbh", 8), ("w0h", 4), ("a0h", 4)]


def _offsets(fields):
    off, d = 0, {}
    for n, w in fields:
        d[n] = off
        off += w
    return d, off


PV_OFF, PV_W = _offsets(PV_FIELDS)
DV_OFF, DV_W = _offsets(DV_FIELDS)


def pv_col(L, name, l, i=0):
    if name == "final_g":
        return L * PV_W + i
    return l * PV_W + PV_OFF[name] + i


def dv_col(name, l, i=0):
    return l * DV_W + DV_OFF[name] + i


CST_FIELDS = [("ident", 128), ("blk64", 128), ("ones", 128), ("swap", 128), ("perm", 128), ("mfwd", 256),
              ("mbwd", 256), ("mfwdT", 64), ("mbwdT", 64), ("rst", 512)]
CST_OFF, CST_W = _offsets(CST_FIELDS)


def make_consts(T):
    c = np.zeros((128, CST_W), np.float32)
    alibi = np.zeros((128, 4 * 384), np.float32)
    p = np.arange(128)
    o = CST_OFF
    c[:, o["ident"]:o["ident"] + 128] = np.eye(128)
    c[:, o["blk64"]:o["blk64"] + 128] = (p[:, None] // 64 == p[None, :] // 64)
    c[:, o["ones"]:o["ones"] + 128] = 1.0
    c[:, o["swap"]:o["swap"] + 128] = (p[:, None] == (p[None, :] + 64) % 128)
    d = p % 64
    partner = np.where((d % 32) < 16, p + 16, p - 16)
    c[:, o["perm"]:o["perm"] + 128] = (p[:, None] == partner[None, :])
    s = (p % 64)[:, None]
    t = np.arange(64)[None, :]
    strict_f, incl_f = (s < t), (s <= t)
    strict_b, incl_b = (s > t), (s >= t)
    c[:, o["mfwd"]:o["mfwd"] + 256] = np.concatenate([strict_f, incl_f, strict_f, incl_f], 1)
    c[:, o["mbwd"]:o["mbwd"] + 256] = np.concatenate([strict_b, incl_b, strict_b, incl_b], 1)
    c[:, o["mfwdT"]:o["mfwdT"] + 64] = (t < s)
    c[:, o["mbwdT"]:o["mbwdT"] + 64] = (t > s)
    c[:, o["rst"]:o["rst"] + 512] = (np.arange(512)[None, :] % 64 != 0)
    cc = np.arange(384)[None, :]
    dist = np.abs(cc - 128 - p[:, None]).astype(np.float64)
    for h in range(4):
        slope = 2.0 ** (-8.0 * (h + 1) / 4)
        e = np.where(dist <= 128, np.exp(-slope * dist), 0.0)
        alibi[:, h * 384:(h + 1) * 384] = e
    row = (np.arange(T) // GRID_W).astype(np.float64)
    col = (np.arange(T) % GRID_W).astype(np.float64)
    inv = 10000.0 ** (-np.arange(0, 32, 2, dtype=np.float64) / 32)
    C = np.zeros((128, T), np.float32)
    S = np.zeros((128, T), np.float32)
    for pp in range(128):
        dd = pp % 64
        pos = row if dd < 32 else col
        f = inv[dd % 16]
        ang = pos * f
        C[pp] = np.cos(ang)
        S[pp] = (-np.sin(ang)) if (dd % 32) < 16 else np.sin(ang)
    return c, alibi, C, S


def pack_params(inp, L):
    pv = np.zeros((128, L * PV_W + 8), np.float32)

    def put(name, l, i, vec128):
        pv[:, pv_col(L, name, l, i)] = vec128

    for l in range(L):
        for i in range(8):
            put("norm_g", l, i, inp["norm_g"][l, i * 128:(i + 1) * 128])
        for i in range(7):
            put("sh0", l, i, inp["rwkv_shift"][l, 0, i * 128:(i + 1) * 128])
            put("sh1", l, i, inp["rwkv_shift"][l, 1, i * 128:(i + 1) * 128])
        for d in range(2):
            for hp in range(2):
                put("w0", l, d * 2 + hp, inp["rwkv_w0"][l, d, hp * 128:(hp + 1) * 128])
                put("a0", l, d * 2 + hp, inp["rwkv_a0"][l, d, hp * 128:(hp + 1) * 128])
        rk = inp["rwkv_r_k"][l].reshape(256)
        for hp in range(2):
            sl = slice(hp * 128, (hp + 1) * 128)
            put("k_k", l, hp, inp["rwkv_k_k"][l, sl])
            put("k_a", l, hp, inp["rwkv_k_a"][l, sl])
            put("r_k", l, hp, rk[sl])
            put("ln_g", l, hp, inp["rwkv_ln_g"][l, sl])
            put("ln_b", l, hp, inp["rwkv_ln_b"][l, sl])
        put("qn", l, 0, np.tile(inp["attn_q_norm"][l], 2))
        put("kn", l, 0, np.tile(inp["attn_k_norm"][l], 2))
        for j in range(4):
            for cc in range(2):
                put("cw", l, j * 2 + cc, inp["lru_conv_w"][l, j, cc * 128:(cc + 1) * 128])
        for cc in range(2):
            put("cb", l, cc, inp["lru_conv_b"][l, cc * 128:(cc + 1) * 128])
        for d in range(2):
            for k in range(2):
                for cc in range(2):
                    put("gb", l, (d * 2 + k) * 2 + cc, inp["lru_gate_b"][l, d, k, cc * 128:(cc + 1) * 128])
            for cc in range(2):
                put("lam", l, d * 2 + cc, inp["lru_lambda"][l, d, cc * 128:(cc + 1) * 128])
        sk = inp["swa_sink"][l]
        for cc in range(2):
            put("sink", l, cc, np.repeat(sk[2 * cc:2 * cc + 2], 64))
            put("sinksw", l, cc, np.repeat(sk[2 * cc:2 * cc + 2][::-1], 64))
    for i in range(8):
        pv[:, L * PV_W + i] = inp["final_g"][i * 128:(i + 1) * 128]
    upw = np.zeros((128, L * 2 * 256), np.float32)
    for l in range(L):
        for d in range(2):
            upw[0:64, (l * 2 + d) * 256:(l * 2 + d + 1) * 256] = inp["rwkv_w_up"][l, d]
            upw[64:128, (l * 2 + d) * 256:(l * 2 + d + 1) * 256] = inp["rwkv_a_up"][l, d]
    gw = np.zeros((128, L * 8 * 128), np.float32)
    for l in range(L):
        for d in range(2):
            for k in range(2):
                for cc in range(2):
                    base = (((l * 2 + d) * 2 + k) * 2 + cc) * 128
                    for b in range(2):
                        gw[b * 64:(b + 1) * 64, base + b * 64:base + (b + 1) * 64] = inp["lru_gate_w"][l, d, k, 2 * cc + b]
    return pv, upw, gw


class SemCounter:
    def __init__(self, bld, name, step):
        self.bld, self.name, self.step = bld, name, step
        self.n = 0
        self._new()

    def _new(self):
        self.sem = self.bld.es.enter_context(self.bld.nc.semaphore(f"{self.name}_{self.n}"))
        self.sname = f"{self.name}_{self.n}"
        self.n += 1
        self.val = 0

    def next(self):
        if self.val + self.step > SEM_LIMIT:
            self._new()
        self.val += self.step
        return (self.sname, self.sem, self.val)


class Builder:
    def __init__(self, nc, es):
        self.nc, self.es = nc, es
        self.eng = {"pe": nc.tensor, "act": nc.scalar, "dve": nc.vector, "pool": nc.gpsimd, "sp": nc.sync}
        self.cnt = {e: SemCounter(self, e, 1) for e in ("pe", "act", "dve", "pool")}
        self.dcnt = {}
        self.waited = {e: {} for e in self.eng}
        self.last_w = {}
        self.readers = {}
        self.pending = {e: ([], []) for e in self.eng}
        self.n_ins = 0

    def _deps(self, reads, writes):
        toks = []
        for k in reads:
            t = self.last_w.get(k)
            if t is not None:
                toks.append(t)
        for k in writes:
            t = self.last_w.get(k)
            if t is not None:
                toks.append(t)
            r = self.readers.get(k)
            if r:
                toks.extend(r.values())
        return toks

    def _wait(self, e, toks):
        w = self.waited[e]
        need = {}
        for (sn, sem, val) in toks:
            if w.get(sn, 0) < val and need.get(sn, (None, 0))[1] < val:
                need[sn] = (sem, val)
        for sn, (sem, val) in need.items():
            self.eng[e].wait_ge(sem, val)
            w[sn] = val

    def _commit(self, tok, reads, writes):
        for k in writes:
            self.last_w[k] = tok
            self.readers[k] = {}
        for k in reads:
            self.readers.setdefault(k, {})[tok[0]] = tok

    def op(self, e, fn, reads=(), writes=(), inc=True):
        self._wait(e, self._deps(reads, writes))
        ins = fn(self.eng[e])
        self.n_ins += 1
        pr, pw = self.pending[e]
        if not inc:
            pr.extend(reads)
            pw.extend(writes)
            return
        tok = self.cnt[e].next()
        ins.then_inc(tok[1], 1)
        self._commit(tok, list(reads) + pr, list(writes) + pw)
        self.pending[e] = ([], [])

    def dma(self, out, in_, reads=(), writes=(), q="sp", semkey=None):
        self._wait(q, self._deps(reads, writes))
        ins = self.eng[q].dma_start(out=out, in_=in_)
        self.n_ins += 1
        if semkey not in self.dcnt:
            self.dcnt[semkey] = SemCounter(self, "d" + semkey, 16)
        tok = self.dcnt[semkey].next()
        ins.then_inc(tok[1], 16)
        self._commit(tok, reads, writes)
        return tok

    def barrier(self):
        toks = []
        for c in list(self.cnt.values()) + list(self.dcnt.values()):
            if c.val > 0:
                toks.append((c.sname, c.sem, c.val))
        for e in self.eng:
            self._wait(e, toks)

    def final_wait(self, e, keys):
        toks = []
        for k in keys:
            t = self.last_w.get(k)
            if t is not None:
                toks.append(t)
        self._wait(e, toks)


class NullBuilder:
    def __init__(self):
        self.readers, self.last_w, self.n_ins = {}, {}, 0

    def op(self, *a, **k):
        pass

    def dma(self, *a, **k):
        pass

    def barrier(self):
        pass

    def _wait(self, *a, **k):
        pass


SCRW = 2048
NSCR = 8


def build(cfg):
    T, NSEQ, L = cfg.T, cfg.NSEQ, cfg.DEPTH
    NSEG, NT, NCH = T // 512, T // 128, T // 64
    assert T % 512 == 0 and T <= 2048
    nc = bass.Bass("TRN2", target_bir_lowering=False)

    def din(name, shape, dt=F32):
        return nc.dram_tensor(name, list(shape), dt, kind="ExternalInput").ap()

    x_d = din("x", [NSEQ * 1024, T])
    win_d = din("w_in", [L * 1024, D_IN])
    wout_d = din("w_out", [L * 1024, 1024])
    pv_d = din("pvec", [128, L * PV_W + 8])
    upw_d = din("upw", [128, L * 512])
    gw_d = din("gw", [128, L * 1024])
    cst_d = din("cst", [128, CST_W])
    alibi_d = din("alibi", [128, 1536])
    ropeC_d = din("ropeC", [128, T])
    ropeS_d = din("ropeS", [128, T])
    y_d = nc.dram_tensor("y", [NSEQ * 1024, T], F32, kind="ExternalOutput").ap()
    tap_d = {}
    for (tname, tshape) in cfg.taps:
        tap_d[tname] = nc.dram_tensor("tap_" + tname, list(tshape), F32, kind="ExternalOutput").ap()

    es = ExitStack()
    with es:
        B = Builder(nc, es)

        def sb(name, shape, dt):
            return es.enter_context(nc.sbuf_tensor(name, list(shape), dt))

        xT = sb("xT", [128, 8, T], F32)
        hT = sb("hT", [128, 8, max(T, 2048)], BF16)
        pvt = sb("pvt", [128, L * PV_W + 8], F32)
        dvt = sb("dvt", [128, L * DV_W], F32)
        CBW = CST_W
        cbf = sb("cbf", [128, CBW], BF16)
        swapf = sb("swapf", [128, 128], F32)
        upw_f = sb("upw_f", [128, 512], F32)
        upw_b = sb("upw_b", [128, 512], BF16)
        gw_b = sb("gw_b", [128, 1024], BF16)
        NWS = 3
        wst = [sb(f"wst{i}", [128, 8, 128], F32) for i in range(2)]
        wbf = [sb(f"wbf{i}", [128, 8, 128], BF16) for i in range(NWS)]
        wost = [sb(f"wost{i}", [128, 256], F32) for i in range(2)]
        wobf = [sb(f"wobf{i}", [128, 256], BF16) for i in range(2)]
        obuf = sb("obuf", [128, 2, T], BF16)
        sqb = sb("sqb", [128, 2, 512], BF16)
        rstd = sb("rstd", [128, 512], F32)
        epsb = sb("epsb", [128, 4], F32)
        tmpA = sb("tmpA", [128, 512], F32)
        tmpB = sb("tmpB", [128, 512], F32)
        gam = sb("gam", [128, 3, 8], F32)
        scr = [sb(f"scr{i}", [128, SCRW], F32) for i in range(NSCR)]
        psum = es.enter_context(nc.psum_tensor("psum", [128, 8 * 512], F32))

        def sk(i):
            return ("scr", i)

        def sbf(i):
            return scr[i][:].bitcast(BF16)

        def PS(b, n=1):
            return psum[:, b * 512:(b + n) * 512]

        def psk(b, n=1):
            return [("ps", b + i) for i in range(n)]

        def cb(name, w=None, off=0):
            o = CST_OFF[name] + off
            return cbf[:, o:o + (w if w is not None else dict(CST_FIELDS)[name])]

        def pvc(name, l, i=0):
            c = pv_col(L, name, l, i)
            return pvt[:, c:c + 1]

        def dvc(name, l, i=0):
            c = dv_col(name, l, i)
            return dvt[:, c:c + 1]

        def tap(name, ap_sb, rkeys, rows=128):
            if name in tap_d:
                B.dma(tap_d[name], ap_sb, reads=rkeys, writes=[("tap", name)], q="pool", semkey="tap" + name)

        wepoch = [0]
        wplan = []
        wstate = {"rec": True, "next": 0, "issued": 0}

        def _issue_w(i):
            l, ranges, ep = wplan[i]
            si, bi = i % 2, i % NWS
            off = 0
            src = win_d[l * 1024:(l + 1) * 1024, :].rearrange("(k p) c -> p k c", p=128)
            for (c0, n) in ranges:
                B.dma(wst[si][:, :, off:off + n], src[:, :, c0:c0 + n], writes=[f"wst{si}"], q="sp", semkey=f"wst{si}")
                off += n
            B.op("pool", lambda e: e.tensor_copy(out=wbf[bi][:, :, 0:off], in_=wst[si][:, :, 0:off]),
                 reads=[f"wst{si}"], writes=[f"wbf{bi}"])

        def load_win(l, ranges):
            m = sum(n for _, n in ranges)
            if wstate["rec"]:
                wplan.append((l, tuple(ranges), wepoch[0]))
                return 0, m
            i = wstate["next"]
            wstate["next"] += 1
            assert wplan[i][0] == l and wplan[i][1] == tuple(ranges), (i, wplan[i], l, ranges)
            hi = i
            while hi + 1 < len(wplan) and hi + 1 <= i + 2 and wplan[hi + 1][2] == wplan[i][2]:
                hi += 1
            while wstate["issued"] <= hi:
                _issue_w(wstate["issued"])
                wstate["issued"] += 1
            return i % NWS, m

        ipb = [0]

        def next_bank():
            b = ipb[0]
            ipb[0] = 4 - b
            return b

        def inproj(l, ranges, bank0):
            bi, m = load_win(l, ranges)
            for sg in range(NSEG):
                for k in range(8):
                    B.op("pe", lambda e, k=k, sg=sg: e.matmul(PS(bank0 + sg)[0:m, :], lhsT=wbf[bi][:, k, 0:m], rhs=hT[:, k, sg * 512:(sg + 1) * 512],
                                                              start=(k == 0), stop=(k == 7)),
                         reads=[f"wbf{bi}", "hT"], writes=psk(bank0 + sg), inc=(k == 7))
            return m

        def inproj_tok(l, ranges, dst_fn, post):
            bi, m = load_win(l, ranges)
            assert m == 64
            tb0 = next_bank()
            for tb in range((NT + 7) // 8):
                bank = tb0 + tb % 2
                ntt = min(8, NT - tb * 8)
                for q in range(ntt):
                    tt = tb * 8 + q
                    for k in range(8):
                        B.op("pe", lambda e, k=k, tt=tt, q=q, bank=bank: e.matmul(PS(bank)[:, q * 64:(q + 1) * 64], lhsT=hT[:, k, tt * 128:(tt + 1) * 128],
                                                                                  rhs=wbf[bi][:, k, 0:64], start=(k == 0), stop=(k == 7)),
                             reads=[f"wbf{bi}", "hT"], writes=psk(bank), inc=(k == 7))
                post(tb, bank, ntt)

        wo_slot = [0]

        def outproj(l, g):
            for co in range(8):
                si = wo_slot[0] % 2
                wo_slot[0] += 1
                for kc in range(2):
                    r0 = l * 1024 + g * 256 + kc * 128
                    B.dma(wost[si][:, kc * 128:(kc + 1) * 128], wout_d[r0:r0 + 128, co * 128:(co + 1) * 128],
                          writes=[f"wost{si}"], q="sp", semkey=f"wost{si}")
                B.op("pool", lambda e, si=si: e.tensor_copy(out=wobf[si][:], in_=wost[si][:]),
                     reads=[f"wost{si}"], writes=[f"wobf{si}"])
                nb = min(2, NSEG)
                for half in range(NSEG // nb):
                    pb = 4 + 2 * ((co * (NSEG // nb) + half) % 2)
                    for j in range(nb):
                        sg = half * nb + j
                        for kc in range(2):
                            B.op("pe", lambda e, si=si, kc=kc, sg=sg, j=j, pb=pb: e.matmul(
                                PS(pb + j), lhsT=wobf[si][:, kc * 128:(kc + 1) * 128], rhs=obuf[:, kc, sg * 512:(sg + 1) * 512],
                                start=(kc == 0), stop=(kc == 1)),
                                reads=[f"wobf{si}", "obuf"], writes=psk(pb + j), inc=(kc == 1))
                    t0 = half * nb * 512
                    w = nb * 512
                    B.op("dve", lambda e, co=co, pb=pb, t0=t0, w=w, nb=nb: e.tensor_tensor(
                        out=xT[:, co, t0:t0 + w], in0=PS(pb, nb), in1=xT[:, co, t0:t0 + w], op=ALU.add),
                        reads=psk(pb, nb) + [("xT", co)], writes=[("xT", co)])

        def rmsnorm_to(l, dest_kind):
            for sg in range(NSEG):
                sl = slice(sg * 512, (sg + 1) * 512)
                for c in range(8):
                    B.op("act", lambda e, c=c: e.activation(out=sqb[:, c % 2, :], in_=xT[:, c, sl], func=AF.Square),
                         reads=[("xT", c)], writes=[("sqb", c % 2)])
                    B.op("pe", lambda e, c=c: e.matmul(PS(0), lhsT=cb("ones"), rhs=sqb[:, c % 2, :], start=(c == 0), stop=(c == 7)),
                         reads=[("sqb", c % 2), "cbf"], writes=psk(0), inc=True)
                B.op("act", lambda e: e.activation(out=rstd[:], in_=PS(0), func=AF.Ln, scale=1.0 / 1024, bias=epsb[:, 0:1]),
                     reads=psk(0) + ["epsb"], writes=["rstd"])
                B.op("act", lambda e: e.activation(out=rstd[:], in_=rstd[:], func=AF.Exp, scale=-0.5), reads=["rstd"], writes=["rstd"])
                for c in range(8):
                    if dest_kind == "h":
                        gcol = pvc("norm_g", l, c)
                        B.op("dve", lambda e, c=c, gcol=gcol: e.scalar_tensor_tensor(out=hT[:, c, sl], in0=xT[:, c, sl], scalar=gcol, in1=rstd[:],
                                                                                      op0=ALU.mult, op1=ALU.mult),
                             reads=[("xT", c), "rstd", "pvt"], writes=["hT"])
                    else:
                        gcol = pvt[:, L * PV_W + c:L * PV_W + c + 1]
                        B.op("dve", lambda e, c=c, gcol=gcol: e.scalar_tensor_tensor(out=xT[:, c, sl], in0=xT[:, c, sl], scalar=gcol, in1=rstd[:],
                                                                                      op0=ALU.mult, op1=ALU.mult),
                             reads=[("xT", c), "rstd", "pvt"], writes=[("xT", c)])

        def mixer_C(l):
            for cc in range(2):
                zb = next_bank()
                inproj(l, [(OFF_C + cc * 128, 128)], zb)
                Z = PS(zb, NSEG)
                zk = psk(zb, NSEG)
                xc = scr[0][:, 0:T]
                B.op("act", lambda e: e.activation(out=xc, in_=Z, func=AF.Identity, scale=pvc("cw", l, 2 * 2 + cc), bias=pvc("cb", l, cc)),
                     reads=zk + ["pvt"], writes=[sk(0)])
                for (j, sh) in ((0, -2), (1, -1), (3, 1)):
                    if sh < 0:
                        o_ap, i_ap = xc[:, -sh:T], Z[:, 0:T + sh]
                    else:
                        o_ap, i_ap = xc[:, 0:T - sh], Z[:, sh:T]
                    B.op("dve", lambda e, o_ap=o_ap, i_ap=i_ap, j=j: e.scalar_tensor_tensor(out=o_ap, in0=i_ap, scalar=pvc("cw", l, j * 2 + cc), in1=o_ap,
                                                                                            op0=ALU.mult, op1=ALU.add),
                         reads=zk + [sk(0), "pvt"], writes=[sk(0)])
                xcb = sbf(1)[:, 0:T]
                B.op("act", lambda e: e.activation(out=xcb, in_=xc, func=AF.Copy), reads=[sk(0)], writes=[sk(1)])
                for d in range(2):
                    rb, ib, sbuf_, hb = scr[2][:, 0:T], scr[3][:, 0:T], scr[4][:, 0:T], scr[5 + d][:, 0:T]
                    for sg in range(NSEG):
                        sl = slice(sg * 512, (sg + 1) * 512)
                        for k in range(2):
                            bank = 4 + 2 * k + sg % 2
                            wcol = (((d * 2 + k) * 2 + cc)) * 128
                            B.op("pe", lambda e, bank=bank, wcol=wcol, sl=sl: e.matmul(PS(bank), lhsT=gw_b[:, wcol:wcol + 128], rhs=xcb[:, sl], start=True, stop=True),
                                 reads=["gw_b", sk(1)], writes=psk(bank))
                            dst = rb if k == 0 else ib
                            B.op("act", lambda e, bank=bank, dst=dst, sl=sl, k=k: e.activation(out=dst[:, sl], in_=PS(bank), func=AF.Sigmoid,
                                                                                               bias=pvc("gb", l, (d * 2 + k) * 2 + cc)),
                                 reads=psk(bank) + ["pvt"], writes=[sk(2 + k)])
                    B.op("act", lambda e: e.activation(out=rb, in_=rb, func=AF.Exp, scale=dvc("cneg", l, d * 2 + cc)), reads=[sk(2), "dvt"], writes=[sk(2)])
                    B.op("act", lambda e: e.activation(out=sbuf_, in_=rb, func=AF.Square), reads=[sk(2)], writes=[sk(4)])
                    B.op("act", lambda e: e.activation(out=sbuf_, in_=sbuf_, func=AF.Ln, scale=-1.0, bias=epsb[:, 1:2]), reads=[sk(4), "epsb"], writes=[sk(4)])
                    B.op("act", lambda e: e.activation(out=sbuf_, in_=sbuf_, func=AF.Exp, scale=0.5), reads=[sk(4)], writes=[sk(4)])
                    B.op("dve", lambda e: e.tensor_tensor(out=ib, in0=ib, in1=xc, op=ALU.mult), reads=[sk(3), sk(0)], writes=[sk(3)])
                    B.op("dve", lambda e: e.tensor_tensor(out=ib, in0=ib, in1=sbuf_, op=ALU.mult), reads=[sk(3), sk(4)], writes=[sk(3)])
                    if d == 0:
                        B.op("dve", lambda e: e.tensor_tensor_scan(out=hb, data0=rb, data1=ib, initial=0.0, op0=ALU.mult, op1=ALU.add),
                             reads=[sk(2), sk(3)], writes=[sk(5 + d)])
                    else:
                        B.op("dve", lambda e: e.tensor_tensor_scan(out=hb[:, ::-1], data0=rb[:, ::-1], data1=ib[:, ::-1], initial=0.0, op0=ALU.mult, op1=ALU.add),
                             reads=[sk(2), sk(3)], writes=[sk(5 + d)])
                hf, hbk = scr[5][:, 0:T], scr[6][:, 0:T]
                B.op("dve", lambda e: e.tensor_tensor(out=hf, in0=hf, in1=hbk, op=ALU.add), reads=[sk(5), sk(6)], writes=[sk(5)])
                gb_ = next_bank()
                inproj(l, [(OFF_C + 256 + cc * 128, 128)], gb_)
                sgt = scr[2][:, 0:T]
                B.op("act", lambda e: e.activation(out=sgt, in_=PS(gb_, NSEG), func=AF.Silu), reads=psk(gb_, NSEG), writes=[sk(2)])
                B.op("dve", lambda e: e.tensor_tensor(out=obuf[:, cc, :], in0=hf, in1=sgt, op=ALU.mult), reads=[sk(5), sk(2)], writes=["obuf"])

        def kT_bufs():
            return [sbf(0)[:, SCRW:SCRW + T], sbf(5)[:, SCRW:SCRW + T]]

        def load_k_all(l, off_k, prep):
            kTs = kT_bufs()
            prep([(off_k, 128)], None, False)
            B.dma(kTs[0][64:128, :], kTs[0][0:64, :], reads=[("kT", 0)], writes=[("kT", 0)], q="sp", semkey="kdup0")
            B.dma(kTs[1][0:64, :], kTs[1][64:128, :], reads=[("kT", 1)], writes=[("kT", 1)], q="sp", semkey="kdup1")
            return kTs

        def load_q(l, off_q, cc, prep):
            qT = sbf(0)[:, 0:T]
            prep([(off_q + cc * 128, 128)], qT, True)
            return qT

        def load_vaug(l, off_v, cc):
            va = sbf(1)
            vav = va[:, 0:NT * 192].rearrange("p (t c) -> p t c", c=192)
            B.op("pool", lambda e: e.memset(vav[:, :, 64:128], 1.0), writes=[sk(1)])

            def post(tb, bank, ntt):
                src = PS(bank)[:, 0:ntt * 64].rearrange("p (t c) -> p t c", c=64)
                B.op("act", lambda e: e.activation(out=vav[:, tb * 8:tb * 8 + ntt, 0:64], in_=src, func=AF.Copy), reads=psk(bank), writes=[sk(1)])
                B.op("dve", lambda e: e.tensor_copy(out=vav[:, tb * 8:tb * 8 + ntt, 128:192], in_=src), reads=psk(bank), writes=[sk(1)])
            inproj_tok(l, [(off_v + cc * 64, 64)], None, post)
            return vav

        def load_gate(l, off_g, cc):
            gb_ = next_bank()
            inproj(l, [(off_g + cc * 128, 128)], gb_)
            sgt = scr[2][:, 0:T]
            B.op("act", lambda e: e.activation(out=sgt, in_=PS(gb_, NSEG), func=AF.Silu), reads=psk(gb_, NSEG), writes=[sk(2)])
            return sgt

        def attn_post(l, cc, qg, bx, by, sgt, sink):
            X, Y = PS(bx), PS(by)
            gsl = slice(qg * 512, (qg + 1) * 512)
            if sink:
                B.op("dve", lambda e: e.tensor_scalar(out=tmpA[0:64, :], in0=Y[0:64, :], scalar1=dvc("esinksw", l, cc)[0:64, :], scalar2=None, op0=ALU.add),
                     reads=psk(by) + ["dvt"], writes=["tmpA"])
                B.op("dve", lambda e: e.tensor_scalar(out=tmpA[64:128, :], in0=X[64:128, :], scalar1=dvc("esinksw", l, cc)[64:128, :], scalar2=None, op0=ALU.add),
                     reads=psk(bx) + ["dvt"], writes=["tmpA"])
                B.op("act", lambda e: e.activation(out=tmpA[:], in_=tmpA[:], func=AF.Ln), reads=["tmpA"], writes=["tmpA"])
            else:
                B.op("act", lambda e: e.activation(out=tmpA[0:64, :], in_=Y[0:64, :], func=AF.Ln), reads=psk(by), writes=["tmpA"])
                B.op("act", lambda e: e.activation(out=tmpA[64:128, :], in_=X[64:128, :], func=AF.Ln), reads=psk(bx), writes=["tmpA"])
            B.op("act", lambda e: e.activation(out=tmpA[:], in_=tmpA[:], func=AF.Exp, scale=-1.0), reads=["tmpA"], writes=["tmpA"])
            B.op("pe", lambda e: e.matmul(PS(0), lhsT=swapf[:], rhs=tmpA[:], start=True, stop=True), reads=["swapf", "tmpA"], writes=psk(0))
            B.op("dve", lambda e: e.tensor_tensor(out=tmpB[:], in0=PS(0), in1=sgt[:, gsl], op=ALU.mult), reads=psk(0) + [sk(2)], writes=["tmpB"])
            B.op("dve", lambda e: e.tensor_tensor(out=obuf[0:64, cc, gsl], in0=X[0:64, :], in1=tmpB[0:64, :], op=ALU.mult),
                 reads=psk(bx) + ["tmpB"], writes=["obuf"])
            B.op("dve", lambda e: e.tensor_tensor(out=obuf[64:128, cc, gsl], in0=Y[64:128, :], in1=tmpB[64:128, :], op=ALU.mult),
                 reads=psk(by) + ["tmpB"], writes=["obuf"])

        def mixer_D(l):
            B.dma(scr[3][:, 0:1536], alibi_d[:, :], writes=[sk(3)], q="pool", semkey="alibi")

            def prep(ranges, dst, isq):
                pb_ = next_bank()
                inproj(l, ranges, pb_)
                if isq:
                    B.op("act", lambda e: e.activation(out=dst, in_=PS(pb_, NSEG), func=AF.Copy), reads=psk(pb_, NSEG), writes=[sk(0)])
                else:
                    kTs_ = kT_bufs()
                    B.op("act", lambda e: e.activation(out=kTs_[0][0:64, :], in_=PS(pb_, NSEG)[0:64, :], func=AF.Copy), reads=psk(pb_, NSEG), writes=[("kT", 0)])
                    B.op("act", lambda e: e.activation(out=kTs_[1][64:128, :], in_=PS(pb_, NSEG)[64:128, :], func=AF.Copy), reads=psk(pb_, NSEG), writes=[("kT", 1)])
            kTs = load_k_all(l, OFF_D + 256, prep)
            for cc in range(2):
                qT = load_q(l, OFF_D, cc, prep)
                kT = kTs[cc]
                kkey = ("kT", cc)
                vav = load_vaug(l, OFF_D + 384, cc)
                sgt = load_gate(l, OFF_D + 512, cc)
                ptv = sbf(4)
                exv = sbf(5)
                atab = scr[3][:, 2 * cc * 384:(2 * cc + 2) * 384].rearrange("p (h c) -> p h c", h=2)

                def qrange(j):
                    return max(j - 1, 0), min(j + 1, NT - 1)

                def qk(j):
                    b0, b1 = qrange(j)
                    ncol = (b1 - b0 + 1) * 128
                    for hh in range(2):
                        ph = slice(hh * 64, hh * 64 + 64)
                        bank = 2 * (j % 2) + hh
                        B.op("pe", lambda e: e.matmul(PS(bank)[:, 0:ncol], lhsT=kT[ph, j * 128:(j + 1) * 128], rhs=qT[ph, b0 * 128:b0 * 128 + ncol], start=True, stop=True),
                             reads=[sk(0), kkey], writes=psk(bank), inc=(hh == 1))

                def ex(j):
                    b0, b1 = qrange(j)
                    ncol = (b1 - b0 + 1) * 128
                    tcol0 = (b0 - (j - 1)) * 128
                    src = PS(2 * (j % 2), 2).rearrange("p (b c) -> p b c", b=2)[:, :, 0:ncol]
                    exs = exv[:, (j % 2) * 768:(j % 2 + 1) * 768].rearrange("p (h c) -> p h c", h=2)[:, :, 0:ncol]
                    pts = ptv[:, (j % 4) * 768:(j % 4 + 1) * 768].rearrange("p (h c) -> p h c", h=2)[:, :, 0:ncol]
                    B.op("act", lambda e: e.activation(out=exs, in_=src, func=AF.Exp, scale=0.125), reads=psk(2 * (j % 2), 2), writes=[("ex", j % 2)])
                    B.op("dve", lambda e: e.tensor_tensor(out=pts, in0=exs, in1=atab[:, :, tcol0:tcol0 + ncol], op=ALU.mult),
                         reads=[("ex", j % 2), sk(3)], writes=[("pt", j % 4)])

                def pv_block(i):
                    qg = i // 4
                    js = [j for j in (i - 1, i, i + 1) if 0 <= j < NT]
                    for hh in range(2):
                        bank = 4 + 2 * (qg % 2) + hh
                        for n, j in enumerate(js):
                            b0, _ = qrange(j)
                            c0 = (j % 4) * 768 + hh * 384 + (i - b0) * 128
                            B.op("pe", lambda e: e.matmul(PS(bank)[:, (i % 4) * 128:(i % 4 + 1) * 128], lhsT=vav[:, j, hh * 64:hh * 64 + 128], rhs=ptv[:, c0:c0 + 128],
                                                         start=(n == 0), stop=(n == len(js) - 1)),
                                 reads=[sk(1), ("pt", j % 4)], writes=psk(bank), inc=(n == len(js) - 1))

                qk(0)
                for j in range(NT):
                    if j + 1 < NT:
                        qk(j + 1)
                    ex(j)
                    blocks = []
                    if j >= 1:
                        blocks.append(j - 1)
                    if j == NT - 1:
                        blocks.append(j)
                    for i in blocks:
                        pv_block(i)
                        if i % 4 == 3:
                            qg = i // 4
                            attn_post(l, cc, qg, 4 + 2 * (qg % 2), 4 + 2 * (qg % 2) + 1, sgt, True)

        def mixer_B(l):
            B.dma(scr[6][:, 0:T], ropeC_d[:, :], writes=[sk(6)], q="pool", semkey="ropeC")
            B.dma(scr[7][:, 0:T], ropeS_d[:, :], writes=[sk(7)], q="pool", semkey="ropeS")
            t1s = [scr[3][:, 0:512], scr[3][:, 1536:2048]]
            t2s = [scr[3][:, 512:1024], scr[5][:, 0:512]]
            qgs = [sbf(3)[:, 2048:2560], sbf(3)[:, 2560:3072]]
            rss = [rstd, tmpA]
            rsk = ["rstd", "tmpA"]

            def prep(ranges, dst, isq):
                inproj(l, ranges, 0)
                gcol = pvc("qn" if isq else "kn", l, 0)
                for sg in range(NSEG):
                    sl = slice(sg * 512, (sg + 1) * 512)
                    Z = PS(sg)
                    u = sg % 2
                    qgb, t1b, t2b, rs_, rk_ = qgs[u], t1s[u], t2s[u], rss[u], rsk[u]
                    b4, b5 = 4 + 2 * u, 5 + 2 * u
                    B.op("act", lambda e: e.activation(out=qgb, in_=Z, func=AF.Copy, scale=gcol), reads=psk(sg) + ["pvt"], writes=[("qgb", u)])
                    B.op("act", lambda e: e.activation(out=sqb[:, u, :], in_=Z, func=AF.Square), reads=psk(sg), writes=[("sqb", u)])
                    B.op("pe", lambda e: e.matmul(PS(b4), lhsT=cb("blk64"), rhs=sqb[:, u, :], start=True, stop=True), reads=[("sqb", u), "cbf"], writes=psk(b4))
                    B.op("pe", lambda e: e.matmul(PS(b5), lhsT=cb("perm"), rhs=qgb, start=True, stop=True), reads=[("qgb", u), "cbf"], writes=psk(b5))
                    B.op("act", lambda e: e.activation(out=rs_[:], in_=PS(b4), func=AF.Ln, scale=1.0 / 64, bias=epsb[:, 0:1]), reads=psk(b4) + ["epsb"], writes=[rk_])
                    B.op("act", lambda e: e.activation(out=rs_[:], in_=rs_[:], func=AF.Exp, scale=-0.5), reads=[rk_], writes=[rk_])
                    B.op("pool", lambda e: e.tensor_tensor(out=t1b, in0=qgb, in1=scr[6][:, sl], op=ALU.mult), reads=[("qgb", u), sk(6)], writes=[("t1b", u)])
                    B.op("dve", lambda e: e.tensor_tensor(out=t2b, in0=PS(b5), in1=scr[7][:, sl], op=ALU.mult), reads=psk(b5) + [sk(7)], writes=[("t2b", u)])
                    B.op("dve", lambda e: e.tensor_tensor(out=t1b, in0=t1b, in1=t2b, op=ALU.add), reads=[("t1b", u), ("t2b", u)], writes=[("t1b", u)])
                    if isq:
                        B.op("dve", lambda e: e.tensor_tensor(out=dst[:, sl], in0=t1b, in1=rs_[:], op=ALU.mult), reads=[("t1b", u), rk_], writes=[sk(0)])
                    else:
                        kTs_ = kT_bufs()
                        B.op("dve", lambda e: e.tensor_tensor(out=kTs_[0][0:64, sl], in0=t1b[0:64, :], in1=rs_[0:64, :], op=ALU.mult), reads=[("t1b", u), rk_], writes=[("kT", 0)])
                        B.op("dve", lambda e: e.tensor_tensor(out=kTs_[1][64:128, sl], in0=t1b[64:128, :], in1=rs_[64:128, :], op=ALU.mult), reads=[("t1b", u), rk_], writes=[("kT", 1)])
            kTs = load_k_all(l, OFF_B + 256, prep)
            for cc in range(2):
                qT = load_q(l, OFF_B, cc, prep)
                kT = kTs[cc]
                kkey = ("kT", cc)
                vav = load_vaug(l, OFF_B + 384, cc)
                sgt = load_gate(l, OFF_B + 512, cc)
                ptv = sbf(4)
                for qg in range(NSEG):
                    gsl = slice(qg * 512, (qg + 1) * 512)
                    ab = 4 + 2 * (qg % 2)

                    def qk(j):
                        for hh in range(2):
                            ph = slice(hh * 64, hh * 64 + 64)
                            bank = 2 * (j % 2) + hh
                            B.op("pe", lambda e: e.matmul(PS(bank), lhsT=kT[ph, j * 128:(j + 1) * 128], rhs=qT[ph, gsl], start=True, stop=True),
                                 reads=[sk(0), kkey], writes=psk(bank), inc=(hh == 1))

                    def ex(j):
                        slot = j % 3
                        B.op("act", lambda e: e.activation(out=ptv[:, slot * 1024:(slot + 1) * 1024], in_=PS(2 * (j % 2), 2), func=AF.Exp, scale=0.125),
                             reads=psk(2 * (j % 2), 2), writes=[("pt", slot)])

                    def pv(j):
                        slot = j % 3
                        for hh in range(2):
                            B.op("pe", lambda e: e.matmul(PS(ab + hh), lhsT=vav[:, j, hh * 64:hh * 64 + 128], rhs=ptv[:, slot * 1024 + hh * 512:slot * 1024 + (hh + 1) * 512],
                                                         start=(j == 0), stop=(j == NT - 1)),
                                 reads=[sk(1), ("pt", slot)], writes=psk(ab + hh), inc=(hh == 1))
                    qk(0)
                    for j in range(NT):
                        if j + 1 < NT:
                            qk(j + 1)
                        ex(j)
                        pv(j)
                    attn_post(l, cc, qg, ab, ab + 1, sgt, False)

        def mixer_A(l):
            SA = min(512, T)
            NS = T // SA
            nch = SA // 64
            arena_b = hT[:].rearrange("p k t -> p (k t)")

            def AF32(off, w):
                return arena_b[:, 2 * off:2 * (off + w)].bitcast(F32)

            def ABF(off, w):
                return arena_b[:, 2 * off:2 * off + w]

            rB = [sbf(0)[:, 0:T], sbf(0)[:, SCRW:SCRW + T]]
            kB = [sbf(1)[:, 0:T], sbf(1)[:, SCRW:SCRW + T]]
            vB = [sbf(2)[:, 0:T], sbf(2)[:, SCRW:SCRW + T]]
            waB = sbf(3)[:, 0:T]
            kkB = [sbf(4)[:, 0:T], sbf(4)[:, SCRW:SCRW + T]]
            yfB = [sbf(5)[:, 0:T], sbf(5)[:, SCRW:SCRW + T]]
            bcB = [sbf(6)[:, 0:T], sbf(6)[:, SCRW:SCRW + T]]
            for ci in range(7):
                pb0 = 4 * (ci % 2)
                xi = 7 if ci % 2 == 0 else 5
                Xt = scr[xi][:, 0:T]
                inproj(l, [(OFF_A + ci * 128, 128)], pb0)
                Z = PS(pb0, NSEG)
                zk = psk(pb0, NSEG)
                B.op("act", lambda e: e.activation(out=Xt, in_=Z, func=AF.Copy, scale=dvc("c0", l, ci)), reads=zk + ["dvt"], writes=[sk(xi)])
                B.op("dve", lambda e: e.scalar_tensor_tensor(out=Xt[:, 1:T], in0=Z[:, 0:T - 1], scalar=pvc("sh0", l, ci), in1=Xt[:, 1:T], op0=ALU.mult, op1=ALU.add),
                     reads=zk + [sk(xi), "pvt"], writes=[sk(xi)])
                if ci < 6:
                    dst = (rB, kB, vB)[ci // 2][ci % 2]
                    dk_ = sk(ci // 2)
                    B.op("dve", lambda e: e.scalar_tensor_tensor(out=dst[:, 0:T - 1], in0=Z[:, 1:T], scalar=pvc("sh1", l, ci), in1=Xt[:, 0:T - 1], op0=ALU.mult, op1=ALU.add),
                         reads=zk + [sk(xi), "pvt"], writes=[dk_])
                    B.op("dve", lambda e: e.tensor_copy(out=dst[:, T - 1:T], in_=Xt[:, T - 1:T]), reads=[sk(xi)], writes=[dk_])
                else:
                    B.op("dve", lambda e: e.scalar_tensor_tensor(out=Xt[:, 0:T - 1], in0=Z[:, 1:T], scalar=pvc("sh1", l, ci), in1=Xt[:, 0:T - 1], op0=ALU.mult, op1=ALU.add),
                         reads=zk + [sk(xi), "pvt"], writes=[sk(xi)])
                    B.op("act", lambda e: e.activation(out=waB[0:64, :], in_=Xt[0:64, :], func=AF.Tanh), reads=[sk(xi)], writes=[sk(3)])
                    B.op("act", lambda e: e.activation(out=waB[64:128, :], in_=Xt[64:128, :], func=AF.Copy), reads=[sk(xi)], writes=[sk(3)])
            for hp in range(2):
                for sg in range(NSEG):
                    sl = slice(sg * 512, (sg + 1) * 512)
                    B.op("dve", lambda e: e.tensor_scalar(out=tmpA[:], in0=kB[hp][:, sl], scalar1=pvc("k_k", l, hp), scalar2=None, op0=ALU.mult),
                         reads=[sk(1), "pvt"], writes=["tmpA"])
                    B.op("act", lambda e: e.activation(out=sqb[:, 0, :], in_=tmpA[:], func=AF.Square), reads=["tmpA"], writes=[("sqb", 0)])
                    B.op("pe", lambda e: e.matmul(PS(4), lhsT=cb("blk64"), rhs=sqb[:, 0, :], start=True, stop=True), reads=[("sqb", 0), "cbf"], writes=psk(4))
                    B.op("act", lambda e: e.activation(out=rstd[:], in_=PS(4), func=AF.Ln, bias=epsb[:, 3:4]), reads=psk(4) + ["epsb"], writes=["rstd"])
                    B.op("act", lambda e: e.activation(out=rstd[:], in_=rstd[:], func=AF.Exp, scale=-0.5), reads=["rstd"], writes=["rstd"])
                    B.op("dve", lambda e: e.tensor_tensor(out=kkB[hp][:, sl], in0=tmpA[:], in1=rstd[:], op=ALU.mult), reads=["tmpA", "rstd"], writes=[sk(4)])
            for cc in range(2):
                gb_ = next_bank()
                inproj(l, [(OFF_A + 896 + cc * 128, 128)], gb_)
                B.op("act", lambda e: e.activation(out=obuf[:, cc, :], in_=PS(gb_, NSEG), func=AF.Silu), reads=psk(gb_, NSEG), writes=["obuf"])
            wepoch[0] += 1
            B.barrier()
            o = 0
            Fb = []
            for i in range(5):
                Fb.append(AF32(o, SA)); o += SA
            H0 = AF32(o, SA); o += SA
            H1 = AF32(o, SA); o += SA
            H2 = ABF(o, SA); o += SA // 2
            BK = ABF(o, 2 * SA); o += SA
            ARb = ABF(o, 2 * SA); o += SA
            HK = ABF(o, 2 * SA); o += SA
            Gm = ABF(o, 4 * SA); o += 2 * SA
            GT = ABF(o, SA); o += SA // 2
            Xs, XTs, Ps_ = [], [], []
            for i in range(2):
                Xs.append(ABF(o, SA)); o += SA // 2
                XTs.append(ABF(o, SA)); o += SA // 2
                Ps_.append(ABF(o, SA)); o += SA // 2
            assert o <= 8192, o
            s7 = sbf(7)
            VT, BhT, KhT = s7[:, 0:SA], s7[:, SA:2 * SA], s7[:, 2 * SA:3 * SA]
            Q1 = scr[7][:, 768:768 + SA]
            Qb = s7[:, 2560:2624]
            UTb = [s7[:, 2624:2688], s7[:, 2688:2752]]
            STf = [[scr[7][:, 1408 + (hp * 2 + i) * 64:1408 + (hp * 2 + i + 1) * 64] for i in range(2)] for hp in range(2)]
            STb = [[s7[:, 3328 + (hp * 2 + i) * 64:3328 + (hp * 2 + i + 1) * 64] for i in range(2)] for hp in range(2)]
            id64 = s7[:, 3600:3664]
            blkf = scr[7][:, 1856:1984]
            F0, F1, F2, F3, F4 = Fb
            B.op("dve", lambda e: e.tensor_tensor(out=id64, in0=cb("mfwd", 64, 64), in1=cb("mfwd", 64, 0), op=ALU.subtract), reads=["cbf"], writes=["id64"])
            B.op("dve", lambda e: e.tensor_copy(out=blkf, in_=cb("blk64")), reads=["cbf"], writes=["blkf"])
            BK4 = BK.rearrange("p (c two j) -> p c two j", two=2, j=64)
            AR4 = ARb.rearrange("p (c two j) -> p c two j", two=2, j=64)
            HK4 = HK.rearrange("p (c two j) -> p c two j", two=2, j=64)
            Gm3 = Gm.rearrange("p (c x) -> p c x", x=256)

            def v3(ap):
                return ap.rearrange("p (c j) -> p c j", j=64)

            def hs(hd):
                return slice(hd * 64, hd * 64 + 64)

            Wf = [wst[i][:].rearrange("p k c -> p (k c)") for i in range(2)]
            fin = [Wf[0][:, 0:SA], Wf[0][:, SA:2 * SA], Wf[1][:, 0:SA], Wf[1][:, SA:2 * SA]]
            fink = ["wst0", "wst0", "wst1", "wst1"]
            sets = []
            sets.append(dict(BK=BK, AR=ARb, HK=HK, kBK="BK", kAR="AR", kHK="HK", gi=0))
            wfl = [wbf[i][:].rearrange("p k c -> p (k c)") for i in range(3)]
            sets.append(dict(BK=wfl[0][:, 0:2 * SA], AR=wfl[1][:, 0:2 * SA], HK=wfl[2][:, 0:2 * SA], kBK="wbf0", kAR="wbf1", kHK="wbf2", gi=1))
            for S_ in sets:
                for nm in ("BK", "AR", "HK"):
                    S_[nm + "4"] = S_[nm].rearrange("p (c two j) -> p c two j", two=2, j=64)
            upwv = upw_f[:].bitcast(BF16)
            AR4s = [ARb.rearrange("p (c two j) -> p c two j", two=2, j=64), wfl[1][:, 0:2 * SA].rearrange("p (c two j) -> p c two j", two=2, j=64),
                    upwv[:, 0:2 * SA].rearrange("p (c two j) -> p c two j", two=2, j=64)]
            kARs = ["AR", "wbf1", "upw_f"]
            Gms = [Gm, sbf(3)[:, 2 * SCRW:2 * SCRW + 4 * SA] if False else sbf(3)[:, SCRW:SCRW + 4 * SA]]
            kGms = ["Gm", "scr3hi"]
            tAv, tBv = tmpA[:].bitcast(BF16), tmpB[:].bitcast(BF16)
            VTs, BhTs, KhTs = [VT, tAv[:, 0:SA]], [BhT, tAv[:, SA:2 * SA]], [KhT, tBv[:, 0:SA]]
            kVTs, kBhTs, kKhTs = ["VT", "tmpA_lo"], ["BhT", "tmpA_hi"], ["KhT", "tmpB"]
            Q1s, kQ1s = [Q1, rstd[:, 0:SA]], ["Q1", "rstd"]
            Tms, kTms = [sqb[:, 0, 0:SA], sqb[:, 1, 0:SA]], [("sqb", 0), ("sqb", 1)]
            iters = []
            for d in range(2):
                for sg in (range(NS) if d == 0 else range(NS - 1, -1, -1)):
                    for hp in range(2):
                        iters.append((d, sg, hp))
            cur = [0, 0]

            def pre_gen(ii):
                d, sg, hp = iters[ii]
                S_ = sets[ii % 2]
                BK4, HK4 = S_["BK4"], S_["HK4"]
                AR4 = AR4s[ii % 3]
                kAR_ = kARs[ii % 3]
                gi = ii % 3
                endc = 63 if d == 0 else 0
                ssl = slice(sg * SA, (sg + 1) * SA)
                first_of_dir = (sg == (0 if d == 0 else NS - 1))
                wc = d * 256 + hp * 128
                B.op("pe", lambda e: e.matmul(PS(2)[:, 0:SA], lhsT=upw_b[0:64, wc:wc + 128], rhs=waB[0:64, ssl], start=True, stop=True),
                     reads=["upw_b", sk(3)], writes=psk(2))
                B.op("pe", lambda e: e.matmul(PS(3)[:, 0:SA], lhsT=upw_b[64:128, wc:wc + 128], rhs=waB[64:128, ssl], start=True, stop=True),
                     reads=["upw_b", sk(3)], writes=psk(3))
                yield
                B.op("act", lambda e: e.activation(out=F0, in_=PS(2)[:, 0:SA], func=AF.Tanh, scale=0.5, bias=dvc("w0h", l, d * 2 + hp)), reads=psk(2) + ["dvt"], writes=["F0"])
                B.op("act", lambda e: e.activation(out=F1, in_=PS(3)[:, 0:SA], func=AF.Tanh, scale=0.5, bias=dvc("a0h", l, d * 2 + hp)), reads=psk(3) + ["dvt"], writes=["F1"])
                yield
                B.op("pool", lambda e: e.tensor_scalar(out=F0, in0=F0, scalar1=0.5, scalar2=0.5, op0=ALU.mult, op1=ALU.add), reads=["F0"], writes=["F0"])
                B.op("pool", lambda e: e.tensor_scalar(out=F1, in0=F1, scalar1=0.5, scalar2=0.5, op0=ALU.mult, op1=ALU.add), reads=["F1"], writes=["F1"])
                yield
                rst = cb("rst", SA)
                if d == 0:
                    B.op("dve", lambda e: e.tensor_tensor_scan(out=F2, data0=rst, data1=F0, initial=0.0, op0=ALU.mult, op1=ALU.add), reads=["cbf", "F0"], writes=["F2"])
                else:
                    B.op("dve", lambda e: e.tensor_tensor_scan(out=F2[:, ::-1], data0=rst, data1=F0[:, ::-1], initial=0.0, op0=ALU.mult, op1=ALU.add),
                         reads=["cbf", "F0"], writes=["F2"])
                yield
                B.op("dve", lambda e: e.tensor_tensor(out=F3, in0=F2, in1=F0, op=ALU.subtract), reads=["F2", "F0"], writes=["F3"])
                yield
                endb = v3(F2)[:, :, endc:endc + 1].broadcast_to([128, nch, 64])
                B.op("dve", lambda e: e.tensor_tensor(out=v3(F0), in0=endb, in1=v3(F2), op=ALU.subtract), reads=["F2"], writes=["F0"])
                yield
                B.op("act", lambda e: e.activation(out=F4, in_=F2, func=AF.Exp, scale=LAMW), reads=["F2"], writes=["F4"])
                B.op("act", lambda e: e.activation(out=F2, in_=F2, func=AF.Exp, scale=-LAMW), reads=["F2"], writes=["F2"])
                yield
                B.op("act", lambda e: e.activation(out=F3, in_=F3, func=AF.Exp, scale=-LAMW), reads=["F3"], writes=["F3"])
                B.op("act", lambda e: e.activation(out=F0, in_=F0, func=AF.Exp, scale=-LAMW), reads=["F0"], writes=["F0"])
                yield
                gv = gam[:, gi, 0:nch]
                B.op("dve", lambda e: e.tensor_copy(out=gv, in_=v3(F2)[:, :, endc]), reads=["F2"], writes=[("gam", gi)])
                kk_, r_, k_, v_ = kkB[hp][:, ssl], rB[hp][:, ssl], kB[hp][:, ssl], vB[hp][:, ssl]
                B.op("pool", lambda e: e.tensor_tensor(out=H0, in0=kk_, in1=F1, op=ALU.mult), reads=[sk(4), "F1"], writes=["H0"])
                yield
                B.op("pool", lambda e: e.tensor_scalar(out=F1, in0=F1, scalar1=pvc("k_a", l, hp), scalar2=dvc("omka", l, hp), op0=ALU.mult, op1=ALU.add),
                     reads=["F1", "pvt", "dvt"], writes=["F1"])
                yield
                B.op("pool", lambda e: e.tensor_tensor(out=H1, in0=F1, in1=k_, op=ALU.mult), reads=["F1", sk(1)], writes=["H1"])
                yield
                B.op("pool", lambda e: e.tensor_tensor(out=BK4[:, :, 0, :], in0=v3(H0), in1=v3(F4), op=ALU.mult), reads=["H0", "F4"], writes=[S_["kBK"]])
                yield
                B.op("pool", lambda e: e.tensor_tensor(out=BK4[:, :, 1, :], in0=v3(H1), in1=v3(F4), op=ALU.mult), reads=["H1", "F4"], writes=[S_["kBK"]])
                yield
                B.op("dve", lambda e: e.scalar_tensor_tensor(out=AR4[:, :, 0, :], in0=v3(kk_), scalar=-1.0, in1=v3(F3), op0=ALU.mult, op1=ALU.mult),
                     reads=[sk(4), "F3"], writes=[kAR_])
                yield
                B.op("pool", lambda e: e.tensor_tensor(out=AR4[:, :, 1, :], in0=v3(r_), in1=v3(F2), op=ALU.mult), reads=[sk(0), "F2"], writes=[kAR_])
                yield
                B.op("pool", lambda e: e.tensor_tensor(out=HK4[:, :, 0, :], in0=v3(H0), in1=v3(F0), op=ALU.mult), reads=["H0", "F0"], writes=[S_["kHK"]])
                yield
                B.op("pool", lambda e: e.tensor_tensor(out=HK4[:, :, 1, :], in0=v3(H1), in1=v3(F0), op=ALU.mult), reads=["H1", "F0"], writes=[S_["kHK"]])
                yield
                B.op("dve", lambda e: e.scalar_tensor_tensor(out=H2, in0=r_, scalar=pvc("r_k", l, hp), in1=H1, op0=ALU.mult, op1=ALU.mult),
                     reads=[sk(0), "pvt", "H1"], writes=["H2"])
                B.op("pe", lambda e: e.matmul(PS(2)[:, 0:SA], lhsT=cb("blk64"), rhs=H2, start=True, stop=True), reads=["cbf", "H2"], writes=psk(2))
                yield
                if d == 0:
                    B.op("act", lambda e: e.activation(out=bcB[hp][:, ssl], in_=PS(2)[:, 0:SA], func=AF.Copy), reads=psk(2), writes=[("bc", hp)])
                else:
                    B.op("dve", lambda e: e.tensor_tensor(out=bcB[hp][:, ssl], in0=PS(2)[:, 0:SA], in1=bcB[hp][:, ssl], op=ALU.add), reads=psk(2) + [("bc", hp)], writes=[("bc", hp)])
                yield

            def inv_gen(ii):
                d, sg, hp = iters[ii]
                S_ = sets[ii % 2]
                BK4, HK4 = S_["BK4"], S_["HK4"]
                kBK, kHK = S_["kBK"], S_["kHK"]
                AR4, kAR = AR4s[ii % 3], kARs[ii % 3]
                gi = ii % 3
                Gm, kGm = Gms[ii % 2], kGms[ii % 2]
                Gm3 = Gm.rearrange("p (c x) -> p c x", x=256)
                VT, BhT, KhT = VTs[ii % 2], BhTs[ii % 2], KhTs[ii % 2]
                kVT, kBhT, kKhT = kVTs[ii % 2], kBhTs[ii % 2], kKhTs[ii % 2]
                Q1, kQ1 = Q1s[ii % 2], kQ1s[ii % 2]
                Tm, tkey = Tms[ii % 2], kTms[ii % 2]
                mname = "mfwd" if d == 0 else "mbwd"
                mTname = "mfwdT" if d == 0 else "mbwdT"
                ssl = slice(sg * SA, (sg + 1) * SA)
                v_ = vB[hp][:, ssl]
                n_items = 2 * nch
                Gps = PS(4, 4)
                n_items = 2 * nch
                it = 0
                for c in range(nch):
                    for hd in range(2):
                        it += 1
                        last = (it == n_items)
                        B.op("pe", lambda e: e.matmul(Gps[hs(hd), c * 256:c * 256 + 128], lhsT=BK4[hs(hd), c, 0, :], rhs=AR4[hs(hd), c, :, :], start=True, stop=True),
                             reads=[kBK, kAR], writes=psk(4, 4), inc=False)
                        B.op("pe", lambda e: e.matmul(Gps[hs(hd), c * 256 + 128:c * 256 + 256], lhsT=BK4[hs(hd), c, 1, :], rhs=AR4[hs(hd), c, :, :], start=True, stop=True),
                             reads=[kBK, kAR], writes=psk(4, 4), inc=last)
                mk = cb(mname).rearrange("p (o x) -> p o x", o=1).broadcast_to([128, nch, 256])
                B.op("dve", lambda e: e.tensor_tensor(out=Gm3, in0=Gps[:, 0:nch * 256].rearrange("p (c x) -> p c x", x=256), in1=mk, op=ALU.mult),
                     reads=psk(4, 4) + ["cbf"], writes=[kGm])
                it = 0
                for c in range(nch):
                    for hd in range(2):
                        it += 1
                        B.op("pe", lambda e: e.matmul(PS(4)[hs(hd), c * 64:(c + 1) * 64], lhsT=AR4[hs(hd), c, 0, :], rhs=BK4[hs(hd), c, 0, :], start=True, stop=True),
                             reads=[kBK, kAR], writes=psk(4), inc=(it == n_items))
                mkT = cb(mTname).rearrange("p (o x) -> p o x", o=1).broadcast_to([128, nch, 64])
                B.op("dve", lambda e: e.tensor_tensor(out=v3(GT), in0=v3(PS(4)[:, 0:SA]), in1=mkT, op=ALU.mult), reads=psk(4) + ["cbf"], writes=["GT"])
                yield
                idb = id64.rearrange("p (o x) -> p o x", o=1).broadcast_to([128, nch, 64])
                B.op("dve", lambda e: e.tensor_tensor(out=v3(Ps_[0]), in0=Gm3[:, :, 0:64], in1=idb, op=ALU.add), reads=[kGm, "id64"], writes=[("P", 0)])

                def Xap(level, buf, hd, c):
                    if level == 0:
                        return Gm[hs(hd), c * 256:c * 256 + 64]
                    return Xs[buf][hs(hd), c * 64:(c + 1) * 64]

                def XTap(level, buf, hd, c):
                    if level == 0:
                        return GT[hs(hd), c * 64:(c + 1) * 64]
                    return XTs[buf][hs(hd), c * 64:(c + 1) * 64]
                for r in range(1, 7):
                    src_b, dst_b = (r - 1) % 2, r % 2
                    xk_src = [kGm] if r == 1 else [("X", src_b)]
                    xtk_src = ["GT"] if r == 1 else [("XT", src_b)]
                    do_x, do_xt, do_p = (r <= 4), (r <= 5), (r >= 2)
                    psrc, pdst = (r - 2) % 2, (r - 1) % 2
                    it = 0
                    for c in range(nch):
                        for hd in range(2):
                            it += 1
                            last = (it == n_items)
                            if do_x:
                                B.op("pe", lambda e: e.matmul(PS(5)[hs(hd), c * 64:(c + 1) * 64], lhsT=XTap(r - 1, src_b, hd, c), rhs=Xap(r - 1, src_b, hd, c), start=True, stop=True),
                                     reads=xk_src + xtk_src, writes=psk(5), inc=False)
                            if do_xt:
                                B.op("pe", lambda e: e.matmul(PS(6)[hs(hd), c * 64:(c + 1) * 64], lhsT=Xap(r - 1, src_b, hd, c), rhs=XTap(r - 1, src_b, hd, c), start=True, stop=True),
                                     reads=xk_src + xtk_src, writes=psk(6), inc=(last and not do_p))
                            if do_p:
                                B.op("pe", lambda e: e.matmul(PS(7)[hs(hd), c * 64:(c + 1) * 64], lhsT=XTap(r - 1, src_b, hd, c), rhs=Ps_[psrc][hs(hd), c * 64:(c + 1) * 64], start=True, stop=True),
                                     reads=xtk_src + [("P", psrc)], writes=psk(7), inc=last)
                            if it % 4 == 0 and not last:
                                yield
                    if do_x:
                        B.op("act", lambda e: e.activation(out=Xs[dst_b], in_=PS(5)[:, 0:SA], func=AF.Copy), reads=psk(5), writes=[("X", dst_b)])
                    if do_xt:
                        B.op("act", lambda e: e.activation(out=XTs[dst_b], in_=PS(6)[:, 0:SA], func=AF.Copy),
                             reads=psk(6), writes=[("XT", dst_b)])
                    if do_p:
                        B.op("dve", lambda e: e.tensor_tensor(out=(Tm if r == 6 else Ps_[pdst]), in0=PS(7)[:, 0:SA], in1=Ps_[psrc], op=ALU.add), reads=psk(7) + [("P", psrc)], writes=[(tkey if r == 6 else ("P", pdst))])
                    yield
                for (srcf, dstb, dkey, rk_, bank) in ((lambda c, hd: v_[hs(hd), c * 64:(c + 1) * 64], VT, kVT, [sk(2)], 4),
                                                       (lambda c, hd: HK4[hs(hd), c, 0, :], BhT, kBhT, [kHK], 5),
                                                       (lambda c, hd: HK4[hs(hd), c, 1, :], KhT, kKhT, [kHK], 6)):
                    pb = PS(bank).bitcast(BF16)
                    it = 0
                    for c in range(nch):
                        for hd in range(2):
                            it += 1
                            B.op("pe", lambda e: e.transpose(pb[hs(hd), c * 64:(c + 1) * 64], srcf(c, hd), cb("ident")[hs(hd), hd * 64:hd * 64 + 64]),
                                 reads=rk_ + ["cbf"], writes=psk(bank), inc=(it == n_items))
                    B.op("act", lambda e: e.activation(out=dstb, in_=pb[:, 0:SA], func=AF.Copy), reads=psk(bank), writes=[dkey])
                    yield
                it = 0
                for c in range(nch):
                    for hd in range(2):
                        it += 1
                        B.op("pe", lambda e: e.matmul(PS(7)[hs(hd), c * 64:(c + 1) * 64], lhsT=Gm[hs(hd), c * 256 + 128:c * 256 + 192], rhs=VT[hs(hd), c * 64:(c + 1) * 64], start=True, stop=True),
                             reads=[kGm, kVT], writes=psk(7), inc=(it == n_items))
                B.op("dve", lambda e: e.tensor_copy(out=Q1, in_=PS(7)[:, 0:SA]), reads=psk(7), writes=[kQ1])
                yield

            def chain_gen(ii):
                d, sg, hp = iters[ii]
                S_ = sets[ii % 2]
                BK4, HK4 = S_["BK4"], S_["HK4"]
                kBK, kHK = S_["kBK"], S_["kHK"]
                AR4, kAR = AR4s[ii % 3], kARs[ii % 3]
                gi = ii % 3
                Gm, kGm = Gms[ii % 2], kGms[ii % 2]
                Gm3 = Gm.rearrange("p (c x) -> p c x", x=256)
                VT, BhT, KhT = VTs[ii % 2], BhTs[ii % 2], KhTs[ii % 2]
                kVT, kBhT, kKhT = kVTs[ii % 2], kBhTs[ii % 2], kKhTs[ii % 2]
                Q1, kQ1 = Q1s[ii % 2], kQ1s[ii % 2]
                Tm, tkey = Tms[ii % 2], kTms[ii % 2]
                mname = "mfwd" if d == 0 else "mbwd"
                mTname = "mfwdT" if d == 0 else "mbwdT"
                ssl = slice(sg * SA, (sg + 1) * SA)
                v_ = vB[hp][:, ssl]
                n_items = 2 * nch
                if sg == (0 if d == 0 else NS - 1):
                    cur[hp] = 0
                    B.op("dve", lambda e: e.memset(STf[hp][0], 0.0), writes=[("STf", hp, 0)])
                    B.op("dve", lambda e: e.memset(STb[hp][0], 0.0), writes=[("STb", hp, 0)])
                corder = range(nch) if d == 0 else range(nch - 1, -1, -1)
                for ci_, c in enumerate(corder):
                    cu = cur[hp]
                    nx = 1 - cu
                    cs_ = slice(c * 64, (c + 1) * 64)
                    ub = ci_ % 2
                    for hd in range(2):
                        B.op("pe", lambda e: e.matmul(PS(0)[hs(hd), 0:64], lhsT=AR4[hs(hd), c, 0, :], rhs=STb[hp][cu][hs(hd), :], start=True, stop=True),
                             reads=[kAR, ("STb", hp, cu)], writes=psk(0), inc=(hd == 1))
                    yield
                    B.op("dve", lambda e: e.tensor_tensor(out=Qb, in0=PS(0)[:, 0:64], in1=Q1[:, cs_], op=ALU.add), reads=psk(0) + [kQ1], writes=["Qb"])
                    for hd in range(2):
                        B.op("pe", lambda e: e.matmul(PS(0)[hs(hd), 64:128], lhsT=Tm[hs(hd), cs_], rhs=Qb[hs(hd), :], start=True, stop=True),
                             reads=[tkey, "Qb"], writes=psk(0), inc=(hd == 1))
                    yield
                    B.op("dve", lambda e: e.tensor_copy(out=UTb[ub], in_=PS(0)[:, 64:128]), reads=psk(0), writes=[("UTb", ub)])
                    for hd in range(2):
                        B.op("pe", lambda e: e.matmul(PS(0)[hs(hd), 128:192], lhsT=BhT[hs(hd), cs_], rhs=UTb[ub][hs(hd), :], start=True, stop=False),
                             reads=[kBhT, ("UTb", ub)], writes=psk(0), inc=False)
                        B.op("pe", lambda e: e.matmul(PS(0)[hs(hd), 128:192], lhsT=KhT[hs(hd), cs_], rhs=VT[hs(hd), cs_], start=False, stop=True),
                             reads=[kKhT, kVT], writes=psk(0), inc=(hd == 1))
                    yield
                    gcol = gam[:, gi, c:c + 1]
                    B.op("dve", lambda e: e.scalar_tensor_tensor(out=STb[hp][nx], in0=STf[hp][cu], scalar=gcol, in1=PS(0)[:, 128:192], op0=ALU.mult, op1=ALU.add),
                         reads=[("STf", hp, cu), ("gam", gi)] + psk(0), writes=[("STb", hp, nx)])
                    B.op("dve", lambda e: e.scalar_tensor_tensor(out=STf[hp][nx], in0=STf[hp][cu], scalar=gcol, in1=PS(0)[:, 128:192], op0=ALU.mult, op1=ALU.add),
                         reads=[("STf", hp, cu), ("gam", gi)] + psk(0), writes=[("STf", hp, nx)])
                    for hd in range(2):
                        B.op("pe", lambda e: e.matmul(PS(1)[hs(hd), cs_], lhsT=STb[hp][cu][hs(hd), :], rhs=AR4[hs(hd), c, 1, :], start=True, stop=False),
                             reads=[("STb", hp, cu), kAR], writes=psk(1), inc=False)
                        B.op("pe", lambda e: e.matmul(PS(1)[hs(hd), cs_], lhsT=UTb[ub][hs(hd), :], rhs=Gm[hs(hd), c * 256 + 64:c * 256 + 128], start=False, stop=False),
                             reads=[("UTb", ub), kGm], writes=psk(1), inc=False)
                        B.op("pe", lambda e: e.matmul(PS(1)[hs(hd), cs_], lhsT=VT[hs(hd), cs_], rhs=Gm[hs(hd), c * 256 + 192:c * 256 + 256], start=False, stop=True),
                             reads=[kVT, kGm], writes=psk(1), inc=(hd == 1))
                    cur[hp] = nx
                    yield
                if d == 0:
                    B.op("act", lambda e: e.activation(out=yfB[hp][:, ssl], in_=PS(1)[:, 0:SA], func=AF.Copy), reads=psk(1), writes=[("yf", hp)])
                else:
                    B.op("dve", lambda e: e.tensor_tensor(out=fin[0], in0=PS(1)[:, 0:SA], in1=yfB[hp][:, ssl], op=ALU.add), reads=psk(1) + [("yf", hp)], writes=[fink[0]])
                    B.op("pe", lambda e: e.matmul(PS(0)[:, 0:SA], lhsT=blkf, rhs=fin[0], start=True, stop=True), reads=["blkf", fink[0]], writes=psk(0))
                    B.op("dve", lambda e: e.scalar_tensor_tensor(out=fin[1], in0=PS(0)[:, 0:SA], scalar=-1.0 / 64, in1=fin[0], op0=ALU.mult, op1=ALU.add), reads=psk(0) + [fink[0]], writes=[fink[1]])
                    B.op("act", lambda e: e.activation(out=fin[2], in_=fin[1], func=AF.Square), reads=[fink[1]], writes=[fink[2]])
                    B.op("pe", lambda e: e.matmul(PS(0)[:, 0:SA], lhsT=blkf, rhs=fin[2], start=True, stop=True), reads=["blkf", fink[2]], writes=psk(0))
                    B.op("act", lambda e: e.activation(out=fin[3], in_=PS(0)[:, 0:SA], func=AF.Ln, scale=1.0 / 64, bias=epsb[:, 2:3]), reads=psk(0) + ["epsb"], writes=[fink[3]])
                    B.op("act", lambda e: e.activation(out=fin[3], in_=fin[3], func=AF.Exp, scale=-0.5), reads=[fink[3]], writes=[fink[3]])
                    B.op("dve", lambda e: e.tensor_tensor(out=fin[1], in0=fin[1], in1=fin[3], op=ALU.mult), reads=[fink[1], fink[3]], writes=[fink[1]])
                    B.op("dve", lambda e: e.tensor_scalar(out=fin[1], in0=fin[1], scalar1=pvc("ln_g", l, hp), scalar2=pvc("ln_b", l, hp), op0=ALU.mult, op1=ALU.add), reads=[fink[1], "pvt"], writes=[fink[1]])
                    B.op("dve", lambda e: e.tensor_tensor(out=fin[2], in0=bcB[hp][:, ssl], in1=v_, op=ALU.mult), reads=[("bc", hp), sk(2)], writes=[fink[2]])
                    B.op("dve", lambda e: e.tensor_tensor(out=fin[1], in0=fin[1], in1=fin[2], op=ALU.add), reads=[fink[1], fink[2]], writes=[fink[1]])
                    B.op("dve", lambda e: e.tensor_tensor(out=obuf[:, hp, ssl], in0=fin[1], in1=obuf[:, hp, ssl], op=ALU.mult), reads=[fink[1], "obuf"], writes=["obuf"])


                yield

            def run_all():
                n = len(iters)
                for _ in pre_gen(0):
                    pass
                if n > 1:
                    gens0 = [pre_gen(1), inv_gen(0)]
                else:
                    gens0 = [inv_gen(0)]
                _rr(gens0)
                for s_ in range(n):
                    gens = [chain_gen(s_)]
                    if s_ + 1 < n:
                        gens.append(inv_gen(s_ + 1))
                    if s_ + 2 < n:
                        gens.append(pre_gen(s_ + 2))
                    _rr(gens)

            def _rr(gens):
                import os
                gens = list(gens)
                mode = os.environ.get("A_MODE", "cip")
                if os.environ.get("A_SEQ"):
                    mode = ""
                names = {"chain_gen": "c", "inv_gen": "i", "pre_gen": "p"}
                conc = [g for g in gens if names[g.gi_code.co_name] in mode]
                seq = [g for g in gens if names[g.gi_code.co_name] not in mode]
                while conc:
                    for g in list(conc):
                        try:
                            next(g)
                        except StopIteration:
                            conc.remove(g)
                for g in seq:
                    for _ in g:
                        pass
            run_all()
            B.barrier()
            wepoch[0] += 1

        def program():
            B.dma(scr[0][:, 0:CST_W], cst_d[:, :], writes=[sk(0)], q="pool", semkey="cst")
            B.dma(pvt[:], pv_d[:, :], writes=["pvt"], q="pool", semkey="pvt")
            B.op("dve", lambda e: e.tensor_copy(out=cbf[:], in_=scr[0][:, 0:CST_W]), reads=[sk(0)], writes=["cbf"])
            B.op("dve", lambda e: e.tensor_copy(out=swapf[:], in_=scr[0][:, CST_OFF["swap"]:CST_OFF["swap"] + 128]), reads=[sk(0)], writes=["swapf"])
            B.op("dve", lambda e: e.memset(epsb[:, 0:1], NORM_EPS), writes=["epsb"])
            B.op("dve", lambda e: e.memset(epsb[:, 1:2], 1.0), writes=["epsb"])
            B.op("dve", lambda e: e.memset(epsb[:, 2:3], GN_EPS), writes=["epsb"])
            B.op("dve", lambda e: e.memset(epsb[:, 3:4], 1e-18), writes=["epsb"])
            for l in range(L):
                for i in range(7):
                    B.op("dve", lambda e, l=l, i=i: e.tensor_tensor(out=dvc("c0", l, i), in0=pvc("sh0", l, i), in1=pvc("sh1", l, i), op=ALU.add),
                         reads=["pvt"], writes=["dvt"])
                    B.op("dve", lambda e, l=l, i=i: e.tensor_scalar(out=dvc("c0", l, i), in0=dvc("c0", l, i), scalar1=-1.0, scalar2=1.0, op0=ALU.mult, op1=ALU.add),
                         reads=["dvt"], writes=["dvt"])
                for i in range(2):
                    B.op("dve", lambda e, l=l, i=i: e.tensor_scalar(out=dvc("omka", l, i), in0=pvc("k_a", l, i), scalar1=-1.0, scalar2=1.0, op0=ALU.mult, op1=ALU.add),
                         reads=["pvt"], writes=["dvt"])
                    B.op("act", lambda e, l=l, i=i: e.activation(out=dvc("esinksw", l, i), in_=pvc("sinksw", l, i), func=AF.Exp),
                         reads=["pvt"], writes=["dvt"])
                for i in range(4):
                    B.op("dve", lambda e, l=l, i=i: e.tensor_scalar(out=dvc("w0h", l, i), in0=pvc("w0", l, i), scalar1=0.5, scalar2=None, op0=ALU.mult),
                         reads=["pvt"], writes=["dvt"])
                    B.op("dve", lambda e, l=l, i=i: e.tensor_scalar(out=dvc("a0h", l, i), in0=pvc("a0", l, i), scalar1=0.5, scalar2=None, op0=ALU.mult),
                         reads=["pvt"], writes=["dvt"])
                    B.op("act", lambda e, l=l, i=i: e.activation(out=dvc("cneg", l, i), in_=pvc("lam", l, i), func=AF.Exp, scale=-1.0),
                         reads=["pvt"], writes=["dvt"])
                    B.op("act", lambda e, l=l, i=i: e.activation(out=dvc("cneg", l, i), in_=dvc("cneg", l, i), func=AF.Ln, bias=epsb[:, 1:2]),
                         reads=["dvt", "epsb"], writes=["dvt"])
                    B.op("dve", lambda e, l=l, i=i: e.tensor_scalar(out=dvc("cneg", l, i), in0=dvc("cneg", l, i), scalar1=-8.0, scalar2=None, op0=ALU.mult),
                         reads=["dvt"], writes=["dvt"])


            for s in range(NSEQ):
                for c in range(8):
                    B.dma(xT[:, c, :], x_d[s * 1024 + c * 128:s * 1024 + (c + 1) * 128, :], writes=[("xT", c)], q="pool", semkey=f"xin{c}")
                for l in range(L):
                    rmsnorm_to(l, "h")
                    B.dma(scr[7][:, 0:1024], gw_d[:, l * 1024:(l + 1) * 1024], writes=[sk(7)], q="pool", semkey="gw")
                    B.op("pool", lambda e: e.tensor_copy(out=gw_b[:], in_=scr[7][:, 0:1024]), reads=[sk(7)], writes=["gw_b"])
                    B.dma(upw_f[:], upw_d[:, l * 512:(l + 1) * 512], writes=["upw_f"], q="pool", semkey="upw")
                    B.op("pool", lambda e: e.tensor_copy(out=upw_b[:], in_=upw_f[:]), reads=["upw_f"], writes=["upw_b"])
                    if "C" in cfg.MIX:
                        mixer_C(l)
                        B.barrier()
                        outproj(l, 2)
                    if "D" in cfg.MIX:
                        mixer_D(l)
                        B.barrier()
                        outproj(l, 3)
                    if "B" in cfg.MIX:
                        mixer_B(l)
                        B.barrier()
                        outproj(l, 1)
                    if "A" in cfg.MIX:
                        mixer_A(l)
                        B.barrier()
                        outproj(l, 0)
                rmsnorm_to(None, "x")
                for c in range(8):
                    B.dma(y_d[s * 1024 + c * 128:s * 1024 + (c + 1) * 128, :], xT[:, c, :], reads=[("xT", c)], q="pool", semkey=f"yout{c}")
            toks = []
            for c in range(8):
                toks.extend(B.readers.get(("xT", c), {}).values())
            for tname in tap_d:
                t = B.last_w.get(("tap", tname))
                if t is not None:
                    toks.append(t)
            B._wait("pool", toks)

        realB = B
        B = NullBuilder()
        wstate["rec"] = True
        program()
        B = realB
        wstate["rec"] = False
        wepoch[0] = 0
        wo_slot[0] = 0
        program()
        print(f"[build] instructions: {B.n_ins}")
    return nc


_NC_CACHE = {}


def run_trunk(cfg, xs, inp):
    T, L = cfg.T, cfg.DEPTH
    ncore = cfg.NCORES
    assert xs.shape[0] == ncore * cfg.NSEQ
    key = (cfg.T, cfg.NSEQ, cfg.DEPTH, cfg.MIX, cfg.taps)
    if key not in _NC_CACHE:
        _NC_CACHE[key] = build(cfg)
    nc = _NC_CACHE[key]
    cst, alibi, C, S = make_consts(T)
    pv, upw, gw = pack_params(inp, L)
    w_in = np.ascontiguousarray(np.asarray(inp["w_in"], np.float32)[:L].reshape(L * 1024, D_IN))
    w_out = np.ascontiguousarray(np.asarray(inp["w_out"], np.float32)[:L].reshape(L * 1024, 1024))
    in_maps = []
    for c in range(ncore):
        xc = xs[c * cfg.NSEQ:(c + 1) * cfg.NSEQ]
        xt = np.ascontiguousarray(xc.transpose(0, 2, 1)).reshape(cfg.NSEQ * 1024, T)
        in_maps.append({"x": xt, "w_in": w_in, "w_out": w_out, "pvec": pv, "upw": upw, "gw": gw, "cst": cst,
                        "alibi": alibi, "ropeC": C, "ropeS": S})
    res = run_bass_kernel_spmd(nc, in_maps, core_ids=list(range(ncore)))
    outs = []
    for c in range(ncore):
        yt = np.asarray(res.results[c]["y"]).reshape(cfg.NSEQ, 1024, T)
        outs.append(yt.transpose(0, 2, 1))
    return np.ascontiguousarray(np.concatenate(outs, 0)).astype(np.float32), res


def kernel(**inputs):
    inp = {k: np.asarray(v) for k, v in inputs.items()}
    xp, xs_ = inp["x_prompt"], inp["x_sample"]
    xs = np.concatenate([xp, xs_], 0).astype(np.float32)
    cfg = Cfg(T=xs.shape[1], NSEQ=xs.shape[0] // 8, DEPTH=inp["w_in"].shape[0], MIX="ABCD", NCORES=8)
    y, _ = run_trunk(cfg, xs, inp)
    return (y[:xp.shape[0]], y[xp.shape[0]:])
```

```python
import math
from contextlib import ExitStack
import numpy as np
import ml_dtypes
import concourse.bass as bass
import concourse.mybir as mybir
from concourse.bass_utils import run_bass_kernel_spmd

F32 = mybir.dt.float32
BF16 = mybir.dt.bfloat16
AF = mybir.ActivationFunctionType
ALU = mybir.AluOpType

D_MODEL = 1024
GRID_W = 64
D_IN = 3200
A_W, B_W, C_W, D_W = 1152, 768, 512, 768
OFF_A, OFF_B, OFF_C, OFF_D = 0, 1152, 1920, 2432
NORM_EPS = 1e-6
GN_EPS = 64e-5
LAMW = math.exp(-0.5)
SEM_LIMIT = 30000


class Cfg:
    def __init__(self, T=2048, NSEQ=5, DEPTH=4, MIX="ABCD", NCORES=8, taps=()):
        self.T, self.NSEQ, self.DEPTH, self.MIX, self.NCORES = T, NSEQ, DEPTH, MIX, NCORES
        self.taps = tuple(taps)


PV_FIELDS = [("norm_g", 8), ("sh0", 7), ("sh1", 7), ("w0", 4), ("a0", 4), ("k_k", 2), ("k_a", 2), ("r_k", 2),
             ("ln_g", 2), ("ln_b", 2), ("qn", 1), ("kn", 1), ("cw", 8), ("cb", 2), ("gb", 8), ("lam", 4),
             ("sink", 2), ("sinksw", 2)]
DV_FIELDS = [("c0", 7), ("omka", 2), ("cneg", 4), ("esinksw", 2), ("gbh", 8), ("w0h", 4), ("a0h", 4)]


def _offsets(fields):
    off, d = 0, {}
    for n, w in fields:
        d[n] = off
        off += w
    return d, off


PV_OFF, PV_W = _offsets(PV_FIELDS)
DV_OFF, DV_W = _offsets(DV_FIELDS)


def pv_col(L, name, l, i=0):
    if name == "final_g":
        return L * PV_W + i
    return l * PV_W + PV_OFF[name] + i


def dv_col(name, l, i=0):
    return l * DV_W + DV_OFF[name] + i


CST_FIELDS = [("ident", 128), ("blk64", 128), ("ones", 128), ("swap", 128), ("perm", 128), ("mfwd", 256),
              ("mbwd", 256), ("mfwdT", 64), ("mbwdT", 64), ("rst", 512)]
CST_OFF, CST_W = _offsets(CST_FIELDS)


def make_consts(T):
    c = np.zeros((128, CST_W), np.float32)
    alibi = np.zeros((128, 4 * 384), np.float32)
    p = np.arange(128)
    o = CST_OFF
    c[:, o["ident"]:o["ident"] + 128] = np.eye(128)
    c[:, o["blk64"]:o["blk64"] + 128] = (p[:, None] // 64 == p[None, :] // 64)
    c[:, o["ones"]:o["ones"] + 128] = 1.0
    c[:, o["swap"]:o["swap"] + 128] = (p[:, None] == (p[None, :] + 64) % 128)
    d = p % 64
    partner = np.where((d % 32) < 16, p + 16, p - 16)
    c[:, o["perm"]:o["perm"] + 128] = (p[:, None] == partner[None, :])
    s = (p % 64)[:, None]
    t = np.arange(64)[None, :]
    strict_f, incl_f = (s < t), (s <= t)
    strict_b, incl_b = (s > t), (s >= t)
    c[:, o["mfwd"]:o["mfwd"] + 256] = np.concatenate([strict_f, incl_f, strict_f, incl_f], 1)
    c[:, o["mbwd"]:o["mbwd"] + 256] = np.concatenate([strict_b, incl_b, strict_b, incl_b], 1)
    c[:, o["mfwdT"]:o["mfwdT"] + 64] = (t < s)
    c[:, o["mbwdT"]:o["mbwdT"] + 64] = (t > s)
    c[:, o["rst"]:o["rst"] + 512] = (np.arange(512)[None, :] % 64 != 0)
    cc = np.arange(384)[None, :]
    dist = np.abs(cc - 128 - p[:, None]).astype(np.float64)
    for h in range(4):
        slope = 2.0 ** (-8.0 * (h + 1) / 4)
        e = np.where(dist <= 128, np.exp(-slope * dist), 0.0)
        alibi[:, h * 384:(h + 1) * 384] = e
    row = (np.arange(T) // GRID_W).astype(np.float64)
    col = (np.arange(T) % GRID_W).astype(np.float64)
    inv = 10000.0 ** (-np.arange(0, 32, 2, dtype=np.float64) / 32)
    C = np.zeros((128, T), np.float32)
    S = np.zeros((128, T), np.float32)
    for pp in range(128):
        dd = pp % 64
        pos = row if dd < 32 else col
        f = inv[dd % 16]
        ang = pos * f
        C[pp] = np.cos(ang)
        S[pp] = (-np.sin(ang)) if (dd % 32) < 16 else np.sin(ang)
    return c, alibi, C, S


def pack_params(inp, L):
    pv = np.zeros((128, L * PV_W + 8), np.float32)

    def put(name, l, i, vec128):
        pv[:, pv_col(L, name, l, i)] = vec128

    for l in range(L):
        for i in range(8):
            put("norm_g", l, i, inp["norm_g"][l, i * 128:(i + 1) * 128])
        for i in range(7):
            put("sh0", l, i, inp["rwkv_shift"][l, 0, i * 128:(i + 1) * 128])
            put("sh1", l, i, inp["rwkv_shift"][l, 1, i * 128:(i + 1) * 128])
        for d in range(2):
            for hp in range(2):
                put("w0", l, d * 2 + hp, inp["rwkv_w0"][l, d, hp * 128:(hp + 1) * 128])
                put("a0", l, d * 2 + hp, inp["rwkv_a0"][l, d, hp * 128:(hp + 1) * 128])
        rk = inp["rwkv_r_k"][l].reshape(256)
        for hp in range(2):
            sl = slice(hp * 128, (hp + 1) * 128)
            put("k_k", l, hp, inp["rwkv_k_k"][l, sl])
            put("k_a", l, hp, inp["rwkv_k_a"][l, sl])
            put("r_k", l, hp, rk[sl])
            put("ln_g", l, hp, inp["rwkv_ln_g"][l, sl])
            put("ln_b", l, hp, inp["rwkv_ln_b"][l, sl])
        put("qn", l, 0, np.tile(inp["attn_q_norm"][l], 2))
        put("kn", l, 0, np.tile(inp["attn_k_norm"][l], 2))
        for j in range(4):
            for cc in range(2):
                put("cw", l, j * 2 + cc, inp["lru_conv_w"][l, j, cc * 128:(cc + 1) * 128])
        for cc in range(2):
            put("cb", l, cc, inp["lru_conv_b"][l, cc * 128:(cc + 1) * 128])
        for d in range(2):
            for k in range(2):
                for cc in range(2):
                    put("gb", l, (d * 2 + k) * 2 + cc, inp["lru_gate_b"][l, d, k, cc * 128:(cc + 1) * 128])
            for cc in range(2):
                put("lam", l, d * 2 + cc, inp["lru_lambda"][l, d, cc * 128:(cc + 1) * 128])
        sk = inp["swa_sink"][l]
        for cc in range(2):
            put("sink", l, cc, np.repeat(sk[2 * cc:2 * cc + 2], 64))
            put("sinksw", l, cc, np.repeat(sk[2 * cc:2 * cc + 2][::-1], 64))
    for i in range(8):
        pv[:, L * PV_W + i] = inp["final_g"][i * 128:(i + 1) * 128]
    upw = np.zeros((128, L * 2 * 256), np.float32)
    for l in range(L):
        for d in range(2):
            upw[0:64, (l * 2 + d) * 256:(l * 2 + d + 1) * 256] = inp["rwkv_w_up"][l, d]
            upw[64:128, (l * 2 + d) * 256:(l * 2 + d + 1) * 256] = inp["rwkv_a_up"][l, d]
    gw = np.zeros((128, L * 8 * 128), np.float32)
    for l in range(L):
        for d in range(2):
            for k in range(2):
                for cc in range(2):
                    base = (((l * 2 + d) * 2 + k) * 2 + cc) * 128
                    for b in range(2):
                        gw[b * 64:(b + 1) * 64, base + b * 64:base + (b + 1) * 64] = inp["lru_gate_w"][l, d, k, 2 * cc + b]
    return pv, upw, gw


class SemCounter:
    def __init__(self, bld, name, step):
        self.bld, self.name, self.step = bld, name, step
        self.n = 0
        self._new()

    def _new(self):
        self.sem = self.bld.es.enter_context(self.bld.nc.semaphore(f"{self.name}_{self.n}"))
        self.sname = f"{self.name}_{self.n}"
        self.n += 1
        self.val = 0

    def next(self):
        if self.val + self.step > SEM_LIMIT:
            self._new()
        self.val += self.step
        return (self.sname, self.sem, self.val)


class Builder:
    def __init__(self, nc, es):
        self.nc, self.es = nc, es
        self.eng = {"pe": nc.tensor, "act": nc.scalar, "dve": nc.vector, "pool": nc.gpsimd, "sp": nc.sync}
        self.cnt = {e: SemCounter(self, e, 1) for e in ("pe", "act", "dve", "pool")}
        self.dcnt = {}
        self.waited = {e: {} for e in self.eng}
        self.last_w = {}
        self.readers = {}
        self.pending = {e: ([], []) for e in self.eng}
        self.n_ins = 0

    def _deps(self, reads, writes):
        toks = []
        for k in reads:
            t = self.last_w.get(k)
            if t is not None:
                toks.append(t)
        for k in writes:
            t = self.last_w.get(k)
            if t is not None:
                toks.append(t)
            r = self.readers.get(k)
            if r:
                toks.extend(r.values())
        return toks

    def _wait(self, e, toks):
        w = self.waited[e]
        need = {}
        for (sn, sem, val) in toks:
            if w.get(sn, 0) < val and need.get(sn, (None, 0))[1] < val:
                need[sn] = (sem, val)
        for sn, (sem, val) in need.items():
            self.eng[e].wait_ge(sem, val)
            w[sn] = val

    def _commit(self, tok, reads, writes):
        for k in writes:
            self.last_w[k] = tok
            self.readers[k] = {}
        for k in reads:
            self.readers.setdefault(k, {})[tok[0]] = tok

    def op(self, e, fn, reads=(), writes=(), inc=True):
        self._wait(e, self._deps(reads, writes))
        ins = fn(self.eng[e])
        self.n_ins += 1
        pr, pw = self.pending[e]
        if not inc:
            pr.extend(reads)
            pw.extend(writes)
            return
        tok = self.cnt[e].next()
        ins.then_inc(tok[1], 1)
        self._commit(tok, list(reads) + pr, list(writes) + pw)
        self.pending[e] = ([], [])

    def dma(self, out, in_, reads=(), writes=(), q="sp", semkey=None):
        self._wait(q, self._deps(reads, writes))
        ins = self.eng[q].dma_start(out=out, in_=in_)
        self.n_ins += 1
        if semkey not in self.dcnt:
            self.dcnt[semkey] = SemCounter(self, "d" + semkey, 16)
        tok = self.dcnt[semkey].next()
        ins.then_inc(tok[1], 16)
        self._commit(tok, reads, writes)
        return tok

    def barrier(self):
        toks = []
        for c in list(self.cnt.values()) + list(self.dcnt.values()):
            if c.val > 0:
                toks.append((c.sname, c.sem, c.val))
        for e in self.eng:
            self._wait(e, toks)

    def final_wait(self, e, keys):
        toks = []
        for k in keys:
            t = self.last_w.get(k)
            if t is not None:
                toks.append(t)
        self._wait(e, toks)


class NullBuilder:
    def __init__(self):
        self.readers, self.last_w, self.n_ins = {}, {}, 0

    def op(self, *a, **k):
        pass

    def dma(self, *a, **k):
        pass

    def barrier(self):
        pass

    def _wait(self, *a, **k):
        pass


SCRW = 2048
NSCR = 8


def build(cfg):
    T, NSEQ, L = cfg.T, cfg.NSEQ, cfg.DEPTH
    NSEG, NT, NCH = T // 512, T // 128, T // 64
    assert T % 512 == 0 and T <= 2048
    nc = bass.Bass("TRN2", target_bir_lowering=False)

    def din(name, shape, dt=F32):
        return nc.dram_tensor(name, list(shape), dt, kind="ExternalInput").ap()

    x_d = din("x", [NSEQ * 1024, T])
    win_d = din("w_in", [L * 1024, D_IN])
    wout_d = din("w_out", [L * 1024, 1024])
    pv_d = din("pvec", [128, L * PV_W + 8])
    upw_d = din("upw", [128, L * 512])
    gw_d = din("gw", [128, L * 1024])
    cst_d = din("cst", [128, CST_W])
    alibi_d = din("alibi", [128, 1536])
    ropeC_d = din("ropeC", [128, T])
    ropeS_d = din("ropeS", [128, T])
    y_d = nc.dram_tensor("y", [NSEQ * 1024, T], F32, kind="ExternalOutput").ap()
    tap_d = {}
    for (tname, tshape) in cfg.taps:
        tap_d[tname] = nc.dram_tensor("tap_" + tname, list(tshape), F32, kind="ExternalOutput").ap()

    es = ExitStack()
    with es:
        B = Builder(nc, es)

        def sb(name, shape, dt):
            return es.enter_context(nc.sbuf_tensor(name, list(shape), dt))

        xT = sb("xT", [128, 8, T], F32)
        hT = sb("hT", [128, 8, max(T, 2048)], BF16)
        pvt = sb("pvt", [128, L * PV_W + 8], F32)
        dvt = sb("dvt", [128, L * DV_W], F32)
        CBW = CST_W
        cbf = sb("cbf", [128, CBW], BF16)
        swapf = sb("swapf", [128, 128], F32)
        upw_f = sb("upw_f", [128, 512], F32)
        upw_b = sb("upw_b", [128, 512], BF16)
        gw_b = sb("gw_b", [128, 1024], BF16)
        NWS = 3
        wst = [sb(f"wst{i}", [128, 8, 128], F32) for i in range(2)]
        wbf = [sb(f"wbf{i}", [128, 8, 128], BF16) for i in range(NWS)]
        wost = [sb(f"wost{i}", [128, 256], F32) for i in range(2)]
        wobf = [sb(f"wobf{i}", [128, 256], BF16) for i in range(2)]
        obuf = sb("obuf", [128, 2, T], BF16)
        sqb = sb("sqb", [128, 2, 512], BF16)
        rstd = sb("rstd", [128, 512], F32)
        epsb = sb("epsb", [128, 4], F32)
        tmpA = sb("tmpA", [128, 512], F32)
        tmpB = sb("tmpB", [128, 512], F32)
        gam = sb("gam", [128, 3, 8], F32)
        scr = [sb(f"scr{i}", [128, SCRW], F32) for i in range(NSCR)]
        psum = es.enter_context(nc.psum_tensor("psum", [128, 8 * 512], F32))

        def sk(i):
            return ("scr", i)

        def sbf(i):
            return scr[i][:].bitcast(BF16)

        def PS(b, n=1):
            return psum[:, b * 512:(b + n) * 512]

        def psk(b, n=1):
            return [("ps", b + i) for i in range(n)]

        def cb(name, w=None, off=0):
            o = CST_OFF[name] + off
            return cbf[:, o:o + (w if w is not None else dict(CST_FIELDS)[name])]

        def pvc(name, l, i=0):
            c = pv_col(L, name, l, i)
            return pvt[:, c:c + 1]

        def dvc(name, l, i=0):
            c = dv_col(name, l, i)
            return dvt[:, c:c + 1]

        def tap(name, ap_sb, rkeys, rows=128):
            if name in tap_d:
                B.dma(tap_d[name], ap_sb, reads=rkeys, writes=[("tap", name)], q="pool", semkey="tap" + name)

        wepoch = [0]
        wplan = []
        wstate = {"rec": True, "next": 0, "issued": 0}

        def _issue_w(i):
            l, ranges, ep = wplan[i]
            si, bi = i % 2, i % NWS
            off = 0
            src = win_d[l * 1024:(l + 1) * 1024, :].rearrange("(k p) c -> p k c", p=128)
            for (c0, n) in ranges:
                B.dma(wst[si][:, :, off:off + n], src[:, :, c0:c0 + n], writes=[f"wst{si}"], q="sp", semkey=f"wst{si}")
                off += n
            B.op("pool", lambda e: e.tensor_copy(out=wbf[bi][:, :, 0:off], in_=wst[si][:, :, 0:off]),
                 reads=[f"wst{si}"], writes=[f"wbf{bi}"])

        def load_win(l, ranges):
            m = sum(n for _, n in ranges)
            if wstate["rec"]:
                wplan.append((l, tuple(ranges), wepoch[0]))
                return 0, m
            i = wstate["next"]
            wstate["next"] += 1
            assert wplan[i][0] == l and wplan[i][1] == tuple(ranges), (i, wplan[i], l, ranges)
            hi = i
            while hi + 1 < len(wplan) and hi + 1 <= i + 2 and wplan[hi + 1][2] == wplan[i][2]:
                hi += 1
            while wstate["issued"] <= hi:
                _issue_w(wstate["issued"])
                wstate["issued"] += 1
            return i % NWS, m

        ipb = [0]

        def next_bank():
            b = ipb[0]
            ipb[0] = 4 - b
            return b

        def inproj(l, ranges, bank0):
            bi, m = load_win(l, ranges)
            for sg in range(NSEG):
                for k in range(8):
                    B.op("pe", lambda e, k=k, sg=sg: e.matmul(PS(bank0 + sg)[0:m, :], lhsT=wbf[bi][:, k, 0:m], rhs=hT[:, k, sg * 512:(sg + 1) * 512],
                                                              start=(k == 0), stop=(k == 7)),
                         reads=[f"wbf{bi}", "hT"], writes=psk(bank0 + sg), inc=(k == 7))
            return m

        def inproj_tok(l, ranges, dst_fn, post):
            bi, m = load_win(l, ranges)
            assert m == 64
            tb0 = next_bank()
            for tb in range((NT + 7) // 8):
                bank = tb0 + tb % 2
                ntt = min(8, NT - tb * 8)
                for q in range(ntt):
                    tt = tb * 8 + q
                    for k in range(8):
                        B.op("pe", lambda e, k=k, tt=tt, q=q, bank=bank: e.matmul(PS(bank)[:, q * 64:(q + 1) * 64], lhsT=hT[:, k, tt * 128:(tt + 1) * 128],
                                                                                  rhs=wbf[bi][:, k, 0:64], start=(k == 0), stop=(k == 7)),
                             reads=[f"wbf{bi}", "hT"], writes=psk(bank), inc=(k == 7))
                post(tb, bank, ntt)

        wo_slot = [0]

        def outproj(l, g):
            for co in range(8):
                si = wo_slot[0] % 2
                wo_slot[0] += 1
                for kc in range(2):
                    r0 = l * 1024 + g * 256 + kc * 128
                    B.dma(wost[si][:, kc * 128:(kc + 1) * 128], wout_d[r0:r0 + 128, co * 128:(co + 1) * 128],
                          writes=[f"wost{si}"], q="sp", semkey=f"wost{si}")
                B.op("pool", lambda e, si=si: e.tensor_copy(out=wobf[si][:], in_=wost[si][:]),
                     reads=[f"wost{si}"], writes=[f"wobf{si}"])
                nb = min(2, NSEG)
                for half in range(NSEG // nb):
                    pb = 4 + 2 * ((co * (NSEG // nb) + half) % 2)
                    for j in range(nb):
                        sg = half * nb + j
                        for kc in range(2):
                            B.op("pe", lambda e, si=si, kc=kc, sg=sg, j=j, pb=pb: e.matmul(
                                PS(pb + j), lhsT=wobf[si][:, kc * 128:(kc + 1) * 128], rhs=obuf[:, kc, sg * 512:(sg + 1) * 512],
                                start=(kc == 0), stop=(kc == 1)),
                                reads=[f"wobf{si}", "obuf"], writes=psk(pb + j), inc=(kc == 1))
                    t0 = half * nb * 512
                    w = nb * 512
                    B.op("dve", lambda e, co=co, pb=pb, t0=t0, w=w, nb=nb: e.tensor_tensor(
                        out=xT[:, co, t0:t0 + w], in0=PS(pb, nb), in1=xT[:, co, t0:t0 + w], op=ALU.add),
                        reads=psk(pb, nb) + [("xT", co)], writes=[("xT", co)])

        def rmsnorm_to(l, dest_kind):
            for sg in range(NSEG):
                sl = slice(sg * 512, (sg + 1) * 512)
                rs_, rk_ = (rstd, "rstd") if sg % 2 == 0 else (tmpA, "tmpA")
                for c in range(8):
                    B.op("act", lambda e, c=c: e.activation(out=sqb[:, c % 2, :], in_=xT[:, c, sl], func=AF.Square),
                         reads=[("xT", c)], writes=[("sqb", c % 2)])
                    B.op("pe", lambda e, c=c: e.matmul(PS(0), lhsT=cb("ones"), rhs=sqb[:, c % 2, :], start=(c == 0), stop=(c == 7)),
                         reads=[("sqb", c % 2), "cbf"], writes=psk(0), inc=True)
                B.op("act", lambda e: e.activation(out=rs_[:], in_=PS(0), func=AF.Ln, scale=1.0 / 1024, bias=epsb[:, 0:1]),
                     reads=psk(0) + ["epsb"], writes=[rk_])
                B.op("act", lambda e: e.activation(out=rs_[:], in_=rs_[:], func=AF.Exp, scale=-0.5), reads=[rk_], writes=[rk_])
                for c in range(8):
                    if dest_kind == "h":
                        gcol = pvc("norm_g", l, c)
                        B.op("dve", lambda e, c=c, gcol=gcol: e.scalar_tensor_tensor(out=hT[:, c, sl], in0=xT[:, c, sl], scalar=gcol, in1=rs_[:],
                                                                                      op0=ALU.mult, op1=ALU.mult),
                             reads=[("xT", c), rk_, "pvt"], writes=["hT"])
                    else:
                        gcol = pvt[:, L * PV_W + c:L * PV_W + c + 1]
                        B.op("dve", lambda e, c=c, gcol=gcol: e.scalar_tensor_tensor(out=xT[:, c, sl], in0=xT[:, c, sl], scalar=gcol, in1=rs_[:],
                                                                                      op0=ALU.mult, op1=ALU.mult),
                             reads=[("xT", c), rk_, "pvt"], writes=[("xT", c)])

        def mixer_C(l):
            for cc in range(2):
                zb = next_bank()
                inproj(l, [(OFF_C + cc * 128, 128)], zb)
                Z = PS(zb, NSEG)
                zk = psk(zb, NSEG)
                xc = scr[0][:, 0:T]
                B.op("act", lambda e: e.activation(out=xc, in_=Z, func=AF.Identity, scale=pvc("cw", l, 2 * 2 + cc), bias=pvc("cb", l, cc)),
                     reads=zk + ["pvt"], writes=[sk(0)])
                for (j, sh) in ((0, -2), (1, -1), (3, 1)):
                    if sh < 0:
                        o_ap, i_ap = xc[:, -sh:T], Z[:, 0:T + sh]
                    else:
                        o_ap, i_ap = xc[:, 0:T - sh], Z[:, sh:T]
                    B.op("dve", lambda e, o_ap=o_ap, i_ap=i_ap, j=j: e.scalar_tensor_tensor(out=o_ap, in0=i_ap, scalar=pvc("cw", l, j * 2 + cc), in1=o_ap,
                                                                                            op0=ALU.mult, op1=ALU.add),
                         reads=zk + [sk(0), "pvt"], writes=[sk(0)])
                xcb = sbf(1)[:, 0:T]
                B.op("act", lambda e: e.activation(out=xcb, in_=xc, func=AF.Copy), reads=[sk(0)], writes=[sk(1)])
                for d in range(2):
                    rb, ib, sbuf_, hb = scr[2][:, 0:T], scr[3][:, 0:T], scr[4][:, 0:T], scr[5 + d][:, 0:T]
                    for sg in range(NSEG):
                        sl = slice(sg * 512, (sg + 1) * 512)
                        for k in range(2):
                            bank = 4 + 2 * k + sg % 2
                            wcol = (((d * 2 + k) * 2 + cc)) * 128
                            B.op("pe", lambda e, bank=bank, wcol=wcol, sl=sl: e.matmul(PS(bank), lhsT=gw_b[:, wcol:wcol + 128], rhs=xcb[:, sl], start=True, stop=True),
                                 reads=["gw_b", sk(1)], writes=psk(bank))
                            dst = rb if k == 0 else ib
                            B.op("act", lambda e, bank=bank, dst=dst, sl=sl, k=k: e.activation(out=dst[:, sl], in_=PS(bank), func=AF.Sigmoid,
                                                                                               bias=pvc("gb", l, (d * 2 + k) * 2 + cc)),
                                 reads=psk(bank) + ["pvt"], writes=[sk(2 + k)])
                    B.op("act", lambda e: e.activation(out=rb, in_=rb, func=AF.Exp, scale=dvc("cneg", l, d * 2 + cc)), reads=[sk(2), "dvt"], writes=[sk(2)])
                    B.op("act", lambda e: e.activation(out=sbuf_, in_=rb, func=AF.Square), reads=[sk(2)], writes=[sk(4)])
                    B.op("act", lambda e: e.activation(out=sbuf_, in_=sbuf_, func=AF.Ln, scale=-1.0, bias=epsb[:, 1:2]), reads=[sk(4), "epsb"], writes=[sk(4)])
                    B.op("act", lambda e: e.activation(out=sbuf_, in_=sbuf_, func=AF.Exp, scale=0.5), reads=[sk(4)], writes=[sk(4)])
                    B.op("dve", lambda e: e.tensor_tensor(out=ib, in0=ib, in1=xc, op=ALU.mult), reads=[sk(3), sk(0)], writes=[sk(3)])
                    B.op("dve", lambda e: e.tensor_tensor(out=ib, in0=ib, in1=sbuf_, op=ALU.mult), reads=[sk(3), sk(4)], writes=[sk(3)])
                    if d == 0:
                        B.op("dve", lambda e: e.tensor_tensor_scan(out=hb, data0=rb, data1=ib, initial=0.0, op0=ALU.mult, op1=ALU.add),
                             reads=[sk(2), sk(3)], writes=[sk(5 + d)])
                    else:
                        B.op("dve", lambda e: e.tensor_tensor_scan(out=hb[:, ::-1], data0=rb[:, ::-1], data1=ib[:, ::-1], initial=0.0, op0=ALU.mult, op1=ALU.add),
                             reads=[sk(2), sk(3)], writes=[sk(5 + d)])
                hf, hbk = scr[5][:, 0:T], scr[6][:, 0:T]
                B.op("dve", lambda e: e.tensor_tensor(out=hf, in0=hf, in1=hbk, op=ALU.add), reads=[sk(5), sk(6)], writes=[sk(5)])
                gb_ = next_bank()
                inproj(l, [(OFF_C + 256 + cc * 128, 128)], gb_)
                sgt = scr[2][:, 0:T]
                B.op("act", lambda e: e.activation(out=sgt, in_=PS(gb_, NSEG), func=AF.Silu), reads=psk(gb_, NSEG), writes=[sk(2)])
                B.op("dve", lambda e: e.tensor_tensor(out=obuf[:, cc, :], in0=hf, in1=sgt, op=ALU.mult), reads=[sk(5), sk(2)], writes=["obuf"])

        def kT_bufs():
            return [sbf(0)[:, SCRW:SCRW + T], sbf(5)[:, SCRW:SCRW + T]]

        def load_k_all(l, off_k, prep):
            kTs = kT_bufs()
            prep([(off_k, 128)], None, False)
            B.dma(kTs[0][64:128, :], kTs[0][0:64, :], reads=[("kT", 0)], writes=[("kT", 0)], q="sp", semkey="kdup0")
            B.dma(kTs[1][0:64, :], kTs[1][64:128, :], reads=[("kT", 1)], writes=[("kT", 1)], q="sp", semkey="kdup1")
            return kTs

        def load_q(l, off_q, cc, prep):
            qT = sbf(0)[:, 0:T]
            prep([(off_q + cc * 128, 128)], qT, True)
            return qT

        def load_vaug(l, off_v, cc):
            va = sbf(1)
            vav = va[:, 0:NT * 192].rearrange("p (t c) -> p t c", c=192)
            B.op("pool", lambda e: e.memset(vav[:, :, 64:128], 1.0), writes=[sk(1)])

            def post(tb, bank, ntt):
                src = PS(bank)[:, 0:ntt * 64].rearrange("p (t c) -> p t c", c=64)
                B.op("act", lambda e: e.activation(out=vav[:, tb * 8:tb * 8 + ntt, 0:64], in_=src, func=AF.Copy), reads=psk(bank), writes=[sk(1)])
                B.op("dve", lambda e: e.tensor_copy(out=vav[:, tb * 8:tb * 8 + ntt, 128:192], in_=src), reads=psk(bank), writes=[sk(1)])
            inproj_tok(l, [(off_v + cc * 64, 64)], None, post)
            return vav

        def load_gate(l, off_g, cc):
            gb_ = next_bank()
            inproj(l, [(off_g + cc * 128, 128)], gb_)
            sgt = scr[2][:, 0:T]
            B.op("act", lambda e: e.activation(out=sgt, in_=PS(gb_, NSEG), func=AF.Silu), reads=psk(gb_, NSEG), writes=[sk(2)])
            return sgt

        def attn_post(l, cc, qg, bx, by, sgt, sink):
            X, Y = PS(bx), PS(by)
            gsl = slice(qg * 512, (qg + 1) * 512)
            if sink:
                B.op("dve", lambda e: e.tensor_scalar(out=tmpA[0:64, :], in0=Y[0:64, :], scalar1=dvc("esinksw", l, cc)[0:64, :], scalar2=None, op0=ALU.add),
                     reads=psk(by) + ["dvt"], writes=["tmpA"])
                B.op("dve", lambda e: e.tensor_scalar(out=tmpA[64:128, :], in0=X[64:128, :], scalar1=dvc("esinksw", l, cc)[64:128, :], scalar2=None, op0=ALU.add),
                     reads=psk(bx) + ["dvt"], writes=["tmpA"])
                B.op("act", lambda e: e.activation(out=tmpA[:], in_=tmpA[:], func=AF.Ln), reads=["tmpA"], writes=["tmpA"])
            else:
                B.op("act", lambda e: e.activation(out=tmpA[0:64, :], in_=Y[0:64, :], func=AF.Ln), reads=psk(by), writes=["tmpA"])
                B.op("act", lambda e: e.activation(out=tmpA[64:128, :], in_=X[64:128, :], func=AF.Ln), reads=psk(bx), writes=["tmpA"])
            B.op("act", lambda e: e.activation(out=tmpA[:], in_=tmpA[:], func=AF.Exp, scale=-1.0), reads=["tmpA"], writes=["tmpA"])
            B.op("pe", lambda e: e.matmul(PS(0), lhsT=swapf[:], rhs=tmpA[:], start=True, stop=True), reads=["swapf", "tmpA"], writes=psk(0))
            B.op("dve", lambda e: e.tensor_tensor(out=tmpB[:], in0=PS(0), in1=sgt[:, gsl], op=ALU.mult), reads=psk(0) + [sk(2)], writes=["tmpB"])
            B.op("dve", lambda e: e.tensor_tensor(out=obuf[0:64, cc, gsl], in0=X[0:64, :], in1=tmpB[0:64, :], op=ALU.mult),
                 reads=psk(bx) + ["tmpB"], writes=["obuf"])
            B.op("dve", lambda e: e.tensor_tensor(out=obuf[64:128, cc, gsl], in0=Y[64:128, :], in1=tmpB[64:128, :], op=ALU.mult),
                 reads=psk(by) + ["tmpB"], writes=["obuf"])

        def mixer_D(l):
            B.dma(scr[3][:, 0:1536], alibi_d[:, :], writes=[sk(3)], q="pool", semkey="alibi")

            def prep(ranges, dst, isq):
                pb_ = next_bank()
                inproj(l, ranges, pb_)
                if isq:
                    B.op("act", lambda e: e.activation(out=dst, in_=PS(pb_, NSEG), func=AF.Copy), reads=psk(pb_, NSEG), writes=[sk(0)])
                else:
                    kTs_ = kT_bufs()
                    B.op("act", lambda e: e.activation(out=kTs_[0][0:64, :], in_=PS(pb_, NSEG)[0:64, :], func=AF.Copy), reads=psk(pb_, NSEG), writes=[("kT", 0)])
                    B.op("act", lambda e: e.activation(out=kTs_[1][64:128, :], in_=PS(pb_, NSEG)[64:128, :], func=AF.Copy), reads=psk(pb_, NSEG), writes=[("kT", 1)])
            kTs = load_k_all(l, OFF_D + 256, prep)
            for cc in range(2):
                qT = load_q(l, OFF_D, cc, prep)
                kT = kTs[cc]
                kkey = ("kT", cc)
                vav = load_vaug(l, OFF_D + 384, cc)
                sgt = load_gate(l, OFF_D + 512, cc)
                ptv = sbf(4)
                exv = sbf(5)
                atab = scr[3][:, 2 * cc * 384:(2 * cc + 2) * 384].rearrange("p (h c) -> p h c", h=2)

                def qrange(j):
                    return max(j - 1, 0), min(j + 1, NT - 1)

                def qk(j):
                    b0, b1 = qrange(j)
                    ncol = (b1 - b0 + 1) * 128
                    for hh in range(2):
                        ph = slice(hh * 64, hh * 64 + 64)
                        bank = 2 * (j % 2) + hh
                        B.op("pe", lambda e: e.matmul(PS(bank)[:, 0:ncol], lhsT=kT[ph, j * 128:(j + 1) * 128], rhs=qT[ph, b0 * 128:b0 * 128 + ncol], start=True, stop=True),
                             reads=[sk(0), kkey], writes=psk(bank), inc=(hh == 1))

                def ex(j):
                    b0, b1 = qrange(j)
                    ncol = (b1 - b0 + 1) * 128
                    tcol0 = (b0 - (j - 1)) * 128
                    src = PS(2 * (j % 2), 2).rearrange("p (b c) -> p b c", b=2)[:, :, 0:ncol]
                    exs = exv[:, (j % 2) * 768:(j % 2 + 1) * 768].rearrange("p (h c) -> p h c", h=2)[:, :, 0:ncol]
                    pts = ptv[:, (j % 4) * 768:(j % 4 + 1) * 768].rearrange("p (h c) -> p h c", h=2)[:, :, 0:ncol]
                    B.op("act", lambda e: e.activation(out=exs, in_=src, func=AF.Exp, scale=0.125), reads=psk(2 * (j % 2), 2), writes=[("ex", j % 2)])
                    B.op("dve", lambda e: e.tensor_tensor(out=pts, in0=exs, in1=atab[:, :, tcol0:tcol0 + ncol], op=ALU.mult),
                         reads=[("ex", j % 2), sk(3)], writes=[("pt", j % 4)])

                def pv_block(i):
                    qg = i // 4
                    js = [j for j in (i - 1, i, i + 1) if 0 <= j < NT]
                    for hh in range(2):
                        bank = 4 + 2 * (qg % 2) + hh
                        for n, j in enumerate(js):
                            b0, _ = qrange(j)
                            c0 = (j % 4) * 768 + hh * 384 + (i - b0) * 128
                            B.op("pe", lambda e: e.matmul(PS(bank)[:, (i % 4) * 128:(i % 4 + 1) * 128], lhsT=vav[:, j, hh * 64:hh * 64 + 128], rhs=ptv[:, c0:c0 + 128],
                                                         start=(n == 0), stop=(n == len(js) - 1)),
                                 reads=[sk(1), ("pt", j % 4)], writes=psk(bank), inc=(n == len(js) - 1))

                qk(0)
                for j in range(NT):
                    if j + 1 < NT:
                        qk(j + 1)
                    ex(j)
                    blocks = []
                    if j >= 1:
                        blocks.append(j - 1)
                    if j == NT - 1:
                        blocks.append(j)
                    for i in blocks:
                        pv_block(i)
                        if i % 4 == 3:
                            qg = i // 4
                            attn_post(l, cc, qg, 4 + 2 * (qg % 2), 4 + 2 * (qg % 2) + 1, sgt, True)

        def mixer_B(l):
            B.dma(scr[6][:, 0:T], ropeC_d[:, :], writes=[sk(6)], q="pool", semkey="ropeC")
            B.dma(scr[7][:, 0:T], ropeS_d[:, :], writes=[sk(7)], q="pool", semkey="ropeS")
            t1s = [scr[3][:, 0:512], scr[3][:, 1536:2048]]
            t2s = [scr[3][:, 512:1024], scr[5][:, 0:512]]
            qgs = [sbf(3)[:, 2048:2560], sbf(3)[:, 2560:3072]]
            rss = [rstd, tmpA]
            rsk = ["rstd", "tmpA"]

            def prep(ranges, dst, isq):
                inproj(l, ranges, 0)
                gcol = pvc("qn" if isq else "kn", l, 0)
                for sg in range(NSEG):
                    sl = slice(sg * 512, (sg + 1) * 512)
                    Z = PS(sg)
                    u = sg % 2
                    qgb, t1b, t2b, rs_, rk_ = qgs[u], t1s[u], t2s[u], rss[u], rsk[u]
                    b4, b5 = 4 + 2 * u, 5 + 2 * u
                    B.op("act", lambda e: e.activation(out=qgb, in_=Z, func=AF.Copy, scale=gcol), reads=psk(sg) + ["pvt"], writes=[("qgb", u)])
                    B.op("act", lambda e: e.activation(out=sqb[:, u, :], in_=Z, func=AF.Square), reads=psk(sg), writes=[("sqb", u)])
                    B.op("pe", lambda e: e.matmul(PS(b4), lhsT=cb("blk64"), rhs=sqb[:, u, :], start=True, stop=True), reads=[("sqb", u), "cbf"], writes=psk(b4))
                    B.op("pe", lambda e: e.matmul(PS(b5), lhsT=cb("perm"), rhs=qgb, start=True, stop=True), reads=[("qgb", u), "cbf"], writes=psk(b5))
                    B.op("act", lambda e: e.activation(out=rs_[:], in_=PS(b4), func=AF.Ln, scale=1.0 / 64, bias=epsb[:, 0:1]), reads=psk(b4) + ["epsb"], writes=[rk_])
                    B.op("act", lambda e: e.activation(out=rs_[:], in_=rs_[:], func=AF.Exp, scale=-0.5), reads=[rk_], writes=[rk_])
                    B.op("pool", lambda e: e.tensor_tensor(out=t1b, in0=qgb, in1=scr[6][:, sl], op=ALU.mult), reads=[("qgb", u), sk(6)], writes=[("t1b", u)])
                    B.op("dve", lambda e: e.tensor_tensor(out=t2b, in0=PS(b5), in1=scr[7][:, sl], op=ALU.mult), reads=psk(b5) + [sk(7)], writes=[("t2b", u)])
                    B.op("dve", lambda e: e.tensor_tensor(out=t1b, in0=t1b, in1=t2b, op=ALU.add), reads=[("t1b", u), ("t2b", u)], writes=[("t1b", u)])
                    if isq:
                        B.op("dve", lambda e: e.tensor_tensor(out=dst[:, sl], in0=t1b, in1=rs_[:], op=ALU.mult), reads=[("t1b", u), rk_], writes=[sk(0)])
                    else:
                        kTs_ = kT_bufs()
                        B.op("dve", lambda e: e.tensor_tensor(out=kTs_[0][0:64, sl], in0=t1b[0:64, :], in1=rs_[0:64, :], op=ALU.mult), reads=[("t1b", u), rk_], writes=[("kT", 0)])
                        B.op("dve", lambda e: e.tensor_tensor(out=kTs_[1][64:128, sl], in0=t1b[64:128, :], in1=rs_[64:128, :], op=ALU.mult), reads=[("t1b", u), rk_], writes=[("kT", 1)])
            kTs = load_k_all(l, OFF_B + 256, prep)
            for cc in range(2):
                qT = load_q(l, OFF_B, cc, prep)
                kT = kTs[cc]
                kkey = ("kT", cc)
                vav = load_vaug(l, OFF_B + 384, cc)
                sgt = load_gate(l, OFF_B + 512, cc)
                ptv = sbf(4)
                for qg in range(NSEG):
                    gsl = slice(qg * 512, (qg + 1) * 512)
                    ab = 4 + 2 * (qg % 2)

                    def qk(j):
                        for hh in range(2):
                            ph = slice(hh * 64, hh * 64 + 64)
                            bank = 2 * (j % 2) + hh
                            B.op("pe", lambda e: e.matmul(PS(bank), lhsT=kT[ph, j * 128:(j + 1) * 128], rhs=qT[ph, gsl], start=True, stop=True),
                                 reads=[sk(0), kkey], writes=psk(bank), inc=(hh == 1))

                    def ex(j):
                        slot = j % 3
                        B.op("act", lambda e: e.activation(out=ptv[:, slot * 1024:(slot + 1) * 1024], in_=PS(2 * (j % 2), 2), func=AF.Exp, scale=0.125),
                             reads=psk(2 * (j % 2), 2), writes=[("pt", slot)])

                    def pv(j):
                        slot = j % 3
                        for hh in range(2):
                            B.op("pe", lambda e: e.matmul(PS(ab + hh), lhsT=vav[:, j, hh * 64:hh * 64 + 128], rhs=ptv[:, slot * 1024 + hh * 512:slot * 1024 + (hh + 1) * 512],
                                                         start=(j == 0), stop=(j == NT - 1)),
                                 reads=[sk(1), ("pt", slot)], writes=psk(ab + hh), inc=(hh == 1))
                    qk(0)
                    for j in range(NT):
                        if j + 1 < NT:
                            qk(j + 1)
                        ex(j)
                        pv(j)
                    attn_post(l, cc, qg, ab, ab + 1, sgt, False)

        def mixer_A(l):
            SA = min(512, T)
            NS = T // SA
            nch = SA // 64
            arena_b = hT[:].rearrange("p k t -> p (k t)")

            def AF32(off, w):
                return arena_b[:, 2 * off:2 * (off + w)].bitcast(F32)

            def ABF(off, w):
                return arena_b[:, 2 * off:2 * off + w]

            rB = [sbf(0)[:, 0:T], sbf(0)[:, SCRW:SCRW + T]]
            kB = [sbf(1)[:, 0:T], sbf(1)[:, SCRW:SCRW + T]]
            vB = [sbf(2)[:, 0:T], sbf(2)[:, SCRW:SCRW + T]]
            waB = sbf(3)[:, 0:T]
            kkB = [sbf(4)[:, 0:T], sbf(4)[:, SCRW:SCRW + T]]
            yfB = [sbf(5)[:, 0:T], sbf(5)[:, SCRW:SCRW + T]]
            bcB = [sbf(6)[:, 0:T], sbf(6)[:, SCRW:SCRW + T]]
            for ci in range(7):
                pb0 = 4 * (ci % 2)
                xi = 7 if ci % 2 == 0 else 5
                Xt = scr[xi][:, 0:T]
                inproj(l, [(OFF_A + ci * 128, 128)], pb0)
                Z = PS(pb0, NSEG)
                zk = psk(pb0, NSEG)
                B.op("act", lambda e: e.activation(out=Xt, in_=Z, func=AF.Copy, scale=dvc("c0", l, ci)), reads=zk + ["dvt"], writes=[sk(xi)])
                B.op("dve", lambda e: e.scalar_tensor_tensor(out=Xt[:, 1:T], in0=Z[:, 0:T - 1], scalar=pvc("sh0", l, ci), in1=Xt[:, 1:T], op0=ALU.mult, op1=ALU.add),
                     reads=zk + [sk(xi), "pvt"], writes=[sk(xi)])
                if ci < 6:
                    dst = (rB, kB, vB)[ci // 2][ci % 2]
                    dk_ = sk(ci // 2)
                    B.op("dve", lambda e: e.scalar_tensor_tensor(out=dst[:, 0:T - 1], in0=Z[:, 1:T], scalar=pvc("sh1", l, ci), in1=Xt[:, 0:T - 1], op0=ALU.mult, op1=ALU.add),
                         reads=zk + [sk(xi), "pvt"], writes=[dk_])
                    B.op("dve", lambda e: e.tensor_copy(out=dst[:, T - 1:T], in_=Xt[:, T - 1:T]), reads=[sk(xi)], writes=[dk_])
                else:
                    B.op("dve", lambda e: e.scalar_tensor_tensor(out=Xt[:, 0:T - 1], in0=Z[:, 1:T], scalar=pvc("sh1", l, ci), in1=Xt[:, 0:T - 1], op0=ALU.mult, op1=ALU.add),
                         reads=zk + [sk(xi), "pvt"], writes=[sk(xi)])
                    B.op("act", lambda e: e.activation(out=waB[0:64, :], in_=Xt[0:64, :], func=AF.Tanh), reads=[sk(xi)], writes=[sk(3)])
                    B.op("act", lambda e: e.activation(out=waB[64:128, :], in_=Xt[64:128, :], func=AF.Copy), reads=[sk(xi)], writes=[sk(3)])
            for hp in range(2):
                for sg in range(NSEG):
                    sl = slice(sg * 512, (sg + 1) * 512)
                    B.op("dve", lambda e: e.tensor_scalar(out=tmpA[:], in0=kB[hp][:, sl], scalar1=pvc("k_k", l, hp), scalar2=None, op0=ALU.mult),
                         reads=[sk(1), "pvt"], writes=["tmpA"])
                    B.op("act", lambda e: e.activation(out=sqb[:, 0, :], in_=tmpA[:], func=AF.Square), reads=["tmpA"], writes=[("sqb", 0)])
                    B.op("pe", lambda e: e.matmul(PS(4), lhsT=cb("blk64"), rhs=sqb[:, 0, :], start=True, stop=True), reads=[("sqb", 0), "cbf"], writes=psk(4))
                    B.op("act", lambda e: e.activation(out=rstd[:], in_=PS(4), func=AF.Ln, bias=epsb[:, 3:4]), reads=psk(4) + ["epsb"], writes=["rstd"])
                    B.op("act", lambda e: e.activation(out=rstd[:], in_=rstd[:], func=AF.Exp, scale=-0.5), reads=["rstd"], writes=["rstd"])
                    B.op("dve", lambda e: e.tensor_tensor(out=kkB[hp][:, sl], in0=tmpA[:], in1=rstd[:], op=ALU.mult), reads=["tmpA", "rstd"], writes=[sk(4)])
            for cc in range(2):
                gb_ = next_bank()
                inproj(l, [(OFF_A + 896 + cc * 128, 128)], gb_)
                B.op("act", lambda e: e.activation(out=obuf[:, cc, :], in_=PS(gb_, NSEG), func=AF.Silu), reads=psk(gb_, NSEG), writes=["obuf"])
            wepoch[0] += 1
            B.barrier()
            o = 0
            Fb = []
            for i in range(5):
                Fb.append(AF32(o, SA)); o += SA
            H0 = AF32(o, SA); o += SA
            H1 = AF32(o, SA); o += SA
            H2 = ABF(o, SA); o += SA // 2
            BK = ABF(o, 2 * SA); o += SA
            ARb = ABF(o, 2 * SA); o += SA
            HK = ABF(o, 2 * SA); o += SA
            Gm = ABF(o, 4 * SA); o += 2 * SA
            GT = ABF(o, SA); o += SA // 2
            Xs, XTs, Ps_ = [], [], []
            for i in range(2):
                Xs.append(ABF(o, SA)); o += SA // 2
                XTs.append(ABF(o, SA)); o += SA // 2
                Ps_.append(ABF(o, SA)); o += SA // 2
            assert o <= 8192, o
            s7 = sbf(7)
            VT, BhT, KhT = s7[:, 0:SA], s7[:, SA:2 * SA], s7[:, 2 * SA:3 * SA]
            Q1 = scr[7][:, 768:768 + SA]
            Qb = s7[:, 2560:2624]
            UTb = [s7[:, 2624:2688], s7[:, 2688:2752]]
            STf = [[scr[7][:, 1408 + (hp * 2 + i) * 64:1408 + (hp * 2 + i + 1) * 64] for i in range(2)] for hp in range(2)]
            STb = [[s7[:, 3328 + (hp * 2 + i) * 64:3328 + (hp * 2 + i + 1) * 64] for i in range(2)] for hp in range(2)]
            id64 = s7[:, 3600:3664]
            blkf = scr[7][:, 1856:1984]
            F0, F1, F2, F3, F4 = Fb
            B.op("dve", lambda e: e.tensor_tensor(out=id64, in0=cb("mfwd", 64, 64), in1=cb("mfwd", 64, 0), op=ALU.subtract), reads=["cbf"], writes=["id64"])
            B.op("dve", lambda e: e.tensor_copy(out=blkf, in_=cb("blk64")), reads=["cbf"], writes=["blkf"])
            BK4 = BK.rearrange("p (c two j) -> p c two j", two=2, j=64)
            AR4 = ARb.rearrange("p (c two j) -> p c two j", two=2, j=64)
            HK4 = HK.rearrange("p (c two j) -> p c two j", two=2, j=64)
            Gm3 = Gm.rearrange("p (c x) -> p c x", x=256)

            def v3(ap):
                return ap.rearrange("p (c j) -> p c j", j=64)

            def hs(hd):
                return slice(hd * 64, hd * 64 + 64)

            Wf = [wst[i][:].rearrange("p k c -> p (k c)") for i in range(2)]
            fin = [Wf[0][:, 0:SA], Wf[0][:, SA:2 * SA], Wf[1][:, 0:SA], Wf[1][:, SA:2 * SA]]
            fink = ["wst0", "wst0", "wst1", "wst1"]
            sets = []
            sets.append(dict(BK=BK, AR=ARb, HK=HK, kBK="BK", kAR="AR", kHK="HK", gi=0))
            wfl = [wbf[i][:].rearrange("p k c -> p (k c)") for i in range(3)]
            sets.append(dict(BK=wfl[0][:, 0:2 * SA], AR=wfl[1][:, 0:2 * SA], HK=wfl[2][:, 0:2 * SA], kBK="wbf0", kAR="wbf1", kHK="wbf2", gi=1))
            for S_ in sets:
                for nm in ("BK", "AR", "HK"):
                    S_[nm + "4"] = S_[nm].rearrange("p (c two j) -> p c two j", two=2, j=64)
            upwv = upw_f[:].bitcast(BF16)
            AR4s = [ARb.rearrange("p (c two j) -> p c two j", two=2, j=64), wfl[1][:, 0:2 * SA].rearrange("p (c two j) -> p c two j", two=2, j=64),
                    upwv[:, 0:2 * SA].rearrange("p (c two j) -> p c two j", two=2, j=64)]
            kARs = ["AR", "wbf1", "upw_f"]
            Gms = [Gm, sbf(3)[:, 2 * SCRW:2 * SCRW + 4 * SA] if False else sbf(3)[:, SCRW:SCRW + 4 * SA]]
            kGms = ["Gm", "scr3hi"]
            tAv, tBv = tmpA[:].bitcast(BF16), tmpB[:].bitcast(BF16)
            VTs, BhTs, KhTs = [VT, tAv[:, 0:SA]], [BhT, tAv[:, SA:2 * SA]], [KhT, tBv[:, 0:SA]]
            kVTs, kBhTs, kKhTs = ["VT", "tmpA_lo"], ["BhT", "tmpA_hi"], ["KhT", "tmpB"]
            Q1s, kQ1s = [Q1, rstd[:, 0:SA]], ["Q1", "rstd"]
            Tms, kTms = [sqb[:, 0, 0:SA], sqb[:, 1, 0:SA]], [("sqb", 0), ("sqb", 1)]
            iters = []
            for d in range(2):
                for sg in (range(NS) if d == 0 else range(NS - 1, -1, -1)):
                    for hp in range(2):
                        iters.append((d, sg, hp))
            cur = [0, 0]

            def pre_gen(ii):
                d, sg, hp = iters[ii]
                S_ = sets[ii % 2]
                BK4, HK4 = S_["BK4"], S_["HK4"]
                AR4 = AR4s[ii % 3]
                kAR_ = kARs[ii % 3]
                gi = ii % 3
                endc = 63 if d == 0 else 0
                ssl = slice(sg * SA, (sg + 1) * SA)
                first_of_dir = (sg == (0 if d == 0 else NS - 1))
                wc = d * 256 + hp * 128
                B.op("pe", lambda e: e.matmul(PS(2)[:, 0:SA], lhsT=upw_b[0:64, wc:wc + 128], rhs=waB[0:64, ssl], start=True, stop=True),
                     reads=["upw_b", sk(3)], writes=psk(2))
                B.op("pe", lambda e: e.matmul(PS(3)[:, 0:SA], lhsT=upw_b[64:128, wc:wc + 128], rhs=waB[64:128, ssl], start=True, stop=True),
                     reads=["upw_b", sk(3)], writes=psk(3))
                yield
                B.op("act", lambda e: e.activation(out=F0, in_=PS(2)[:, 0:SA], func=AF.Tanh, scale=0.5, bias=dvc("w0h", l, d * 2 + hp)), reads=psk(2) + ["dvt"], writes=["F0"])
                B.op("act", lambda e: e.activation(out=F1, in_=PS(3)[:, 0:SA], func=AF.Tanh, scale=0.5, bias=dvc("a0h", l, d * 2 + hp)), reads=psk(3) + ["dvt"], writes=["F1"])
                yield
                B.op("pool", lambda e: e.tensor_scalar(out=F0, in0=F0, scalar1=0.5, scalar2=0.5, op0=ALU.mult, op1=ALU.add), reads=["F0"], writes=["F0"])
                B.op("pool", lambda e: e.tensor_scalar(out=F1, in0=F1, scalar1=0.5, scalar2=0.5, op0=ALU.mult, op1=ALU.add), reads=["F1"], writes=["F1"])
                yield
                rst = cb("rst", SA)
                if d == 0:
                    B.op("dve", lambda e: e.tensor_tensor_scan(out=F2, data0=rst, data1=F0, initial=0.0, op0=ALU.mult, op1=ALU.add), reads=["cbf", "F0"], writes=["F2"])
                else:
                    B.op("dve", lambda e: e.tensor_tensor_scan(out=F2[:, ::-1], data0=rst, data1=F0[:, ::-1], initial=0.0, op0=ALU.mult, op1=ALU.add),
                         reads=["cbf", "F0"], writes=["F2"])
                yield
                B.op("dve", lambda e: e.tensor_tensor(out=F3, in0=F2, in1=F0, op=ALU.subtract), reads=["F2", "F0"], writes=["F3"])
                yield
                endb = v3(F2)[:, :, endc:endc + 1].broadcast_to([128, nch, 64])
                B.op("dve", lambda e: e.tensor_tensor(out=v3(F0), in0=endb, in1=v3(F2), op=ALU.subtract), reads=["F2"], writes=["F0"])
                yield
                B.op("act", lambda e: e.activation(out=F4, in_=F2, func=AF.Exp, scale=LAMW), reads=["F2"], writes=["F4"])
                B.op("act", lambda e: e.activation(out=F2, in_=F2, func=AF.Exp, scale=-LAMW), reads=["F2"], writes=["F2"])
                yield
                B.op("act", lambda e: e.activation(out=F3, in_=F3, func=AF.Exp, scale=-LAMW), reads=["F3"], writes=["F3"])
                B.op("act", lambda e: e.activation(out=F0, in_=F0, func=AF.Exp, scale=-LAMW), reads=["F0"], writes=["F0"])
                yield
                gv = gam[:, gi, 0:nch]
                B.op("dve", lambda e: e.tensor_copy(out=gv, in_=v3(F2)[:, :, endc]), reads=["F2"], writes=[("gam", gi)])
                kk_, r_, k_, v_ = kkB[hp][:, ssl], rB[hp][:, ssl], kB[hp][:, ssl], vB[hp][:, ssl]
                B.op("pool", lambda e: e.tensor_tensor(out=H0, in0=kk_, in1=F1, op=ALU.mult), reads=[sk(4), "F1"], writes=["H0"])
                yield
                B.op("pool", lambda e: e.tensor_scalar(out=F1, in0=F1, scalar1=pvc("k_a", l, hp), scalar2=dvc("omka", l, hp), op0=ALU.mult, op1=ALU.add),
                     reads=["F1", "pvt", "dvt"], writes=["F1"])
                yield
                B.op("pool", lambda e: e.tensor_tensor(out=H1, in0=F1, in1=k_, op=ALU.mult), reads=["F1", sk(1)], writes=["H1"])
                yield
                B.op("pool", lambda e: e.tensor_tensor(out=BK4[:, :, 0, :], in0=v3(H0), in1=v3(F4), op=ALU.mult), reads=["H0", "F4"], writes=[S_["kBK"]])
                yield
                B.op("pool", lambda e: e.tensor_tensor(out=BK4[:, :, 1, :], in0=v3(H1), in1=v3(F4), op=ALU.mult), reads=["H1", "F4"], writes=[S_["kBK"]])
                yield
                B.op("dve", lambda e: e.scalar_tensor_tensor(out=AR4[:, :, 0, :], in0=v3(kk_), scalar=-1.0, in1=v3(F3), op0=ALU.mult, op1=ALU.mult),
                     reads=[sk(4), "F3"], writes=[kAR_])
                yield
                B.op("pool", lambda e: e.tensor_tensor(out=AR4[:, :, 1, :], in0=v3(r_), in1=v3(F2), op=ALU.mult), reads=[sk(0), "F2"], writes=[kAR_])
                yield
                B.op("pool", lambda e: e.tensor_tensor(out=HK4[:, :, 0, :], in0=v3(H0), in1=v3(F0), op=ALU.mult), reads=["H0", "F0"], writes=[S_["kHK"]])
                yield
                B.op("pool", lambda e: e.tensor_tensor(out=HK4[:, :, 1, :], in0=v3(H1), in1=v3(F0), op=ALU.mult), reads=["H1", "F0"], writes=[S_["kHK"]])
                yield
                B.op("dve", lambda e: e.scalar_tensor_tensor(out=H2, in0=r_, scalar=pvc("r_k", l, hp), in1=H1, op0=ALU.mult, op1=ALU.mult),
                     reads=[sk(0), "pvt", "H1"], writes=["H2"])
                B.op("pe", lambda e: e.matmul(PS(2)[:, 0:SA], lhsT=cb("blk64"), rhs=H2, start=True, stop=True), reads=["cbf", "H2"], writes=psk(2))
                yield
                if d == 0:
                    B.op("act", lambda e: e.activation(out=bcB[hp][:, ssl], in_=PS(2)[:, 0:SA], func=AF.Copy), reads=psk(2), writes=[("bc", hp)])
                else:
                    B.op("dve", lambda e: e.tensor_tensor(out=bcB[hp][:, ssl], in0=PS(2)[:, 0:SA], in1=bcB[hp][:, ssl], op=ALU.add), reads=psk(2) + [("bc", hp)], writes=[("bc", hp)])
                yield

            def inv_gen(ii):
                d, sg, hp = iters[ii]
                S_ = sets[ii % 2]
                BK4, HK4 = S_["BK4"], S_["HK4"]
                kBK, kHK = S_["kBK"], S_["kHK"]
                AR4, kAR = AR4s[ii % 3], kARs[ii % 3]
                gi = ii % 3
                Gm, kGm = Gms[ii % 2], kGms[ii % 2]
                Gm3 = Gm.rearrange("p (c x) -> p c x", x=256)
                VT, BhT, KhT = VTs[ii % 2], BhTs[ii % 2], KhTs[ii % 2]
                kVT, kBhT, kKhT = kVTs[ii % 2], kBhTs[ii % 2], kKhTs[ii % 2]
                Q1, kQ1 = Q1s[ii % 2], kQ1s[ii % 2]
                Tm, tkey = Tms[ii % 2], kTms[ii % 2]
                mname = "mfwd" if d == 0 else "mbwd"
                mTname = "mfwdT" if d == 0 else "mbwdT"
                ssl = slice(sg * SA, (sg + 1) * SA)
                v_ = vB[hp][:, ssl]
                n_items = 2 * nch
                Gps = PS(4, 4)
                n_items = 2 * nch
                it = 0
                for c in range(nch):
                    for hd in range(2):
                        it += 1
                        last = (it == n_items)
                        B.op("pe", lambda e: e.matmul(Gps[hs(hd), c * 256:c * 256 + 128], lhsT=BK4[hs(hd), c, 0, :], rhs=AR4[hs(hd), c, :, :], start=True, stop=True),
                             reads=[kBK, kAR], writes=psk(4, 4), inc=False)
                        B.op("pe", lambda e: e.matmul(Gps[hs(hd), c * 256 + 128:c * 256 + 256], lhsT=BK4[hs(hd), c, 1, :], rhs=AR4[hs(hd), c, :, :], start=True, stop=True),
                             reads=[kBK, kAR], writes=psk(4, 4), inc=last)
                mk = cb(mname).rearrange("p (o x) -> p o x", o=1).broadcast_to([128, nch, 256])
                B.op("dve", lambda e: e.tensor_tensor(out=Gm3, in0=Gps[:, 0:nch * 256].rearrange("p (c x) -> p c x", x=256), in1=mk, op=ALU.mult),
                     reads=psk(4, 4) + ["cbf"], writes=[kGm])
                it = 0
                for c in range(nch):
                    for hd in range(2):
                        it += 1
                        B.op("pe", lambda e: e.matmul(PS(4)[hs(hd), c * 64:(c + 1) * 64], lhsT=AR4[hs(hd), c, 0, :], rhs=BK4[hs(hd), c, 0, :], start=True, stop=True),
                             reads=[kBK, kAR], writes=psk(4), inc=(it == n_items))
                mkT = cb(mTname).rearrange("p (o x) -> p o x", o=1).broadcast_to([128, nch, 64])
                B.op("dve", lambda e: e.tensor_tensor(out=v3(GT), in0=v3(PS(4)[:, 0:SA]), in1=mkT, op=ALU.mult), reads=psk(4) + ["cbf"], writes=["GT"])
                yield
                idb = id64.rearrange("p (o x) -> p o x", o=1).broadcast_to([128, nch, 64])
                B.op("dve", lambda e: e.tensor_tensor(out=v3(Ps_[0]), in0=Gm3[:, :, 0:64], in1=idb, op=ALU.add), reads=[kGm, "id64"], writes=[("P", 0)])

                def Xap(level, buf, hd, c):
                    if level == 0:
                        return Gm[hs(hd), c * 256:c * 256 + 64]
                    return Xs[buf][hs(hd), c * 64:(c + 1) * 64]

                def XTap(level, buf, hd, c):
                    if level == 0:
                        return GT[hs(hd), c * 64:(c + 1) * 64]
                    return XTs[buf][hs(hd), c * 64:(c + 1) * 64]
                for r in range(1, 7):
                    src_b, dst_b = (r - 1) % 2, r % 2
                    xk_src = [kGm] if r == 1 else [("X", src_b)]
                    xtk_src = ["GT"] if r == 1 else [("XT", src_b)]
                    do_x, do_xt, do_p = (r <= 4), (r <= 5), (r >= 2)
                    psrc, pdst = (r - 2) % 2, (r - 1) % 2
                    it = 0
                    for c in range(nch):
                        for hd in range(2):
                            it += 1
                            last = (it == n_items)
                            if do_x:
                                B.op("pe", lambda e: e.matmul(PS(5)[hs(hd), c * 64:(c + 1) * 64], lhsT=XTap(r - 1, src_b, hd, c), rhs=Xap(r - 1, src_b, hd, c), start=True, stop=True),
                                     reads=xk_src + xtk_src, writes=psk(5), inc=False)
                            if do_xt:
                                B.op("pe", lambda e: e.matmul(PS(6)[hs(hd), c * 64:(c + 1) * 64], lhsT=Xap(r - 1, src_b, hd, c), rhs=XTap(r - 1, src_b, hd, c), start=True, stop=True),
                                     reads=xk_src + xtk_src, writes=psk(6), inc=(last and not do_p))
                            if do_p:
                                B.op("pe", lambda e: e.matmul(PS(7)[hs(hd), c * 64:(c + 1) * 64], lhsT=XTap(r - 1, src_b, hd, c), rhs=Ps_[psrc][hs(hd), c * 64:(c + 1) * 64], start=True, stop=True),
                                     reads=xtk_src + [("P", psrc)], writes=psk(7), inc=last)
                            if it % 4 == 0 and not last:
                                yield
                    if do_x:
                        B.op("act", lambda e: e.activation(out=Xs[dst_b], in_=PS(5)[:, 0:SA], func=AF.Copy), reads=psk(5), writes=[("X", dst_b)])
                    if do_xt:
                        B.op("act", lambda e: e.activation(out=XTs[dst_b], in_=PS(6)[:, 0:SA], func=AF.Copy),
                             reads=psk(6), writes=[("XT", dst_b)])
                    if do_p:
                        B.op("dve", lambda e: e.tensor_tensor(out=(Tm if r == 6 else Ps_[pdst]), in0=PS(7)[:, 0:SA], in1=Ps_[psrc], op=ALU.add), reads=psk(7) + [("P", psrc)], writes=[(tkey if r == 6 else ("P", pdst))])
                    yield
                for (srcf, dstb, dkey, rk_, bank) in ((lambda c, hd: v_[hs(hd), c * 64:(c + 1) * 64], VT, kVT, [sk(2)], 4),
                                                       (lambda c, hd: HK4[hs(hd), c, 0, :], BhT, kBhT, [kHK], 5),
                                                       (lambda c, hd: HK4[hs(hd), c, 1, :], KhT, kKhT, [kHK], 6)):
                    pb = PS(bank).bitcast(BF16)
                    it = 0
                    for c in range(nch):
                        for hd in range(2):
                            it += 1
                            B.op("pe", lambda e: e.transpose(pb[hs(hd), c * 64:(c + 1) * 64], srcf(c, hd), cb("ident")[hs(hd), hd * 64:hd * 64 + 64]),
                                 reads=rk_ + ["cbf"], writes=psk(bank), inc=(it == n_items))
                    B.op("act", lambda e: e.activation(out=dstb, in_=pb[:, 0:SA], func=AF.Copy), reads=psk(bank), writes=[dkey])
                    yield
                it = 0
                for c in range(nch):
                    for hd in range(2):
                        it += 1
                        B.op("pe", lambda e: e.matmul(PS(7)[hs(hd), c * 64:(c + 1) * 64], lhsT=Gm[hs(hd), c * 256 + 128:c * 256 + 192], rhs=VT[hs(hd), c * 64:(c + 1) * 64], start=True, stop=True),
                             reads=[kGm, kVT], writes=psk(7), inc=(it == n_items))
                B.op("dve", lambda e: e.tensor_copy(out=Q1, in_=PS(7)[:, 0:SA]), reads=psk(7), writes=[kQ1])
                yield

            def chain_gen(ii):
                d, sg, hp = iters[ii]
                S_ = sets[ii % 2]
                BK4, HK4 = S_["BK4"], S_["HK4"]
                kBK, kHK = S_["kBK"], S_["kHK"]
                AR4, kAR = AR4s[ii % 3], kARs[ii % 3]
                gi = ii % 3
                Gm, kGm = Gms[ii % 2], kGms[ii % 2]
                Gm3 = Gm.rearrange("p (c x) -> p c x", x=256)
                VT, BhT, KhT = VTs[ii % 2], BhTs[ii % 2], KhTs[ii % 2]
                kVT, kBhT, kKhT = kVTs[ii % 2], kBhTs[ii % 2], kKhTs[ii % 2]
                Q1, kQ1 = Q1s[ii % 2], kQ1s[ii % 2]
                Tm, tkey = Tms[ii % 2], kTms[ii % 2]
                mname = "mfwd" if d == 0 else "mbwd"
                mTname = "mfwdT" if d == 0 else "mbwdT"
                ssl = slice(sg * SA, (sg + 1) * SA)
                v_ = vB[hp][:, ssl]
                n_items = 2 * nch
                if sg == (0 if d == 0 else NS - 1):
                    cur[hp] = 0
                    B.op("dve", lambda e: e.memset(STf[hp][0], 0.0), writes=[("STf", hp, 0)])
                    B.op("dve", lambda e: e.memset(STb[hp][0], 0.0), writes=[("STb", hp, 0)])
                corder = range(nch) if d == 0 else range(nch - 1, -1, -1)
                for ci_, c in enumerate(corder):
                    cu = cur[hp]
                    nx = 1 - cu
                    cs_ = slice(c * 64, (c + 1) * 64)
                    ub = ci_ % 2
                    for hd in range(2):
                        B.op("pe", lambda e: e.matmul(PS(0)[hs(hd), 0:64], lhsT=AR4[hs(hd), c, 0, :], rhs=STb[hp][cu][hs(hd), :], start=True, stop=True),
                             reads=[kAR, ("STb", hp, cu)], writes=psk(0), inc=(hd == 1))
                    yield
                    B.op("dve", lambda e: e.tensor_tensor(out=Qb, in0=PS(0)[:, 0:64], in1=Q1[:, cs_], op=ALU.add), reads=psk(0) + [kQ1], writes=["Qb"])
                    for hd in range(2):
                        B.op("pe", lambda e: e.matmul(PS(0)[hs(hd), 64:128], lhsT=Tm[hs(hd), cs_], rhs=Qb[hs(hd), :], start=True, stop=True),
                             reads=[tkey, "Qb"], writes=psk(0), inc=(hd == 1))
                    yield
                    B.op("dve", lambda e: e.tensor_copy(out=UTb[ub], in_=PS(0)[:, 64:128]), reads=psk(0), writes=[("UTb", ub)])
                    for hd in range(2):
                        B.op("pe", lambda e: e.matmul(PS(0)[hs(hd), 128:192], lhsT=BhT[hs(hd), cs_], rhs=UTb[ub][hs(hd), :], start=True, stop=False),
                             reads=[kBhT, ("UTb", ub)], writes=psk(0), inc=False)
                        B.op("pe", lambda e: e.matmul(PS(0)[hs(hd), 128:192], lhsT=KhT[hs(hd), cs_], rhs=VT[hs(hd), cs_], start=False, stop=True),
                             reads=[kKhT, kVT], writes=psk(0), inc=(hd == 1))
                    yield
                    gcol = gam[:, gi, c:c + 1]
                    B.op("dve", lambda e: e.scalar_tensor_tensor(out=STb[hp][nx], in0=STf[hp][cu], scalar=gcol, in1=PS(0)[:, 128:192], op0=ALU.mult, op1=ALU.add),
                         reads=[("STf", hp, cu), ("gam", gi)] + psk(0), writes=[("STb", hp, nx)])
                    B.op("dve", lambda e: e.scalar_tensor_tensor(out=STf[hp][nx], in0=STf[hp][cu], scalar=gcol, in1=PS(0)[:, 128:192], op0=ALU.mult, op1=ALU.add),
                         reads=[("STf", hp, cu), ("gam", gi)] + psk(0), writes=[("STf", hp, nx)])
                    for hd in range(2):
                        B.op("pe", lambda e: e.matmul(PS(1)[hs(hd), cs_], lhsT=STb[hp][cu][hs(hd), :], rhs=AR4[hs(hd), c, 1, :], start=True, stop=False),
                             reads=[("STb", hp, cu), kAR], writes=psk(1), inc=False)
                        B.op("pe", lambda e: e.matmul(PS(1)[hs(hd), cs_], lhsT=UTb[ub][hs(hd), :], rhs=Gm[hs(hd), c * 256 + 64:c * 256 + 128], start=False, stop=False),
                             reads=[("UTb", ub), kGm], writes=psk(1), inc=False)
                        B.op("pe", lambda e: e.matmul(PS(1)[hs(hd), cs_], lhsT=VT[hs(hd), cs_], rhs=Gm[hs(hd), c * 256 + 192:c * 256 + 256], start=False, stop=True),
                             reads=[kVT, kGm], writes=psk(1), inc=(hd == 1))
                    cur[hp] = nx
                    yield
                if d == 0:
                    B.op("act", lambda e: e.activation(out=yfB[hp][:, ssl], in_=PS(1)[:, 0:SA], func=AF.Copy), reads=psk(1), writes=[("yf", hp)])
                else:
                    B.op("dve", lambda e: e.tensor_tensor(out=fin[0], in0=PS(1)[:, 0:SA], in1=yfB[hp][:, ssl], op=ALU.add), reads=psk(1) + [("yf", hp)], writes=[fink[0]])
                    B.op("pe", lambda e: e.matmul(PS(0)[:, 0:SA], lhsT=blkf, rhs=fin[0], start=True, stop=True), reads=["blkf", fink[0]], writes=psk(0))
                    B.op("dve", lambda e: e.scalar_tensor_tensor(out=fin[1], in0=PS(0)[:, 0:SA], scalar=-1.0 / 64, in1=fin[0], op0=ALU.mult, op1=ALU.add), reads=psk(0) + [fink[0]], writes=[fink[1]])
                    B.op("act", lambda e: e.activation(out=fin[2], in_=fin[1], func=AF.Square), reads=[fink[1]], writes=[fink[2]])
                    B.op("pe", lambda e: e.matmul(PS(0)[:, 0:SA], lhsT=blkf, rhs=fin[2], start=True, stop=True), reads=["blkf", fink[2]], writes=psk(0))
                    B.op("act", lambda e: e.activation(out=fin[3], in_=PS(0)[:, 0:SA], func=AF.Ln, scale=1.0 / 64, bias=epsb[:, 2:3]), reads=psk(0) + ["epsb"], writes=[fink[3]])
                    B.op("act", lambda e: e.activation(out=fin[3], in_=fin[3], func=AF.Exp, scale=-0.5), reads=[fink[3]], writes=[fink[3]])
                    B.op("dve", lambda e: e.tensor_tensor(out=fin[1], in0=fin[1], in1=fin[3], op=ALU.mult), reads=[fink[1], fink[3]], writes=[fink[1]])
                    B.op("dve", lambda e: e.tensor_scalar(out=fin[1], in0=fin[1], scalar1=pvc("ln_g", l, hp), scalar2=pvc("ln_b", l, hp), op0=ALU.mult, op1=ALU.add), reads=[fink[1], "pvt"], writes=[fink[1]])
                    B.op("dve", lambda e: e.tensor_tensor(out=fin[2], in0=bcB[hp][:, ssl], in1=v_, op=ALU.mult), reads=[("bc", hp), sk(2)], writes=[fink[2]])
                    B.op("dve", lambda e: e.tensor_tensor(out=fin[1], in0=fin[1], in1=fin[2], op=ALU.add), reads=[fink[1], fink[2]], writes=[fink[1]])
                    B.op("dve", lambda e: e.tensor_tensor(out=obuf[:, hp, ssl], in0=fin[1], in1=obuf[:, hp, ssl], op=ALU.mult), reads=[fink[1], "obuf"], writes=["obuf"])


                yield

            def run_all():
                n = len(iters)
                for _ in pre_gen(0):
                    pass
                if n > 1:
                    gens0 = [pre_gen(1), inv_gen(0)]
                else:
                    gens0 = [inv_gen(0)]
                _rr(gens0)
                for s_ in range(n):
                    gens = [chain_gen(s_)]
                    if s_ + 1 < n:
                        gens.append(inv_gen(s_ + 1))
                    if s_ + 2 < n:
                        gens.append(pre_gen(s_ + 2))
                    _rr(gens)

            def _rr(gens):
                import os
                gens = list(gens)
                mode = os.environ.get("A_MODE", "cip")
                if os.environ.get("A_SEQ"):
                    mode = ""
                names = {"chain_gen": "c", "inv_gen": "i", "pre_gen": "p"}
                conc = [g for g in gens if names[g.gi_code.co_name] in mode]
                seq = [g for g in gens if names[g.gi_code.co_name] not in mode]
                while conc:
                    for g in list(conc):
                        try:
                            next(g)
                        except StopIteration:
                            conc.remove(g)
                for g in seq:
                    for _ in g:
                        pass
            run_all()
            B.barrier()
            wepoch[0] += 1

        def program():
            B.dma(scr[0][:, 0:CST_W], cst_d[:, :], writes=[sk(0)], q="pool", semkey="cst")
            B.dma(pvt[:], pv_d[:, :], writes=["pvt"], q="pool", semkey="pvt")
            B.op("dve", lambda e: e.tensor_copy(out=cbf[:], in_=scr[0][:, 0:CST_W]), reads=[sk(0)], writes=["cbf"])
            B.op("dve", lambda e: e.tensor_copy(out=swapf[:], in_=scr[0][:, CST_OFF["swap"]:CST_OFF["swap"] + 128]), reads=[sk(0)], writes=["swapf"])
            B.op("dve", lambda e: e.memset(epsb[:, 0:1], NORM_EPS), writes=["epsb"])
            B.op("dve", lambda e: e.memset(epsb[:, 1:2], 1.0), writes=["epsb"])
            B.op("dve", lambda e: e.memset(epsb[:, 2:3], GN_EPS), writes=["epsb"])
            B.op("dve", lambda e: e.memset(epsb[:, 3:4], 1e-18), writes=["epsb"])
            for l in range(L):
                for i in range(7):
                    B.op("dve", lambda e, l=l, i=i: e.tensor_tensor(out=dvc("c0", l, i), in0=pvc("sh0", l, i), in1=pvc("sh1", l, i), op=ALU.add),
                         reads=["pvt"], writes=["dvt"])
                    B.op("dve", lambda e, l=l, i=i: e.tensor_scalar(out=dvc("c0", l, i), in0=dvc("c0", l, i), scalar1=-1.0, scalar2=1.0, op0=ALU.mult, op1=ALU.add),
                         reads=["dvt"], writes=["dvt"])
                for i in range(2):
                    B.op("dve", lambda e, l=l, i=i: e.tensor_scalar(out=dvc("omka", l, i), in0=pvc("k_a", l, i), scalar1=-1.0, scalar2=1.0, op0=ALU.mult, op1=ALU.add),
                         reads=["pvt"], writes=["dvt"])
                    B.op("act", lambda e, l=l, i=i: e.activation(out=dvc("esinksw", l, i), in_=pvc("sinksw", l, i), func=AF.Exp),
                         reads=["pvt"], writes=["dvt"])
                for i in range(4):
                    B.op("dve", lambda e, l=l, i=i: e.tensor_scalar(out=dvc("w0h", l, i), in0=pvc("w0", l, i), scalar1=0.5, scalar2=None, op0=ALU.mult),
                         reads=["pvt"], writes=["dvt"])
                    B.op("dve", lambda e, l=l, i=i: e.tensor_scalar(out=dvc("a0h", l, i), in0=pvc("a0", l, i), scalar1=0.5, scalar2=None, op0=ALU.mult),
                         reads=["pvt"], writes=["dvt"])
                    B.op("act", lambda e, l=l, i=i: e.activation(out=dvc("cneg", l, i), in_=pvc("lam", l, i), func=AF.Exp, scale=-1.0),
                         reads=["pvt"], writes=["dvt"])
                    B.op("act", lambda e, l=l, i=i: e.activation(out=dvc("cneg", l, i), in_=dvc("cneg", l, i), func=AF.Ln, bias=epsb[:, 1:2]),
                         reads=["dvt", "epsb"], writes=["dvt"])
                    B.op("dve", lambda e, l=l, i=i: e.tensor_scalar(out=dvc("cneg", l, i), in0=dvc("cneg", l, i), scalar1=-8.0, scalar2=None, op0=ALU.mult),
                         reads=["dvt"], writes=["dvt"])


            for s in range(NSEQ):
                for c in range(8):
                    B.dma(xT[:, c, :], x_d[s * 1024 + c * 128:s * 1024 + (c + 1) * 128, :], writes=[("xT", c)], q="pool", semkey=f"xin{c}")
                for l in range(L):
                    rmsnorm_to(l, "h")
                    B.dma(scr[7][:, 0:1024], gw_d[:, l * 1024:(l + 1) * 1024], writes=[sk(7)], q="pool", semkey="gw")
                    B.op("pool", lambda e: e.tensor_copy(out=gw_b[:], in_=scr[7][:, 0:1024]), reads=[sk(7)], writes=["gw_b"])
                    B.dma(upw_f[:], upw_d[:, l * 512:(l + 1) * 512], writes=["upw_f"], q="pool", semkey="upw")
                    B.op("pool", lambda e: e.tensor_copy(out=upw_b[:], in_=upw_f[:]), reads=["upw_f"], writes=["upw_b"])
                    if "C" in cfg.MIX:
                        mixer_C(l)
                        B.barrier()
                        outproj(l, 2)
                    if "D" in cfg.MIX:
                        mixer_D(l)
                        B.barrier()
                        outproj(l, 3)
                    if "B" in cfg.MIX:
                        mixer_B(l)
                        B.barrier()
                        outproj(l, 1)
                    if "A" in cfg.MIX:
                        mixer_A(l)
                        B.barrier()
                        outproj(l, 0)
                rmsnorm_to(None, "x")
                for c in range(8):
                    B.dma(y_d[s * 1024 + c * 128:s * 1024 + (c + 1) * 128, :], xT[:, c, :], reads=[("xT", c)], q="pool", semkey=f"yout{c}")
            toks = []
            for c in range(8):
                toks.extend(B.readers.get(("xT", c), {}).values())
            for tname in tap_d:
                t = B.last_w.get(("tap", tname))
                if t is not None:
                    toks.append(t)
            B._wait("pool", toks)

        realB = B
        B = NullBuilder()
        wstate["rec"] = True
        program()
        B = realB
        wstate["rec"] = False
        wepoch[0] = 0
        wo_slot[0] = 0
        program()
        print(f"[build] instructions: {B.n_ins}")
    return nc


_NC_CACHE = {}


def run_trunk(cfg, xs, inp):
    T, L = cfg.T, cfg.DEPTH
    ncore = cfg.NCORES
    assert xs.shape[0] == ncore * cfg.NSEQ
    key = (cfg.T, cfg.NSEQ, cfg.DEPTH, cfg.MIX, cfg.taps)
    if key not in _NC_CACHE:
        _NC_CACHE[key] = build(cfg)
    nc = _NC_CACHE[key]
    cst, alibi, C, S = make_consts(T)
    pv, upw, gw = pack_params(inp, L)
    w_in = np.ascontiguousarray(np.asarray(inp["w_in"], np.float32)[:L].reshape(L * 1024, D_IN))
    w_out = np.ascontiguousarray(np.asarray(inp["w_out"], np.float32)[:L].reshape(L * 1024, 1024))
    in_maps = []
    for c in range(ncore):
        xc = xs[c * cfg.NSEQ:(c + 1) * cfg.NSEQ]
        xt = np.ascontiguousarray(xc.transpose(0, 2, 1)).reshape(cfg.NSEQ * 1024, T)
        in_maps.append({"x": xt, "w_in": w_in, "w_out": w_out, "pvec": pv, "upw": upw, "gw": gw, "cst": cst,
                        "alibi": alibi, "ropeC": C, "ropeS": S})
    res = run_bass_kernel_spmd(nc, in_maps, core_ids=list(range(ncore)))
    outs = []
    for c in range(ncore):
        yt = np.asarray(res.results[c]["y"]).reshape(cfg.NSEQ, 1024, T)
        outs.append(yt.transpose(0, 2, 1))
    return np.ascontiguousarray(np.concatenate(outs, 0)).astype(np.float32), res


def kernel(**inputs):
    inp = {k: np.asarray(v) for k, v in inputs.items()}
    xp, xs_ = inp["x_prompt"], inp["x_sample"]
    xs = np.concatenate([xp, xs_], 0).astype(np.float32)
    cfg = Cfg(T=xs.shape[1], NSEQ=xs.shape[0] // 8, DEPTH=inp["w_in"].shape[0], MIX="ABCD", NCORES=8)
    y, _ = run_trunk(cfg, xs, inp)
    return (y[:xp.shape[0]], y[xp.shape[0]:])
```
